# Optimizing a Trainium2 kernel written in Bass

```python
import math
import jax, jax.numpy as jnp
from jax import lax
import numpy as np

D_MODEL = 1024
BATCH = 2
SEQ = 8192
DEPTH = 2

CHUNK = 64
N_MEM = 256
SB_HEADS = 8
SB_HEAD_DIM = 64
SB_WIDTH = SB_HEADS * SB_HEAD_DIM
SB_BLOCK = 128
CONV_CH = 256
CONV_WIDTH = 31
SSM_CH = 256
SSM_GROUP = 16
SSM_GROUPS = SSM_CH // SSM_GROUP
SSM_STATE = 64
MIX_WIDTH = SB_WIDTH + CONV_CH + SSM_CH
IN_PROJ = 3 * SB_WIDTH + 2 * CONV_CH + SSM_CH
XA_HEADS = 4
XA_HEAD_DIM = D_MODEL // XA_HEADS
XA_WIDTH = XA_HEADS * XA_HEAD_DIM
FFN_HIDDEN = ((int(math.ceil(8 * D_MODEL / 3)) + 255) // 256) * 256
EPS = 1e-6

kernel_name = "hybrid_sb_conformer_s5_encoder"


def rms_norm(x, g):
    xf = x.astype(jnp.float32)
    y = xf * lax.rsqrt(jnp.mean(xf * xf, axis=-1, keepdims=True) + EPS)
    return (y * g.astype(jnp.float32)).astype(x.dtype)


def layer_norm(x, g, b):
    xf = x.astype(jnp.float32)
    mu = jnp.mean(xf, axis=-1, keepdims=True)
    var = jnp.mean(jnp.square(xf - mu), axis=-1, keepdims=True)
    y = (xf - mu) * lax.rsqrt(var + EPS)
    return (y * g.astype(jnp.float32) + b.astype(jnp.float32)).astype(x.dtype)


def stick_breaking_attention(q, k, v):
    bsz, L, H, hd = q.shape
    nb = L // SB_BLOCK
    kh = k.transpose(0, 2, 1, 3)
    vh = v.transpose(0, 2, 1, 3)
    q_blocks = q.transpose(0, 2, 1, 3).reshape(bsz, H, nb, SB_BLOCK, hd).transpose(2, 0, 1, 3, 4)
    key_pos = jnp.arange(L)
    scale = hd ** -0.5

    def one_block(args):
        qb, blk = args
        z = jnp.einsum('bhqd,bhkd->bhqk', qb, kh).astype(jnp.float32) * scale
        q_pos = blk * SB_BLOCK + jnp.arange(SB_BLOCK)
        mask = key_pos[None, :] < q_pos[:, None]
        log_1mb = jnp.where(mask, jax.nn.log_sigmoid(-z), 0.0)
        later = lax.cumsum(log_1mb, axis=3, reverse=True) - log_1mb
        w = jnp.where(mask, jnp.exp(jax.nn.log_sigmoid(z) + later), 0.0)
        return jnp.einsum('bhqk,bhkd->bhqd', w.astype(vh.dtype), vh)

    out = lax.map(one_block, (q_blocks, jnp.arange(nb)))
    return out.transpose(1, 0, 3, 2, 4).reshape(bsz, L, H * hd)


def conformer_conv(u2, dw_w, dw_b, ln_g, ln_b, pw2_w):
    a, b = jnp.split(u2, 2, axis=-1)
    h = a * jax.nn.sigmoid(b)
    h = lax.conv_general_dilated(
        h, dw_w[:, None, :].astype(h.dtype), window_strides=(1,),
        padding=((CONV_WIDTH - 1, 0),), dimension_numbers=('NWC', 'WIO', 'NWC'),
        feature_group_count=CONV_CH) + dw_b
    h = jax.nn.silu(layer_norm(h, ln_g, ln_b))
    return h @ pw2_w


def s5_ssm(u, lam_re, lam_im, log_dt, b_re, b_im, c_re, c_im, d, glu_w):
    bsz, L, _ = u.shape
    f32 = jnp.float32
    uf = u.astype(f32).reshape(bsz, L, SSM_GROUPS, SSM_GROUP)
    lr, li = lam_re.astype(f32), lam_im.astype(f32)
    dt = jnp.exp(log_dt.astype(f32))[:, None]
    mag = jnp.exp(lr * dt)
    ar, ai = mag * jnp.cos(li * dt), mag * jnp.sin(li * dt)
    den = lr * lr + li * li
    fr = ((ar - 1.0) * lr + ai * li) / den
    fi = (ai * lr - (ar - 1.0) * li) / den
    br, bi = b_re.astype(f32), b_im.astype(f32)
    bbr = fr[..., None] * br - fi[..., None] * bi
    bbi = fr[..., None] * bi + fi[..., None] * br
    bu_r = jnp.einsum('blgh,gph->blgp', uf, bbr)
    bu_i = jnp.einsum('blgh,gph->blgp', uf, bbi)
    shape = bu_r.shape
    a_r = jnp.broadcast_to(ar, shape)
    a_i = jnp.broadcast_to(ai, shape)

    def combine(e1, e2):
        a1r, a1i, b1r, b1i = e1
        a2r, a2i, b2r, b2i = e2
        return (a2r * a1r - a2i * a1i,
                a2r * a1i + a2i * a1r,
                a2r * b1r - a2i * b1i + b2r,
                a2r * b1i + a2i * b1r + b2i)

    _, _, xr, xi = lax.associative_scan(combine, (a_r, a_i, bu_r, bu_i), axis=1)
    y = (jnp.einsum('blgp,ghp->blgh', xr, c_re.astype(f32))
         - jnp.einsum('blgp,ghp->blgh', xi, c_im.astype(f32)))
    y = y.reshape(bsz, L, SSM_CH) + d.astype(f32) * uf.reshape(bsz, L, SSM_CH)
    y = y.astype(u.dtype)
    ya, yb = jnp.split(y @ glu_w, 2, axis=-1)
    return ya * jax.nn.sigmoid(yb)


def memory_cross_attention(h, m, wq, wk, wv, q_g, k_g, wo):
    bsz, L, _ = h.shape
    q = rms_norm((h @ wq).reshape(bsz, L, XA_HEADS, XA_HEAD_DIM), q_g)
    k = rms_norm((m @ wk).reshape(bsz, N_MEM, XA_HEADS, XA_HEAD_DIM), k_g)
    v = (m @ wv).reshape(bsz, N_MEM, XA_HEADS, XA_HEAD_DIM)
    s = jnp.einsum('blhd,bmhd->bhlm', q, k).astype(jnp.float32) * (XA_HEAD_DIM ** -0.5)
    p = jax.nn.softmax(s, axis=-1).astype(v.dtype)
    o = jnp.einsum('bhlm,bmhd->blhd', p, v).reshape(bsz, L, XA_WIDTH)
    return o @ wo


def setup_inputs(seed: int = 0) -> dict:
    key = jax.random.key(seed)
    ks = iter(jax.random.split(key, 40))
    f32 = jnp.float32

    def nrm(shape, scale):
        return jax.random.normal(next(ks), shape, f32) * scale

    def gain(shape):
        return 1.0 + 0.02 * jax.random.normal(next(ks), shape, f32)

    n_idx = jnp.arange(SSM_STATE, dtype=f32)
    return {
        "x": nrm((BATCH, SEQ, D_MODEL), 1.0),
        "mem": nrm((BATCH, N_MEM, D_MODEL), 1.0),
        "norm_mix_g": gain((DEPTH, D_MODEL)),
        "w_in": nrm((DEPTH, D_MODEL, IN_PROJ), D_MODEL ** -0.5),
        "sb_q_norm_g": gain((DEPTH, SB_HEAD_DIM)),
        "sb_k_norm_g": gain((DEPTH, SB_HEAD_DIM)),
        "conv_dw_w": nrm((DEPTH, CONV_WIDTH, CONV_CH), CONV_WIDTH ** -0.5),
        "conv_dw_b": nrm((DEPTH, CONV_CH), 0.02),
        "conv_ln_g": gain((DEPTH, CONV_CH)),
        "conv_ln_b": nrm((DEPTH, CONV_CH), 0.02),
        "conv_pw2_w": nrm((DEPTH, CONV_CH, CONV_CH), CONV_CH ** -0.5),
        "ssm_lam_re": -0.5 * jnp.exp(nrm((DEPTH, SSM_GROUPS, SSM_STATE), 0.05)),
        "ssm_lam_im": jnp.pi * n_idx * jnp.exp(nrm((DEPTH, SSM_GROUPS, SSM_STATE), 0.01)),
        "ssm_log_dt": jax.random.uniform(next(ks), (DEPTH, SSM_GROUPS), f32,
                                         math.log(1e-3), math.log(1e-1)),
        "ssm_b_re": nrm((DEPTH, SSM_GROUPS, SSM_STATE, SSM_GROUP), (2 * SSM_GROUP) ** -0.5),
        "ssm_b_im": nrm((DEPTH, SSM_GROUPS, SSM_STATE, SSM_GROUP), (2 * SSM_GROUP) ** -0.5),
        "ssm_c_re": nrm((DEPTH, SSM_GROUPS, SSM_GROUP, SSM_STATE), (2 * SSM_STATE) ** -0.5),
        "ssm_c_im": nrm((DEPTH, SSM_GROUPS, SSM_GROUP, SSM_STATE), (2 * SSM_STATE) ** -0.5),
        "ssm_d": nrm((DEPTH, SSM_CH), 1.0),
        "ssm_glu_w": nrm((DEPTH, SSM_CH, 2 * SSM_CH), SSM_CH ** -0.5),
        "branch_norm_g": gain((DEPTH, MIX_WIDTH)),
        "w_out": nrm((DEPTH, MIX_WIDTH, D_MODEL), MIX_WIDTH ** -0.5),
        "norm_xa_g": gain((DEPTH, D_MODEL)),
        "norm_mem_g": gain((DEPTH, D_MODEL)),
        "xa_wq": nrm((DEPTH, D_MODEL, XA_WIDTH), D_MODEL ** -0.5),
        "xa_wk": nrm((DEPTH, D_MODEL, XA_WIDTH), D_MODEL ** -0.5),
        "xa_wv": nrm((DEPTH, D_MODEL, XA_WIDTH), D_MODEL ** -0.5),
        "xa_q_norm_g": gain((DEPTH, XA_HEAD_DIM)),
        "xa_k_norm_g": gain((DEPTH, XA_HEAD_DIM)),
        "xa_wo": nrm((DEPTH, XA_WIDTH, D_MODEL), XA_WIDTH ** -0.5),
        "norm_ffn_g": gain((DEPTH, D_MODEL)),
        "ffn_w_in": nrm((DEPTH, D_MODEL, 2 * FFN_HIDDEN), D_MODEL ** -0.5),
        "ffn_w_out": nrm((DEPTH, FFN_HIDDEN, D_MODEL), FFN_HIDDEN ** -0.5),
    }


def reference(x, mem, norm_mix_g, w_in, sb_q_norm_g, sb_k_norm_g, conv_dw_w, conv_dw_b,
              conv_ln_g, conv_ln_b, conv_pw2_w, ssm_lam_re, ssm_lam_im, ssm_log_dt,
              ssm_b_re, ssm_b_im, ssm_c_re, ssm_c_im, ssm_d, ssm_glu_w, branch_norm_g,
              w_out, norm_xa_g, norm_mem_g, xa_wq, xa_wk, xa_wv, xa_q_norm_g, xa_k_norm_g,
              xa_wo, norm_ffn_g, ffn_w_in, ffn_w_out):
    bsz, L, _ = x.shape
    s1 = SB_WIDTH
    s2 = 2 * SB_WIDTH
    s3 = 3 * SB_WIDTH
    s4 = s3 + 2 * CONV_CH
    for l in range(DEPTH):
        h = rms_norm(x, norm_mix_g[l])
        p = h @ w_in[l]
        q = rms_norm(p[..., :s1].reshape(bsz, L, SB_HEADS, SB_HEAD_DIM), sb_q_norm_g[l])
        k = rms_norm(p[..., s1:s2].reshape(bsz, L, SB_HEADS, SB_HEAD_DIM), sb_k_norm_g[l])
        v = p[..., s2:s3].reshape(bsz, L, SB_HEADS, SB_HEAD_DIM)
        o_sb = stick_breaking_attention(q, k, v)
        o_conv = conformer_conv(p[..., s3:s4], conv_dw_w[l], conv_dw_b[l],
                                conv_ln_g[l], conv_ln_b[l], conv_pw2_w[l])
        o_ssm = s5_ssm(p[..., s4:], ssm_lam_re[l], ssm_lam_im[l], ssm_log_dt[l],
                       ssm_b_re[l], ssm_b_im[l], ssm_c_re[l], ssm_c_im[l],
                       ssm_d[l], ssm_glu_w[l])
        g = branch_norm_g[l]
        mixed = jnp.concatenate([
            rms_norm(o_sb, g[:SB_WIDTH]),
            rms_norm(o_conv, g[SB_WIDTH:SB_WIDTH + CONV_CH]),
            rms_norm(o_ssm, g[SB_WIDTH + CONV_CH:]),
        ], axis=-1)
        x = x + mixed @ w_out[l]
        hx = rms_norm(x, norm_xa_g[l])
        hm = rms_norm(mem, norm_mem_g[l])
        x = x + memory_cross_attention(hx, hm, xa_wq[l], xa_wk[l], xa_wv[l],
                                       xa_q_norm_g[l], xa_k_norm_g[l], xa_wo[l])
        hf = rms_norm(x, norm_ffn_g[l])
        gate, up = jnp.split(hf @ ffn_w_in[l], 2, axis=-1)
        x = x + (jax.nn.silu(gate) * up) @ ffn_w_out[l]
    return x
```

```python
import numpy as np
from contextlib import ExitStack
import concourse.bass as bass
import concourse.mybir as mybir
from concourse.bass_utils import run_bass_kernel_spmd

F32 = mybir.dt.float32
BF16 = mybir.dt.bfloat16
AF = mybir.ActivationFunctionType
ALU = mybir.AluOpType
AX = mybir.AxisListType

NCORES = 8
D = 1024
TOK = 2048
NT = TOK // 128
SEQ = 8192
NINP = 2304
FFN_H = 2816
EPS = 1e-6
QKV_SZ = 64 * TOK
HALO_OFF = 3 * QKV_SZ
SB_SZ = HALO_OFF + 256 * 32
NDMA_SEM = 12


class _Op:
    __slots__ = ("eng", "fn", "deps", "dma", "signal", "seq", "dsem", "dval", "qi")

    def __init__(self, eng, fn, dma):
        self.eng = eng
        self.fn = fn
        self.deps = []
        self.dma = dma
        self.signal = False
        self.seq = 0
        self.dsem = None
        self.dval = 0
        self.qi = 0


class Prog:
    ENGS = ("pe", "act", "dve", "pool", "sp")

    def __init__(self, nc):
        self.nc = nc
        self.ops = {e: [] for e in self.ENGS}
        self.last_w = {}
        self.readers = {}
        self.ndma = {e: 0 for e in self.ENGS}
        self.bar = []
        self._pid = {}

    def op(self, eng, fn, reads=(), writes=(), dma=False):
        o = _Op(eng, fn, dma)
        if dma:
            o.qi = self.ndma[eng]
            self.ndma[eng] += 1

        def add(p, kind):
            if p is o:
                return
            if not p.dma and p.eng == eng and not dma:
                if eng == "pe":
                    return
            if p not in o.deps:
                o.deps.append(p)
                p.signal = True

        for p in self.bar:
            add(p, "bar")
        for k in reads:
            p = self.last_w.get(k)
            if p is not None:
                add(p, "raw")
        for k in writes:
            p = self.last_w.get(k)
            if p is not None:
                add(p, "waw")
            for r in self.readers.get(k, ()):
                add(r, "war")
        for k in reads:
            self.readers.setdefault(k, []).append(o)
        for k in writes:
            self.last_w[k] = o
            self.readers[k] = []
        self.ops[eng].append(o)
        return o

    def pid(self, eng):
        k = id(eng)
        if k not in self._pid:
            self._pid[k] = eng.partition_id() % NCORES
        return self._pid[k]

    def barrier(self):
        self.bar = [self.ops[e][-1] for e in self.ENGS if self.ops[e]]
        for e in self.ENGS:
            for o in self.ops[e][-NDMA_SEM:]:
                if o.dma and o not in self.bar:
                    self.bar.append(o)
        self.last_w = {}
        self.readers = {}

    def emit(self, es):
        nc = self.nc
        esem = {e: es.enter_context(nc.semaphore("s_" + e)) for e in self.ENGS}
        dsems = {}
        for e in self.ENGS:
            if self.ndma[e]:
                dsems[e] = [es.enter_context(nc.semaphore("d_%s%d" % (e, i)))
                            for i in range(min(NDMA_SEM, self.ndma[e]))]
        for e in self.ENGS:
            n = 0
            for o in self.ops[e]:
                if o.dma:
                    o.dsem = dsems[e][o.qi % NDMA_SEM]
                    o.dval = 16 * (o.qi // NDMA_SEM + 1)
                elif o.signal:
                    n += 1
                    o.seq = n
        final_waits = []
        for e in self.ENGS:
            if self.ndma[e]:
                for i, s in enumerate(dsems[e]):
                    cnt = len(range(i, self.ndma[e], NDMA_SEM))
                    final_waits.append((s, 16 * cnt))

        def run(e, eng):
            waited = {}

            def wait(sem, val):
                key = id(sem)
                if waited.get(key, 0) >= val:
                    return
                eng.wait_ge(sem, val)
                waited[key] = val

            for o in self.ops[e]:
                for p in o.deps:
                    if p.dma:
                        wait(p.dsem, p.dval)
                    else:
                        wait(esem[p.eng], p.seq)
                if o.dma:
                    if o.qi >= NDMA_SEM:
                        wait(o.dsem, o.dval - 16)
                    o.fn(eng).then_inc(o.dsem, 16)
                else:
                    ins = o.fn(eng)
                    if o.signal:
                        ins.then_inc(esem[e], 1)
            if e == "sp":
                for s, v in final_waits:
                    wait(s, v)

        with nc.Block() as block:
            @block.tensor
            def _(eng):
                run("pe", eng)

            @block.scalar
            def _(eng):
                run("act", eng)

            @block.vector
            def _(eng):
                run("dve", eng)

            @block.gpsimd
            def _(eng):
                run("pool", eng)

            @block.sync
            def _(eng):
                run("sp", eng)


ARENA_F32 = 50688


class Ctx:
    def __init__(self, nc, es):
        self.nc = nc
        self.es = es
        self.P = Prog(nc)
        self.n = 0
        self.banks = [es.enter_context(nc.psum_tensor("bank%d" % i, [128, 512], F32))[:]
                      for i in range(8)]
        self.arena = es.enter_context(nc.sbuf_tensor("arena", [128, ARENA_F32], F32))[:]
        self.off = 0

    def sb(self, shape, dtype, name=None):
        shape = list(shape)
        esz = mybir.dt.size(dtype)
        n = 1
        for d in shape[1:]:
            n *= d
        nf = (n * esz + 31) // 32 * 8
        assert self.off + nf <= ARENA_F32, "SBUF arena overflow (%s)" % name
        v = self.arena[0:shape[0], self.off:self.off + nf]
        self.off += nf
        if dtype != F32:
            v = v.bitcast(dtype)
        v = v[:, 0:n]
        if len(shape) > 2:
            names = " ".join("d%d" % i for i in range(1, len(shape)))
            v = v.rearrange("p (%s) -> p %s" % (names, names),
                            **{"d%d" % i: shape[i] for i in range(1, len(shape))})
        return v

    def mark(self):
        return self.off

    def release(self, m):
        self.off = m

    def din(self, name, shape, dtype):
        return self.nc.dram_tensor(name, list(shape), dtype, kind="ExternalInput").ap()

    def dout(self, name, shape, dtype):
        return self.nc.dram_tensor(name, list(shape), dtype, kind="ExternalOutput").ap()

    def dint(self, name, shape, dtype):
        return self.nc.dram_tensor(name, list(shape), dtype, kind="Internal").ap()


def bank_bf(bank):
    return bank.bitcast(BF16)


def rstd_ops(P, st_in, st_tmp, st_out, inv_n, key_in, key_tmp, key_out):
    kl = lambda k: list(k) if isinstance(k, list) else [k]
    key_in, key_tmp, key_out = kl(key_in), kl(key_tmp), kl(key_out)
    P.op("dve", lambda e: e.tensor_scalar(out=st_tmp, in0=st_in, scalar1=inv_n, scalar2=EPS,
                                          op0=ALU.mult, op1=ALU.add),
         reads=key_in, writes=key_tmp)
    P.op("act", lambda e: e.activation(out=st_tmp, in_=st_tmp, func=AF.Ln),
         reads=key_tmp, writes=key_tmp)
    P.op("act", lambda e: e.activation(out=st_out, in_=st_tmp, func=AF.Exp, scale=-0.5),
         reads=key_tmp, writes=key_out)


def phase_A(C, x, w_in, g_mix, gq, gk, ident, send_bf, send_u, hc_loc):
    P = C.P
    W = C.sb([128, 8, NINP], BF16, "A_W")
    gB = C.sb([128, D], F32, "A_gB")
    gqB = C.sb([128, 8, 64], F32, "A_gqB")
    gkB = C.sb([128, 8, 64], F32, "A_gkB")
    idB = C.sb([128, 128], BF16, "A_idB")
    idF = C.sb([128, 128], F32, "A_idF")
    xt = [C.sb([128, D], F32, "A_xt%d" % i) for i in range(2)]
    junk = C.sb([128, D], BF16, "A_junk")
    hb = [C.sb([128, D], BF16, "A_hb%d" % i) for i in range(2)]
    hT = [C.sb([128, 8, 128], BF16, "A_hT%d" % i) for i in range(2)]
    st = [C.sb([128, 32], F32, "A_st%d" % i) for i in range(2)]
    qsq = C.sb([128, 512], F32, "A_qsq")
    qtmp = C.sb([128, 512], F32, "A_qtmp")
    qn = C.sb([128, 512], BF16, "A_qn")
    ge = C.sb([128, 256], F32, "A_ge")
    hcn = C.sb([128, 256], BF16, "A_hcn")
    uf = C.sb([128, 256], F32, "A_uf")
    qT_all = C.sb([128, 4, TOK], BF16, "A_qT")
    kT_all = C.sb([128, 4, TOK], BF16, "A_kT")
    v_all = C.sb([128, NT, 512], BF16, "A_v")
    hcT_all = C.sb([128, 2, TOK], BF16, "A_hcT")
    uT_all = C.sb([128, 2, TOK], F32, "A_uT")

    for k in range(8):
        P.op("pool", lambda e, k=k: e.dma_start(out=W[:, k, :], in_=w_in[k * 128:(k + 1) * 128, :]),
             writes=[("W", k)], dma=True)
    P.op("sp", lambda e: e.dma_start(out=gB, in_=g_mix.to_broadcast([128, D])), writes=["gB"], dma=True)
    P.op("sp", lambda e: e.dma_start(out=gqB, in_=gq.unsqueeze(1).to_broadcast([128, 8, 64])),
         writes=["gqB"], dma=True)
    P.op("sp", lambda e: e.dma_start(out=gkB, in_=gk.unsqueeze(1).to_broadcast([128, 8, 64])),
         writes=["gkB"], dma=True)
    P.op("pool", lambda e: e.dma_start(out=idB, in_=ident), writes=["idB"], dma=True)
    P.op("sp", lambda e: e.dma_start(out=idF, in_=ident), writes=["idF"], dma=True)
    P.op("dve", lambda e: e.tensor_scalar(out=gqB, in0=gqB, scalar1=0.125, scalar2=None, op0=ALU.mult),
         reads=["gqB"], writes=["gqB"])
    Wk = [("W", k) for k in range(8)]

    chunks = [(0, 512), (512, 512), (1024, 512), (1536, 512), (2048, 256)]
    nbank = [0]

    def obank():
        i = 2 + (nbank[0] % 3)
        nbank[0] += 1
        return i
    ntb = [0]

    def tbank():
        i = 5 + (ntb[0] % 2)
        ntb[0] += 1
        return i

    def load_x(t):
        par = t % 2
        P.op("sp", lambda e: e.dma_start(out=xt[par], in_=x[t * 128:(t + 1) * 128, :]),
             writes=[("xt", par)], dma=True)

    load_x(0)
    for t in range(NT):
        par = t % 2
        if t + 1 < NT:
            load_x(t + 1)
        s = st[par]
        P.op("act", lambda e, par=par, s=s: e.activation(out=junk, in_=xt[par], func=AF.Square,
                                                          accum_out=s[:, 0:1]),
             reads=[("xt", par)], writes=["junk", ("st", par, 0)])
        rstd_ops(P, s[:, 0:1], s[:, 1:2], s[:, 2:3], 1.0 / D, ("st", par, 0), ("st", par, 1), ("st", par, 2))
        P.op("dve", lambda e, par=par, s=s: e.scalar_tensor_tensor(
            out=hb[par], in0=xt[par], scalar=s[:, 2:3], in1=gB, op0=ALU.mult, op1=ALU.mult),
            reads=[("xt", par), ("st", par, 2), "gB"], writes=[("hb", par)])
        pT = bank_bf(C.banks[par]).rearrange("p (k t) -> p k t", k=8)
        for k in range(8):
            P.op("pe", lambda e, k=k, pT=pT, par=par: e.transpose(
                out=pT[:, k, :], in_=hb[par][:, k * 128:(k + 1) * 128], identity=idB),
                reads=[("hb", par), "idB"], writes=[("bank", par)])
        P.op("act", lambda e, pT=pT, par=par: e.copy(out=hT[par], in_=pT),
             reads=[("bank", par)], writes=[("hT", par)])
        for ci, (c0, cw) in enumerate(chunks):
            bi = obank()
            pO = C.banks[bi][:, 0:cw]
            for k in range(8):
                P.op("pe", lambda e, k=k, pO=pO, par=par, c0=c0, cw=cw: e.matmul(
                    pO, lhsT=hT[par][:, k, :], rhs=W[:, k, c0:c0 + cw], start=(k == 0), stop=(k == 7)),
                    reads=[("hT", par), ("W", k)], writes=[("bank", bi)])
            bk = ("bank", bi)
            if ci in (0, 1):
                dstT = qT_all if ci == 0 else kT_all
                gBt = gqB if ci == 0 else gkB
                gkey = "gqB" if ci == 0 else "gkB"
                P.op("act", lambda e, pO=pO: e.activation(out=qsq, in_=pO, func=AF.Square),
                     reads=[bk], writes=["qsq"])
                P.op("dve", lambda e, s=s: e.tensor_reduce(
                    out=s[:, 8:16], in_=qsq.rearrange("p (h d) -> p h d", h=8), axis=AX.X, op=ALU.add),
                    reads=["qsq"], writes=[("st", par, 3)])
                rstd_ops(P, s[:, 8:16], s[:, 16:24], s[:, 24:32], 1.0 / 64,
                         ("st", par, 3), ("st", par, 4), ("st", par, 5))
                P.op("dve", lambda e, pO=pO, s=s: e.tensor_tensor(
                    out=qtmp.rearrange("p (h d) -> p h d", h=8),
                    in0=pO.rearrange("p (h d) -> p h d", h=8),
                    in1=s[:, 24:32].unsqueeze(2).to_broadcast([128, 8, 64]), op=ALU.mult),
                    reads=[bk, ("st", par, 5)], writes=["qtmp"])
                P.op("dve", lambda e, gBt=gBt: e.tensor_tensor(
                    out=qn, in0=qtmp, in1=gBt.rearrange("p h d -> p (h d)"), op=ALU.mult),
                    reads=["qtmp", gkey], writes=["qn"])
                tb = tbank()
                pQ = bank_bf(C.banks[tb])[:, 0:512].rearrange("p (j t) -> p j t", j=4)
                for j in range(4):
                    P.op("pe", lambda e, j=j, pQ=pQ: e.transpose(
                        out=pQ[:, j, :], in_=qn[:, j * 128:(j + 1) * 128], identity=idB),
                        reads=["qn", "idB"], writes=[("bank", tb)])
                P.op("act", lambda e, pQ=pQ, dstT=dstT, t=t: e.copy(
                    out=dstT[:, :, t * 128:(t + 1) * 128], in_=pQ),
                    reads=[("bank", tb)], writes=[("qkT", ci, t)])
            elif ci == 2:
                P.op("act", lambda e, pO=pO, t=t: e.copy(out=v_all[:, t, :], in_=pO),
                     reads=[bk], writes=[("v", t)])
            elif ci == 3:
                P.op("act", lambda e, pO=pO: e.activation(out=ge, in_=pO[:, 256:512], func=AF.Exp, scale=-1.0),
                     reads=[bk], writes=["ge"])
                P.op("dve", lambda e: e.tensor_scalar(out=ge, in0=ge, scalar1=1.0, scalar2=None, op0=ALU.add),
                     reads=["ge"], writes=["ge"])
                P.op("dve", lambda e: e.reciprocal(out=ge, in_=ge), reads=["ge"], writes=["ge"])
                P.op("dve", lambda e, pO=pO: e.tensor_tensor(out=hcn, in0=pO[:, 0:256], in1=ge, op=ALU.mult),
                     reads=[bk, "ge"], writes=["hcn"])
                tb = tbank()
                pQ = bank_bf(C.banks[tb])[:, 0:256].rearrange("p (j t) -> p j t", j=2)
                for j in range(2):
                    P.op("pe", lambda e, j=j, pQ=pQ: e.transpose(
                        out=pQ[:, j, :], in_=hcn[:, j * 128:(j + 1) * 128], identity=idB),
                        reads=["hcn", "idB"], writes=[("bank", tb)])
                P.op("act", lambda e, pQ=pQ, t=t: e.copy(out=hcT_all[:, :, t * 128:(t + 1) * 128], in_=pQ),
                     reads=[("bank", tb)], writes=[("hcT", t)])
            else:
                P.op("act", lambda e, pO=pO: e.copy(out=uf, in_=pO), reads=[bk], writes=["uf"])
                tb = tbank()
                pQ = C.banks[tb][:, 0:256].rearrange("p (j t) -> p j t", j=2)
                for j in range(2):
                    P.op("pe", lambda e, j=j, pQ=pQ: e.transpose(
                        out=pQ[:, j, :], in_=uf[:, j * 128:(j + 1) * 128], identity=idF),
                        reads=["uf", "idF"], writes=[("bank", tb)])
                P.op("act", lambda e, pQ=pQ, t=t: e.copy(out=uT_all[:, :, t * 128:(t + 1) * 128], in_=pQ),
                     reads=[("bank", tb)], writes=[("uT", t)])

    allq = [("qkT", 0, t) for t in range(NT)]
    allk = [("qkT", 1, t) for t in range(NT)]
    allv = [("v", t) for t in range(NT)]
    allh = [("hcT", t) for t in range(NT)]
    allu = [("uT", t) for t in range(NT)]
    for j in range(8):
        r0 = (j % 2) * 64
        P.op("sp", lambda e, j=j, r0=r0: e.dma_start(
            out=send_bf[j, 0:QKV_SZ].rearrange("(d t) -> d t", t=TOK), in_=qT_all[r0:r0 + 64, j // 2, :]),
            reads=allq, dma=True)
        P.op("sp", lambda e, j=j, r0=r0: e.dma_start(
            out=send_bf[j, QKV_SZ:2 * QKV_SZ].rearrange("(d t) -> d t", t=TOK), in_=kT_all[r0:r0 + 64, j // 2, :]),
            reads=allk, dma=True)
        P.op("sp", lambda e, j=j: e.dma_start(
            out=send_bf[j, 2 * QKV_SZ:3 * QKV_SZ].rearrange("(n p d) -> p n d", p=128, d=64),
            in_=v_all[:, :, j * 64:(j + 1) * 64]), reads=allv, dma=True)
        P.op("sp", lambda e, j=j: e.dma_start(
            out=send_bf[j, HALO_OFF:SB_SZ].rearrange("(c p t) -> p c t", p=128, t=32),
            in_=hcT_all[:, :, TOK - 32:TOK]), reads=allh, dma=True)
        u0 = (j % 4) * 32
        P.op("sp", lambda e, j=j, u0=u0: e.dma_start(out=send_u[j], in_=uT_all[u0:u0 + 32, j // 4, :]),
             reads=allu, dma=True)
    P.op("sp", lambda e: e.dma_start(out=hc_loc.rearrange("(c p) t -> p c t", p=128), in_=hcT_all),
         reads=allh, dma=True)


def build_A():
    nc = bass.Bass("TRN2", target_bir_lowering=False)
    es = ExitStack()
    C = Ctx(nc, es)
    x = C.din("x", [TOK, D], F32)
    w_in = C.din("w_in", [D, NINP], F32)
    g_mix = C.din("g_mix", [1, D], F32)
    gq = C.din("gq", [1, 64], F32)
    gk = C.din("gk", [1, 64], F32)
    ident = C.din("ident", [128, 128], F32)
    send_bf = C.dout("send_bf", [8, SB_SZ], BF16)
    send_u = C.dout("send_u", [8, 32, TOK], F32)
    hc_loc = C.dout("hc_loc", [256, TOK], BF16)
    phase_A(C, x, w_in, g_mix, gq, gk, ident, send_bf, send_u, hc_loc)
    C.P.emit(es)
    es.close()
    return nc


def phase_B_attn(C, recv_bf, consts_bf, send_o, side=None, side_rate=1):
    P = C.P
    qT = C.sb([64, 2 * SEQ], BF16, "B_qT")
    kT = C.sb([64, 2 * SEQ], BF16, "B_kT")
    v = C.sb([128, 128, 64], BF16, "B_v")
    cst = C.sb([128, 256 + 4 * 512], BF16, "B_cst")
    negtri = cst[:, 0:128]
    negones = cst[:, 128:256]
    mask = cst[:, 256:].rearrange("p (i q) -> p i q", i=4)
    eb = [C.sb([128, 512], F32, "B_e%d" % i) for i in range(2)]
    spb = [C.sb([128, 512], BF16, "B_sp%d" % i) for i in range(3)]
    wb = [C.sb([128, 512], BF16, "B_w%d" % i) for i in range(3)]
    ssb = [C.sb([128, 512], BF16, "B_ss%d" % i) for i in range(3)]
    ost = [C.sb([64, 512], F32, "B_ost%d" % i) for i in range(2)]

    P.op("pool", lambda e: e.dma_start(out=cst, in_=consts_bf), writes=["cst"], dma=True)
    for c in range(8):
        P.op("sp", lambda e, c=c: e.dma_start(
            out=kT[:, c * TOK:(c + 1) * TOK], in_=recv_bf[c, QKV_SZ:2 * QKV_SZ].rearrange("(d t) -> d t", t=TOK)),
            writes=[("kT", c)], dma=True)
        P.op("sp", lambda e, c=c: e.dma_start(
            out=qT[:, c * TOK:(c + 1) * TOK], in_=recv_bf[c, 0:QKV_SZ].rearrange("(d t) -> d t", t=TOK)),
            writes=[("qT", c)], dma=True)
        P.op("sp", lambda e, c=c: e.dma_start(
            out=v[:, c * 16:(c + 1) * 16, :],
            in_=recv_bf[c, 2 * QKV_SZ:3 * QKV_SZ].rearrange("(n p d) -> p n d", p=128, d=64)),
            writes=[("v", c)], dma=True)

    units = []
    for b in range(2):
        for qc in range(16):
            kbs = [(4 * qc + i, i) for i in (3, 2, 1, 0)] + [(kb, None) for kb in range(4 * qc - 1, -1, -1)]
            for n, (kb, di) in enumerate(kbs):
                units.append(dict(idx=len(units), b=b, qc=qc, kb=kb, diag=di, first=(n == 0),
                                  last=(n == len(kbs) - 1)))

    def unit_cols(u):
        kc = u["b"] * SEQ + u["kb"] * 128
        qc0 = u["b"] * SEQ + u["qc"] * 512
        return kc, qc0

    def emit_Z(u):
        i = u["idx"]
        kc, qc0 = unit_cols(u)
        zb = i % 2
        Z = C.banks[zb]
        P.op("pe", lambda e: e.matmul(Z, lhsT=kT[:, kc:kc + 128], rhs=qT[:, qc0:qc0 + 512], start=True, stop=True),
             reads=[("kT", kc // TOK), ("qT", qc0 // TOK)], writes=[("bank", zb)])

    def emit_soft(u):
        i = u["idx"]
        zb = i % 2
        Z = C.banks[zb]
        ee = eb[i % 2]
        sp = spb[i % 3]
        P.op("act", lambda e: e.activation(out=ee, in_=Z, func=AF.Exp), reads=[("bank", zb)], writes=[("e", i % 2)])
        P.op("act", lambda e: e.activation(out=sp, in_=ee, func=AF.Ln, bias=1.0),
             reads=[("e", i % 2)], writes=[("sp", i % 3)])
        if u["diag"] is not None:
            m = mask[:, u["diag"], :]
            P.op("dve", lambda e: e.tensor_tensor(out=sp, in0=sp, in1=m, op=ALU.mult),
                 reads=[("sp", i % 3), "cst"], writes=[("sp", i % 3)])
        ss = ssb[i % 3]
        if u["first"]:
            P.op("dve", lambda e: e.tensor_copy(out=ss, in_=sp), reads=[("sp", i % 3)], writes=[("ss", i % 3)])
        else:
            sprev = ssb[(i - 1) % 3]
            P.op("dve", lambda e: e.tensor_tensor(out=ss, in0=sprev, in1=sp, op=ALU.add),
                 reads=[("sp", i % 3), ("ss", (i - 1) % 3)], writes=[("ss", i % 3)])

    def emit_W(u):
        i = u["idx"]
        kc, qc0 = unit_cols(u)
        wbk = 2 + i % 2
        Wp = C.banks[wbk]
        sp = spb[i % 3]
        first = u["first"]
        P.op("pe", lambda e: e.matmul(Wp, lhsT=kT[:, kc:kc + 128], rhs=qT[:, qc0:qc0 + 512], start=True, stop=False),
             reads=[("kT", kc // TOK), ("qT", qc0 // TOK)], writes=[("bank", wbk)])
        P.op("pe", lambda e: e.matmul(Wp, lhsT=negtri, rhs=sp, start=False, stop=first),
             reads=["cst", ("sp", i % 3)], writes=[("bank", wbk)])
        if not first:
            sprev = ssb[(i - 1) % 3]
            P.op("pe", lambda e: e.matmul(Wp, lhsT=negones, rhs=sprev, start=False, stop=True),
                 reads=["cst", ("ss", (i - 1) % 3)], writes=[("bank", wbk)])

    def emit_expW(u):
        i = u["idx"]
        wbk = 2 + i % 2
        Wp = C.banks[wbk]
        w = wb[i % 3]
        P.op("act", lambda e: e.activation(out=w, in_=Wp, func=AF.Exp), reads=[("bank", wbk)], writes=[("w", i % 3)])
        if u["diag"] is not None:
            m = mask[:, u["diag"], :]
            P.op("dve", lambda e: e.tensor_tensor(out=w, in0=w, in1=m, op=ALU.mult),
                 reads=[("w", i % 3), "cst"], writes=[("w", i % 3)])

    def emit_PV(u):
        i = u["idx"]
        qc = u["qc"]
        G = u["b"] * 64 + u["kb"]
        ob = 4 + qc % 2
        O = C.banks[ob][0:64, :]
        w = wb[i % 3]
        P.op("pe", lambda e: e.matmul(O, lhsT=v[:, G, :], rhs=w, start=u["first"], stop=u["last"]),
             reads=[("v", G // 16), ("w", i % 3)], writes=[("bank", ob)])
        if u["last"]:
            os_ = ost[qc % 2]
            P.op("dve", lambda e: e.tensor_copy(out=os_, in_=O), reads=[("bank", ob)], writes=[("ost", qc % 2)])
            dest = u["b"] * 4 + qc // 4
            c0 = (qc % 4) * 512
            P.op("sp", lambda e: e.dma_start(out=send_o[dest, 0:64, c0:c0 + 512], in_=os_),
                 reads=[("ost", qc % 2)], dma=True)

    n = len(units)
    emit_Z(units[0])
    for s in range(n + 1):
        if s + 1 < n:
            emit_Z(units[s + 1])
        if s < n:
            emit_soft(units[s])
            emit_W(units[s])
        if 0 <= s - 1 < n:
            emit_expW(units[s - 1])
            emit_PV(units[s - 1])
        if side is not None:
            for _ in range(side_rate):
                next(side, None)
    if side is not None:
        for _ in side:
            pass


def attn_consts():
    k = np.arange(128)
    negtri = -(k[:, None] >= k[None, :]).astype(np.float32)
    negones = -np.ones((128, 128), np.float32)
    q = np.arange(512)
    mask = np.stack([(128 * i + k[:, None] < q[None, :]).astype(np.float32) for i in range(4)], 1)
    return np.concatenate([negtri, negones, mask.reshape(128, 2048)], 1)


def build_B(with_ssm=True):
    nc = bass.Bass("TRN2", target_bir_lowering=False)
    es = ExitStack()
    C = Ctx(nc, es)
    recv_bf = C.din("recv_bf", [8, SB_SZ], BF16)
    consts_bf = C.din("consts_bf", [128, 256 + 2048], F32)
    send_o = C.dout("send_o", [8, 96, TOK], F32)
    side = None
    if with_ssm:
        ssm_io = ssm_decl(C)
        side = phase_B_ssm(C, ssm_io, send_o)
    phase_B_attn(C, recv_bf, consts_bf, send_o, side=side)
    C.P.emit(es)
    es.close()
    return nc


import math

SSM_NP = 6 + 1 + 32 + 32 + 32 + 2 + 128 + 128
PC_VEC, PC_SGN, PC_BRI, PC_BIR, PC_CT, PC_D, PC_I, PC_SW = 0, 6, 7, 39, 71, 103, 105, 233
NLEV = 13


def ssm_decl(C):
    return dict(recv_u=C.din("recv_u", [8, 32, TOK], F32), ssm_par=C.din("ssm_par", [128, SSM_NP], F32))


def ssm_pack(lam_re, lam_im, log_dt, b_re, b_im, c_re, c_im, d, core):
    par = np.zeros((128, SSM_NP), np.float32)
    for gi in range(2):
        g = 2 * core + gi
        par[:, PC_VEC + 3 * gi + 0] = np.concatenate([lam_re[g], lam_re[g]])
        par[:, PC_VEC + 3 * gi + 1] = np.concatenate([lam_im[g], lam_im[g]])
        par[:, PC_VEC + 3 * gi + 2] = log_dt[g]
        par[:, PC_BRI + 16 * gi:PC_BRI + 16 * gi + 16] = np.concatenate([b_re[g], b_im[g]], 0)
        par[:, PC_BIR + 16 * gi:PC_BIR + 16 * gi + 16] = np.concatenate([b_im[g], b_re[g]], 0)
        par[:, PC_CT + 16 * gi:PC_CT + 16 * gi + 16] = np.concatenate([c_re[g].T, c_im[g].T], 0)
        par[16 * gi:16 * gi + 16, PC_D + gi] = d[16 * g:16 * g + 16]
    par[:64, PC_SGN] = 1.0
    par[64:, PC_SGN] = -1.0
    par[:, PC_I:PC_I + 128] = np.eye(128, dtype=np.float32)
    par[:, PC_SW:PC_SW + 128] = np.roll(np.eye(128, dtype=np.float32), 64, axis=1)
    return par


def phase_B_ssm(C, io, send_o):
    P = C.P
    recv_u, ssm_par = io["recv_u"], io["ssm_par"]
    par = C.sb([128, SSM_NP], F32, "S_par")
    sc = C.sb([128, 128], F32, "S_sc")
    sci = C.sb([128, 4], mybir.dt.int32, "S_sci")
    AT = C.sb([128, 2, NLEV, 128], F32, "S_AT")
    Bpad = C.sb([128, 2, 32], F32, "S_Bpad")
    Cpad = C.sb([128, 2, 32], F32, "S_Cpad")
    lB = C.sb([32, 2, 128], F32, "S_lB")
    X = C.sb([128, SEQ], F32, "S_X")
    ub = C.sb([32, SEQ], F32, "S_u")
    yst = [C.sb([32, 512], F32, "S_y%d" % i) for i in range(2)]
    I128 = par[:, PC_I:PC_I + 128]
    SW = par[:, PC_SW:PC_SW + 128]
    sgn = par[:, PC_SGN:PC_SGN + 1]
    TWO_PI = 2.0 * math.pi

    P.op("sp", lambda e: e.dma_start(out=par, in_=ssm_par), writes=["par"], dma=True)
    P.op("dve", lambda e: e.memset(Bpad, 0.0), writes=["Bpad"])
    P.op("dve", lambda e: e.memset(Cpad, 0.0), writes=["Cpad"])

    ncol = [0]

    def col():
        ncol[0] += 1
        assert ncol[0] <= 128
        return ncol[0] - 1

    def cs(i):
        return sc[:, i:i + 1]

    def K(i):
        return ("sc", i)

    def ts(o, a, s1, s2, op0, op1=None, extra=()):
        if op1 is None:
            P.op("dve", lambda e: e.tensor_scalar(out=cs(o), in0=cs(a), scalar1=s1, scalar2=None, op0=op0),
                 reads=[K(a)] + list(extra), writes=[K(o)])
        else:
            P.op("dve", lambda e: e.tensor_scalar(out=cs(o), in0=cs(a), scalar1=s1, scalar2=s2, op0=op0, op1=op1),
                 reads=[K(a)] + list(extra), writes=[K(o)])

    def tt(o, a, b, op):
        P.op("dve", lambda e: e.tensor_tensor(out=cs(o), in0=cs(a), in1=cs(b), op=op),
             reads=[K(a), K(b)], writes=[K(o)])

    for gi in range(2):
        ncol[0] = 0
        lr = par[:, PC_VEC + 3 * gi:PC_VEC + 3 * gi + 1]
        li = par[:, PC_VEC + 3 * gi + 1:PC_VEC + 3 * gi + 2]
        ldt = par[:, PC_VEC + 3 * gi + 2:PC_VEC + 3 * gi + 3]
        c_dt, c_mag, c_th = col(), col(), col()
        P.op("act", lambda e, c_dt=c_dt, ldt=ldt: e.activation(out=cs(c_dt), in_=ldt, func=AF.Exp),
             reads=["par"], writes=[K(c_dt)])
        P.op("act", lambda e, c_mag=c_mag, c_dt=c_dt, lr=lr: e.activation(out=cs(c_mag), in_=lr, func=AF.Exp, scale=cs(c_dt)),
             reads=["par", K(c_dt)], writes=[K(c_mag)])
        P.op("dve", lambda e, c_th=c_th, c_dt=c_dt, li=li: e.tensor_tensor(out=cs(c_th), in0=li, in1=cs(c_dt), op=ALU.mult),
             reads=["par", K(c_dt)], writes=[K(c_th)])
        trig = []
        for shift in (0.5 * math.pi, 0.0):
            c_a, c_y, c_n, c_r, c_w, c_o = col(), col(), col(), col(), col(), col()
            ts(c_a, c_th, shift, None, ALU.add)
            ts(c_y, c_a, 1.0 / TWO_PI, None, ALU.mult)
            ii = 0 if shift else 1
            P.op("dve", lambda e, ii=ii, c_y=c_y: e.tensor_copy(out=sci[:, ii:ii + 1], in_=cs(c_y)),
                 reads=[K(c_y)], writes=[("sci", ii)])
            P.op("dve", lambda e, ii=ii, c_n=c_n: e.tensor_copy(out=cs(c_n), in_=sci[:, ii:ii + 1]),
                 reads=[("sci", ii)], writes=[K(c_n)])
            P.op("dve", lambda e, c_r=c_r, c_n=c_n, c_a=c_a: e.scalar_tensor_tensor(
                out=cs(c_r), in0=cs(c_n), scalar=-TWO_PI, in1=cs(c_a), op0=ALU.mult, op1=ALU.add),
                reads=[K(c_n), K(c_a)], writes=[K(c_r)])
            ts(c_w, c_r, math.pi, -TWO_PI, ALU.is_gt, ALU.mult)
            tt(c_r, c_r, c_w, ALU.add)
            P.op("act", lambda e, c_o=c_o, c_r=c_r: e.activation(out=cs(c_o), in_=cs(c_r), func=AF.Sin),
                 reads=[K(c_r)], writes=[K(c_o)])
            trig.append(c_o)
        c_ar, c_ai = col(), col()
        tt(c_ar, c_mag, trig[0], ALU.mult)
        tt(c_ai, c_mag, trig[1], ALU.mult)
        c_l2, c_i2, c_den, c_am1, c_t1, c_t2, c_fr, c_fi, c_f2 = [col() for _ in range(9)]
        P.op("dve", lambda e, lr=lr, c_l2=c_l2: e.tensor_tensor(out=cs(c_l2), in0=lr, in1=lr, op=ALU.mult),
             reads=["par"], writes=[K(c_l2)])
        P.op("dve", lambda e, li=li, c_i2=c_i2: e.tensor_tensor(out=cs(c_i2), in0=li, in1=li, op=ALU.mult),
             reads=["par"], writes=[K(c_i2)])
        tt(c_den, c_l2, c_i2, ALU.add)
        P.op("dve", lambda e, c_den=c_den: e.reciprocal(out=cs(c_den), in_=cs(c_den)), reads=[K(c_den)], writes=[K(c_den)])
        ts(c_am1, c_ar, -1.0, None, ALU.add)
        P.op("dve", lambda e, lr=lr, c_t1=c_t1, c_am1=c_am1: e.tensor_tensor(out=cs(c_t1), in0=cs(c_am1), in1=lr, op=ALU.mult),
             reads=["par", K(c_am1)], writes=[K(c_t1)])
        P.op("dve", lambda e, li=li, c_t2=c_t2, c_ai=c_ai: e.tensor_tensor(out=cs(c_t2), in0=cs(c_ai), in1=li, op=ALU.mult),
             reads=["par", K(c_ai)], writes=[K(c_t2)])
        tt(c_fr, c_t1, c_t2, ALU.add)
        tt(c_fr, c_fr, c_den, ALU.mult)
        P.op("dve", lambda e, lr=lr, c_t1=c_t1, c_ai=c_ai: e.tensor_tensor(out=cs(c_t1), in0=cs(c_ai), in1=lr, op=ALU.mult),
             reads=["par", K(c_ai)], writes=[K(c_t1)])
        P.op("dve", lambda e, li=li, c_t2=c_t2, c_am1=c_am1: e.tensor_tensor(out=cs(c_t2), in0=cs(c_am1), in1=li, op=ALU.mult),
             reads=["par", K(c_am1)], writes=[K(c_t2)])
        tt(c_fi, c_t1, c_t2, ALU.subtract)
        tt(c_fi, c_fi, c_den, ALU.mult)
        P.op("dve", lambda e, c_f2=c_f2, c_fi=c_fi: e.scalar_tensor_tensor(
            out=cs(c_f2), in0=cs(c_fi), scalar=-1.0, in1=sgn, op0=ALU.mult, op1=ALU.mult),
            reads=[K(c_fi), "par"], writes=[K(c_f2)])
        bsl = Bpad[:, gi, 16 * gi:16 * gi + 16]
        P.op("dve", lambda e, bsl=bsl, gi=gi, c_fr=c_fr: e.tensor_scalar(
            out=bsl, in0=par[:, PC_BRI + 16 * gi:PC_BRI + 16 * gi + 16], scalar1=cs(c_fr), scalar2=None, op0=ALU.mult),
            reads=["par", K(c_fr), "Bpad"], writes=[("Bpad", gi)])
        P.op("dve", lambda e, bsl=bsl, gi=gi, c_f2=c_f2: e.scalar_tensor_tensor(
            out=bsl, in0=par[:, PC_BIR + 16 * gi:PC_BIR + 16 * gi + 16], scalar=cs(c_f2), in1=bsl, op0=ALU.mult, op1=ALU.add),
            reads=["par", K(c_f2), ("Bpad", gi)], writes=[("Bpad", gi)])
        pb = C.banks[6][0:32, 0:128]
        P.op("pe", lambda e, pb=pb, gi=gi: e.transpose(out=pb, in_=Bpad[:, gi, :], identity=I128),
             reads=[("Bpad", gi), "par"], writes=[("bank", 6)])
        P.op("dve", lambda e, pb=pb, gi=gi: e.tensor_copy(out=lB[:, gi, :], in_=pb), reads=[("bank", 6)], writes=[("lB", gi)])
        P.op("dve", lambda e, gi=gi: e.tensor_scalar(
            out=Cpad[:, gi, 16 * gi:16 * gi + 16], in0=par[:, PC_CT + 16 * gi:PC_CT + 16 * gi + 16],
            scalar1=sgn, scalar2=None, op0=ALU.mult), reads=["par", "Cpad"], writes=[("Cpad", gi)])
        c_pr, c_pi = c_ar, c_ai
        for k in range(NLEV):
            c_s2 = col()
            P.op("dve", lambda e, c_s2=c_s2, c_pi=c_pi: e.tensor_tensor(out=cs(c_s2), in0=cs(c_pi), in1=sgn, op=ALU.mult),
                 reads=[K(c_pi), "par"], writes=[K(c_s2)])
            A = AT[:, gi, k, :]
            P.op("dve", lambda e, A=A, c_pr=c_pr: e.tensor_scalar(out=A, in0=I128, scalar1=cs(c_pr), scalar2=None, op0=ALU.mult),
                 reads=["par", K(c_pr)], writes=[("AT", gi, k)])
            P.op("dve", lambda e, A=A, c_s2=c_s2: e.scalar_tensor_tensor(
                out=A, in0=SW, scalar=cs(c_s2), in1=A, op0=ALU.mult, op1=ALU.add),
                reads=["par", K(c_s2), ("AT", gi, k)], writes=[("AT", gi, k)])
            if k + 1 < NLEV:
                c_a2, c_b2, c_nr, c_ni = col(), col(), col(), col()
                tt(c_a2, c_pr, c_pr, ALU.mult)
                tt(c_b2, c_pi, c_pi, ALU.mult)
                tt(c_nr, c_a2, c_b2, ALU.subtract)
                P.op("dve", lambda e, c_ni=c_ni, c_pr=c_pr, c_pi=c_pi: e.scalar_tensor_tensor(
                    out=cs(c_ni), in0=cs(c_pr), scalar=2.0, in1=cs(c_pi), op0=ALU.mult, op1=ALU.mult),
                    reads=[K(c_pr), K(c_pi)], writes=[K(c_ni)])
                c_pr, c_pi = c_nr, c_ni

    def scan_gen():
      if True:
        nb = [0]

        def sbank():
            nb[0] += 1
            return 6 + nb[0] % 2

        for b in range(2):
            for j in range(4):
                P.op("sp", lambda e, b=b, j=j: e.dma_start(out=ub[:, j * TOK:(j + 1) * TOK], in_=recv_u[4 * b + j]),
                     writes=[("ub", j)], dma=True)
            for gi in range(2):
                for ch in range(16):
                    bk = sbank()
                    ps = C.banks[bk]
                    P.op("pe", lambda e, ps=ps, gi=gi, ch=ch: e.matmul(
                        ps, lhsT=lB[:, gi, :], rhs=ub[:, ch * 512:(ch + 1) * 512], start=True, stop=True),
                        reads=[("lB", gi), ("ub", ch // 4)], writes=[("bank", bk)])
                    P.op("dve", lambda e, ps=ps, ch=ch: e.tensor_copy(out=X[:, ch * 512:(ch + 1) * 512], in_=ps),
                         reads=[("bank", bk)], writes=[("X", ch)])
                    yield

                for k in range(NLEV):
                    s_ = 1 << k
                    for ch in range(15, -1, -1):
                        lo = max(512 * ch, s_)
                        hi = 512 * (ch + 1)
                        if lo >= hi:
                            continue
                        n_ = hi - lo
                        bk = sbank()
                        ps = C.banks[bk][:, 0:n_]
                        src = X[:, lo - s_:hi - s_]
                        dst = X[:, lo:hi]
                        rk = sorted(set([("X", (lo - s_) // 512), ("X", (hi - s_ - 1) // 512)]))
                        P.op("pe", lambda e, ps=ps, src=src, k=k, gi=gi: e.matmul(
                            ps, lhsT=AT[:, gi, k, :], rhs=src, start=True, stop=True),
                            reads=[("AT", gi, k)] + rk, writes=[("bank", bk)])
                        P.op("dve", lambda e, ps=ps, dst=dst: e.tensor_tensor(out=dst, in0=dst, in1=ps, op=ALU.add),
                             reads=[("bank", bk), ("X", ch)], writes=[("X", ch)])
                        yield
                for ch in range(16):
                    bk = sbank()
                    ps = C.banks[bk][0:32, :]
                    P.op("pe", lambda e, ps=ps, ch=ch, gi=gi: e.matmul(
                        ps, lhsT=Cpad[:, gi, :], rhs=X[:, ch * 512:(ch + 1) * 512], start=True, stop=True),
                        reads=[("Cpad", gi), ("X", ch)], writes=[("bank", bk)])
                    ys = yst[ch % 2]
                    P.op("dve", lambda e, ps=ps, ch=ch, gi=gi, ys=ys: e.scalar_tensor_tensor(
                        out=ys, in0=ub[:, ch * 512:(ch + 1) * 512], scalar=par[0:32, PC_D + gi:PC_D + gi + 1], in1=ps,
                        op0=ALU.mult, op1=ALU.add),
                        reads=[("bank", bk), ("ub", ch // 4), "par"], writes=[("yst", ch % 2)])
                    dest = 4 * b + ch // 4
                    c0 = (ch % 4) * 512
                    P.op("sp", lambda e, ys=ys, dest=dest, c0=c0, gi=gi: e.dma_start(
                        out=send_o[dest, 64 + 16 * gi:64 + 16 * gi + 16, c0:c0 + 512], in_=ys[16 * gi:16 * gi + 16, :]),
                        reads=[("yst", ch % 2)], dma=True)
                    yield

    return scan_gen()


def phase_C(C, io):
    P = C.P
    x, x_out, recv_o, hc_loc, halo, mem = io["x"], io["x_out"], io["recv_o"], io["hc_loc"], io["halo"], io["mem"]
    bcast = lambda ap, n: ap.to_broadcast([128, n])

    xres = C.sb([128, NT, D], F32, "C_x")
    idB = C.sb([128, 128], BF16, "C_idB")
    idF = C.sb([128, 128], F32, "C_idF")
    onesF = C.sb([128, 1], F32, "C_ones")
    st = C.sb([128, 64], F32, "C_st")
    junk = C.sb([128, D], BF16, "C_junk")
    hb = C.sb([128, D], BF16, "C_hb")
    tT = [C.sb([128, 8, 128], BF16, "C_tT%d" % i) for i in range(2)]
    base_mark = C.mark()

    for t in range(NT):
        P.op("sp", lambda e, t=t: e.dma_start(out=xres[:, t, :], in_=x[t * 128:(t + 1) * 128, :]),
             writes=[("x", t)], dma=True)
    P.op("pool", lambda e: e.dma_start(out=idB, in_=io["ident"]), writes=["idB"], dma=True)
    P.op("sp", lambda e: e.dma_start(out=idF, in_=io["ident"]), writes=["idF"], dma=True)
    P.op("dve", lambda e: e.memset(onesF, 1.0), writes=["ones"])

    nst = [0]

    def stcol(n=1):
        c = nst[0] % 64
        if c + n > 64:
            c = 0
        nst[0] = c + n
        return c

    def sk(c, n=1):
        return [("st", c + i) for i in range(n)]

    def rstd_col(ss_c, n, inv_n):
        tmp, out = stcol(n), stcol(n)
        rstd_ops(P, st[:, ss_c:ss_c + n], st[:, tmp:tmp + n], st[:, out:out + n], inv_n,
                 sk(ss_c, n), sk(tmp, n), sk(out, n))
        return out

    def load_w(dst, src, nk, key, eng="pool"):
        for k in range(nk):
            P.op(eng, lambda e, k=k: e.dma_start(out=dst[:, k, :], in_=src[k * 128:(k + 1) * 128, :]),
                 writes=[(key, k)], dma=True)
        return [(key, k) for k in range(nk)]

    def transposes(src, n, dst, bank, rk, wk, ident=None, f32=False, evac="act"):
        if f32:
            pv = C.banks[bank][:, 0:n * 128].rearrange("p (i t) -> p i t", i=n)
        else:
            pv = bank_bf(C.banks[bank])[:, 0:n * 128].rearrange("p (i t) -> p i t", i=n)
        idt, idk = (idF, "idF") if f32 else (idB, "idB")
        for i in range(n):
            P.op("pe", lambda e, i=i: e.transpose(out=pv[:, i, :], in_=src[:, i * 128:(i + 1) * 128], identity=idt),
                 reads=list(rk) + [idk], writes=[("bank", bank)])
        if evac == "act":
            P.op("act", lambda e: e.copy(out=dst, in_=pv), reads=[("bank", bank)], writes=list(wk))
        else:
            P.op("dve", lambda e: e.tensor_copy(out=dst, in_=pv), reads=[("bank", bank)], writes=list(wk))

    def norm_T(src, gB, gkey, dstT, bank, rk, wk):
        c = stcol()
        P.op("act", lambda e: e.activation(out=junk, in_=src, func=AF.Square, accum_out=st[:, c:c + 1]),
             reads=list(rk), writes=["junk", ("st", c)])
        r = rstd_col(c, 1, 1.0 / D)
        P.op("dve", lambda e: e.scalar_tensor_tensor(out=hb, in0=src, scalar=st[:, r:r + 1], in1=gB,
                                                     op0=ALU.mult, op1=ALU.mult),
             reads=list(rk) + [("st", r), gkey], writes=["hb"])
        transposes(hb, 8, dstT, bank, ["hb"], wk)

    hcT = C.sb([128, 2, 32 + TOK], BF16, "C_hcT")
    Dg = C.sb([128, 2, 31, 128], BF16, "C_Dg")
    cw = C.sb([128, 2, 36], F32, "C_cw")
    cT = C.sb([128, 2, TOK], F32, "C_cT")
    lngB = C.sb([128, 256], F32, "C_lng")
    lnbB = C.sb([128, 256], F32, "C_lnb")
    gbrB = C.sb([128, 512], F32, "C_gbrB")
    gbrT = C.sb([128, 8], F32, "C_gbrT")
    pw2 = C.sb([128, 2, 256], BF16, "C_pw2")
    gluw = C.sb([128, 2, 512], BF16, "C_gluw")
    wout = C.sb([128, 8, D], BF16, "C_wout")
    yT = C.sb([128, 2, TOK], BF16, "C_yT")
    oT = [C.sb([128, 4, 128], F32, "C_oT%d" % i) for i in range(2)]
    osq = C.sb([128, 4, 128], F32, "C_osq")
    og = C.sb([128, 4, 128], BF16, "C_og")
    yn = C.sb([128, 256], F32, "C_yn")
    ge = C.sb([128, 256], F32, "C_ge")
    sw = C.sb([128, 256], BF16, "C_sw")
    swT = C.sb([128, 2, 128], BF16, "C_swT")
    osm = C.sb([128, 256], F32, "C_osm")
    mixed = C.sb([128, 512], BF16, "C_mixed")

    P.op("sp", lambda e: e.dma_start(out=hcT[:, :, 0:32], in_=halo.rearrange("(c p) t -> p c t", p=128)),
         writes=["hcT_h"], dma=True)
    P.op("sp", lambda e: e.dma_start(out=hcT[:, :, 32:], in_=hc_loc.rearrange("(c p) t -> p c t", p=128)),
         writes=["hcT"], dma=True)
    P.op("sp", lambda e: e.dma_start(out=cw[:, :, 0:31], in_=io["conv_dw_wT"].rearrange("(c p) t -> p c t", p=128)),
         writes=["cw"], dma=True)
    for ct in range(2):
        P.op("sp", lambda e, ct=ct: e.dma_start(out=cw[:, ct, 31:32], in_=io["conv_dw_b"][ct * 128:(ct + 1) * 128, :]),
             writes=[("cwb", ct)], dma=True)
    P.op("sp", lambda e: e.dma_start(out=lngB, in_=bcast(io["conv_ln_g"], 256)), writes=["lng"], dma=True)
    P.op("sp", lambda e: e.dma_start(out=lnbB, in_=bcast(io["conv_ln_b"], 256)), writes=["lnb"], dma=True)
    P.op("sp", lambda e: e.dma_start(out=gbrB, in_=bcast(io["branch_g"][:, 512:1024], 512)), writes=["gbrB"], dma=True)
    P.op("sp", lambda e: e.dma_start(out=gbrT, in_=io["branch_gT"]), writes=["gbrT"], dma=True)
    load_w(pw2, io["conv_pw2_w"], 2, "pw2")
    load_w(gluw, io["ssm_glu_w"], 2, "gluw")
    kw_out = load_w(wout, io["w_out"], 8, "wout")
    for j in range(8):
        P.op("pool", lambda e, j=j: e.dma_start(out=yT[(j % 4) * 32:(j % 4) * 32 + 32, j // 4, :], in_=recv_o[j, 64:96, :]),
             writes=[("yT", j)], dma=True)
    for ct in range(2):
        for j in range(31):
            eng = "dve" if (j % 2) else "pool"
            P.op(eng, lambda e, ct=ct, j=j: e.tensor_scalar(out=Dg[:, ct, j, :], in0=idF, scalar1=cw[:, ct, j:j + 1],
                                                           scalar2=None, op0=ALU.mult),
                 reads=["idF", "cw"], writes=[("Dg", ct, j)])
    nb = [0]
    for ct in range(2):
        for ch in range(4):
            bk = nb[0] % 2
            nb[0] += 1
            ps = C.banks[bk]
            for j in range(31):
                c0 = 512 * ch + j + 2
                P.op("pe", lambda e, ct=ct, j=j, c0=c0, ps=ps: e.matmul(
                    ps, lhsT=Dg[:, ct, j, :], rhs=hcT[:, ct, c0:c0 + 512], start=(j == 0), stop=(j == 30)),
                    reads=[("Dg", ct, j), "hcT", "hcT_h"], writes=[("bank", bk)])
            P.op("act", lambda e, ct=ct, ch=ch, ps=ps: e.activation(
                out=cT[:, ct, ch * 512:(ch + 1) * 512], in_=ps, func=AF.Identity, bias=cw[:, ct, 31:32]),
                reads=[("bank", bk), ("cwb", ct)], writes=[("cT", ct, ch)])

    def load_o(t):
        par = t % 2
        for k in range(4):
            for hh in range(2):
                P.op("sp", lambda e, k=k, hh=hh, par=par, t=t: e.dma_start(
                    out=oT[par][hh * 64:hh * 64 + 64, k, :], in_=recv_o[2 * k + hh, 0:64, t * 128:(t + 1) * 128]),
                    writes=[("oT", par)], dma=True)

    load_o(0)
    for t in range(NT):
        par = t % 2
        if t + 1 < NT:
            load_o(t + 1)
        tc = slice(t * 128, (t + 1) * 128)
        pc = C.banks[2][:, 0:256]
        for ct in range(2):
            P.op("pe", lambda e, ct=ct, tc=tc: e.transpose(out=pc[:, ct * 128:(ct + 1) * 128], in_=cT[:, ct, tc], identity=idF),
                 reads=[("cT", ct, t // 4), "idF"], writes=[("bank", 2)])
        c_s = stcol()
        P.op("dve", lambda e, c_s=c_s: e.tensor_reduce(out=st[:, c_s:c_s + 1], in_=pc, axis=AX.X, op=ALU.add),
             reads=[("bank", 2)], writes=[("st", c_s)])
        c_m = stcol()
        P.op("dve", lambda e, c_s=c_s, c_m=c_m: e.tensor_scalar(out=st[:, c_m:c_m + 1], in0=st[:, c_s:c_s + 1],
                                                              scalar1=-1.0 / 256, scalar2=None, op0=ALU.mult),
             reads=[("st", c_s)], writes=[("st", c_m)])
        c_v = stcol()
        P.op("act", lambda e, c_m=c_m, c_v=c_v: e.activation(out=junk[:, 0:256], in_=pc, func=AF.Square,
                                                            bias=st[:, c_m:c_m + 1], accum_out=st[:, c_v:c_v + 1]),
             reads=[("bank", 2), ("st", c_m)], writes=["junk", ("st", c_v)])
        c_r = rstd_col(c_v, 1, 1.0 / 256)
        P.op("dve", lambda e, c_m=c_m, c_r=c_r: e.tensor_scalar(out=yn, in0=pc, scalar1=st[:, c_m:c_m + 1],
                                                              scalar2=st[:, c_r:c_r + 1], op0=ALU.add, op1=ALU.mult),
             reads=[("bank", 2), ("st", c_m), ("st", c_r)], writes=["yn"])
        P.op("dve", lambda e: e.tensor_tensor(out=yn, in0=yn, in1=lngB, op=ALU.mult), reads=["yn", "lng"], writes=["yn"])
        P.op("dve", lambda e: e.tensor_tensor(out=yn, in0=yn, in1=lnbB, op=ALU.add), reads=["yn", "lnb"], writes=["yn"])
        P.op("act", lambda e: e.activation(out=ge, in_=yn, func=AF.Exp, scale=-1.0), reads=["yn"], writes=["ge"])
        P.op("dve", lambda e: e.tensor_scalar(out=ge, in0=ge, scalar1=1.0, scalar2=None, op0=ALU.add), reads=["ge"], writes=["ge"])
        P.op("dve", lambda e: e.reciprocal(out=ge, in_=ge), reads=["ge"], writes=["ge"])
        P.op("dve", lambda e: e.tensor_tensor(out=sw, in0=yn, in1=ge, op=ALU.mult), reads=["yn", "ge"], writes=["sw"])
        transposes(sw, 2, swT, 3, ["sw"], ["swT"])
        po = C.banks[2][:, 256:512]
        for ct in range(2):
            P.op("pe", lambda e, ct=ct: e.matmul(po, lhsT=swT[:, ct, :], rhs=pw2[:, ct, :], start=(ct == 0), stop=(ct == 1)),
                 reads=["swT", ("pw2", ct)], writes=[("bank", 2)])
        c_q = stcol()
        P.op("act", lambda e, c_q=c_q: e.activation(out=junk[:, 0:256], in_=po, func=AF.Square, accum_out=st[:, c_q:c_q + 1]),
             reads=[("bank", 2)], writes=["junk", ("st", c_q)])
        c_r2 = rstd_col(c_q, 1, 1.0 / 256)
        P.op("dve", lambda e, c_r2=c_r2: e.scalar_tensor_tensor(out=mixed[:, 0:256], in0=po, scalar=st[:, c_r2:c_r2 + 1],
                                                              in1=gbrB[:, 0:256], op0=ALU.mult, op1=ALU.mult),
             reads=[("bank", 2), ("st", c_r2), "gbrB"], writes=["mixed0"])
        pg = C.banks[3]
        for ct in range(2):
            P.op("pe", lambda e, ct=ct, tc=tc: e.matmul(pg, lhsT=yT[:, ct, tc], rhs=gluw[:, ct, :], start=(ct == 0), stop=(ct == 1)),
                 reads=[("yT", j) for j in range(8)] + [("gluw", ct)], writes=[("bank", 3)])
        P.op("act", lambda e: e.activation(out=ge, in_=pg[:, 256:512], func=AF.Exp, scale=-1.0), reads=[("bank", 3)], writes=["ge"])
        P.op("dve", lambda e: e.tensor_scalar(out=ge, in0=ge, scalar1=1.0, scalar2=None, op0=ALU.add), reads=["ge"], writes=["ge"])
        P.op("dve", lambda e: e.reciprocal(out=ge, in_=ge), reads=["ge"], writes=["ge"])
        P.op("dve", lambda e: e.tensor_tensor(out=osm, in0=pg[:, 0:256], in1=ge, op=ALU.mult), reads=[("bank", 3), "ge"], writes=["osm"])
        c_q3 = stcol()
        P.op("act", lambda e, c_q3=c_q3: e.activation(out=junk[:, 0:256], in_=osm, func=AF.Square, accum_out=st[:, c_q3:c_q3 + 1]),
             reads=["osm"], writes=["junk", ("st", c_q3)])
        c_r3 = rstd_col(c_q3, 1, 1.0 / 256)
        P.op("dve", lambda e, c_r3=c_r3: e.scalar_tensor_tensor(out=mixed[:, 256:512], in0=osm, scalar=st[:, c_r3:c_r3 + 1],
                                                              in1=gbrB[:, 256:512], op0=ALU.mult, op1=ALU.mult),
             reads=["osm", ("st", c_r3), "gbrB"], writes=["mixed1"])
        mT = tT[0]
        transposes(mixed, 4, mT[:, 0:4, :], 3, ["mixed0", "mixed1"], ["mT"])
        P.op("act", lambda e, par=par: e.activation(out=osq, in_=oT[par], func=AF.Square), reads=[("oT", par)], writes=["osq"])
        pss = C.banks[2][:, 0:1]
        for k in range(4):
            P.op("pe", lambda e, k=k: e.matmul(pss, lhsT=osq[:, k, :], rhs=onesF, start=(k == 0), stop=(k == 3)),
                 reads=["osq", "ones"], writes=[("bank", 2)])
        c_q1 = stcol()
        P.op("dve", lambda e, c_q1=c_q1: e.tensor_copy(out=st[:, c_q1:c_q1 + 1], in_=pss), reads=[("bank", 2)], writes=[("st", c_q1)])
        c_r1 = rstd_col(c_q1, 1, 1.0 / 512)
        P.op("dve", lambda e, par=par: e.tensor_tensor(out=og, in0=oT[par], in1=gbrT[:, 0:4].unsqueeze(2).to_broadcast([128, 4, 128]),
                                                      op=ALU.mult), reads=[("oT", par), "gbrT"], writes=["og"])
        for n in range(2):
            p1 = C.banks[4 + n]
            p2 = C.banks[6 + n]
            for k in range(4):
                P.op("pe", lambda e, k=k, n=n, p1=p1: e.matmul(p1, lhsT=og[:, k, :], rhs=wout[:, k, n * 512:(n + 1) * 512],
                                                               start=(k == 0), stop=(k == 3)),
                     reads=["og", ("wout", k)], writes=[("bank", 4 + n)])
            for k in range(4):
                P.op("pe", lambda e, k=k, n=n, p2=p2: e.matmul(p2, lhsT=mT[:, k, :], rhs=wout[:, 4 + k, n * 512:(n + 1) * 512],
                                                               start=(k == 0), stop=(k == 3)),
                     reads=["mT", ("wout", 4 + k)], writes=[("bank", 6 + n)])
            xs = xres[:, t, n * 512:(n + 1) * 512]
            P.op("dve", lambda e, xs=xs, p2=p2: e.tensor_tensor(out=xs, in0=xs, in1=p2, op=ALU.add),
                 reads=[("bank", 6 + n), ("x", t)], writes=[("x", t)])
            P.op("dve", lambda e, xs=xs, p1=p1, c_r1=c_r1: e.scalar_tensor_tensor(
                out=xs, in0=p1, scalar=st[:, c_r1:c_r1 + 1], in1=xs, op0=ALU.mult, op1=ALU.add),
                reads=[("bank", 4 + n), ("st", c_r1), ("x", t)], writes=[("x", t)])

    P.barrier()
    C.release(base_mark)
    wq = C.sb([128, 8, D], BF16, "C_wq")
    wo = C.sb([128, 8, D], BF16, "C_wo")
    kTp = C.sb([128, 8, 256], BF16, "C_kTp")
    Vm = C.sb([128, 2, D], BF16, "C_V")
    gxaB = C.sb([128, D], F32, "C_gxa")
    gkq = C.sb([128, 4, 256], F32, "C_gkq")
    qb = C.sb([128, D], BF16, "C_qb")
    Pb = C.sb([128, 4, 256], BF16, "C_Pb")
    PT = C.sb([128, 8, 128], BF16, "C_PT")
    ob = C.sb([128, D], BF16, "C_ob")
    s2_mark = C.mark()
    wk_ = C.sb([128, 8, D], BF16, "C_wk")
    wv_ = C.sb([128, 8, D], BF16, "C_wv")
    gmB = C.sb([128, D], F32, "C_gm")
    memt = C.sb([128, D], F32, "C_memt")
    hmT = C.sb([128, 8, 256], BF16, "C_hmT")
    kf = C.sb([128, D], F32, "C_kf")
    gq4 = C.sb([128, 256], F32, "C_gq4")

    kwq = load_w(wq, io["xa_wq"], 8, "wq")
    kwk = load_w(wk_, io["xa_wk"], 8, "wk")
    kwv = load_w(wv_, io["xa_wv"], 8, "wv")
    kwo = load_w(wo, io["xa_wo"], 8, "wo")
    P.op("sp", lambda e: e.dma_start(out=gxaB, in_=bcast(io["norm_xa_g"], D)), writes=["gxa"], dma=True)
    P.op("sp", lambda e: e.dma_start(out=gmB, in_=bcast(io["norm_mem_g"], D)), writes=["gm"], dma=True)
    P.op("sp", lambda e: e.dma_start(out=gkq[:, 0, :], in_=bcast(io["xa_k_g"], 256)), writes=["gkq"], dma=True)
    P.op("sp", lambda e: e.dma_start(out=gq4, in_=bcast(io["xa_q_g"], 256)), writes=["gq4"], dma=True)
    P.op("dve", lambda e: e.scalar_tensor_tensor(out=gkq[:, 0, :], in0=gkq[:, 0, :], scalar=1.0 / 16, in1=gq4,
                                                 op0=ALU.mult, op1=ALU.mult), reads=["gkq", "gq4"], writes=["gkq"])
    for h in range(1, 4):
        P.op("dve", lambda e, h=h: e.tensor_copy(out=gkq[:, h, :], in_=gkq[:, 0, :]), reads=["gkq"], writes=[("gkq", h)])
    gkq_keys = ["gkq"] + [("gkq", h) for h in range(1, 4)]
    for mt in range(2):
        P.op("sp", lambda e, mt=mt: e.dma_start(out=memt, in_=mem[mt * 128:(mt + 1) * 128, :]), writes=["memt"], dma=True)
        norm_T(memt, gmB, "gm", tT[0], 4, ["memt"], ["tT0"])
        P.op("act", lambda e, mt=mt: e.copy(out=hmT[:, :, mt * 128:(mt + 1) * 128], in_=tT[0]), reads=["tT0"], writes=[("hmT", mt)])
        for n in range(2):
            pk = C.banks[n]
            for k in range(8):
                P.op("pe", lambda e, k=k, n=n, pk=pk: e.matmul(pk, lhsT=tT[0][:, k, :], rhs=wk_[:, k, n * 512:(n + 1) * 512],
                                                               start=(k == 0), stop=(k == 7)),
                     reads=["tT0", ("wk", k)], writes=[("bank", n)])
            P.op("act", lambda e, n=n, pk=pk: e.copy(out=kf[:, n * 512:(n + 1) * 512], in_=pk), reads=[("bank", n)], writes=[("kf", n)])
        c_k = stcol(4)
        for h in range(4):
            P.op("act", lambda e, h=h, c_k=c_k: e.activation(out=junk[:, 0:256], in_=kf[:, h * 256:(h + 1) * 256], func=AF.Square,
                                                            accum_out=st[:, c_k + h:c_k + h + 1]),
                 reads=[("kf", h // 2)], writes=["junk", ("st", c_k + h)])
        outc = rstd_col(c_k, 4, 1.0 / 256)
        P.op("dve", lambda e, outc=outc: e.tensor_tensor(
            out=kf.rearrange("p (h d) -> p h d", h=4), in0=kf.rearrange("p (h d) -> p h d", h=4),
            in1=st[:, outc:outc + 4].unsqueeze(2).to_broadcast([128, 4, 256]), op=ALU.mult),
            reads=[("kf", 0), ("kf", 1)] + sk(outc, 4), writes=[("kf", 0), ("kf", 1)])
        P.op("dve", lambda e: e.tensor_tensor(out=hb, in0=kf, in1=gkq.rearrange("p h d -> p (h d)"), op=ALU.mult),
             reads=[("kf", 0), ("kf", 1)] + gkq_keys, writes=["hb"])
        transposes(hb, 8, kTp[:, :, mt * 128:(mt + 1) * 128], 5, ["hb"], [("kTp", mt)])
        for n in range(2):
            pv_ = C.banks[2 + n]
            for k in range(8):
                P.op("pe", lambda e, k=k, n=n, pv_=pv_: e.matmul(pv_, lhsT=tT[0][:, k, :], rhs=wv_[:, k, n * 512:(n + 1) * 512],
                                                                 start=(k == 0), stop=(k == 7)),
                     reads=["tT0", ("wv", k)], writes=[("bank", 2 + n)])
            P.op("act", lambda e, n=n, mt=mt, pv_=pv_: e.copy(out=Vm[:, mt, n * 512:(n + 1) * 512], in_=pv_),
                 reads=[("bank", 2 + n)], writes=[("V", mt)])

    for t in range(NT):
        xt_ = xres[:, t, :]
        norm_T(xt_, gxaB, "gxa", tT[1], 4, [("x", t)], ["tT1"])
        for n in range(2):
            pq = C.banks[n]
            for k in range(8):
                P.op("pe", lambda e, k=k, n=n, pq=pq: e.matmul(pq, lhsT=tT[1][:, k, :], rhs=wq[:, k, n * 512:(n + 1) * 512],
                                                               start=(k == 0), stop=(k == 7)),
                     reads=["tT1", ("wq", k)], writes=[("bank", n)])
        c_q = stcol(4)
        for h in range(4):
            pqh = C.banks[h // 2][:, (h % 2) * 256:(h % 2) * 256 + 256]
            P.op("act", lambda e, h=h, c_q=c_q, pqh=pqh: e.activation(out=junk[:, 0:256], in_=pqh, func=AF.Square,
                                                                    accum_out=st[:, c_q + h:c_q + h + 1]),
                 reads=[("bank", h // 2)], writes=["junk", ("st", c_q + h)])
        rq = rstd_col(c_q, 4, 1.0 / 256)
        for n in range(2):
            P.op("act", lambda e, n=n: e.copy(out=qb[:, n * 512:(n + 1) * 512], in_=C.banks[n]), reads=[("bank", n)], writes=[("qb", n)])
        transposes(qb, 8, tT[0], 5, [("qb", 0), ("qb", 1)], ["tT0"])
        for h in range(4):
            psc = C.banks[2 + h // 2][:, (h % 2) * 256:(h % 2) * 256 + 256]
            for j in range(2):
                P.op("pe", lambda e, h=h, j=j, psc=psc: e.matmul(psc, lhsT=tT[0][:, 2 * h + j, :], rhs=kTp[:, 2 * h + j, :],
                                                                 start=(j == 0), stop=(j == 1)),
                     reads=["tT0", ("kTp", 0), ("kTp", 1)], writes=[("bank", 2 + h // 2)])
        c_mx = stcol(4)
        for b2 in range(2):
            P.op("dve", lambda e, b2=b2, c_mx=c_mx: e.tensor_reduce(
                out=st[:, c_mx + 2 * b2:c_mx + 2 * b2 + 2], in_=C.banks[2 + b2].rearrange("p (h m) -> p h m", h=2),
                axis=AX.X, op=ALU.max), reads=[("bank", 2 + b2)], writes=sk(c_mx + 2 * b2, 2))
        c_nb = stcol(4)
        P.op("dve", lambda e, c_mx=c_mx, c_nb=c_nb, rq=rq: e.scalar_tensor_tensor(
            out=st[:, c_nb:c_nb + 4], in0=st[:, c_mx:c_mx + 4], scalar=-1.0, in1=st[:, rq:rq + 4], op0=ALU.mult, op1=ALU.mult),
            reads=sk(c_mx, 4) + sk(rq, 4), writes=sk(c_nb, 4))
        c_rs = stcol(4)
        for h in range(4):
            psc = C.banks[2 + h // 2][:, (h % 2) * 256:(h % 2) * 256 + 256]
            P.op("act", lambda e, h=h, psc=psc, rq=rq, c_nb=c_nb, c_rs=c_rs: e.activation(
                out=Pb[:, h, :], in_=psc, func=AF.Exp, scale=st[:, rq + h:rq + h + 1], bias=st[:, c_nb + h:c_nb + h + 1],
                accum_out=st[:, c_rs + h:c_rs + h + 1]),
                reads=[("bank", 2 + h // 2), ("st", rq + h), ("st", c_nb + h)], writes=[("Pb", h), ("st", c_rs + h)])
        c_ri = stcol(4)
        P.op("dve", lambda e, c_rs=c_rs, c_ri=c_ri: e.reciprocal(out=st[:, c_ri:c_ri + 4], in_=st[:, c_rs:c_rs + 4]),
             reads=sk(c_rs, 4), writes=sk(c_ri, 4))
        transposes(Pb.rearrange("p h m -> p (h m)"), 8, PT, 6, [("Pb", h) for h in range(4)], ["PT"])
        for h in range(4):
            po_ = C.banks[h // 2][:, (h % 2) * 256:(h % 2) * 256 + 256]
            for mt in range(2):
                P.op("pe", lambda e, h=h, mt=mt, po_=po_: e.matmul(po_, lhsT=PT[:, 2 * h + mt, :], rhs=Vm[:, mt, h * 256:(h + 1) * 256],
                                                                   start=(mt == 0), stop=(mt == 1)),
                     reads=["PT", ("V", mt)], writes=[("bank", h // 2)])
        for n in range(2):
            P.op("dve", lambda e, n=n, c_ri=c_ri: e.tensor_tensor(
                out=ob[:, n * 512:(n + 1) * 512].rearrange("p (h d) -> p h d", h=2),
                in0=C.banks[n].rearrange("p (h d) -> p h d", h=2),
                in1=st[:, c_ri + 2 * n:c_ri + 2 * n + 2].unsqueeze(2).to_broadcast([128, 2, 256]), op=ALU.mult),
                reads=[("bank", n)] + sk(c_ri + 2 * n, 2), writes=[("ob", n)])
        transposes(ob, 8, tT[1], 7, [("ob", 0), ("ob", 1)], ["tT1"])
        for n in range(2):
            pw_ = C.banks[2 + n]
            for k in range(8):
                P.op("pe", lambda e, k=k, n=n, pw_=pw_: e.matmul(pw_, lhsT=tT[1][:, k, :], rhs=wo[:, k, n * 512:(n + 1) * 512],
                                                                 start=(k == 0), stop=(k == 7)),
                     reads=["tT1", ("wo", k)], writes=[("bank", 2 + n)])
            xs = xres[:, t, n * 512:(n + 1) * 512]
            P.op("dve", lambda e, xs=xs, pw_=pw_: e.tensor_tensor(out=xs, in0=xs, in1=pw_, op=ALU.add),
                 reads=[("bank", 2 + n), ("x", t)], writes=[("x", t)])

    P.barrier()
    C.release(base_mark)
    gfB = C.sb([128, D], F32, "C_gf")
    hfT = C.sb([128, 8, 1024], BF16, "C_hfT")
    hid = C.sb([128, 22, 1024], BF16, "C_hid")
    Wo_ = C.sb([128, 22, D], BF16, "C_Wo")
    Wg = [C.sb([128, 8, 128], BF16, "C_Wg%d" % i) for i in range(2)]
    Wu = [C.sb([128, 8, 128], BF16, "C_Wu%d" % i) for i in range(2)]
    sg = [C.sb([128, 512], F32, "C_sg%d" % i) for i in range(2)]
    P.op("sp", lambda e: e.dma_start(out=gfB, in_=bcast(io["norm_ffn_g"], D)), writes=["gf"], dma=True)
    kWo = load_w(Wo_, io["ffn_w_out"], 22, "Wo")
    w_in_v = io["ffn_w_in"].rearrange("(k p) n -> p k n", p=128)

    def load_gu(j):
        par = j % 2
        P.op("pool", lambda e, j=j, par=par: e.dma_start(out=Wg[par], in_=w_in_v[:, :, j * 128:(j + 1) * 128]),
             writes=[("Wg", par)], dma=True)
        P.op("pool", lambda e, j=j, par=par: e.dma_start(out=Wu[par], in_=w_in_v[:, :, FFN_H + j * 128:FFN_H + (j + 1) * 128]),
             writes=[("Wu", par)], dma=True)

    nsg = [0]
    for half in range(2):
        for tt in range(8):
            t = half * 8 + tt
            norm_T(xres[:, t, :], gfB, "gf", hfT[:, :, tt * 128:(tt + 1) * 128], 4 + tt % 2, [("x", t)], [("hfT", tt)])
        hkeys = [("hfT", tt) for tt in range(8)]
        load_gu(0)
        for j in range(22):
            par = j % 2
            if j + 1 < 22:
                load_gu(j + 1)
            for tc2 in range(2):
                pg_ = C.banks[0 + tc2]
                pu_ = C.banks[2 + tc2]
                for k in range(8):
                    P.op("pe", lambda e, k=k, par=par, tc2=tc2, pg_=pg_: e.matmul(
                        pg_, lhsT=Wg[par][:, k, :], rhs=hfT[:, k, tc2 * 512:(tc2 + 1) * 512], start=(k == 0), stop=(k == 7)),
                        reads=[("Wg", par)] + hkeys[tc2 * 4:tc2 * 4 + 4], writes=[("bank", tc2)])
                for k in range(8):
                    P.op("pe", lambda e, k=k, par=par, tc2=tc2, pu_=pu_: e.matmul(
                        pu_, lhsT=Wu[par][:, k, :], rhs=hfT[:, k, tc2 * 512:(tc2 + 1) * 512], start=(k == 0), stop=(k == 7)),
                        reads=[("Wu", par)] + hkeys[tc2 * 4:tc2 * 4 + 4], writes=[("bank", 2 + tc2)])
                sp_ = nsg[0] % 2
                nsg[0] += 1
                P.op("act", lambda e, sp_=sp_, pg_=pg_: e.activation(out=sg[sp_], in_=pg_, func=AF.Silu),
                     reads=[("bank", tc2)], writes=[("sg", sp_)])
                P.op("dve", lambda e, sp_=sp_, pu_=pu_, j=j, tc2=tc2: e.tensor_tensor(
                    out=hid[:, j, tc2 * 512:(tc2 + 1) * 512], in0=sg[sp_], in1=pu_, op=ALU.mult),
                    reads=[("sg", sp_), ("bank", 2 + tc2)], writes=[("hid", j, tc2)])
        for tt in range(8):
            t = half * 8 + tt
            for n in range(2):
                bk = 4 + (2 * tt + n) % 4
                pd = C.banks[bk]
                for j in range(22):
                    P.op("pe", lambda e, j=j, n=n, tt=tt, pd=pd: e.matmul(
                        pd, lhsT=hid[:, j, tt * 128:(tt + 1) * 128], rhs=Wo_[:, j, n * 512:(n + 1) * 512],
                        start=(j == 0), stop=(j == 21)),
                        reads=[("hid", j, tt // 4), ("Wo", j)], writes=[("bank", bk)])
                xs = xres[:, t, n * 512:(n + 1) * 512]
                P.op("dve", lambda e, xs=xs, pd=pd: e.tensor_tensor(out=xs, in0=xs, in1=pd, op=ALU.add),
                     reads=[("bank", bk), ("x", t)], writes=[("x", t)])
            P.op("sp", lambda e, t=t: e.dma_start(out=x_out[t * 128:(t + 1) * 128, :], in_=xres[:, t, :]),
                 reads=[("x", t)], dma=True)


C_IN = [("x", [TOK, D], F32), ("recv_o", [8, 96, TOK], F32), ("hc_loc", [256, TOK], BF16), ("halo", [256, 32], BF16),
        ("mem", [256, D], F32), ("ident", [128, 128], F32),
        ("conv_dw_wT", [256, 31], F32), ("conv_dw_b", [256, 1], F32), ("conv_ln_g", [1, 256], F32), ("conv_ln_b", [1, 256], F32),
        ("conv_pw2_w", [256, 256], F32), ("ssm_glu_w", [256, 512], F32), ("branch_g", [1, D], F32), ("branch_gT", [128, 8], F32),
        ("w_out", [D, D], F32), ("norm_xa_g", [1, D], F32), ("norm_mem_g", [1, D], F32),
        ("xa_wq", [D, D], F32), ("xa_wk", [D, D], F32), ("xa_wv", [D, D], F32), ("xa_wo", [D, D], F32),
        ("xa_q_g", [1, 256], F32), ("xa_k_g", [1, 256], F32), ("norm_ffn_g", [1, D], F32),
        ("ffn_w_in", [D, 2 * FFN_H], F32), ("ffn_w_out", [FFN_H, D], F32)]


def build_C():
    nc = bass.Bass("TRN2", target_bir_lowering=False)
    es = ExitStack()
    C = Ctx(nc, es)
    io = {n: C.din(n, s, d) for n, s, d in C_IN}
    io["x_out"] = C.dout("x_out", [TOK, D], F32)
    phase_C(C, io)
    C.P.emit(es)
    es.close()
    return nc


_PROGS = {}


def _prog(name, fn):
    if name not in _PROGS:
        _PROGS[name] = fn()
    return _PROGS[name]


def _run(nc, in_maps):
    res = run_bass_kernel_spmd(nc, in_maps, core_ids=list(range(NCORES)))
    return res.results


def _c_weights(w, l):
    f = np.ascontiguousarray
    return dict(
        conv_dw_wT=f(w["conv_dw_w"][l].T), conv_dw_b=f(w["conv_dw_b"][l][:, None]),
        conv_ln_g=f(w["conv_ln_g"][l][None]), conv_ln_b=f(w["conv_ln_b"][l][None]),
        conv_pw2_w=f(w["conv_pw2_w"][l]), ssm_glu_w=f(w["ssm_glu_w"][l]),
        branch_g=f(w["branch_norm_g"][l][None]), branch_gT=f(w["branch_norm_g"][l].reshape(8, 128).T),
        w_out=f(w["w_out"][l]), norm_xa_g=f(w["norm_xa_g"][l][None]), norm_mem_g=f(w["norm_mem_g"][l][None]),
        xa_wq=f(w["xa_wq"][l]), xa_wk=f(w["xa_wk"][l]), xa_wv=f(w["xa_wv"][l]), xa_wo=f(w["xa_wo"][l]),
        xa_q_g=f(w["xa_q_norm_g"][l][None]), xa_k_g=f(w["xa_k_norm_g"][l][None]),
        norm_ffn_g=f(w["norm_ffn_g"][l][None]), ffn_w_in=f(w["ffn_w_in"][l]), ffn_w_out=f(w["ffn_w_out"][l]))


def kernel(**inputs):
    w = {k: np.asarray(v, dtype=np.float32) for k, v in inputs.items()}
    x = w["x"]
    mem = w["mem"]
    bsz, L, _ = x.shape
    assert (bsz, L) == (2, SEQ)
    ident = np.eye(128, dtype=np.float32)
    consts = attn_consts()
    xs = [np.ascontiguousarray(x.reshape(NCORES, TOK, D)[c]) for c in range(NCORES)]
    ncA = _prog("A", build_A)
    ncB = _prog("B", build_B)
    ncC = _prog("C", build_C)
    for l in range(2):
        ra = _run(ncA, [dict(x=xs[c], w_in=w["w_in"][l], g_mix=w["norm_mix_g"][l][None], gq=w["sb_q_norm_g"][l][None],
                             gk=w["sb_k_norm_g"][l][None], ident=ident) for c in range(NCORES)])
        send_bf = [np.asarray(ra[c]["send_bf"]) for c in range(NCORES)]
        send_u = [np.asarray(ra[c]["send_u"]) for c in range(NCORES)]
        hc_loc = [np.asarray(ra[c]["hc_loc"]) for c in range(NCORES)]
        in_b = []
        for j in range(NCORES):
            in_b.append(dict(
                recv_bf=np.ascontiguousarray(np.stack([send_bf[c][j] for c in range(NCORES)])),
                recv_u=np.ascontiguousarray(np.stack([send_u[c][j] for c in range(NCORES)])),
                consts_bf=consts,
                ssm_par=ssm_pack(w["ssm_lam_re"][l], w["ssm_lam_im"][l], w["ssm_log_dt"][l], w["ssm_b_re"][l],
                                 w["ssm_b_im"][l], w["ssm_c_re"][l], w["ssm_c_im"][l], w["ssm_d"][l], j)))
        rb = _run(ncB, in_b)
        send_o = [np.asarray(rb[j]["send_o"]) for j in range(NCORES)]
        cw = _c_weights(w, l)
        in_c = []
        for c in range(NCORES):
            if c % 4 == 0:
                halo = np.zeros((256, 32), dtype=send_bf[0].dtype)
            else:
                halo = np.ascontiguousarray(send_bf[c - 1][0][HALO_OFF:].reshape(256, 32))
            m = dict(x=xs[c], recv_o=np.ascontiguousarray(np.stack([send_o[j][c] for j in range(NCORES)])),
                     hc_loc=hc_loc[c], halo=halo, mem=np.ascontiguousarray(mem[c // 4]), ident=ident)
            m.update(cw)
            in_c.append(m)
        rc = _run(ncC, in_c)
        xs = [np.asarray(rc[c]["x_out"]) for c in range(NCORES)]
    return np.stack(xs).reshape(bsz, L, D).astype(np.float32)
```

```python
import numpy as np
from contextlib import ExitStack
import concourse.bass as bass
import concourse.mybir as mybir
from concourse.bass_utils import run_bass_kernel_spmd

F32 = mybir.dt.float32
BF16 = mybir.dt.bfloat16
AF = mybir.ActivationFunctionType
ALU = mybir.AluOpType
AX = mybir.AxisListType

NCORES = 8
D = 1024
TOK = 2048
NT = TOK // 128
SEQ = 8192
NINP = 2304
FFN_H = 2816
EPS = 1e-6
QKV_SZ = 64 * TOK
HALO_OFF = 3 * QKV_SZ
SB_SZ = HALO_OFF + 256 * 32
NDMA_SEM = 12


class _Op:
    __slots__ = ("eng", "fn", "deps", "dma", "signal", "seq", "dsem", "dval", "qi")

    def __init__(self, eng, fn, dma):
        self.eng = eng
        self.fn = fn
        self.deps = []
        self.dma = dma
        self.signal = False
        self.seq = 0
        self.dsem = None
        self.dval = 0
        self.qi = 0


class Prog:
    ENGS = ("pe", "act", "dve", "pool", "sp")

    def __init__(self, nc):
        self.nc = nc
        self.ops = {e: [] for e in self.ENGS}
        self.last_w = {}
        self.readers = {}
        self.ndma = {e: 0 for e in self.ENGS}
        self.bar = []
        self._pid = {}

    def op(self, eng, fn, reads=(), writes=(), dma=False):
        o = _Op(eng, fn, dma)
        if dma:
            o.qi = self.ndma[eng]
            self.ndma[eng] += 1

        def add(p, kind):
            if p is o:
                return
            if not p.dma and p.eng == eng and not dma:
                if eng == "pe":
                    return
            if p not in o.deps:
                o.deps.append(p)
                p.signal = True

        for p in self.bar:
            add(p, "bar")
        for k in reads:
            p = self.last_w.get(k)
            if p is not None:
                add(p, "raw")
        for k in writes:
            p = self.last_w.get(k)
            if p is not None:
                add(p, "waw")
            for r in self.readers.get(k, ()):
                add(r, "war")
        for k in reads:
            self.readers.setdefault(k, []).append(o)
        for k in writes:
            self.last_w[k] = o
            self.readers[k] = []
        self.ops[eng].append(o)
        return o

    def pid(self, eng):
        k = id(eng)
        if k not in self._pid:
            self._pid[k] = eng.partition_id() % NCORES
        return self._pid[k]

    def barrier(self):
        self.bar = [self.ops[e][-1] for e in self.ENGS if self.ops[e]]
        for e in self.ENGS:
            for o in self.ops[e][-NDMA_SEM:]:
                if o.dma and o not in self.bar:
                    self.bar.append(o)
        self.last_w = {}
        self.readers = {}

    def emit(self, es):
        nc = self.nc
        esem = {e: es.enter_context(nc.semaphore("s_" + e)) for e in self.ENGS}
        dsems = {}
        for e in self.ENGS:
            if self.ndma[e]:
                dsems[e] = [es.enter_context(nc.semaphore("d_%s%d" % (e, i)))
                            for i in range(min(NDMA_SEM, self.ndma[e]))]
        for e in self.ENGS:
            n = 0
            for o in self.ops[e]:
                if o.dma:
                    o.dsem = dsems[e][o.qi % NDMA_SEM]
                    o.dval = 16 * (o.qi // NDMA_SEM + 1)
                elif o.signal:
                    n += 1
                    o.seq = n
        final_waits = []
        for e in self.ENGS:
            if self.ndma[e]:
                for i, s in enumerate(dsems[e]):
                    cnt = len(range(i, self.ndma[e], NDMA_SEM))
                    final_waits.append((s, 16 * cnt))

        def run(e, eng):
            waited = {}

            def wait(sem, val):
                key = id(sem)
                if waited.get(key, 0) >= val:
                    return
                eng.wait_ge(sem, val)
                waited[key] = val

            for o in self.ops[e]:
                for p in o.deps:
                    if p.dma:
                        wait(p.dsem, p.dval)
                    else:
                        wait(esem[p.eng], p.seq)
                if o.dma:
                    if o.qi >= NDMA_SEM:
                        wait(o.dsem, o.dval - 16)
                    o.fn(eng).then_inc(o.dsem, 16)
                else:
                    ins = o.fn(eng)
                    if o.signal:
                        ins.then_inc(esem[e], 1)
            if e == "sp":
                for s, v in final_waits:
                    wait(s, v)

        with nc.Block() as block:
            @block.tensor
            def _(eng):
                run("pe", eng)

            @block.scalar
            def _(eng):
                run("act", eng)

            @block.vector
            def _(eng):
                run("dve", eng)

            @block.gpsimd
            def _(eng):
                run("pool", eng)

            @block.sync
            def _(eng):
                run("sp", eng)


ARENA_F32 = 50688


class Ctx:
    def __init__(self, nc, es):
        self.nc = nc
        self.es = es
        self.P = Prog(nc)
        self.n = 0
        self.banks = [es.enter_context(nc.psum_tensor("bank%d" % i, [128, 512], F32))[:]
                      for i in range(8)]
        self.arena = es.enter_context(nc.sbuf_tensor("arena", [128, ARENA_F32], F32))[:]
        self.off = 0

    def sb(self, shape, dtype, name=None):
        shape = list(shape)
        esz = mybir.dt.size(dtype)
        n = 1
        for d in shape[1:]:
            n *= d
        nf = (n * esz + 31) // 32 * 8
        assert self.off + nf <= ARENA_F32, "SBUF arena overflow (%s)" % name
        v = self.arena[0:shape[0], self.off:self.off + nf]
        self.off += nf
        if dtype != F32:
            v = v.bitcast(dtype)
        v = v[:, 0:n]
        if len(shape) > 2:
            names = " ".join("d%d" % i for i in range(1, len(shape)))
            v = v.rearrange("p (%s) -> p %s" % (names, names),
                            **{"d%d" % i: shape[i] for i in range(1, len(shape))})
        return v

    def mark(self):
        return self.off

    def release(self, m):
        self.off = m

    def din(self, name, shape, dtype):
        return self.nc.dram_tensor(name, list(shape), dtype, kind="ExternalInput").ap()

    def dout(self, name, shape, dtype):
        return self.nc.dram_tensor(name, list(shape), dtype, kind="ExternalOutput").ap()

    def dint(self, name, shape, dtype):
        return self.nc.dram_tensor(name, list(shape), dtype, kind="Internal").ap()


def bank_bf(bank):
    return bank.bitcast(BF16)


def rstd_ops(P, st_in, st_tmp, st_out, inv_n, key_in, key_tmp, key_out):
    kl = lambda k: list(k) if isinstance(k, list) else [k]
    key_in, key_tmp, key_out = kl(key_in), kl(key_tmp), kl(key_out)
    P.op("dve", lambda e: e.tensor_scalar(out=st_tmp, in0=st_in, scalar1=inv_n, scalar2=EPS,
                                          op0=ALU.mult, op1=ALU.add),
         reads=key_in, writes=key_tmp)
    P.op("act", lambda e: e.activation(out=st_tmp, in_=st_tmp, func=AF.Ln),
         reads=key_tmp, writes=key_tmp)
    P.op("act", lambda e: e.activation(out=st_out, in_=st_tmp, func=AF.Exp, scale=-0.5),
         reads=key_tmp, writes=key_out)


def phase_A(C, x, w_in, g_mix, gq, gk, ident, send_bf, send_u, hc_loc):
    P = C.P
    W = C.sb([128, 8, NINP], BF16, "A_W")
    gB = C.sb([128, D], F32, "A_gB")
    gqB = C.sb([128, 8, 64], F32, "A_gqB")
    gkB = C.sb([128, 8, 64], F32, "A_gkB")
    idB = C.sb([128, 128], BF16, "A_idB")
    idF = C.sb([128, 128], F32, "A_idF")
    xt = [C.sb([128, D], F32, "A_xt%d" % i) for i in range(3)]
    junk_ = [C.sb([128, D], BF16, "A_junk%d" % i) for i in range(3)]
    hb = [C.sb([128, D], BF16, "A_hb%d" % i) for i in range(3)]
    hT = [C.sb([128, 8, 128], BF16, "A_hT%d" % i) for i in range(3)]
    st = [C.sb([128, 32], F32, "A_st%d" % i) for i in range(3)]
    qsq_ = [C.sb([128, 512], F32, "A_qsq%d" % i) for i in range(3)]
    qtmp_ = [C.sb([128, 512], F32, "A_qtmp%d" % i) for i in range(3)]
    qn_ = [C.sb([128, 512], BF16, "A_qn%d" % i) for i in range(3)]
    ge_ = [C.sb([128, 256], F32, "A_ge%d" % i) for i in range(3)]
    hcn_ = [C.sb([128, 256], BF16, "A_hcn%d" % i) for i in range(3)]
    uf_ = [C.sb([128, 256], F32, "A_uf%d" % i) for i in range(3)]
    qT_all = C.sb([128, 4, TOK], BF16, "A_qT")
    kT_all = C.sb([128, 4, TOK], BF16, "A_kT")
    v_all = C.sb([128, NT, 512], BF16, "A_v")
    hcT_all = C.sb([128, 2, TOK], BF16, "A_hcT")
    uT_all = C.sb([128, 2, TOK], F32, "A_uT")

    for ci, (c0, cw) in enumerate([(0, 512), (512, 512), (1024, 512), (1536, 512), (2048, 256)]):
        for k in range(8):
            P.op("pool", lambda e, k=k, c0=c0, cw=cw: e.dma_start(out=W[:, k, c0:c0 + cw],
                                                                 in_=w_in[k * 128:(k + 1) * 128, c0:c0 + cw]),
                 writes=[("W", k, ci)], dma=True)
    P.op("sp", lambda e: e.dma_start(out=gB, in_=g_mix.to_broadcast([128, D])), writes=["gB"], dma=True)
    P.op("sp", lambda e: e.dma_start(out=gqB, in_=gq.unsqueeze(1).to_broadcast([128, 8, 64])),
         writes=["gqB"], dma=True)
    P.op("sp", lambda e: e.dma_start(out=gkB, in_=gk.unsqueeze(1).to_broadcast([128, 8, 64])),
         writes=["gkB"], dma=True)
    P.op("pool", lambda e: e.dma_start(out=idB, in_=ident), writes=["idB"], dma=True)
    P.op("sp", lambda e: e.dma_start(out=idF, in_=ident), writes=["idF"], dma=True)
    P.op("dve", lambda e: e.tensor_scalar(out=gqB, in0=gqB, scalar1=0.125, scalar2=None, op0=ALU.mult),
         reads=["gqB"], writes=["gqB"])

    chunks = [(0, 512), (512, 512), (1024, 512), (1536, 512), (2048, 256)]
    nbank = [0]

    def obank():
        i = 3 + (nbank[0] % 3)
        nbank[0] += 1
        return i
    ntb = [0]

    def tbank():
        i = 6 + (ntb[0] % 2)
        ntb[0] += 1
        return i

    def load_x(t):
        p3 = t % 3
        P.op("sp", lambda e: e.dma_start(out=xt[p3], in_=x[t * 128:(t + 1) * 128, :]),
             writes=[("xt", p3)], dma=True)

    def tile_gen(t):
        par = t % 3
        p3 = t % 3
        if t + 2 < NT and t >= 1:
            load_x(t + 2)
        s = st[par]
        junk, qsq, qtmp, qn, ge, hcn, uf = junk_[par], qsq_[par], qtmp_[par], qn_[par], ge_[par], hcn_[par], uf_[par]
        kj, kqs, kqt, kqn, kge, khc, kuf = (("junk", par), ("qsq", par), ("qtmp", par), ("qn", par), ("ge", par),
                                            ("hcn", par), ("uf", par))
        P.op("act", lambda e, par=par, s=s: e.activation(out=junk, in_=xt[p3], func=AF.Square,
                                                          accum_out=s[:, 0:1]),
             reads=[("xt", p3)], writes=[kj, ("st", par, 0)])
        rstd_ops(P, s[:, 0:1], s[:, 1:2], s[:, 2:3], 1.0 / D, ("st", par, 0), ("st", par, 1), ("st", par, 2))
        P.op("dve", lambda e, par=par, s=s: e.scalar_tensor_tensor(
            out=hb[par], in0=xt[p3], scalar=s[:, 2:3], in1=gB, op0=ALU.mult, op1=ALU.mult),
            reads=[("xt", p3), ("st", par, 2), "gB"], writes=[("hb", par)])
        pT = bank_bf(C.banks[par]).rearrange("p (k t) -> p k t", k=8)
        for k in range(8):
            P.op("pe", lambda e, k=k, pT=pT, par=par: e.transpose(
                out=pT[:, k, :], in_=hb[par][:, k * 128:(k + 1) * 128], identity=idB),
                reads=[("hb", par), "idB"], writes=[("bank", par)])
        P.op("act", lambda e, pT=pT, par=par: e.copy(out=hT[par], in_=pT),
             reads=[("bank", par)], writes=[("hT", par)])
        yield
        for ci, (c0, cw) in enumerate(chunks):
            bi = obank()
            pO = C.banks[bi][:, 0:cw]
            for k in range(8):
                P.op("pe", lambda e, k=k, pO=pO, par=par, c0=c0, cw=cw: e.matmul(
                    pO, lhsT=hT[par][:, k, :], rhs=W[:, k, c0:c0 + cw], start=(k == 0), stop=(k == 7)),
                    reads=[("hT", par), ("W", k, ci)], writes=[("bank", bi)])
            bk = ("bank", bi)
            yield
            if ci in (0, 1):
                dstT = qT_all if ci == 0 else kT_all
                gBt = gqB if ci == 0 else gkB
                gkey = "gqB" if ci == 0 else "gkB"
                P.op("act", lambda e, pO=pO: e.activation(out=qsq, in_=pO, func=AF.Square),
                     reads=[bk], writes=[kqs])
                P.op("dve", lambda e, s=s: e.tensor_reduce(
                    out=s[:, 8:16], in_=qsq.rearrange("p (h d) -> p h d", h=8), axis=AX.X, op=ALU.add),
                    reads=[kqs], writes=[("st", par, 3)])
                rstd_ops(P, s[:, 8:16], s[:, 16:24], s[:, 24:32], 1.0 / 64,
                         ("st", par, 3), ("st", par, 4), ("st", par, 5))
                P.op("dve", lambda e, pO=pO, s=s: e.tensor_tensor(
                    out=qtmp.rearrange("p (h d) -> p h d", h=8),
                    in0=pO.rearrange("p (h d) -> p h d", h=8),
                    in1=s[:, 24:32].unsqueeze(2).to_broadcast([128, 8, 64]), op=ALU.mult),
                    reads=[bk, ("st", par, 5)], writes=[kqt])
                P.op("dve", lambda e, gBt=gBt: e.tensor_tensor(
                    out=qn, in0=qtmp, in1=gBt.rearrange("p h d -> p (h d)"), op=ALU.mult),
                    reads=[kqt, gkey], writes=[kqn])
                tb = tbank()
                pQ = bank_bf(C.banks[tb])[:, 0:512].rearrange("p (j t) -> p j t", j=4)
                for j in range(4):
                    P.op("pe", lambda e, j=j, pQ=pQ: e.transpose(
                        out=pQ[:, j, :], in_=qn[:, j * 128:(j + 1) * 128], identity=idB),
                        reads=[kqn, "idB"], writes=[("bank", tb)])
                P.op("act", lambda e, pQ=pQ, dstT=dstT, t=t: e.copy(
                    out=dstT[:, :, t * 128:(t + 1) * 128], in_=pQ),
                    reads=[("bank", tb)], writes=[("qkT", ci, t)])
            elif ci == 2:
                P.op("act", lambda e, pO=pO, t=t: e.copy(out=v_all[:, t, :], in_=pO),
                     reads=[bk], writes=[("v", t)])
            elif ci == 3:
                P.op("act", lambda e, pO=pO: e.activation(out=ge, in_=pO[:, 256:512], func=AF.Exp, scale=-1.0),
                     reads=[bk], writes=[kge])
                P.op("dve", lambda e: e.tensor_scalar(out=ge, in0=ge, scalar1=1.0, scalar2=None, op0=ALU.add),
                     reads=[kge], writes=[kge])
                P.op("dve", lambda e: e.reciprocal(out=ge, in_=ge), reads=[kge], writes=[kge])
                P.op("dve", lambda e, pO=pO: e.tensor_tensor(out=hcn, in0=pO[:, 0:256], in1=ge, op=ALU.mult),
                     reads=[bk, kge], writes=[khc])
                tb = tbank()
                pQ = bank_bf(C.banks[tb])[:, 0:256].rearrange("p (j t) -> p j t", j=2)
                for j in range(2):
                    P.op("pe", lambda e, j=j, pQ=pQ: e.transpose(
                        out=pQ[:, j, :], in_=hcn[:, j * 128:(j + 1) * 128], identity=idB),
                        reads=[khc, "idB"], writes=[("bank", tb)])
                P.op("act", lambda e, pQ=pQ, t=t: e.copy(out=hcT_all[:, :, t * 128:(t + 1) * 128], in_=pQ),
                     reads=[("bank", tb)], writes=[("hcT", t)])
            else:
                P.op("act", lambda e, pO=pO: e.copy(out=uf, in_=pO), reads=[bk], writes=[kuf])
                tb = tbank()
                pQ = C.banks[tb][:, 0:256].rearrange("p (j t) -> p j t", j=2)
                for j in range(2):
                    P.op("pe", lambda e, j=j, pQ=pQ: e.transpose(
                        out=pQ[:, j, :], in_=uf[:, j * 128:(j + 1) * 128], identity=idF),
                        reads=[kuf, "idF"], writes=[("bank", tb)])
                P.op("act", lambda e, pQ=pQ, t=t: e.copy(out=uT_all[:, :, t * 128:(t + 1) * 128], in_=pQ),
                     reads=[("bank", tb)], writes=[("uT", t)])
        yield

    load_x(0)
    load_x(1)
    load_x(2)
    pending = list(range(NT))
    active = []
    while pending or active:
        while pending and len(active) < 3:
            active.append(tile_gen(pending.pop(0)))
        for g in list(active):
            try:
                next(g)
            except StopIteration:
                active.remove(g)

    allq = [("qkT", 0, t) for t in range(NT)]
    allk = [("qkT", 1, t) for t in range(NT)]
    allv = [("v", t) for t in range(NT)]
    allh = [("hcT", t) for t in range(NT)]
    allu = [("uT", t) for t in range(NT)]
    for j in range(8):
        r0 = (j % 2) * 64
        P.op("sp", lambda e, j=j, r0=r0: e.dma_start(
            out=send_bf[j, 0:QKV_SZ].rearrange("(d t) -> d t", t=TOK), in_=qT_all[r0:r0 + 64, j // 2, :]),
            reads=allq, dma=True)
        P.op("sp", lambda e, j=j, r0=r0: e.dma_start(
            out=send_bf[j, QKV_SZ:2 * QKV_SZ].rearrange("(d t) -> d t", t=TOK), in_=kT_all[r0:r0 + 64, j // 2, :]),
            reads=allk, dma=True)
        P.op("sp", lambda e, j=j: e.dma_start(
            out=send_bf[j, 2 * QKV_SZ:3 * QKV_SZ].rearrange("(n p d) -> p n d", p=128, d=64),
            in_=v_all[:, :, j * 64:(j + 1) * 64]), reads=allv, dma=True)
        P.op("sp", lambda e, j=j: e.dma_start(
            out=send_bf[j, HALO_OFF:SB_SZ].rearrange("(c p t) -> p c t", p=128, t=32),
            in_=hcT_all[:, :, TOK - 32:TOK]), reads=allh, dma=True)
        u0 = (j % 4) * 32
        P.op("sp", lambda e, j=j, u0=u0: e.dma_start(out=send_u[j], in_=uT_all[u0:u0 + 32, j // 4, :]),
             reads=allu, dma=True)
    P.op("sp", lambda e: e.dma_start(out=hc_loc.rearrange("(c p) t -> p c t", p=128), in_=hcT_all),
         reads=allh, dma=True)


def build_A():
    nc = bass.Bass("TRN2", target_bir_lowering=False)
    es = ExitStack()
    C = Ctx(nc, es)
    x = C.din("x", [TOK, D], F32)
    w_in = C.din("w_in", [D, NINP], F32)
    g_mix = C.din("g_mix", [1, D], F32)
    gq = C.din("gq", [1, 64], F32)
    gk = C.din("gk", [1, 64], F32)
    ident = C.din("ident", [128, 128], F32)
    send_bf = C.dout("send_bf", [8, SB_SZ], BF16)
    send_u = C.dout("send_u", [8, 32, TOK], F32)
    hc_loc = C.dout("hc_loc", [256, TOK], BF16)
    phase_A(C, x, w_in, g_mix, gq, gk, ident, send_bf, send_u, hc_loc)
    C.P.emit(es)
    es.close()
    return nc


def phase_B_attn(C, recv_bf, consts_bf, send_o, side=None, side_rate=1):
    P = C.P
    qT = C.sb([64, 2 * SEQ], BF16, "B_qT")
    kT = C.sb([64, 2 * SEQ], BF16, "B_kT")
    v = C.sb([128, 128, 64], BF16, "B_v")
    cst = C.sb([128, 256 + 4 * 512], BF16, "B_cst")
    negtri = cst[:, 0:128]
    negones = cst[:, 128:256]
    mask = cst[:, 256:].rearrange("p (i q) -> p i q", i=4)
    eb = [C.sb([128, 512], F32, "B_e%d" % i) for i in range(2)]
    spb = [C.sb([128, 512], BF16, "B_sp%d" % i) for i in range(3)]
    wb = [C.sb([128, 512], BF16, "B_w%d" % i) for i in range(3)]
    ssb = [C.sb([128, 512], BF16, "B_ss%d" % i) for i in range(3)]
    ost = [C.sb([64, 512], F32, "B_ost%d" % i) for i in range(2)]

    P.op("pool", lambda e: e.dma_start(out=cst, in_=consts_bf), writes=["cst"], dma=True)
    for c in range(8):
        P.op("sp", lambda e, c=c: e.dma_start(
            out=kT[:, c * TOK:(c + 1) * TOK], in_=recv_bf[c, QKV_SZ:2 * QKV_SZ].rearrange("(d t) -> d t", t=TOK)),
            writes=[("kT", c)], dma=True)
        P.op("sp", lambda e, c=c: e.dma_start(
            out=qT[:, c * TOK:(c + 1) * TOK], in_=recv_bf[c, 0:QKV_SZ].rearrange("(d t) -> d t", t=TOK)),
            writes=[("qT", c)], dma=True)
        P.op("sp", lambda e, c=c: e.dma_start(
            out=v[:, c * 16:(c + 1) * 16, :],
            in_=recv_bf[c, 2 * QKV_SZ:3 * QKV_SZ].rearrange("(n p d) -> p n d", p=128, d=64)),
            writes=[("v", c)], dma=True)

    units = []
    for b in range(2):
        for qc in range(16):
            kbs = [(4 * qc + i, i) for i in (3, 2, 1, 0)] + [(kb, None) for kb in range(4 * qc - 1, -1, -1)]
            for n, (kb, di) in enumerate(kbs):
                units.append(dict(idx=len(units), b=b, qc=qc, kb=kb, diag=di, first=(n == 0),
                                  last=(n == len(kbs) - 1)))

    def unit_cols(u):
        kc = u["b"] * SEQ + u["kb"] * 128
        qc0 = u["b"] * SEQ + u["qc"] * 512
        return kc, qc0

    def emit_Z(u):
        i = u["idx"]
        kc, qc0 = unit_cols(u)
        zb = i % 2
        Z = C.banks[zb]
        P.op("pe", lambda e: e.matmul(Z, lhsT=kT[:, kc:kc + 128], rhs=qT[:, qc0:qc0 + 512], start=True, stop=True),
             reads=[("kT", kc // TOK), ("qT", qc0 // TOK)], writes=[("bank", zb)])

    def emit_expZ(u):
        i = u["idx"]
        zb = i % 2
        Z = C.banks[zb]
        ee = eb[i % 2]
        P.op("act", lambda e: e.activation(out=ee, in_=Z, func=AF.Exp), reads=[("bank", zb)], writes=[("e", i % 2)])

    def emit_ln(u):
        i = u["idx"]
        ee = eb[i % 2]
        sp = spb[i % 3]
        P.op("act", lambda e: e.activation(out=sp, in_=ee, func=AF.Ln, bias=1.0),
             reads=[("e", i % 2)], writes=[("sp", i % 3)])
        if u["diag"] is not None:
            m = mask[:, u["diag"], :]
            P.op("dve", lambda e: e.tensor_tensor(out=sp, in0=sp, in1=m, op=ALU.mult),
                 reads=[("sp", i % 3), "cst"], writes=[("sp", i % 3)])
        ss = ssb[i % 3]
        if u["first"]:
            P.op("dve", lambda e: e.tensor_copy(out=ss, in_=sp), reads=[("sp", i % 3)], writes=[("ss", i % 3)])
        else:
            sprev = ssb[(i - 1) % 3]
            P.op("dve", lambda e: e.tensor_tensor(out=ss, in0=sprev, in1=sp, op=ALU.add),
                 reads=[("sp", i % 3), ("ss", (i - 1) % 3)], writes=[("ss", i % 3)])

    def emit_W(u):
        i = u["idx"]
        kc, qc0 = unit_cols(u)
        wbk = 2 + i % 2
        Wp = C.banks[wbk]
        sp = spb[i % 3]
        first = u["first"]
        P.op("pe", lambda e: e.matmul(Wp, lhsT=kT[:, kc:kc + 128], rhs=qT[:, qc0:qc0 + 512], start=True, stop=False),
             reads=[("kT", kc // TOK), ("qT", qc0 // TOK)], writes=[("bank", wbk)])
        P.op("pe", lambda e: e.matmul(Wp, lhsT=negtri, rhs=sp, start=False, stop=first),
             reads=["cst", ("sp", i % 3)], writes=[("bank", wbk)])
        if not first:
            sprev = ssb[(i - 1) % 3]
            P.op("pe", lambda e: e.matmul(Wp, lhsT=negones, rhs=sprev, start=False, stop=True),
                 reads=["cst", ("ss", (i - 1) % 3)], writes=[("bank", wbk)])

    def emit_expW(u):
        i = u["idx"]
        wbk = 2 + i % 2
        Wp = C.banks[wbk]
        w = wb[i % 3]
        P.op("act", lambda e: e.activation(out=w, in_=Wp, func=AF.Exp), reads=[("bank", wbk)], writes=[("w", i % 3)])
        if u["diag"] is not None:
            m = mask[:, u["diag"], :]
            P.op("dve", lambda e: e.tensor_tensor(out=w, in0=w, in1=m, op=ALU.mult),
                 reads=[("w", i % 3), "cst"], writes=[("w", i % 3)])

    def emit_PV(u):
        i = u["idx"]
        qc = u["qc"]
        G = u["b"] * 64 + u["kb"]
        ob = 4 + qc % 2
        O = C.banks[ob][0:64, :]
        w = wb[i % 3]
        P.op("pe", lambda e: e.matmul(O, lhsT=v[:, G, :], rhs=w, start=u["first"], stop=u["last"]),
             reads=[("v", G // 16), ("w", i % 3)], writes=[("bank", ob)])
        if u["last"]:
            os_ = ost[qc % 2]
            P.op("dve", lambda e: e.tensor_copy(out=os_, in_=O), reads=[("bank", ob)], writes=[("ost", qc % 2)])
            dest = u["b"] * 4 + qc // 4
            c0 = (qc % 4) * 512
            P.op("sp", lambda e: e.dma_start(out=send_o[dest, 0:64, c0:c0 + 512], in_=os_),
                 reads=[("ost", qc % 2)], dma=True)

    n = len(units)
    U = lambda i: units[i] if 0 <= i < n else None
    emit_Z(units[0])
    emit_Z(units[1])
    emit_expZ(units[0])
    for s in range(n + 2):
        if U(s + 2):
            emit_Z(U(s + 2))
        if U(s + 1):
            emit_expZ(U(s + 1))
        if U(s):
            emit_ln(U(s))
        if U(s - 1):
            emit_W(U(s - 1))
        if U(s - 2):
            emit_PV(U(s - 2))
        if U(s - 1):
            emit_expW(U(s - 1))
        if side is not None:
            for _ in range(side_rate):
                next(side, None)
    if side is not None:
        for _ in side:
            pass


def attn_consts():
    k = np.arange(128)
    negtri = -(k[:, None] >= k[None, :]).astype(np.float32)
    negones = -np.ones((128, 128), np.float32)
    q = np.arange(512)
    mask = np.stack([(128 * i + k[:, None] < q[None, :]).astype(np.float32) for i in range(4)], 1)
    return np.concatenate([negtri, negones, mask.reshape(128, 2048)], 1)


def build_B(with_ssm=True):
    nc = bass.Bass("TRN2", target_bir_lowering=False)
    es = ExitStack()
    C = Ctx(nc, es)
    recv_bf = C.din("recv_bf", [8, SB_SZ], BF16)
    consts_bf = C.din("consts_bf", [128, 256 + 2048], F32)
    send_o = C.dout("send_o", [8, 96, TOK], F32)
    side = None
    if with_ssm:
        ssm_io = ssm_decl(C)
        side = phase_B_ssm(C, ssm_io, send_o)
    phase_B_attn(C, recv_bf, consts_bf, send_o, side=side)
    C.P.emit(es)
    es.close()
    return nc


import math

SSM_NP = 6 + 1 + 32 + 32 + 32 + 2 + 128 + 128
PC_VEC, PC_SGN, PC_BRI, PC_BIR, PC_CT, PC_D, PC_I, PC_SW = 0, 6, 7, 39, 71, 103, 105, 233
NLEV = 13


def ssm_decl(C):
    return dict(recv_u=C.din("recv_u", [8, 32, TOK], F32), ssm_par=C.din("ssm_par", [128, SSM_NP], F32))


def ssm_pack(lam_re, lam_im, log_dt, b_re, b_im, c_re, c_im, d, core):
    par = np.zeros((128, SSM_NP), np.float32)
    for gi in range(2):
        g = 2 * core + gi
        par[:, PC_VEC + 3 * gi + 0] = np.concatenate([lam_re[g], lam_re[g]])
        par[:, PC_VEC + 3 * gi + 1] = np.concatenate([lam_im[g], lam_im[g]])
        par[:, PC_VEC + 3 * gi + 2] = log_dt[g]
        par[:, PC_BRI + 16 * gi:PC_BRI + 16 * gi + 16] = np.concatenate([b_re[g], b_im[g]], 0)
        par[:, PC_BIR + 16 * gi:PC_BIR + 16 * gi + 16] = np.concatenate([b_im[g], b_re[g]], 0)
        par[:, PC_CT + 16 * gi:PC_CT + 16 * gi + 16] = np.concatenate([c_re[g].T, c_im[g].T], 0)
        par[16 * gi:16 * gi + 16, PC_D + gi] = d[16 * g:16 * g + 16]
    par[:64, PC_SGN] = 1.0
    par[64:, PC_SGN] = -1.0
    par[:, PC_I:PC_I + 128] = np.eye(128, dtype=np.float32)
    par[:, PC_SW:PC_SW + 128] = np.roll(np.eye(128, dtype=np.float32), 64, axis=1)
    return par


def phase_B_ssm(C, io, send_o):
    P = C.P
    recv_u, ssm_par = io["recv_u"], io["ssm_par"]
    par = C.sb([128, SSM_NP], F32, "S_par")
    sc = C.sb([128, 128], F32, "S_sc")
    sci = C.sb([128, 4], mybir.dt.int32, "S_sci")
    AT = C.sb([128, 2, NLEV, 128], F32, "S_AT")
    Bpad = C.sb([128, 2, 32], F32, "S_Bpad")
    Cpad = C.sb([128, 2, 32], F32, "S_Cpad")
    lB = C.sb([32, 2, 128], F32, "S_lB")
    X = C.sb([128, SEQ], F32, "S_X")
    ub = C.sb([32, SEQ], F32, "S_u")
    yst = [C.sb([32, 512], F32, "S_y%d" % i) for i in range(2)]
    I128 = par[:, PC_I:PC_I + 128]
    SW = par[:, PC_SW:PC_SW + 128]
    sgn = par[:, PC_SGN:PC_SGN + 1]
    TWO_PI = 2.0 * math.pi

    P.op("sp", lambda e: e.dma_start(out=par, in_=ssm_par), writes=["par"], dma=True)
    P.op("dve", lambda e: e.memset(Bpad, 0.0), writes=["Bpad"])
    P.op("dve", lambda e: e.memset(Cpad, 0.0), writes=["Cpad"])

    ncol = [0]

    def col():
        ncol[0] += 1
        assert ncol[0] <= 128
        return ncol[0] - 1

    def cs(i):
        return sc[:, i:i + 1]

    def K(i):
        return ("sc", i)

    def ts(o, a, s1, s2, op0, op1=None, extra=()):
        if op1 is None:
            P.op("dve", lambda e: e.tensor_scalar(out=cs(o), in0=cs(a), scalar1=s1, scalar2=None, op0=op0),
                 reads=[K(a)] + list(extra), writes=[K(o)])
        else:
            P.op("dve", lambda e: e.tensor_scalar(out=cs(o), in0=cs(a), scalar1=s1, scalar2=s2, op0=op0, op1=op1),
                 reads=[K(a)] + list(extra), writes=[K(o)])

    def tt(o, a, b, op):
        P.op("dve", lambda e: e.tensor_tensor(out=cs(o), in0=cs(a), in1=cs(b), op=op),
             reads=[K(a), K(b)], writes=[K(o)])

    for gi in range(2):
        ncol[0] = 0
        lr = par[:, PC_VEC + 3 * gi:PC_VEC + 3 * gi + 1]
        li = par[:, PC_VEC + 3 * gi + 1:PC_VEC + 3 * gi + 2]
        ldt = par[:, PC_VEC + 3 * gi + 2:PC_VEC + 3 * gi + 3]
        c_dt, c_mag, c_th = col(), col(), col()
        P.op("act", lambda e, c_dt=c_dt, ldt=ldt: e.activation(out=cs(c_dt), in_=ldt, func=AF.Exp),
             reads=["par"], writes=[K(c_dt)])
        P.op("act", lambda e, c_mag=c_mag, c_dt=c_dt, lr=lr: e.activation(out=cs(c_mag), in_=lr, func=AF.Exp, scale=cs(c_dt)),
             reads=["par", K(c_dt)], writes=[K(c_mag)])
        P.op("dve", lambda e, c_th=c_th, c_dt=c_dt, li=li: e.tensor_tensor(out=cs(c_th), in0=li, in1=cs(c_dt), op=ALU.mult),
             reads=["par", K(c_dt)], writes=[K(c_th)])
        trig = []
        for shift in (0.5 * math.pi, 0.0):
            c_a, c_y, c_n, c_r, c_w, c_o = col(), col(), col(), col(), col(), col()
            ts(c_a, c_th, shift, None, ALU.add)
            ts(c_y, c_a, 1.0 / TWO_PI, None, ALU.mult)
            ii = 0 if shift else 1
            P.op("dve", lambda e, ii=ii, c_y=c_y: e.tensor_copy(out=sci[:, ii:ii + 1], in_=cs(c_y)),
                 reads=[K(c_y)], writes=[("sci", ii)])
            P.op("dve", lambda e, ii=ii, c_n=c_n: e.tensor_copy(out=cs(c_n), in_=sci[:, ii:ii + 1]),
                 reads=[("sci", ii)], writes=[K(c_n)])
            P.op("dve", lambda e, c_r=c_r, c_n=c_n, c_a=c_a: e.scalar_tensor_tensor(
                out=cs(c_r), in0=cs(c_n), scalar=-TWO_PI, in1=cs(c_a), op0=ALU.mult, op1=ALU.add),
                reads=[K(c_n), K(c_a)], writes=[K(c_r)])
            ts(c_w, c_r, math.pi, -TWO_PI, ALU.is_gt, ALU.mult)
            tt(c_r, c_r, c_w, ALU.add)
            P.op("act", lambda e, c_o=c_o, c_r=c_r: e.activation(out=cs(c_o), in_=cs(c_r), func=AF.Sin),
                 reads=[K(c_r)], writes=[K(c_o)])
            trig.append(c_o)
        c_ar, c_ai = col(), col()
        tt(c_ar, c_mag, trig[0], ALU.mult)
        tt(c_ai, c_mag, trig[1], ALU.mult)
        c_l2, c_i2, c_den, c_am1, c_t1, c_t2, c_fr, c_fi, c_f2 = [col() for _ in range(9)]
        P.op("dve", lambda e, lr=lr, c_l2=c_l2: e.tensor_tensor(out=cs(c_l2), in0=lr, in1=lr, op=ALU.mult),
             reads=["par"], writes=[K(c_l2)])
        P.op("dve", lambda e, li=li, c_i2=c_i2: e.tensor_tensor(out=cs(c_i2), in0=li, in1=li, op=ALU.mult),
             reads=["par"], writes=[K(c_i2)])
        tt(c_den, c_l2, c_i2, ALU.add)
        P.op("dve", lambda e, c_den=c_den: e.reciprocal(out=cs(c_den), in_=cs(c_den)), reads=[K(c_den)], writes=[K(c_den)])
        ts(c_am1, c_ar, -1.0, None, ALU.add)
        P.op("dve", lambda e, lr=lr, c_t1=c_t1, c_am1=c_am1: e.tensor_tensor(out=cs(c_t1), in0=cs(c_am1), in1=lr, op=ALU.mult),
             reads=["par", K(c_am1)], writes=[K(c_t1)])
        P.op("dve", lambda e, li=li, c_t2=c_t2, c_ai=c_ai: e.tensor_tensor(out=cs(c_t2), in0=cs(c_ai), in1=li, op=ALU.mult),
             reads=["par", K(c_ai)], writes=[K(c_t2)])
        tt(c_fr, c_t1, c_t2, ALU.add)
        tt(c_fr, c_fr, c_den, ALU.mult)
        P.op("dve", lambda e, lr=lr, c_t1=c_t1, c_ai=c_ai: e.tensor_tensor(out=cs(c_t1), in0=cs(c_ai), in1=lr, op=ALU.mult),
             reads=["par", K(c_ai)], writes=[K(c_t1)])
        P.op("dve", lambda e, li=li, c_t2=c_t2, c_am1=c_am1: e.tensor_tensor(out=cs(c_t2), in0=cs(c_am1), in1=li, op=ALU.mult),
             reads=["par", K(c_am1)], writes=[K(c_t2)])
        tt(c_fi, c_t1, c_t2, ALU.subtract)
        tt(c_fi, c_fi, c_den, ALU.mult)
        P.op("dve", lambda e, c_f2=c_f2, c_fi=c_fi: e.scalar_tensor_tensor(
            out=cs(c_f2), in0=cs(c_fi), scalar=-1.0, in1=sgn, op0=ALU.mult, op1=ALU.mult),
            reads=[K(c_fi), "par"], writes=[K(c_f2)])
        bsl = Bpad[:, gi, 16 * gi:16 * gi + 16]
        P.op("dve", lambda e, bsl=bsl, gi=gi, c_fr=c_fr: e.tensor_scalar(
            out=bsl, in0=par[:, PC_BRI + 16 * gi:PC_BRI + 16 * gi + 16], scalar1=cs(c_fr), scalar2=None, op0=ALU.mult),
            reads=["par", K(c_fr), "Bpad"], writes=[("Bpad", gi)])
        P.op("dve", lambda e, bsl=bsl, gi=gi, c_f2=c_f2: e.scalar_tensor_tensor(
            out=bsl, in0=par[:, PC_BIR + 16 * gi:PC_BIR + 16 * gi + 16], scalar=cs(c_f2), in1=bsl, op0=ALU.mult, op1=ALU.add),
            reads=["par", K(c_f2), ("Bpad", gi)], writes=[("Bpad", gi)])
        pb = C.banks[6][0:32, 0:128]
        P.op("pe", lambda e, pb=pb, gi=gi: e.transpose(out=pb, in_=Bpad[:, gi, :], identity=I128),
             reads=[("Bpad", gi), "par"], writes=[("bank", 6)])
        P.op("dve", lambda e, pb=pb, gi=gi: e.tensor_copy(out=lB[:, gi, :], in_=pb), reads=[("bank", 6)], writes=[("lB", gi)])
        P.op("dve", lambda e, gi=gi: e.tensor_scalar(
            out=Cpad[:, gi, 16 * gi:16 * gi + 16], in0=par[:, PC_CT + 16 * gi:PC_CT + 16 * gi + 16],
            scalar1=sgn, scalar2=None, op0=ALU.mult), reads=["par", "Cpad"], writes=[("Cpad", gi)])
        c_pr, c_pi = c_ar, c_ai
        for k in range(NLEV):
            c_s2 = col()
            P.op("dve", lambda e, c_s2=c_s2, c_pi=c_pi: e.tensor_tensor(out=cs(c_s2), in0=cs(c_pi), in1=sgn, op=ALU.mult),
                 reads=[K(c_pi), "par"], writes=[K(c_s2)])
            A = AT[:, gi, k, :]
            P.op("dve", lambda e, A=A, c_pr=c_pr: e.tensor_scalar(out=A, in0=I128, scalar1=cs(c_pr), scalar2=None, op0=ALU.mult),
                 reads=["par", K(c_pr)], writes=[("AT", gi, k)])
            P.op("dve", lambda e, A=A, c_s2=c_s2: e.scalar_tensor_tensor(
                out=A, in0=SW, scalar=cs(c_s2), in1=A, op0=ALU.mult, op1=ALU.add),
                reads=["par", K(c_s2), ("AT", gi, k)], writes=[("AT", gi, k)])
            if k + 1 < NLEV:
                c_a2, c_b2, c_nr, c_ni = col(), col(), col(), col()
                tt(c_a2, c_pr, c_pr, ALU.mult)
                tt(c_b2, c_pi, c_pi, ALU.mult)
                tt(c_nr, c_a2, c_b2, ALU.subtract)
                P.op("dve", lambda e, c_ni=c_ni, c_pr=c_pr, c_pi=c_pi: e.scalar_tensor_tensor(
                    out=cs(c_ni), in0=cs(c_pr), scalar=2.0, in1=cs(c_pi), op0=ALU.mult, op1=ALU.mult),
                    reads=[K(c_pr), K(c_pi)], writes=[K(c_ni)])
                c_pr, c_pi = c_nr, c_ni

    def scan_gen():
      if True:
        nb = [0]

        def sbank():
            nb[0] += 1
            return 6 + nb[0] % 2

        for b in range(2):
            for j in range(4):
                P.op("sp", lambda e, b=b, j=j: e.dma_start(out=ub[:, j * TOK:(j + 1) * TOK], in_=recv_u[4 * b + j]),
                     writes=[("ub", j)], dma=True)
            for gi in range(2):
                for ch in range(16):
                    bk = sbank()
                    ps = C.banks[bk]
                    P.op("pe", lambda e, ps=ps, gi=gi, ch=ch: e.matmul(
                        ps, lhsT=lB[:, gi, :], rhs=ub[:, ch * 512:(ch + 1) * 512], start=True, stop=True),
                        reads=[("lB", gi), ("ub", ch // 4)], writes=[("bank", bk)])
                    P.op("dve", lambda e, ps=ps, ch=ch: e.tensor_copy(out=X[:, ch * 512:(ch + 1) * 512], in_=ps),
                         reads=[("bank", bk)], writes=[("X", ch)])
                    yield

                for k in range(NLEV):
                    s_ = 1 << k
                    for ch in range(15, -1, -1):
                        lo = max(512 * ch, s_)
                        hi = 512 * (ch + 1)
                        if lo >= hi:
                            continue
                        n_ = hi - lo
                        bk = sbank()
                        ps = C.banks[bk][:, 0:n_]
                        src = X[:, lo - s_:hi - s_]
                        dst = X[:, lo:hi]
                        rk = sorted(set([("X", (lo - s_) // 512), ("X", (hi - s_ - 1) // 512)]))
                        P.op("pe", lambda e, ps=ps, src=src, k=k, gi=gi: e.matmul(
                            ps, lhsT=AT[:, gi, k, :], rhs=src, start=True, stop=True),
                            reads=[("AT", gi, k)] + rk, writes=[("bank", bk)])
                        P.op("dve", lambda e, ps=ps, dst=dst: e.tensor_tensor(out=dst, in0=dst, in1=ps, op=ALU.add),
                             reads=[("bank", bk), ("X", ch)], writes=[("X", ch)])
                        yield
                for ch in range(16):
                    bk = sbank()
                    ps = C.banks[bk][0:32, :]
                    P.op("pe", lambda e, ps=ps, ch=ch, gi=gi: e.matmul(
                        ps, lhsT=Cpad[:, gi, :], rhs=X[:, ch * 512:(ch + 1) * 512], start=True, stop=True),
                        reads=[("Cpad", gi), ("X", ch)], writes=[("bank", bk)])
                    ys = yst[ch % 2]
                    P.op("dve", lambda e, ps=ps, ch=ch, gi=gi, ys=ys: e.scalar_tensor_tensor(
                        out=ys, in0=ub[:, ch * 512:(ch + 1) * 512], scalar=par[0:32, PC_D + gi:PC_D + gi + 1], in1=ps,
                        op0=ALU.mult, op1=ALU.add),
                        reads=[("bank", bk), ("ub", ch // 4), "par"], writes=[("yst", ch % 2)])
                    dest = 4 * b + ch // 4
                    c0 = (ch % 4) * 512
                    P.op("sp", lambda e, ys=ys, dest=dest, c0=c0, gi=gi: e.dma_start(
                        out=send_o[dest, 64 + 16 * gi:64 + 16 * gi + 16, c0:c0 + 512], in_=ys[16 * gi:16 * gi + 16, :]),
                        reads=[("yst", ch % 2)], dma=True)
                    yield

    return scan_gen()


def phase_C(C, io):
    P = C.P
    x, x_out, recv_o, hc_loc, halo, mem = io["x"], io["x_out"], io["recv_o"], io["hc_loc"], io["halo"], io["mem"]
    bcast = lambda ap, n: ap.to_broadcast([128, n])

    xres = C.sb([128, NT, D], F32, "C_x")
    idB = C.sb([128, 128], BF16, "C_idB")
    idF = C.sb([128, 128], F32, "C_idF")
    onesF = C.sb([128, 1], F32, "C_ones")
    st = C.sb([128, 64], F32, "C_st")
    junk = C.sb([128, D], BF16, "C_junk")
    hb = C.sb([128, D], BF16, "C_hb")
    tT = [C.sb([128, 8, 128], BF16, "C_tT%d" % i) for i in range(2)]
    base_mark = C.mark()
    wq = C.sb([128, 8, D], BF16, "C_wq")
    wo = C.sb([128, 8, D], BF16, "C_wo")
    s12_mark = C.mark()

    for t in range(NT):
        P.op("sp", lambda e, t=t: e.dma_start(out=xres[:, t, :], in_=x[t * 128:(t + 1) * 128, :]),
             writes=[("x", t)], dma=True)
    P.op("pool", lambda e: e.dma_start(out=idB, in_=io["ident"]), writes=["idB"], dma=True)
    P.op("sp", lambda e: e.dma_start(out=idF, in_=io["ident"]), writes=["idF"], dma=True)
    P.op("dve", lambda e: e.memset(onesF, 1.0), writes=["ones"])

    nst = [0]

    def stcol(n=1):
        c = nst[0] % 64
        if c + n > 64:
            c = 0
        nst[0] = c + n
        return c

    def sk(c, n=1):
        return [("st", c + i) for i in range(n)]

    def rstd_col(ss_c, n, inv_n):
        tmp, out = stcol(n), stcol(n)
        rstd_ops(P, st[:, ss_c:ss_c + n], st[:, tmp:tmp + n], st[:, out:out + n], inv_n,
                 sk(ss_c, n), sk(tmp, n), sk(out, n))
        return out

    def load_w(dst, src, nk, key, eng="pool"):
        for k in range(nk):
            P.op(eng, lambda e, k=k: e.dma_start(out=dst[:, k, :], in_=src[k * 128:(k + 1) * 128, :]),
                 writes=[(key, k)], dma=True)
        return [(key, k) for k in range(nk)]

    def transposes(src, n, dst, bank, rk, wk, ident=None, f32=False, evac="act"):
        if f32:
            pv = C.banks[bank][:, 0:n * 128].rearrange("p (i t) -> p i t", i=n)
        else:
            pv = bank_bf(C.banks[bank])[:, 0:n * 128].rearrange("p (i t) -> p i t", i=n)
        idt, idk = (idF, "idF") if f32 else (idB, "idB")
        for i in range(n):
            P.op("pe", lambda e, i=i: e.transpose(out=pv[:, i, :], in_=src[:, i * 128:(i + 1) * 128], identity=idt),
                 reads=list(rk) + [idk], writes=[("bank", bank)])
        if evac == "act":
            P.op("act", lambda e: e.copy(out=dst, in_=pv), reads=[("bank", bank)], writes=list(wk))
        else:
            P.op("dve", lambda e: e.tensor_copy(out=dst, in_=pv), reads=[("bank", bank)], writes=list(wk))

    def norm_T(src, gB, gkey, dstT, bank, rk, wk):
        c = stcol()
        P.op("act", lambda e: e.activation(out=junk, in_=src, func=AF.Square, accum_out=st[:, c:c + 1]),
             reads=list(rk), writes=["junk", ("st", c)])
        r = rstd_col(c, 1, 1.0 / D)
        P.op("dve", lambda e: e.scalar_tensor_tensor(out=hb, in0=src, scalar=st[:, r:r + 1], in1=gB,
                                                     op0=ALU.mult, op1=ALU.mult),
             reads=list(rk) + [("st", r), gkey], writes=["hb"])
        transposes(hb, 8, dstT, bank, ["hb"], wk)

    hcT = C.sb([128, 2, 32 + TOK], BF16, "C_hcT")
    Dg = C.sb([128, 2, 31, 128], BF16, "C_Dg")
    cw = C.sb([128, 2, 36], F32, "C_cw")
    cT = C.sb([128, 2, TOK], F32, "C_cT")
    lngB = C.sb([128, 256], F32, "C_lng")
    lnbB = C.sb([128, 256], F32, "C_lnb")
    gbrB = C.sb([128, 512], F32, "C_gbrB")
    gbrT = C.sb([128, 8], F32, "C_gbrT")
    pw2 = C.sb([128, 2, 256], BF16, "C_pw2")
    gluw = C.sb([128, 2, 512], BF16, "C_gluw")
    wout = C.sb([128, 8, D], BF16, "C_wout")
    yT = C.sb([128, 2, TOK], BF16, "C_yT")
    oT = [C.sb([128, 4, 128], F32, "C_oT%d" % i) for i in range(2)]
    osq = C.sb([128, 4, 128], F32, "C_osq")
    og = C.sb([128, 4, 128], BF16, "C_og")
    yn = C.sb([128, 256], F32, "C_yn")
    ge = C.sb([128, 256], F32, "C_ge")
    sw = C.sb([128, 256], BF16, "C_sw")
    swT = C.sb([128, 2, 128], BF16, "C_swT")
    osm = C.sb([128, 256], F32, "C_osm")
    mixed = C.sb([128, 512], BF16, "C_mixed")

    P.op("sp", lambda e: e.dma_start(out=hcT[:, :, 0:32], in_=halo.rearrange("(c p) t -> p c t", p=128)),
         writes=["hcT_h"], dma=True)
    P.op("sp", lambda e: e.dma_start(out=hcT[:, :, 32:], in_=hc_loc.rearrange("(c p) t -> p c t", p=128)),
         writes=["hcT"], dma=True)
    P.op("sp", lambda e: e.dma_start(out=cw[:, :, 0:31], in_=io["conv_dw_wT"].rearrange("(c p) t -> p c t", p=128)),
         writes=["cw"], dma=True)
    for ct in range(2):
        P.op("sp", lambda e, ct=ct: e.dma_start(out=cw[:, ct, 31:32], in_=io["conv_dw_b"][ct * 128:(ct + 1) * 128, :]),
             writes=[("cwb", ct)], dma=True)
    P.op("sp", lambda e: e.dma_start(out=lngB, in_=bcast(io["conv_ln_g"], 256)), writes=["lng"], dma=True)
    P.op("sp", lambda e: e.dma_start(out=lnbB, in_=bcast(io["conv_ln_b"], 256)), writes=["lnb"], dma=True)
    P.op("sp", lambda e: e.dma_start(out=gbrB, in_=bcast(io["branch_g"][:, 512:1024], 512)), writes=["gbrB"], dma=True)
    P.op("sp", lambda e: e.dma_start(out=gbrT, in_=io["branch_gT"]), writes=["gbrT"], dma=True)
    load_w(pw2, io["conv_pw2_w"], 2, "pw2")
    load_w(gluw, io["ssm_glu_w"], 2, "gluw")
    kw_out = load_w(wout, io["w_out"], 8, "wout")
    for j in range(8):
        P.op("pool", lambda e, j=j: e.dma_start(out=yT[(j % 4) * 32:(j % 4) * 32 + 32, j // 4, :], in_=recv_o[j, 64:96, :]),
             writes=[("yT", j)], dma=True)
    load_w(wq, io["xa_wq"], 8, "wq")
    load_w(wo, io["xa_wo"], 8, "wo")
    for ct in range(2):
        for j in range(31):
            eng = "dve" if (j % 2) else "pool"
            P.op(eng, lambda e, ct=ct, j=j: e.tensor_scalar(out=Dg[:, ct, j, :], in0=idF, scalar1=cw[:, ct, j:j + 1],
                                                           scalar2=None, op0=ALU.mult),
                 reads=["idF", "cw"], writes=[("Dg", ct, j)])
    nb = [0]
    for ct in range(2):
        for ch in range(4):
            bk = nb[0] % 2
            nb[0] += 1
            ps = C.banks[bk]
            for j in range(31):
                c0 = 512 * ch + j + 2
                P.op("pe", lambda e, ct=ct, j=j, c0=c0, ps=ps: e.matmul(
                    ps, lhsT=Dg[:, ct, j, :], rhs=hcT[:, ct, c0:c0 + 512], start=(j == 0), stop=(j == 30)),
                    reads=[("Dg", ct, j), "hcT", "hcT_h"], writes=[("bank", bk)])
            P.op("act", lambda e, ct=ct, ch=ch, ps=ps: e.activation(
                out=cT[:, ct, ch * 512:(ch + 1) * 512], in_=ps, func=AF.Identity, bias=cw[:, ct, 31:32]),
                reads=[("bank", bk), ("cwb", ct)], writes=[("cT", ct, ch)])

    def load_o(t):
        par = t % 2
        for k in range(4):
            for hh in range(2):
                P.op("sp", lambda e, k=k, hh=hh, par=par, t=t: e.dma_start(
                    out=oT[par][hh * 64:hh * 64 + 64, k, :], in_=recv_o[2 * k + hh, 0:64, t * 128:(t + 1) * 128]),
                    writes=[("oT", par)], dma=True)

    load_o(0)
    for t in range(NT):
        par = t % 2
        if t + 1 < NT:
            load_o(t + 1)
        tc = slice(t * 128, (t + 1) * 128)
        pc = C.banks[2][:, 0:256]
        for ct in range(2):
            P.op("pe", lambda e, ct=ct, tc=tc: e.transpose(out=pc[:, ct * 128:(ct + 1) * 128], in_=cT[:, ct, tc], identity=idF),
                 reads=[("cT", ct, t // 4), "idF"], writes=[("bank", 2)])
        c_s = stcol()
        P.op("dve", lambda e, c_s=c_s: e.tensor_reduce(out=st[:, c_s:c_s + 1], in_=pc, axis=AX.X, op=ALU.add),
             reads=[("bank", 2)], writes=[("st", c_s)])
        c_m = stcol()
        P.op("dve", lambda e, c_s=c_s, c_m=c_m: e.tensor_scalar(out=st[:, c_m:c_m + 1], in0=st[:, c_s:c_s + 1],
                                                              scalar1=-1.0 / 256, scalar2=None, op0=ALU.mult),
             reads=[("st", c_s)], writes=[("st", c_m)])
        c_v = stcol()
        P.op("act", lambda e, c_m=c_m, c_v=c_v: e.activation(out=junk[:, 0:256], in_=pc, func=AF.Square,
                                                            bias=st[:, c_m:c_m + 1], accum_out=st[:, c_v:c_v + 1]),
             reads=[("bank", 2), ("st", c_m)], writes=["junk", ("st", c_v)])
        c_r = rstd_col(c_v, 1, 1.0 / 256)
        P.op("dve", lambda e, c_m=c_m, c_r=c_r: e.tensor_scalar(out=yn, in0=pc, scalar1=st[:, c_m:c_m + 1],
                                                              scalar2=st[:, c_r:c_r + 1], op0=ALU.add, op1=ALU.mult),
             reads=[("bank", 2), ("st", c_m), ("st", c_r)], writes=["yn"])
        P.op("dve", lambda e: e.tensor_tensor(out=yn, in0=yn, in1=lngB, op=ALU.mult), reads=["yn", "lng"], writes=["yn"])
        P.op("dve", lambda e: e.tensor_tensor(out=yn, in0=yn, in1=lnbB, op=ALU.add), reads=["yn", "lnb"], writes=["yn"])
        P.op("act", lambda e: e.activation(out=ge, in_=yn, func=AF.Exp, scale=-1.0), reads=["yn"], writes=["ge"])
        P.op("dve", lambda e: e.tensor_scalar(out=ge, in0=ge, scalar1=1.0, scalar2=None, op0=ALU.add), reads=["ge"], writes=["ge"])
        P.op("dve", lambda e: e.reciprocal(out=ge, in_=ge), reads=["ge"], writes=["ge"])
        P.op("dve", lambda e: e.tensor_tensor(out=sw, in0=yn, in1=ge, op=ALU.mult), reads=["yn", "ge"], writes=["sw"])
        transposes(sw, 2, swT, 3, ["sw"], ["swT"])
        po = C.banks[2][:, 256:512]
        for ct in range(2):
            P.op("pe", lambda e, ct=ct: e.matmul(po, lhsT=swT[:, ct, :], rhs=pw2[:, ct, :], start=(ct == 0), stop=(ct == 1)),
                 reads=["swT", ("pw2", ct)], writes=[("bank", 2)])
        c_q = stcol()
        P.op("act", lambda e, c_q=c_q: e.activation(out=junk[:, 0:256], in_=po, func=AF.Square, accum_out=st[:, c_q:c_q + 1]),
             reads=[("bank", 2)], writes=["junk", ("st", c_q)])
        c_r2 = rstd_col(c_q, 1, 1.0 / 256)
        P.op("dve", lambda e, c_r2=c_r2: e.scalar_tensor_tensor(out=mixed[:, 0:256], in0=po, scalar=st[:, c_r2:c_r2 + 1],
                                                              in1=gbrB[:, 0:256], op0=ALU.mult, op1=ALU.mult),
             reads=[("bank", 2), ("st", c_r2), "gbrB"], writes=["mixed0"])
        pg = C.banks[3]
        for ct in range(2):
            P.op("pe", lambda e, ct=ct, tc=tc: e.matmul(pg, lhsT=yT[:, ct, tc], rhs=gluw[:, ct, :], start=(ct == 0), stop=(ct == 1)),
                 reads=[("yT", j) for j in range(8)] + [("gluw", ct)], writes=[("bank", 3)])
        P.op("act", lambda e: e.activation(out=ge, in_=pg[:, 256:512], func=AF.Exp, scale=-1.0), reads=[("bank", 3)], writes=["ge"])
        P.op("dve", lambda e: e.tensor_scalar(out=ge, in0=ge, scalar1=1.0, scalar2=None, op0=ALU.add), reads=["ge"], writes=["ge"])
        P.op("dve", lambda e: e.reciprocal(out=ge, in_=ge), reads=["ge"], writes=["ge"])
        P.op("dve", lambda e: e.tensor_tensor(out=osm, in0=pg[:, 0:256], in1=ge, op=ALU.mult), reads=[("bank", 3), "ge"], writes=["osm"])
        c_q3 = stcol()
        P.op("act", lambda e, c_q3=c_q3: e.activation(out=junk[:, 0:256], in_=osm, func=AF.Square, accum_out=st[:, c_q3:c_q3 + 1]),
             reads=["osm"], writes=["junk", ("st", c_q3)])
        c_r3 = rstd_col(c_q3, 1, 1.0 / 256)
        P.op("dve", lambda e, c_r3=c_r3: e.scalar_tensor_tensor(out=mixed[:, 256:512], in0=osm, scalar=st[:, c_r3:c_r3 + 1],
                                                              in1=gbrB[:, 256:512], op0=ALU.mult, op1=ALU.mult),
             reads=["osm", ("st", c_r3), "gbrB"], writes=["mixed1"])
        mT = tT[0]
        transposes(mixed, 4, mT[:, 0:4, :], 3, ["mixed0", "mixed1"], ["mT"])
        P.op("act", lambda e, par=par: e.activation(out=osq, in_=oT[par], func=AF.Square), reads=[("oT", par)], writes=["osq"])
        pss = C.banks[2][:, 0:1]
        for k in range(4):
            P.op("pe", lambda e, k=k: e.matmul(pss, lhsT=osq[:, k, :], rhs=onesF, start=(k == 0), stop=(k == 3)),
                 reads=["osq", "ones"], writes=[("bank", 2)])
        c_q1 = stcol()
        P.op("dve", lambda e, c_q1=c_q1: e.tensor_copy(out=st[:, c_q1:c_q1 + 1], in_=pss), reads=[("bank", 2)], writes=[("st", c_q1)])
        c_r1 = rstd_col(c_q1, 1, 1.0 / 512)
        P.op("dve", lambda e, par=par: e.tensor_tensor(out=og, in0=oT[par], in1=gbrT[:, 0:4].unsqueeze(2).to_broadcast([128, 4, 128]),
                                                      op=ALU.mult), reads=[("oT", par), "gbrT"], writes=["og"])
        for n in range(2):
            p1 = C.banks[4 + n]
            p2 = C.banks[6 + n]
            for k in range(4):
                P.op("pe", lambda e, k=k, n=n, p1=p1: e.matmul(p1, lhsT=og[:, k, :], rhs=wout[:, k, n * 512:(n + 1) * 512],
                                                               start=(k == 0), stop=(k == 3)),
                     reads=["og", ("wout", k)], writes=[("bank", 4 + n)])
            for k in range(4):
                P.op("pe", lambda e, k=k, n=n, p2=p2: e.matmul(p2, lhsT=mT[:, k, :], rhs=wout[:, 4 + k, n * 512:(n + 1) * 512],
                                                               start=(k == 0), stop=(k == 3)),
                     reads=["mT", ("wout", 4 + k)], writes=[("bank", 6 + n)])
            xs = xres[:, t, n * 512:(n + 1) * 512]
            P.op("dve", lambda e, xs=xs, p2=p2: e.tensor_tensor(out=xs, in0=xs, in1=p2, op=ALU.add),
                 reads=[("bank", 6 + n), ("x", t)], writes=[("x", t)])
            P.op("dve", lambda e, xs=xs, p1=p1, c_r1=c_r1: e.scalar_tensor_tensor(
                out=xs, in0=p1, scalar=st[:, c_r1:c_r1 + 1], in1=xs, op0=ALU.mult, op1=ALU.add),
                reads=[("bank", 4 + n), ("st", c_r1), ("x", t)], writes=[("x", t)])

    P.barrier()
    C.release(s12_mark)
    kTp = C.sb([128, 8, 256], BF16, "C_kTp")
    Vm = C.sb([128, 2, D], BF16, "C_V")
    gxaB = C.sb([128, D], F32, "C_gxa")
    gkq = C.sb([128, 4, 256], F32, "C_gkq")
    qb = C.sb([128, D], BF16, "C_qb")
    Pb = C.sb([128, 4, 256], BF16, "C_Pb")
    PT = C.sb([128, 8, 128], BF16, "C_PT")
    ob = C.sb([128, D], BF16, "C_ob")
    s2_mark = C.mark()
    wk_ = C.sb([128, 8, D], BF16, "C_wk")
    wv_ = C.sb([128, 8, D], BF16, "C_wv")
    gmB = C.sb([128, D], F32, "C_gm")
    memt = C.sb([128, D], F32, "C_memt")
    hmT = C.sb([128, 8, 256], BF16, "C_hmT")
    kf = C.sb([128, D], F32, "C_kf")
    gq4 = C.sb([128, 256], F32, "C_gq4")

    kwk = load_w(wk_, io["xa_wk"], 8, "wk")
    kwv = load_w(wv_, io["xa_wv"], 8, "wv")
    P.op("sp", lambda e: e.dma_start(out=gxaB, in_=bcast(io["norm_xa_g"], D)), writes=["gxa"], dma=True)
    P.op("sp", lambda e: e.dma_start(out=gmB, in_=bcast(io["norm_mem_g"], D)), writes=["gm"], dma=True)
    P.op("sp", lambda e: e.dma_start(out=gkq[:, 0, :], in_=bcast(io["xa_k_g"], 256)), writes=["gkq"], dma=True)
    P.op("sp", lambda e: e.dma_start(out=gq4, in_=bcast(io["xa_q_g"], 256)), writes=["gq4"], dma=True)
    P.op("dve", lambda e: e.scalar_tensor_tensor(out=gkq[:, 0, :], in0=gkq[:, 0, :], scalar=1.0 / 16, in1=gq4,
                                                 op0=ALU.mult, op1=ALU.mult), reads=["gkq", "gq4"], writes=["gkq"])
    for h in range(1, 4):
        P.op("dve", lambda e, h=h: e.tensor_copy(out=gkq[:, h, :], in_=gkq[:, 0, :]), reads=["gkq"], writes=[("gkq", h)])
    gkq_keys = ["gkq"] + [("gkq", h) for h in range(1, 4)]
    for mt in range(2):
        P.op("sp", lambda e, mt=mt: e.dma_start(out=memt, in_=mem[mt * 128:(mt + 1) * 128, :]), writes=["memt"], dma=True)
        norm_T(memt, gmB, "gm", tT[0], 4, ["memt"], ["tT0"])
        P.op("act", lambda e, mt=mt: e.copy(out=hmT[:, :, mt * 128:(mt + 1) * 128], in_=tT[0]), reads=["tT0"], writes=[("hmT", mt)])
        for n in range(2):
            pk = C.banks[n]
            for k in range(8):
                P.op("pe", lambda e, k=k, n=n, pk=pk: e.matmul(pk, lhsT=tT[0][:, k, :], rhs=wk_[:, k, n * 512:(n + 1) * 512],
                                                               start=(k == 0), stop=(k == 7)),
                     reads=["tT0", ("wk", k)], writes=[("bank", n)])
            P.op("act", lambda e, n=n, pk=pk: e.copy(out=kf[:, n * 512:(n + 1) * 512], in_=pk), reads=[("bank", n)], writes=[("kf", n)])
        c_k = stcol(4)
        for h in range(4):
            P.op("act", lambda e, h=h, c_k=c_k: e.activation(out=junk[:, 0:256], in_=kf[:, h * 256:(h + 1) * 256], func=AF.Square,
                                                            accum_out=st[:, c_k + h:c_k + h + 1]),
                 reads=[("kf", h // 2)], writes=["junk", ("st", c_k + h)])
        outc = rstd_col(c_k, 4, 1.0 / 256)
        P.op("dve", lambda e, outc=outc: e.tensor_tensor(
            out=kf.rearrange("p (h d) -> p h d", h=4), in0=kf.rearrange("p (h d) -> p h d", h=4),
            in1=st[:, outc:outc + 4].unsqueeze(2).to_broadcast([128, 4, 256]), op=ALU.mult),
            reads=[("kf", 0), ("kf", 1)] + sk(outc, 4), writes=[("kf", 0), ("kf", 1)])
        P.op("dve", lambda e: e.tensor_tensor(out=hb, in0=kf, in1=gkq.rearrange("p h d -> p (h d)"), op=ALU.mult),
             reads=[("kf", 0), ("kf", 1)] + gkq_keys, writes=["hb"])
        transposes(hb, 8, kTp[:, :, mt * 128:(mt + 1) * 128], 5, ["hb"], [("kTp", mt)])
        for n in range(2):
            pv_ = C.banks[2 + n]
            for k in range(8):
                P.op("pe", lambda e, k=k, n=n, pv_=pv_: e.matmul(pv_, lhsT=tT[0][:, k, :], rhs=wv_[:, k, n * 512:(n + 1) * 512],
                                                                 start=(k == 0), stop=(k == 7)),
                     reads=["tT0", ("wv", k)], writes=[("bank", 2 + n)])
            P.op("act", lambda e, n=n, mt=mt, pv_=pv_: e.copy(out=Vm[:, mt, n * 512:(n + 1) * 512], in_=pv_),
                 reads=[("bank", 2 + n)], writes=[("V", mt)])

    for t in range(NT):
        xt_ = xres[:, t, :]
        norm_T(xt_, gxaB, "gxa", tT[1], 4, [("x", t)], ["tT1"])
        for n in range(2):
            pq = C.banks[n]
            for k in range(8):
                P.op("pe", lambda e, k=k, n=n, pq=pq: e.matmul(pq, lhsT=tT[1][:, k, :], rhs=wq[:, k, n * 512:(n + 1) * 512],
                                                               start=(k == 0), stop=(k == 7)),
                     reads=["tT1", ("wq", k)], writes=[("bank", n)])
        c_q = stcol(4)
        for h in range(4):
            pqh = C.banks[h // 2][:, (h % 2) * 256:(h % 2) * 256 + 256]
            P.op("act", lambda e, h=h, c_q=c_q, pqh=pqh: e.activation(out=junk[:, 0:256], in_=pqh, func=AF.Square,
                                                                    accum_out=st[:, c_q + h:c_q + h + 1]),
                 reads=[("bank", h // 2)], writes=["junk", ("st", c_q + h)])
        rq = rstd_col(c_q, 4, 1.0 / 256)
        for n in range(2):
            P.op("act", lambda e, n=n: e.copy(out=qb[:, n * 512:(n + 1) * 512], in_=C.banks[n]), reads=[("bank", n)], writes=[("qb", n)])
        transposes(qb, 8, tT[0], 5, [("qb", 0), ("qb", 1)], ["tT0"])
        for h in range(4):
            psc = C.banks[2 + h // 2][:, (h % 2) * 256:(h % 2) * 256 + 256]
            for j in range(2):
                P.op("pe", lambda e, h=h, j=j, psc=psc: e.matmul(psc, lhsT=tT[0][:, 2 * h + j, :], rhs=kTp[:, 2 * h + j, :],
                                                                 start=(j == 0), stop=(j == 1)),
                     reads=["tT0", ("kTp", 0), ("kTp", 1)], writes=[("bank", 2 + h // 2)])
        c_mx = stcol(4)
        for b2 in range(2):
            P.op("dve", lambda e, b2=b2, c_mx=c_mx: e.tensor_reduce(
                out=st[:, c_mx + 2 * b2:c_mx + 2 * b2 + 2], in_=C.banks[2 + b2].rearrange("p (h m) -> p h m", h=2),
                axis=AX.X, op=ALU.max), reads=[("bank", 2 + b2)], writes=sk(c_mx + 2 * b2, 2))
        c_nb = stcol(4)
        P.op("dve", lambda e, c_mx=c_mx, c_nb=c_nb, rq=rq: e.scalar_tensor_tensor(
            out=st[:, c_nb:c_nb + 4], in0=st[:, c_mx:c_mx + 4], scalar=-1.0, in1=st[:, rq:rq + 4], op0=ALU.mult, op1=ALU.mult),
            reads=sk(c_mx, 4) + sk(rq, 4), writes=sk(c_nb, 4))
        c_rs = stcol(4)
        for h in range(4):
            psc = C.banks[2 + h // 2][:, (h % 2) * 256:(h % 2) * 256 + 256]
            P.op("act", lambda e, h=h, psc=psc, rq=rq, c_nb=c_nb, c_rs=c_rs: e.activation(
                out=Pb[:, h, :], in_=psc, func=AF.Exp, scale=st[:, rq + h:rq + h + 1], bias=st[:, c_nb + h:c_nb + h + 1],
                accum_out=st[:, c_rs + h:c_rs + h + 1]),
                reads=[("bank", 2 + h // 2), ("st", rq + h), ("st", c_nb + h)], writes=[("Pb", h), ("st", c_rs + h)])
        c_ri = stcol(4)
        P.op("dve", lambda e, c_rs=c_rs, c_ri=c_ri: e.reciprocal(out=st[:, c_ri:c_ri + 4], in_=st[:, c_rs:c_rs + 4]),
             reads=sk(c_rs, 4), writes=sk(c_ri, 4))
        transposes(Pb.rearrange("p h m -> p (h m)"), 8, PT, 6, [("Pb", h) for h in range(4)], ["PT"])
        for h in range(4):
            po_ = C.banks[h // 2][:, (h % 2) * 256:(h % 2) * 256 + 256]
            for mt in range(2):
                P.op("pe", lambda e, h=h, mt=mt, po_=po_: e.matmul(po_, lhsT=PT[:, 2 * h + mt, :], rhs=Vm[:, mt, h * 256:(h + 1) * 256],
                                                                   start=(mt == 0), stop=(mt == 1)),
                     reads=["PT", ("V", mt)], writes=[("bank", h // 2)])
        for n in range(2):
            P.op("dve", lambda e, n=n, c_ri=c_ri: e.tensor_tensor(
                out=ob[:, n * 512:(n + 1) * 512].rearrange("p (h d) -> p h d", h=2),
                in0=C.banks[n].rearrange("p (h d) -> p h d", h=2),
                in1=st[:, c_ri + 2 * n:c_ri + 2 * n + 2].unsqueeze(2).to_broadcast([128, 2, 256]), op=ALU.mult),
                reads=[("bank", n)] + sk(c_ri + 2 * n, 2), writes=[("ob", n)])
        transposes(ob, 8, tT[1], 7, [("ob", 0), ("ob", 1)], ["tT1"])
        for n in range(2):
            pw_ = C.banks[2 + n]
            for k in range(8):
                P.op("pe", lambda e, k=k, n=n, pw_=pw_: e.matmul(pw_, lhsT=tT[1][:, k, :], rhs=wo[:, k, n * 512:(n + 1) * 512],
                                                                 start=(k == 0), stop=(k == 7)),
                     reads=["tT1", ("wo", k)], writes=[("bank", 2 + n)])
            xs = xres[:, t, n * 512:(n + 1) * 512]
            P.op("dve", lambda e, xs=xs, pw_=pw_: e.tensor_tensor(out=xs, in0=xs, in1=pw_, op=ALU.add),
                 reads=[("bank", 2 + n), ("x", t)], writes=[("x", t)])

    P.barrier()
    C.release(base_mark)
    gfB = C.sb([128, D], F32, "C_gf")
    hfT = C.sb([128, 8, 1024], BF16, "C_hfT")
    hid = C.sb([128, 22, 1024], BF16, "C_hid")
    Wo_ = C.sb([128, 22, D], BF16, "C_Wo")
    Wg = [C.sb([128, 8, 128], BF16, "C_Wg%d" % i) for i in range(2)]
    Wu = [C.sb([128, 8, 128], BF16, "C_Wu%d" % i) for i in range(2)]
    sg = [C.sb([128, 512], F32, "C_sg%d" % i) for i in range(2)]
    P.op("sp", lambda e: e.dma_start(out=gfB, in_=bcast(io["norm_ffn_g"], D)), writes=["gf"], dma=True)
    w_in_v = io["ffn_w_in"].rearrange("(k p) n -> p k n", p=128)

    def load_wo(j):
        P.op("pool", lambda e, j=j: e.dma_start(out=Wo_[:, j, :], in_=io["ffn_w_out"][j * 128:(j + 1) * 128, :]),
             writes=[("Wo", j)], dma=True)

    def load_gu(j):
        par = j % 2
        P.op("pool", lambda e, j=j, par=par: e.dma_start(out=Wg[par], in_=w_in_v[:, :, j * 128:(j + 1) * 128]),
             writes=[("Wg", par)], dma=True)
        P.op("pool", lambda e, j=j, par=par: e.dma_start(out=Wu[par], in_=w_in_v[:, :, FFN_H + j * 128:FFN_H + (j + 1) * 128]),
             writes=[("Wu", par)], dma=True)

    nsg = [0]
    for half in range(2):
        for tt in range(8):
            t = half * 8 + tt
            norm_T(xres[:, t, :], gfB, "gf", hfT[:, :, tt * 128:(tt + 1) * 128], 4 + tt % 2, [("x", t)], [("hfT", tt)])
        hkeys = [("hfT", tt) for tt in range(8)]
        load_gu(0)
        for j in range(22):
            par = j % 2
            if j + 1 < 22:
                load_gu(j + 1)
            if half == 0:
                load_wo(j)
            for tc2 in range(2):
                pg_ = C.banks[0 + tc2]
                pu_ = C.banks[2 + tc2]
                for k in range(8):
                    P.op("pe", lambda e, k=k, par=par, tc2=tc2, pg_=pg_: e.matmul(
                        pg_, lhsT=Wg[par][:, k, :], rhs=hfT[:, k, tc2 * 512:(tc2 + 1) * 512], start=(k == 0), stop=(k == 7)),
                        reads=[("Wg", par)] + hkeys[tc2 * 4:tc2 * 4 + 4], writes=[("bank", tc2)])
                for k in range(8):
                    P.op("pe", lambda e, k=k, par=par, tc2=tc2, pu_=pu_: e.matmul(
                        pu_, lhsT=Wu[par][:, k, :], rhs=hfT[:, k, tc2 * 512:(tc2 + 1) * 512], start=(k == 0), stop=(k == 7)),
                        reads=[("Wu", par)] + hkeys[tc2 * 4:tc2 * 4 + 4], writes=[("bank", 2 + tc2)])
                sp_ = nsg[0] % 2
                nsg[0] += 1
                P.op("act", lambda e, sp_=sp_, pg_=pg_: e.activation(out=sg[sp_], in_=pg_, func=AF.Silu),
                     reads=[("bank", tc2)], writes=[("sg", sp_)])
                P.op("dve", lambda e, sp_=sp_, pu_=pu_, j=j, tc2=tc2: e.tensor_tensor(
                    out=hid[:, j, tc2 * 512:(tc2 + 1) * 512], in0=sg[sp_], in1=pu_, op=ALU.mult),
                    reads=[("sg", sp_), ("bank", 2 + tc2)], writes=[("hid", j, tc2)])
        for tt in range(8):
            t = half * 8 + tt
            for n in range(2):
                bk = 4 + (2 * tt + n) % 4
                pd = C.banks[bk]
                for j in range(22):
                    P.op("pe", lambda e, j=j, n=n, tt=tt, pd=pd: e.matmul(
                        pd, lhsT=hid[:, j, tt * 128:(tt + 1) * 128], rhs=Wo_[:, j, n * 512:(n + 1) * 512],
                        start=(j == 0), stop=(j == 21)),
                        reads=[("hid", j, tt // 4), ("Wo", j)], writes=[("bank", bk)])
                xs = xres[:, t, n * 512:(n + 1) * 512]
                P.op("dve", lambda e, xs=xs, pd=pd: e.tensor_tensor(out=xs, in0=xs, in1=pd, op=ALU.add),
                     reads=[("bank", bk), ("x", t)], writes=[("x", t)])
            P.op("sp", lambda e, t=t: e.dma_start(out=x_out[t * 128:(t + 1) * 128, :], in_=xres[:, t, :]),
                 reads=[("x", t)], dma=True)


C_IN = [("x", [TOK, D], F32), ("recv_o", [8, 96, TOK], F32), ("hc_loc", [256, TOK], BF16), ("halo", [256, 32], BF16),
        ("mem", [256, D], F32), ("ident", [128, 128], F32),
        ("conv_dw_wT", [256, 31], F32), ("conv_dw_b", [256, 1], F32), ("conv_ln_g", [1, 256], F32), ("conv_ln_b", [1, 256], F32),
        ("conv_pw2_w", [256, 256], F32), ("ssm_glu_w", [256, 512], F32), ("branch_g", [1, D], F32), ("branch_gT", [128, 8], F32),
        ("w_out", [D, D], F32), ("norm_xa_g", [1, D], F32), ("norm_mem_g", [1, D], F32),
        ("xa_wq", [D, D], F32), ("xa_wk", [D, D], F32), ("xa_wv", [D, D], F32), ("xa_wo", [D, D], F32),
        ("xa_q_g", [1, 256], F32), ("xa_k_g", [1, 256], F32), ("norm_ffn_g", [1, D], F32),
        ("ffn_w_in", [D, 2 * FFN_H], F32), ("ffn_w_out", [FFN_H, D], F32)]


def build_C():
    nc = bass.Bass("TRN2", target_bir_lowering=False)
    es = ExitStack()
    C = Ctx(nc, es)
    io = {n: C.din(n, s, d) for n, s, d in C_IN}
    io["x_out"] = C.dout("x_out", [TOK, D], F32)
    phase_C(C, io)
    C.P.emit(es)
    es.close()
    return nc


_PROGS = {}


def _prog(name, fn):
    if name not in _PROGS:
        _PROGS[name] = fn()
    return _PROGS[name]


def _run(nc, in_maps):
    res = run_bass_kernel_spmd(nc, in_maps, core_ids=list(range(NCORES)))
    return res.results


def _c_weights(w, l):
    f = np.ascontiguousarray
    return dict(
        conv_dw_wT=f(w["conv_dw_w"][l].T), conv_dw_b=f(w["conv_dw_b"][l][:, None]),
        conv_ln_g=f(w["conv_ln_g"][l][None]), conv_ln_b=f(w["conv_ln_b"][l][None]),
        conv_pw2_w=f(w["conv_pw2_w"][l]), ssm_glu_w=f(w["ssm_glu_w"][l]),
        branch_g=f(w["branch_norm_g"][l][None]), branch_gT=f(w["branch_norm_g"][l].reshape(8, 128).T),
        w_out=f(w["w_out"][l]), norm_xa_g=f(w["norm_xa_g"][l][None]), norm_mem_g=f(w["norm_mem_g"][l][None]),
        xa_wq=f(w["xa_wq"][l]), xa_wk=f(w["xa_wk"][l]), xa_wv=f(w["xa_wv"][l]), xa_wo=f(w["xa_wo"][l]),
        xa_q_g=f(w["xa_q_norm_g"][l][None]), xa_k_g=f(w["xa_k_norm_g"][l][None]),
        norm_ffn_g=f(w["norm_ffn_g"][l][None]), ffn_w_in=f(w["ffn_w_in"][l]), ffn_w_out=f(w["ffn_w_out"][l]))


def kernel(**inputs):
    w = {k: np.asarray(v, dtype=np.float32) for k, v in inputs.items()}
    x = w["x"]
    mem = w["mem"]
    bsz, L, _ = x.shape
    assert (bsz, L) == (2, SEQ)
    ident = np.eye(128, dtype=np.float32)
    consts = attn_consts()
    xs = [np.ascontiguousarray(x.reshape(NCORES, TOK, D)[c]) for c in range(NCORES)]
    ncA = _prog("A", build_A)
    ncB = _prog("B", build_B)
    ncC = _prog("C", build_C)
    for l in range(2):
        ra = _run(ncA, [dict(x=xs[c], w_in=w["w_in"][l], g_mix=w["norm_mix_g"][l][None], gq=w["sb_q_norm_g"][l][None],
                             gk=w["sb_k_norm_g"][l][None], ident=ident) for c in range(NCORES)])
        send_bf = [np.asarray(ra[c]["send_bf"]) for c in range(NCORES)]
        send_u = [np.asarray(ra[c]["send_u"]) for c in range(NCORES)]
        hc_loc = [np.asarray(ra[c]["hc_loc"]) for c in range(NCORES)]
        in_b = []
        for j in range(NCORES):
            in_b.append(dict(
                recv_bf=np.ascontiguousarray(np.stack([send_bf[c][j] for c in range(NCORES)])),
                recv_u=np.ascontiguousarray(np.stack([send_u[c][j] for c in range(NCORES)])),
                consts_bf=consts,
                ssm_par=ssm_pack(w["ssm_lam_re"][l], w["ssm_lam_im"][l], w["ssm_log_dt"][l], w["ssm_b_re"][l],
                                 w["ssm_b_im"][l], w["ssm_c_re"][l], w["ssm_c_im"][l], w["ssm_d"][l], j)))
        rb = _run(ncB, in_b)
        send_o = [np.asarray(rb[j]["send_o"]) for j in range(NCORES)]
        cw = _c_weights(w, l)
        in_c = []
        for c in range(NCORES):
            if c % 4 == 0:
                halo = np.zeros((256, 32), dtype=send_bf[0].dtype)
            else:
                halo = np.ascontiguousarray(send_bf[c - 1][0][HALO_OFF:].reshape(256, 32))
            m = dict(x=xs[c], recv_o=np.ascontiguousarray(np.stack([send_o[j][c] for j in range(NCORES)])),
                     hc_loc=hc_loc[c], halo=halo, mem=np.ascontiguousarray(mem[c // 4]), ident=ident)
            m.update(cw)
            in_c.append(m)
        rc = _run(ncC, in_c)
        xs = [np.asarray(rc[c]["x_out"]) for c in range(NCORES)]
    return np.stack(xs).reshape(bsz, L, D).astype(np.float32)
```

```python
import numpy as np
from contextlib import ExitStack
import concourse.bass as bass
import concourse.mybir as mybir
from concourse.bass_utils import run_bass_kernel_spmd

F32 = mybir.dt.float32
BF16 = mybir.dt.bfloat16
AF = mybir.ActivationFunctionType
ALU = mybir.AluOpType
AX = mybir.AxisListType

NCORES = 8
D = 1024
TOK = 2048
NT = TOK // 128
SEQ = 8192
NINP = 2304
FFN_H = 2816
EPS = 1e-6
QKV_SZ = 64 * TOK
HALO_OFF = 3 * QKV_SZ
SB_SZ = HALO_OFF + 256 * 32
NDMA_SEM = 12


class _Op:
    __slots__ = ("eng", "fn", "deps", "dma", "signal", "seq", "dsem", "dval", "qi")

    def __init__(self, eng, fn, dma):
        self.eng = eng
        self.fn = fn
        self.deps = []
        self.dma = dma
        self.signal = False
        self.seq = 0
        self.dsem = None
        self.dval = 0
        self.qi = 0


class Prog:
    ENGS = ("pe", "act", "dve", "pool", "sp")

    def __init__(self, nc):
        self.nc = nc
        self.ops = {e: [] for e in self.ENGS}
        self.last_w = {}
        self.readers = {}
        self.ndma = {e: 0 for e in self.ENGS}
        self.bar = []
        self._pid = {}

    def op(self, eng, fn, reads=(), writes=(), dma=False):
        o = _Op(eng, fn, dma)
        if dma:
            o.qi = self.ndma[eng]
            self.ndma[eng] += 1

        def add(p, kind):
            if p is o:
                return
            if not p.dma and p.eng == eng and not dma:
                if eng == "pe":
                    return
            if p not in o.deps:
                o.deps.append(p)
                p.signal = True

        for p in self.bar:
            add(p, "bar")
        for k in reads:
            p = self.last_w.get(k)
            if p is not None:
                add(p, "raw")
        for k in writes:
            p = self.last_w.get(k)
            if p is not None:
                add(p, "waw")
            for r in self.readers.get(k, ()):
                add(r, "war")
        for k in reads:
            self.readers.setdefault(k, []).append(o)
        for k in writes:
            self.last_w[k] = o
            self.readers[k] = []
        self.ops[eng].append(o)
        return o

    def pid(self, eng):
        k = id(eng)
        if k not in self._pid:
            self._pid[k] = eng.partition_id() % NCORES
        return self._pid[k]

    def barrier(self):
        self.bar = [self.ops[e][-1] for e in self.ENGS if self.ops[e]]
        for e in self.ENGS:
            for o in self.ops[e][-NDMA_SEM:]:
                if o.dma and o not in self.bar:
                    self.bar.append(o)
        self.last_w = {}
        self.readers = {}

    def emit(self, es):
        nc = self.nc
        esem = {e: es.enter_context(nc.semaphore("s_" + e)) for e in self.ENGS}
        dsems = {}
        for e in self.ENGS:
            if self.ndma[e]:
                dsems[e] = [es.enter_context(nc.semaphore("d_%s%d" % (e, i)))
                            for i in range(min(NDMA_SEM, self.ndma[e]))]
        for e in self.ENGS:
            n = 0
            for o in self.ops[e]:
                if o.dma:
                    o.dsem = dsems[e][o.qi % NDMA_SEM]
                    o.dval = 16 * (o.qi // NDMA_SEM + 1)
                elif o.signal:
                    n += 1
                    o.seq = n
        final_waits = []
        for e in self.ENGS:
            if self.ndma[e]:
                for i, s in enumerate(dsems[e]):
                    cnt = len(range(i, self.ndma[e], NDMA_SEM))
                    final_waits.append((s, 16 * cnt))

        def run(e, eng):
            waited = {}

            def wait(sem, val):
                key = id(sem)
                if waited.get(key, 0) >= val:
                    return
                eng.wait_ge(sem, val)
                waited[key] = val

            for o in self.ops[e]:
                for p in o.deps:
                    if p.dma:
                        wait(p.dsem, p.dval)
                    else:
                        wait(esem[p.eng], p.seq)
                if o.dma:
                    if o.qi >= NDMA_SEM:
                        wait(o.dsem, o.dval - 16)
                    o.fn(eng).then_inc(o.dsem, 16)
                else:
                    ins = o.fn(eng)
                    if o.signal:
                        ins.then_inc(esem[e], 1)
            if e == "sp":
                for s, v in final_waits:
                    wait(s, v)

        with nc.Block() as block:
            @block.tensor
            def _(eng):
                run("pe", eng)

            @block.scalar
            def _(eng):
                run("act", eng)

            @block.vector
            def _(eng):
                run("dve", eng)

            @block.gpsimd
            def _(eng):
                run("pool", eng)

            @block.sync
            def _(eng):
                run("sp", eng)


ARENA_F32 = 50688


class Ctx:
    def __init__(self, nc, es):
        self.nc = nc
        self.es = es
        self.P = Prog(nc)
        self.n = 0
        self.banks = [es.enter_context(nc.psum_tensor("bank%d" % i, [128, 512], F32))[:]
                      for i in range(8)]
        self.arena = es.enter_context(nc.sbuf_tensor("arena", [128, ARENA_F32], F32))[:]
        self.off = 0

    def sb(self, shape, dtype, name=None):
        shape = list(shape)
        esz = mybir.dt.size(dtype)
        n = 1
        for d in shape[1:]:
            n *= d
        nf = (n * esz + 31) // 32 * 8
        assert self.off + nf <= ARENA_F32, "SBUF arena overflow (%s)" % name
        v = self.arena[0:shape[0], self.off:self.off + nf]
        self.off += nf
        if dtype != F32:
            v = v.bitcast(dtype)
        v = v[:, 0:n]
        if len(shape) > 2:
            names = " ".join("d%d" % i for i in range(1, len(shape)))
            v = v.rearrange("p (%s) -> p %s" % (names, names),
                            **{"d%d" % i: shape[i] for i in range(1, len(shape))})
        return v

    def mark(self):
        return self.off

    def release(self, m):
        self.off = m

    def din(self, name, shape, dtype):
        return self.nc.dram_tensor(name, list(shape), dtype, kind="ExternalInput").ap()

    def dout(self, name, shape, dtype):
        return self.nc.dram_tensor(name, list(shape), dtype, kind="ExternalOutput").ap()

    def dint(self, name, shape, dtype):
        return self.nc.dram_tensor(name, list(shape), dtype, kind="Internal").ap()


def bank_bf(bank):
    return bank.bitcast(BF16)


def rstd_ops(P, st_in, st_tmp, st_out, inv_n, key_in, key_tmp, key_out):
    kl = lambda k: list(k) if isinstance(k, list) else [k]
    key_in, key_tmp, key_out = kl(key_in), kl(key_tmp), kl(key_out)
    P.op("dve", lambda e: e.tensor_scalar(out=st_tmp, in0=st_in, scalar1=inv_n, scalar2=EPS,
                                          op0=ALU.mult, op1=ALU.add),
         reads=key_in, writes=key_tmp)
    P.op("act", lambda e: e.activation(out=st_tmp, in_=st_tmp, func=AF.Ln),
         reads=key_tmp, writes=key_tmp)
    P.op("act", lambda e: e.activation(out=st_out, in_=st_tmp, func=AF.Exp, scale=-0.5),
         reads=key_tmp, writes=key_out)


def phase_A(C, x, w_in, g_mix, gq, gk, ident, send_bf, send_u, hc_loc):
    P = C.P
    W = C.sb([128, 8, NINP], BF16, "A_W")
    gB = C.sb([128, D], F32, "A_gB")
    gqB = C.sb([128, 8, 64], F32, "A_gqB")
    gkB = C.sb([128, 8, 64], F32, "A_gkB")
    idB = C.sb([128, 128], BF16, "A_idB")
    idF = C.sb([128, 128], F32, "A_idF")
    xt = [C.sb([128, D], F32, "A_xt%d" % i) for i in range(3)]
    junk_ = [C.sb([128, D], BF16, "A_junk%d" % i) for i in range(3)]
    hb = [C.sb([128, D], BF16, "A_hb%d" % i) for i in range(3)]
    hT = [C.sb([128, 8, 128], BF16, "A_hT%d" % i) for i in range(3)]
    st = [C.sb([128, 32], F32, "A_st%d" % i) for i in range(3)]
    qsq_ = [C.sb([128, 512], F32, "A_qsq%d" % i) for i in range(3)]
    qtmp_ = [C.sb([128, 512], F32, "A_qtmp%d" % i) for i in range(3)]
    qn_ = [C.sb([128, 512], BF16, "A_qn%d" % i) for i in range(3)]
    ge_ = [C.sb([128, 256], F32, "A_ge%d" % i) for i in range(3)]
    hcn_ = [C.sb([128, 256], BF16, "A_hcn%d" % i) for i in range(3)]
    uf_ = [C.sb([128, 256], F32, "A_uf%d" % i) for i in range(3)]
    qT_all = C.sb([128, 4, TOK], BF16, "A_qT")
    kT_all = C.sb([128, 4, TOK], BF16, "A_kT")
    v_all = C.sb([128, NT, 512], BF16, "A_v")
    hcT_all = C.sb([128, 2, TOK], BF16, "A_hcT")
    uT_all = C.sb([128, 2, TOK], F32, "A_uT")

    for ci, (c0, cw) in enumerate([(0, 512), (512, 512), (1024, 512), (1536, 512), (2048, 256)]):
        for k in range(8):
            P.op("pool", lambda e, k=k, c0=c0, cw=cw: e.dma_start(out=W[:, k, c0:c0 + cw],
                                                                 in_=w_in[k * 128:(k + 1) * 128, c0:c0 + cw]),
                 writes=[("W", k, ci)], dma=True)
    P.op("sp", lambda e: e.dma_start(out=gB, in_=g_mix.to_broadcast([128, D])), writes=["gB"], dma=True)
    P.op("sp", lambda e: e.dma_start(out=gqB, in_=gq.unsqueeze(1).to_broadcast([128, 8, 64])),
         writes=["gqB"], dma=True)
    P.op("sp", lambda e: e.dma_start(out=gkB, in_=gk.unsqueeze(1).to_broadcast([128, 8, 64])),
         writes=["gkB"], dma=True)
    P.op("pool", lambda e: e.dma_start(out=idB, in_=ident), writes=["idB"], dma=True)
    P.op("sp", lambda e: e.dma_start(out=idF, in_=ident), writes=["idF"], dma=True)
    P.op("dve", lambda e: e.tensor_scalar(out=gqB, in0=gqB, scalar1=0.125, scalar2=None, op0=ALU.mult),
         reads=["gqB"], writes=["gqB"])

    chunks = [(0, 512), (512, 512), (1024, 512), (1536, 512), (2048, 256)]
    nbank = [0]

    def obank():
        i = 3 + (nbank[0] % 3)
        nbank[0] += 1
        return i
    ntb = [0]

    def tbank():
        i = 6 + (ntb[0] % 2)
        ntb[0] += 1
        return i

    def load_x(t):
        p3 = t % 3
        P.op("sp", lambda e: e.dma_start(out=xt[p3], in_=x[t * 128:(t + 1) * 128, :]),
             writes=[("xt", p3)], dma=True)

    def tile_gen(t):
        par = t % 3
        p3 = t % 3
        if t + 2 < NT and t >= 1:
            load_x(t + 2)
        s = st[par]
        junk, qsq, qtmp, qn, ge, hcn, uf = junk_[par], qsq_[par], qtmp_[par], qn_[par], ge_[par], hcn_[par], uf_[par]
        kj, kqs, kqt, kqn, kge, khc, kuf = (("junk", par), ("qsq", par), ("qtmp", par), ("qn", par), ("ge", par),
                                            ("hcn", par), ("uf", par))
        P.op("act", lambda e, par=par, s=s: e.activation(out=junk, in_=xt[p3], func=AF.Square,
                                                          accum_out=s[:, 0:1]),
             reads=[("xt", p3)], writes=[kj, ("st", par, 0)])
        rstd_ops(P, s[:, 0:1], s[:, 1:2], s[:, 2:3], 1.0 / D, ("st", par, 0), ("st", par, 1), ("st", par, 2))
        P.op("dve", lambda e, par=par, s=s: e.scalar_tensor_tensor(
            out=hb[par], in0=xt[p3], scalar=s[:, 2:3], in1=gB, op0=ALU.mult, op1=ALU.mult),
            reads=[("xt", p3), ("st", par, 2), "gB"], writes=[("hb", par)])
        pT = bank_bf(C.banks[par]).rearrange("p (k t) -> p k t", k=8)
        for k in range(8):
            P.op("pe", lambda e, k=k, pT=pT, par=par: e.transpose(
                out=pT[:, k, :], in_=hb[par][:, k * 128:(k + 1) * 128], identity=idB),
                reads=[("hb", par), "idB"], writes=[("bank", par)])
        P.op("act", lambda e, pT=pT, par=par: e.copy(out=hT[par], in_=pT),
             reads=[("bank", par)], writes=[("hT", par)])
        yield
        for ci, (c0, cw) in enumerate(chunks):
            bi = obank()
            pO = C.banks[bi][:, 0:cw]
            for k in range(8):
                P.op("pe", lambda e, k=k, pO=pO, par=par, c0=c0, cw=cw: e.matmul(
                    pO, lhsT=hT[par][:, k, :], rhs=W[:, k, c0:c0 + cw], start=(k == 0), stop=(k == 7)),
                    reads=[("hT", par), ("W", k, ci)], writes=[("bank", bi)])
            bk = ("bank", bi)
            yield
            if ci in (0, 1):
                dstT = qT_all if ci == 0 else kT_all
                gBt = gqB if ci == 0 else gkB
                gkey = "gqB" if ci == 0 else "gkB"
                P.op("act", lambda e, pO=pO: e.activation(out=qsq, in_=pO, func=AF.Square),
                     reads=[bk], writes=[kqs])
                P.op("dve", lambda e, s=s: e.tensor_reduce(
                    out=s[:, 8:16], in_=qsq.rearrange("p (h d) -> p h d", h=8), axis=AX.X, op=ALU.add),
                    reads=[kqs], writes=[("st", par, 3)])
                rstd_ops(P, s[:, 8:16], s[:, 16:24], s[:, 24:32], 1.0 / 64,
                         ("st", par, 3), ("st", par, 4), ("st", par, 5))
                P.op("dve", lambda e, pO=pO, s=s: e.tensor_tensor(
                    out=qtmp.rearrange("p (h d) -> p h d", h=8),
                    in0=pO.rearrange("p (h d) -> p h d", h=8),
                    in1=s[:, 24:32].unsqueeze(2).to_broadcast([128, 8, 64]), op=ALU.mult),
                    reads=[bk, ("st", par, 5)], writes=[kqt])
                P.op("dve", lambda e, gBt=gBt: e.tensor_tensor(
                    out=qn, in0=qtmp, in1=gBt.rearrange("p h d -> p (h d)"), op=ALU.mult),
                    reads=[kqt, gkey], writes=[kqn])
                tb = tbank()
                pQ = bank_bf(C.banks[tb])[:, 0:512].rearrange("p (j t) -> p j t", j=4)
                for j in range(4):
                    P.op("pe", lambda e, j=j, pQ=pQ: e.transpose(
                        out=pQ[:, j, :], in_=qn[:, j * 128:(j + 1) * 128], identity=idB),
                        reads=[kqn, "idB"], writes=[("bank", tb)])
                P.op("act", lambda e, pQ=pQ, dstT=dstT, t=t: e.copy(
                    out=dstT[:, :, t * 128:(t + 1) * 128], in_=pQ),
                    reads=[("bank", tb)], writes=[("qkT", ci, t)])
            elif ci == 2:
                P.op("act", lambda e, pO=pO, t=t: e.copy(out=v_all[:, t, :], in_=pO),
                     reads=[bk], writes=[("v", t)])
            elif ci == 3:
                P.op("act", lambda e, pO=pO: e.activation(out=ge, in_=pO[:, 256:512], func=AF.Exp, scale=-1.0),
                     reads=[bk], writes=[kge])
                P.op("dve", lambda e: e.tensor_scalar(out=ge, in0=ge, scalar1=1.0, scalar2=None, op0=ALU.add),
                     reads=[kge], writes=[kge])
                P.op("dve", lambda e: e.reciprocal(out=ge, in_=ge), reads=[kge], writes=[kge])
                P.op("dve", lambda e, pO=pO: e.tensor_tensor(out=hcn, in0=pO[:, 0:256], in1=ge, op=ALU.mult),
                     reads=[bk, kge], writes=[khc])
                tb = tbank()
                pQ = bank_bf(C.banks[tb])[:, 0:256].rearrange("p (j t) -> p j t", j=2)
                for j in range(2):
                    P.op("pe", lambda e, j=j, pQ=pQ: e.transpose(
                        out=pQ[:, j, :], in_=hcn[:, j * 128:(j + 1) * 128], identity=idB),
                        reads=[khc, "idB"], writes=[("bank", tb)])
                P.op("act", lambda e, pQ=pQ, t=t: e.copy(out=hcT_all[:, :, t * 128:(t + 1) * 128], in_=pQ),
                     reads=[("bank", tb)], writes=[("hcT", t)])
            else:
                P.op("act", lambda e, pO=pO: e.copy(out=uf, in_=pO), reads=[bk], writes=[kuf])
                tb = tbank()
                pQ = C.banks[tb][:, 0:256].rearrange("p (j t) -> p j t", j=2)
                for j in range(2):
                    P.op("pe", lambda e, j=j, pQ=pQ: e.transpose(
                        out=pQ[:, j, :], in_=uf[:, j * 128:(j + 1) * 128], identity=idF),
                        reads=[kuf, "idF"], writes=[("bank", tb)])
                P.op("act", lambda e, pQ=pQ, t=t: e.copy(out=uT_all[:, :, t * 128:(t + 1) * 128], in_=pQ),
                     reads=[("bank", tb)], writes=[("uT", t)])
        yield

    load_x(0)
    load_x(1)
    load_x(2)
    pending = list(range(NT))
    active = []
    while pending or active:
        while pending and len(active) < 3:
            active.append(tile_gen(pending.pop(0)))
        for g in list(active):
            try:
                next(g)
            except StopIteration:
                active.remove(g)

    allq = [("qkT", 0, t) for t in range(NT)]
    allk = [("qkT", 1, t) for t in range(NT)]
    allv = [("v", t) for t in range(NT)]
    allh = [("hcT", t) for t in range(NT)]
    allu = [("uT", t) for t in range(NT)]
    for j in range(8):
        r0 = (j % 2) * 64
        P.op("sp", lambda e, j=j, r0=r0: e.dma_start(
            out=send_bf[j, 0:QKV_SZ].rearrange("(d t) -> d t", t=TOK), in_=qT_all[r0:r0 + 64, j // 2, :]),
            reads=allq, dma=True)
        P.op("sp", lambda e, j=j, r0=r0: e.dma_start(
            out=send_bf[j, QKV_SZ:2 * QKV_SZ].rearrange("(d t) -> d t", t=TOK), in_=kT_all[r0:r0 + 64, j // 2, :]),
            reads=allk, dma=True)
        P.op("sp", lambda e, j=j: e.dma_start(
            out=send_bf[j, 2 * QKV_SZ:3 * QKV_SZ].rearrange("(n p d) -> p n d", p=128, d=64),
            in_=v_all[:, :, j * 64:(j + 1) * 64]), reads=allv, dma=True)
        P.op("sp", lambda e, j=j: e.dma_start(
            out=send_bf[j, HALO_OFF:SB_SZ].rearrange("(c p t) -> p c t", p=128, t=32),
            in_=hcT_all[:, :, TOK - 32:TOK]), reads=allh, dma=True)
        u0 = (j % 4) * 32
        P.op("sp", lambda e, j=j, u0=u0: e.dma_start(out=send_u[j], in_=uT_all[u0:u0 + 32, j // 4, :]),
             reads=allu, dma=True)
    P.op("sp", lambda e: e.dma_start(out=hc_loc.rearrange("(c p) t -> p c t", p=128), in_=hcT_all),
         reads=allh, dma=True)


def build_A():
    nc = bass.Bass("TRN2", target_bir_lowering=False)
    es = ExitStack()
    C = Ctx(nc, es)
    x = C.din("x", [TOK, D], F32)
    w_in = C.din("w_in", [D, NINP], F32)
    g_mix = C.din("g_mix", [1, D], F32)
    gq = C.din("gq", [1, 64], F32)
    gk = C.din("gk", [1, 64], F32)
    ident = C.din("ident", [128, 128], F32)
    send_bf = C.dout("send_bf", [8, SB_SZ], BF16)
    send_u = C.dout("send_u", [8, 32, TOK], F32)
    hc_loc = C.dout("hc_loc", [256, TOK], BF16)
    phase_A(C, x, w_in, g_mix, gq, gk, ident, send_bf, send_u, hc_loc)
    C.P.emit(es)
    es.close()
    return nc


def phase_B_attn(C, recv_bf, consts_bf, send_o, side=None, side_rate=1):
    P = C.P
    qT = C.sb([64, 2 * SEQ], BF16, "B_qT")
    kT = C.sb([64, 2 * SEQ], BF16, "B_kT")
    v = C.sb([128, 128, 64], BF16, "B_v")
    cst = C.sb([128, 256 + 4 * 512], BF16, "B_cst")
    negtri = cst[:, 0:128]
    negones = cst[:, 128:256]
    mask = cst[:, 256:].rearrange("p (i q) -> p i q", i=4)
    eb = [C.sb([128, 512], F32, "B_e%d" % i) for i in range(2)]
    spb = [C.sb([128, 512], BF16, "B_sp%d" % i) for i in range(3)]
    wb = [C.sb([128, 512], BF16, "B_w%d" % i) for i in range(3)]
    ssb = [C.sb([128, 512], BF16, "B_ss%d" % i) for i in range(3)]
    ost = [C.sb([64, 512], F32, "B_ost%d" % i) for i in range(2)]

    P.op("pool", lambda e: e.dma_start(out=cst, in_=consts_bf), writes=["cst"], dma=True)
    for c in range(8):
        P.op("sp", lambda e, c=c: e.dma_start(
            out=kT[:, c * TOK:(c + 1) * TOK], in_=recv_bf[c, QKV_SZ:2 * QKV_SZ].rearrange("(d t) -> d t", t=TOK)),
            writes=[("kT", c)], dma=True)
        P.op("sp", lambda e, c=c: e.dma_start(
            out=qT[:, c * TOK:(c + 1) * TOK], in_=recv_bf[c, 0:QKV_SZ].rearrange("(d t) -> d t", t=TOK)),
            writes=[("qT", c)], dma=True)
        P.op("sp", lambda e, c=c: e.dma_start(
            out=v[:, c * 16:(c + 1) * 16, :],
            in_=recv_bf[c, 2 * QKV_SZ:3 * QKV_SZ].rearrange("(n p d) -> p n d", p=128, d=64)),
            writes=[("v", c)], dma=True)

    units = []
    for b in range(2):
        for qc in range(16):
            kbs = [(4 * qc + i, i) for i in (3, 2, 1, 0)] + [(kb, None) for kb in range(4 * qc - 1, -1, -1)]
            for n, (kb, di) in enumerate(kbs):
                units.append(dict(idx=len(units), b=b, qc=qc, kb=kb, diag=di, first=(n == 0),
                                  last=(n == len(kbs) - 1)))

    def unit_cols(u):
        kc = u["b"] * SEQ + u["kb"] * 128
        qc0 = u["b"] * SEQ + u["qc"] * 512
        return kc, qc0

    def emit_Z(u):
        i = u["idx"]
        kc, qc0 = unit_cols(u)
        zb = i % 4
        Z = C.banks[zb]
        P.op("pe", lambda e: e.matmul(Z, lhsT=kT[:, kc:kc + 128], rhs=qT[:, qc0:qc0 + 512], start=True, stop=True),
             reads=[("kT", kc // TOK), ("qT", qc0 // TOK)], writes=[("bank", zb)])

    def emit_expZ(u):
        i = u["idx"]
        zb = i % 4
        Z = C.banks[zb]
        ee = eb[i % 2]
        P.op("act", lambda e: e.activation(out=ee, in_=Z, func=AF.Exp), reads=[("bank", zb)], writes=[("e", i % 2)])

    def emit_ln(u):
        i = u["idx"]
        ee = eb[i % 2]
        sp = spb[i % 3]
        P.op("act", lambda e: e.activation(out=sp, in_=ee, func=AF.Ln, bias=1.0),
             reads=[("e", i % 2)], writes=[("sp", i % 3)])
        if u["diag"] is not None:
            m = mask[:, u["diag"], :]
            P.op("dve", lambda e: e.tensor_tensor(out=sp, in0=sp, in1=m, op=ALU.mult),
                 reads=[("sp", i % 3), "cst"], writes=[("sp", i % 3)])
        ss = ssb[i % 3]
        if u["first"]:
            P.op("dve", lambda e: e.tensor_copy(out=ss, in_=sp), reads=[("sp", i % 3)], writes=[("ss", i % 3)])
        else:
            sprev = ssb[(i - 1) % 3]
            P.op("dve", lambda e: e.tensor_tensor(out=ss, in0=sprev, in1=sp, op=ALU.add),
                 reads=[("sp", i % 3), ("ss", (i - 1) % 3)], writes=[("ss", i % 3)])

    def emit_W(u):
        i = u["idx"]
        kc, qc0 = unit_cols(u)
        wbk = i % 4
        Wp = C.banks[wbk]
        sp = spb[i % 3]
        first = u["first"]
        P.op("pe", lambda e: e.matmul(Wp, lhsT=negtri, rhs=sp, start=False, stop=first, skip_group_check=True),
             reads=["cst", ("sp", i % 3)], writes=[("bank", wbk)])
        if not first:
            sprev = ssb[(i - 1) % 3]
            P.op("pe", lambda e: e.matmul(Wp, lhsT=negones, rhs=sprev, start=False, stop=True, skip_group_check=True),
                 reads=["cst", ("ss", (i - 1) % 3)], writes=[("bank", wbk)])

    def emit_expW(u):
        i = u["idx"]
        wbk = i % 4
        Wp = C.banks[wbk]
        w = wb[i % 3]
        P.op("act", lambda e: e.activation(out=w, in_=Wp, func=AF.Exp), reads=[("bank", wbk)], writes=[("w", i % 3)])
        if u["diag"] is not None:
            m = mask[:, u["diag"], :]
            P.op("dve", lambda e: e.tensor_tensor(out=w, in0=w, in1=m, op=ALU.mult),
                 reads=[("w", i % 3), "cst"], writes=[("w", i % 3)])

    def emit_PV(u):
        i = u["idx"]
        qc = u["qc"]
        G = u["b"] * 64 + u["kb"]
        ob = 4 + qc % 2
        O = C.banks[ob][0:64, :]
        w = wb[i % 3]
        P.op("pe", lambda e: e.matmul(O, lhsT=v[:, G, :], rhs=w, start=u["first"], stop=u["last"]),
             reads=[("v", G // 16), ("w", i % 3)], writes=[("bank", ob)])
        if u["last"]:
            os_ = ost[qc % 2]
            P.op("dve", lambda e: e.tensor_copy(out=os_, in_=O), reads=[("bank", ob)], writes=[("ost", qc % 2)])
            dest = u["b"] * 4 + qc // 4
            c0 = (qc % 4) * 512
            P.op("sp", lambda e: e.dma_start(out=send_o[dest, 0:64, c0:c0 + 512], in_=os_),
                 reads=[("ost", qc % 2)], dma=True)

    n = len(units)
    U = lambda i: units[i] if 0 <= i < n else None
    emit_Z(units[0])
    emit_Z(units[1])
    emit_expZ(units[0])
    for s in range(n + 2):
        if U(s + 2):
            emit_Z(U(s + 2))
        if U(s + 1):
            emit_expZ(U(s + 1))
        if U(s):
            emit_ln(U(s))
        if U(s - 1):
            emit_W(U(s - 1))
        if U(s - 2):
            emit_PV(U(s - 2))
        if U(s - 1):
            emit_expW(U(s - 1))
        if side is not None:
            for _ in range(side_rate):
                next(side, None)
    if side is not None:
        for _ in side:
            pass


def attn_consts():
    k = np.arange(128)
    negtri = -(k[:, None] >= k[None, :]).astype(np.float32)
    negones = -np.ones((128, 128), np.float32)
    q = np.arange(512)
    mask = np.stack([(128 * i + k[:, None] < q[None, :]).astype(np.float32) for i in range(4)], 1)
    return np.concatenate([negtri, negones, mask.reshape(128, 2048)], 1)


def build_B(with_ssm=True):
    nc = bass.Bass("TRN2", target_bir_lowering=False)
    es = ExitStack()
    C = Ctx(nc, es)
    recv_bf = C.din("recv_bf", [8, SB_SZ], BF16)
    consts_bf = C.din("consts_bf", [128, 256 + 2048], F32)
    send_o = C.dout("send_o", [8, 96, TOK], F32)
    side = None
    if with_ssm:
        ssm_io = ssm_decl(C)
        side = phase_B_ssm(C, ssm_io, send_o)
    phase_B_attn(C, recv_bf, consts_bf, send_o, side=side)
    C.P.emit(es)
    es.close()
    return nc


import math

SSM_NP = 6 + 1 + 32 + 32 + 32 + 2 + 128 + 128
PC_VEC, PC_SGN, PC_BRI, PC_BIR, PC_CT, PC_D, PC_I, PC_SW = 0, 6, 7, 39, 71, 103, 105, 233
NLEV = 13


def ssm_decl(C):
    return dict(recv_u=C.din("recv_u", [8, 32, TOK], F32), ssm_par=C.din("ssm_par", [128, SSM_NP], F32))


def ssm_pack(lam_re, lam_im, log_dt, b_re, b_im, c_re, c_im, d, core):
    par = np.zeros((128, SSM_NP), np.float32)
    for gi in range(2):
        g = 2 * core + gi
        par[:, PC_VEC + 3 * gi + 0] = np.concatenate([lam_re[g], lam_re[g]])
        par[:, PC_VEC + 3 * gi + 1] = np.concatenate([lam_im[g], lam_im[g]])
        par[:, PC_VEC + 3 * gi + 2] = log_dt[g]
        par[:, PC_BRI + 16 * gi:PC_BRI + 16 * gi + 16] = np.concatenate([b_re[g], b_im[g]], 0)
        par[:, PC_BIR + 16 * gi:PC_BIR + 16 * gi + 16] = np.concatenate([b_im[g], b_re[g]], 0)
        par[:, PC_CT + 16 * gi:PC_CT + 16 * gi + 16] = np.concatenate([c_re[g].T, c_im[g].T], 0)
        par[16 * gi:16 * gi + 16, PC_D + gi] = d[16 * g:16 * g + 16]
    par[:64, PC_SGN] = 1.0
    par[64:, PC_SGN] = -1.0
    par[:, PC_I:PC_I + 128] = np.eye(128, dtype=np.float32)
    par[:, PC_SW:PC_SW + 128] = np.roll(np.eye(128, dtype=np.float32), 64, axis=1)
    return par


def phase_B_ssm(C, io, send_o):
    P = C.P
    recv_u, ssm_par = io["recv_u"], io["ssm_par"]
    par = C.sb([128, SSM_NP], F32, "S_par")
    sc = C.sb([128, 128], F32, "S_sc")
    sci = C.sb([128, 4], mybir.dt.int32, "S_sci")
    AT = C.sb([128, 2, NLEV, 128], F32, "S_AT")
    Bpad = C.sb([128, 2, 32], F32, "S_Bpad")
    Cpad = C.sb([128, 2, 32], F32, "S_Cpad")
    lB = C.sb([32, 2, 128], F32, "S_lB")
    X = C.sb([128, SEQ], F32, "S_X")
    ub = C.sb([32, SEQ], F32, "S_u")
    yst = [C.sb([32, 512], F32, "S_y%d" % i) for i in range(2)]
    I128 = par[:, PC_I:PC_I + 128]
    SW = par[:, PC_SW:PC_SW + 128]
    sgn = par[:, PC_SGN:PC_SGN + 1]
    TWO_PI = 2.0 * math.pi

    P.op("sp", lambda e: e.dma_start(out=par, in_=ssm_par), writes=["par"], dma=True)
    P.op("dve", lambda e: e.memset(Bpad, 0.0), writes=["Bpad"])
    P.op("dve", lambda e: e.memset(Cpad, 0.0), writes=["Cpad"])

    ncol = [0]

    def col():
        ncol[0] += 1
        assert ncol[0] <= 128
        return ncol[0] - 1

    def cs(i):
        return sc[:, i:i + 1]

    def K(i):
        return ("sc", i)

    def ts(o, a, s1, s2, op0, op1=None, extra=()):
        if op1 is None:
            P.op("dve", lambda e: e.tensor_scalar(out=cs(o), in0=cs(a), scalar1=s1, scalar2=None, op0=op0),
                 reads=[K(a)] + list(extra), writes=[K(o)])
        else:
            P.op("dve", lambda e: e.tensor_scalar(out=cs(o), in0=cs(a), scalar1=s1, scalar2=s2, op0=op0, op1=op1),
                 reads=[K(a)] + list(extra), writes=[K(o)])

    def tt(o, a, b, op):
        P.op("dve", lambda e: e.tensor_tensor(out=cs(o), in0=cs(a), in1=cs(b), op=op),
             reads=[K(a), K(b)], writes=[K(o)])

    for gi in range(2):
        ncol[0] = 0
        lr = par[:, PC_VEC + 3 * gi:PC_VEC + 3 * gi + 1]
        li = par[:, PC_VEC + 3 * gi + 1:PC_VEC + 3 * gi + 2]
        ldt = par[:, PC_VEC + 3 * gi + 2:PC_VEC + 3 * gi + 3]
        c_dt, c_mag, c_th = col(), col(), col()
        P.op("act", lambda e, c_dt=c_dt, ldt=ldt: e.activation(out=cs(c_dt), in_=ldt, func=AF.Exp),
             reads=["par"], writes=[K(c_dt)])
        P.op("act", lambda e, c_mag=c_mag, c_dt=c_dt, lr=lr: e.activation(out=cs(c_mag), in_=lr, func=AF.Exp, scale=cs(c_dt)),
             reads=["par", K(c_dt)], writes=[K(c_mag)])
        P.op("dve", lambda e, c_th=c_th, c_dt=c_dt, li=li: e.tensor_tensor(out=cs(c_th), in0=li, in1=cs(c_dt), op=ALU.mult),
             reads=["par", K(c_dt)], writes=[K(c_th)])
        trig = []
        for shift in (0.5 * math.pi, 0.0):
            c_a, c_y, c_n, c_r, c_w, c_o = col(), col(), col(), col(), col(), col()
            ts(c_a, c_th, shift, None, ALU.add)
            ts(c_y, c_a, 1.0 / TWO_PI, None, ALU.mult)
            ii = 0 if shift else 1
            P.op("dve", lambda e, ii=ii, c_y=c_y: e.tensor_copy(out=sci[:, ii:ii + 1], in_=cs(c_y)),
                 reads=[K(c_y)], writes=[("sci", ii)])
            P.op("dve", lambda e, ii=ii, c_n=c_n: e.tensor_copy(out=cs(c_n), in_=sci[:, ii:ii + 1]),
                 reads=[("sci", ii)], writes=[K(c_n)])
            P.op("dve", lambda e, c_r=c_r, c_n=c_n, c_a=c_a: e.scalar_tensor_tensor(
                out=cs(c_r), in0=cs(c_n), scalar=-TWO_PI, in1=cs(c_a), op0=ALU.mult, op1=ALU.add),
                reads=[K(c_n), K(c_a)], writes=[K(c_r)])
            ts(c_w, c_r, math.pi, -TWO_PI, ALU.is_gt, ALU.mult)
            tt(c_r, c_r, c_w, ALU.add)
            P.op("act", lambda e, c_o=c_o, c_r=c_r: e.activation(out=cs(c_o), in_=cs(c_r), func=AF.Sin),
                 reads=[K(c_r)], writes=[K(c_o)])
            trig.append(c_o)
        c_ar, c_ai = col(), col()
        tt(c_ar, c_mag, trig[0], ALU.mult)
        tt(c_ai, c_mag, trig[1], ALU.mult)
        c_l2, c_i2, c_den, c_am1, c_t1, c_t2, c_fr, c_fi, c_f2 = [col() for _ in range(9)]
        P.op("dve", lambda e, lr=lr, c_l2=c_l2: e.tensor_tensor(out=cs(c_l2), in0=lr, in1=lr, op=ALU.mult),
             reads=["par"], writes=[K(c_l2)])
        P.op("dve", lambda e, li=li, c_i2=c_i2: e.tensor_tensor(out=cs(c_i2), in0=li, in1=li, op=ALU.mult),
             reads=["par"], writes=[K(c_i2)])
        tt(c_den, c_l2, c_i2, ALU.add)
        P.op("dve", lambda e, c_den=c_den: e.reciprocal(out=cs(c_den), in_=cs(c_den)), reads=[K(c_den)], writes=[K(c_den)])
        ts(c_am1, c_ar, -1.0, None, ALU.add)
        P.op("dve", lambda e, lr=lr, c_t1=c_t1, c_am1=c_am1: e.tensor_tensor(out=cs(c_t1), in0=cs(c_am1), in1=lr, op=ALU.mult),
             reads=["par", K(c_am1)], writes=[K(c_t1)])
        P.op("dve", lambda e, li=li, c_t2=c_t2, c_ai=c_ai: e.tensor_tensor(out=cs(c_t2), in0=cs(c_ai), in1=li, op=ALU.mult),
             reads=["par", K(c_ai)], writes=[K(c_t2)])
        tt(c_fr, c_t1, c_t2, ALU.add)
        tt(c_fr, c_fr, c_den, ALU.mult)
        P.op("dve", lambda e, lr=lr, c_t1=c_t1, c_ai=c_ai: e.tensor_tensor(out=cs(c_t1), in0=cs(c_ai), in1=lr, op=ALU.mult),
             reads=["par", K(c_ai)], writes=[K(c_t1)])
        P.op("dve", lambda e, li=li, c_t2=c_t2, c_am1=c_am1: e.tensor_tensor(out=cs(c_t2), in0=cs(c_am1), in1=li, op=ALU.mult),
             reads=["par", K(c_am1)], writes=[K(c_t2)])
        tt(c_fi, c_t1, c_t2, ALU.subtract)
        tt(c_fi, c_fi, c_den, ALU.mult)
        P.op("dve", lambda e, c_f2=c_f2, c_fi=c_fi: e.scalar_tensor_tensor(
            out=cs(c_f2), in0=cs(c_fi), scalar=-1.0, in1=sgn, op0=ALU.mult, op1=ALU.mult),
            reads=[K(c_fi), "par"], writes=[K(c_f2)])
        bsl = Bpad[:, gi, 16 * gi:16 * gi + 16]
        P.op("dve", lambda e, bsl=bsl, gi=gi, c_fr=c_fr: e.tensor_scalar(
            out=bsl, in0=par[:, PC_BRI + 16 * gi:PC_BRI + 16 * gi + 16], scalar1=cs(c_fr), scalar2=None, op0=ALU.mult),
            reads=["par", K(c_fr), "Bpad"], writes=[("Bpad", gi)])
        P.op("dve", lambda e, bsl=bsl, gi=gi, c_f2=c_f2: e.scalar_tensor_tensor(
            out=bsl, in0=par[:, PC_BIR + 16 * gi:PC_BIR + 16 * gi + 16], scalar=cs(c_f2), in1=bsl, op0=ALU.mult, op1=ALU.add),
            reads=["par", K(c_f2), ("Bpad", gi)], writes=[("Bpad", gi)])
        pb = C.banks[6][0:32, 0:128]
        P.op("pe", lambda e, pb=pb, gi=gi: e.transpose(out=pb, in_=Bpad[:, gi, :], identity=I128),
             reads=[("Bpad", gi), "par"], writes=[("bank", 6)])
        P.op("dve", lambda e, pb=pb, gi=gi: e.tensor_copy(out=lB[:, gi, :], in_=pb), reads=[("bank", 6)], writes=[("lB", gi)])
        P.op("dve", lambda e, gi=gi: e.tensor_scalar(
            out=Cpad[:, gi, 16 * gi:16 * gi + 16], in0=par[:, PC_CT + 16 * gi:PC_CT + 16 * gi + 16],
            scalar1=sgn, scalar2=None, op0=ALU.mult), reads=["par", "Cpad"], writes=[("Cpad", gi)])
        c_pr, c_pi = c_ar, c_ai
        for k in range(NLEV):
            c_s2 = col()
            P.op("dve", lambda e, c_s2=c_s2, c_pi=c_pi: e.tensor_tensor(out=cs(c_s2), in0=cs(c_pi), in1=sgn, op=ALU.mult),
                 reads=[K(c_pi), "par"], writes=[K(c_s2)])
            A = AT[:, gi, k, :]
            P.op("dve", lambda e, A=A, c_pr=c_pr: e.tensor_scalar(out=A, in0=I128, scalar1=cs(c_pr), scalar2=None, op0=ALU.mult),
                 reads=["par", K(c_pr)], writes=[("AT", gi, k)])
            P.op("dve", lambda e, A=A, c_s2=c_s2: e.scalar_tensor_tensor(
                out=A, in0=SW, scalar=cs(c_s2), in1=A, op0=ALU.mult, op1=ALU.add),
                reads=["par", K(c_s2), ("AT", gi, k)], writes=[("AT", gi, k)])
            if k + 1 < NLEV:
                c_a2, c_b2, c_nr, c_ni = col(), col(), col(), col()
                tt(c_a2, c_pr, c_pr, ALU.mult)
                tt(c_b2, c_pi, c_pi, ALU.mult)
                tt(c_nr, c_a2, c_b2, ALU.subtract)
                P.op("dve", lambda e, c_ni=c_ni, c_pr=c_pr, c_pi=c_pi: e.scalar_tensor_tensor(
                    out=cs(c_ni), in0=cs(c_pr), scalar=2.0, in1=cs(c_pi), op0=ALU.mult, op1=ALU.mult),
                    reads=[K(c_pr), K(c_pi)], writes=[K(c_ni)])
                c_pr, c_pi = c_nr, c_ni

    def scan_gen():
      if True:
        nb = [0]

        def sbank():
            nb[0] += 1
            return 6 + nb[0] % 2

        for b in range(2):
            for j in range(4):
                P.op("sp", lambda e, b=b, j=j: e.dma_start(out=ub[:, j * TOK:(j + 1) * TOK], in_=recv_u[4 * b + j]),
                     writes=[("ub", j)], dma=True)
            for gi in range(2):
                for ch in range(16):
                    bk = sbank()
                    ps = C.banks[bk]
                    P.op("pe", lambda e, ps=ps, gi=gi, ch=ch: e.matmul(
                        ps, lhsT=lB[:, gi, :], rhs=ub[:, ch * 512:(ch + 1) * 512], start=True, stop=True),
                        reads=[("lB", gi), ("ub", ch // 4)], writes=[("bank", bk)])
                    P.op("dve", lambda e, ps=ps, ch=ch: e.tensor_copy(out=X[:, ch * 512:(ch + 1) * 512], in_=ps),
                         reads=[("bank", bk)], writes=[("X", ch)])
                    yield

                for k in range(NLEV):
                    s_ = 1 << k
                    for ch in range(15, -1, -1):
                        lo = max(512 * ch, s_)
                        hi = 512 * (ch + 1)
                        if lo >= hi:
                            continue
                        n_ = hi - lo
                        bk = sbank()
                        ps = C.banks[bk][:, 0:n_]
                        src = X[:, lo - s_:hi - s_]
                        dst = X[:, lo:hi]
                        rk = sorted(set([("X", (lo - s_) // 512), ("X", (hi - s_ - 1) // 512)]))
                        P.op("pe", lambda e, ps=ps, src=src, k=k, gi=gi: e.matmul(
                            ps, lhsT=AT[:, gi, k, :], rhs=src, start=True, stop=True),
                            reads=[("AT", gi, k)] + rk, writes=[("bank", bk)])
                        P.op("dve", lambda e, ps=ps, dst=dst: e.tensor_tensor(out=dst, in0=dst, in1=ps, op=ALU.add),
                             reads=[("bank", bk), ("X", ch)], writes=[("X", ch)])
                        yield
                for ch in range(16):
                    bk = sbank()
                    ps = C.banks[bk][0:32, :]
                    P.op("pe", lambda e, ps=ps, ch=ch, gi=gi: e.matmul(
                        ps, lhsT=Cpad[:, gi, :], rhs=X[:, ch * 512:(ch + 1) * 512], start=True, stop=True),
                        reads=[("Cpad", gi), ("X", ch)], writes=[("bank", bk)])
                    ys = yst[ch % 2]
                    P.op("dve", lambda e, ps=ps, ch=ch, gi=gi, ys=ys: e.scalar_tensor_tensor(
                        out=ys, in0=ub[:, ch * 512:(ch + 1) * 512], scalar=par[0:32, PC_D + gi:PC_D + gi + 1], in1=ps,
                        op0=ALU.mult, op1=ALU.add),
                        reads=[("bank", bk), ("ub", ch // 4), "par"], writes=[("yst", ch % 2)])
                    dest = 4 * b + ch // 4
                    c0 = (ch % 4) * 512
                    P.op("sp", lambda e, ys=ys, dest=dest, c0=c0, gi=gi: e.dma_start(
                        out=send_o[dest, 64 + 16 * gi:64 + 16 * gi + 16, c0:c0 + 512], in_=ys[16 * gi:16 * gi + 16, :]),
                        reads=[("yst", ch % 2)], dma=True)
                    yield

    return scan_gen()


def phase_C(C, io):
    P = C.P
    x, x_out, recv_o, hc_loc, halo, mem = io["x"], io["x_out"], io["recv_o"], io["hc_loc"], io["halo"], io["mem"]
    bcast = lambda ap, n: ap.to_broadcast([128, n])

    xres = C.sb([128, NT, D], F32, "C_x")
    idB = C.sb([128, 128], BF16, "C_idB")
    idF = C.sb([128, 128], F32, "C_idF")
    onesF = C.sb([128, 1], F32, "C_ones")
    st = C.sb([128, 64], F32, "C_st")
    junk = C.sb([128, D], BF16, "C_junk")
    hb = C.sb([128, D], BF16, "C_hb")
    tT = [C.sb([128, 8, 128], BF16, "C_tT%d" % i) for i in range(2)]
    base_mark = C.mark()
    wq = C.sb([128, 8, D], BF16, "C_wq")
    wo = C.sb([128, 8, D], BF16, "C_wo")
    s12_mark = C.mark()

    for t in range(NT):
        P.op("sp", lambda e, t=t: e.dma_start(out=xres[:, t, :], in_=x[t * 128:(t + 1) * 128, :]),
             writes=[("x", t)], dma=True)
    P.op("pool", lambda e: e.dma_start(out=idB, in_=io["ident"]), writes=["idB"], dma=True)
    P.op("sp", lambda e: e.dma_start(out=idF, in_=io["ident"]), writes=["idF"], dma=True)
    P.op("dve", lambda e: e.memset(onesF, 1.0), writes=["ones"])

    nst = [0]

    def stcol(n=1):
        c = nst[0] % 64
        if c + n > 64:
            c = 0
        nst[0] = c + n
        return c

    def sk(c, n=1):
        return [("st", c + i) for i in range(n)]

    def rstd_col(ss_c, n, inv_n):
        tmp, out = stcol(n), stcol(n)
        rstd_ops(P, st[:, ss_c:ss_c + n], st[:, tmp:tmp + n], st[:, out:out + n], inv_n,
                 sk(ss_c, n), sk(tmp, n), sk(out, n))
        return out

    def load_w(dst, src, nk, key, eng="pool"):
        for k in range(nk):
            P.op(eng, lambda e, k=k: e.dma_start(out=dst[:, k, :], in_=src[k * 128:(k + 1) * 128, :]),
                 writes=[(key, k)], dma=True)
        return [(key, k) for k in range(nk)]

    def transposes(src, n, dst, bank, rk, wk, ident=None, f32=False, evac="act"):
        if f32:
            pv = C.banks[bank][:, 0:n * 128].rearrange("p (i t) -> p i t", i=n)
        else:
            pv = bank_bf(C.banks[bank])[:, 0:n * 128].rearrange("p (i t) -> p i t", i=n)
        idt, idk = (idF, "idF") if f32 else (idB, "idB")
        for i in range(n):
            P.op("pe", lambda e, i=i: e.transpose(out=pv[:, i, :], in_=src[:, i * 128:(i + 1) * 128], identity=idt),
                 reads=list(rk) + [idk], writes=[("bank", bank)])
        if evac == "act":
            P.op("act", lambda e: e.copy(out=dst, in_=pv), reads=[("bank", bank)], writes=list(wk))
        else:
            P.op("dve", lambda e: e.tensor_copy(out=dst, in_=pv), reads=[("bank", bank)], writes=list(wk))

    def norm_T(src, gB, gkey, dstT, bank, rk, wk):
        c = stcol()
        P.op("act", lambda e: e.activation(out=junk, in_=src, func=AF.Square, accum_out=st[:, c:c + 1]),
             reads=list(rk), writes=["junk", ("st", c)])
        r = rstd_col(c, 1, 1.0 / D)
        P.op("dve", lambda e: e.scalar_tensor_tensor(out=hb, in0=src, scalar=st[:, r:r + 1], in1=gB,
                                                     op0=ALU.mult, op1=ALU.mult),
             reads=list(rk) + [("st", r), gkey], writes=["hb"])
        transposes(hb, 8, dstT, bank, ["hb"], wk)

    hcT = C.sb([128, 2, 32 + TOK], BF16, "C_hcT")
    Dg = C.sb([128, 2, 31, 128], BF16, "C_Dg")
    cw = C.sb([128, 2, 36], F32, "C_cw")
    cT = C.sb([128, 2, TOK], F32, "C_cT")
    lngB = C.sb([128, 256], F32, "C_lng")
    lnbB = C.sb([128, 256], F32, "C_lnb")
    gbrB = C.sb([128, 512], F32, "C_gbrB")
    gbrT = C.sb([128, 8], F32, "C_gbrT")
    pw2 = C.sb([128, 2, 256], BF16, "C_pw2")
    gluw = C.sb([128, 2, 512], BF16, "C_gluw")
    wout = C.sb([128, 8, D], BF16, "C_wout")
    yT = C.sb([128, 2, TOK], BF16, "C_yT")
    oT = [C.sb([128, 4, 128], F32, "C_oT%d" % i) for i in range(2)]
    osq = C.sb([128, 4, 128], F32, "C_osq")
    og = C.sb([128, 4, 128], BF16, "C_og")
    yn = C.sb([128, 256], F32, "C_yn")
    ge = C.sb([128, 256], F32, "C_ge")
    sw = C.sb([128, 256], BF16, "C_sw")
    swT = C.sb([128, 2, 128], BF16, "C_swT")
    osm = C.sb([128, 256], F32, "C_osm")
    mixed = C.sb([128, 512], BF16, "C_mixed")

    P.op("sp", lambda e: e.dma_start(out=hcT[:, :, 0:32], in_=halo.rearrange("(c p) t -> p c t", p=128)),
         writes=["hcT_h"], dma=True)
    P.op("sp", lambda e: e.dma_start(out=hcT[:, :, 32:], in_=hc_loc.rearrange("(c p) t -> p c t", p=128)),
         writes=["hcT"], dma=True)
    P.op("sp", lambda e: e.dma_start(out=cw[:, :, 0:31], in_=io["conv_dw_wT"].rearrange("(c p) t -> p c t", p=128)),
         writes=["cw"], dma=True)
    for ct in range(2):
        P.op("sp", lambda e, ct=ct: e.dma_start(out=cw[:, ct, 31:32], in_=io["conv_dw_b"][ct * 128:(ct + 1) * 128, :]),
             writes=[("cwb", ct)], dma=True)
    P.op("sp", lambda e: e.dma_start(out=lngB, in_=bcast(io["conv_ln_g"], 256)), writes=["lng"], dma=True)
    P.op("sp", lambda e: e.dma_start(out=lnbB, in_=bcast(io["conv_ln_b"], 256)), writes=["lnb"], dma=True)
    P.op("sp", lambda e: e.dma_start(out=gbrB, in_=bcast(io["branch_g"][:, 512:1024], 512)), writes=["gbrB"], dma=True)
    P.op("sp", lambda e: e.dma_start(out=gbrT, in_=io["branch_gT"]), writes=["gbrT"], dma=True)
    load_w(pw2, io["conv_pw2_w"], 2, "pw2")
    load_w(gluw, io["ssm_glu_w"], 2, "gluw")
    kw_out = load_w(wout, io["w_out"], 8, "wout")
    for j in range(8):
        P.op("pool", lambda e, j=j: e.dma_start(out=yT[(j % 4) * 32:(j % 4) * 32 + 32, j // 4, :], in_=recv_o[j, 64:96, :]),
             writes=[("yT", j)], dma=True)
    load_w(wq, io["xa_wq"], 8, "wq")
    load_w(wo, io["xa_wo"], 8, "wo")
    for ct in range(2):
        for j in range(31):
            eng = "dve" if (j % 2) else "pool"
            P.op(eng, lambda e, ct=ct, j=j: e.tensor_scalar(out=Dg[:, ct, j, :], in0=idF, scalar1=cw[:, ct, j:j + 1],
                                                           scalar2=None, op0=ALU.mult),
                 reads=["idF", "cw"], writes=[("Dg", ct, j)])
    nb = [0]
    for ct in range(2):
        for ch in range(4):
            bk = nb[0] % 2
            nb[0] += 1
            ps = C.banks[bk]
            for j in range(31):
                c0 = 512 * ch + j + 2
                P.op("pe", lambda e, ct=ct, j=j, c0=c0, ps=ps: e.matmul(
                    ps, lhsT=Dg[:, ct, j, :], rhs=hcT[:, ct, c0:c0 + 512], start=(j == 0), stop=(j == 30)),
                    reads=[("Dg", ct, j), "hcT", "hcT_h"], writes=[("bank", bk)])
            P.op("act", lambda e, ct=ct, ch=ch, ps=ps: e.activation(
                out=cT[:, ct, ch * 512:(ch + 1) * 512], in_=ps, func=AF.Identity, bias=cw[:, ct, 31:32]),
                reads=[("bank", bk), ("cwb", ct)], writes=[("cT", ct, ch)])

    def load_o(t):
        par = t % 2
        for k in range(4):
            for hh in range(2):
                P.op("sp", lambda e, k=k, hh=hh, par=par, t=t: e.dma_start(
                    out=oT[par][hh * 64:hh * 64 + 64, k, :], in_=recv_o[2 * k + hh, 0:64, t * 128:(t + 1) * 128]),
                    writes=[("oT", par)], dma=True)

    load_o(0)
    for t in range(NT):
        par = t % 2
        if t + 1 < NT:
            load_o(t + 1)
        tc = slice(t * 128, (t + 1) * 128)
        pc = C.banks[2][:, 0:256]
        for ct in range(2):
            P.op("pe", lambda e, ct=ct, tc=tc: e.transpose(out=pc[:, ct * 128:(ct + 1) * 128], in_=cT[:, ct, tc], identity=idF),
                 reads=[("cT", ct, t // 4), "idF"], writes=[("bank", 2)])
        c_s = stcol()
        P.op("dve", lambda e, c_s=c_s: e.tensor_reduce(out=st[:, c_s:c_s + 1], in_=pc, axis=AX.X, op=ALU.add),
             reads=[("bank", 2)], writes=[("st", c_s)])
        c_m = stcol()
        P.op("dve", lambda e, c_s=c_s, c_m=c_m: e.tensor_scalar(out=st[:, c_m:c_m + 1], in0=st[:, c_s:c_s + 1],
                                                              scalar1=-1.0 / 256, scalar2=None, op0=ALU.mult),
             reads=[("st", c_s)], writes=[("st", c_m)])
        c_v = stcol()
        P.op("act", lambda e, c_m=c_m, c_v=c_v: e.activation(out=junk[:, 0:256], in_=pc, func=AF.Square,
                                                            bias=st[:, c_m:c_m + 1], accum_out=st[:, c_v:c_v + 1]),
             reads=[("bank", 2), ("st", c_m)], writes=["junk", ("st", c_v)])
        c_r = rstd_col(c_v, 1, 1.0 / 256)
        P.op("dve", lambda e, c_m=c_m, c_r=c_r: e.tensor_scalar(out=yn, in0=pc, scalar1=st[:, c_m:c_m + 1],
                                                              scalar2=st[:, c_r:c_r + 1], op0=ALU.add, op1=ALU.mult),
             reads=[("bank", 2), ("st", c_m), ("st", c_r)], writes=["yn"])
        P.op("dve", lambda e: e.tensor_tensor(out=yn, in0=yn, in1=lngB, op=ALU.mult), reads=["yn", "lng"], writes=["yn"])
        P.op("dve", lambda e: e.tensor_tensor(out=yn, in0=yn, in1=lnbB, op=ALU.add), reads=["yn", "lnb"], writes=["yn"])
        P.op("act", lambda e: e.activation(out=ge, in_=yn, func=AF.Exp, scale=-1.0), reads=["yn"], writes=["ge"])
        P.op("dve", lambda e: e.tensor_scalar(out=ge, in0=ge, scalar1=1.0, scalar2=None, op0=ALU.add), reads=["ge"], writes=["ge"])
        P.op("dve", lambda e: e.reciprocal(out=ge, in_=ge), reads=["ge"], writes=["ge"])
        P.op("dve", lambda e: e.tensor_tensor(out=sw, in0=yn, in1=ge, op=ALU.mult), reads=["yn", "ge"], writes=["sw"])
        transposes(sw, 2, swT, 3, ["sw"], ["swT"])
        po = C.banks[2][:, 256:512]
        for ct in range(2):
            P.op("pe", lambda e, ct=ct: e.matmul(po, lhsT=swT[:, ct, :], rhs=pw2[:, ct, :], start=(ct == 0), stop=(ct == 1)),
                 reads=["swT", ("pw2", ct)], writes=[("bank", 2)])
        c_q = stcol()
        P.op("act", lambda e, c_q=c_q: e.activation(out=junk[:, 0:256], in_=po, func=AF.Square, accum_out=st[:, c_q:c_q + 1]),
             reads=[("bank", 2)], writes=["junk", ("st", c_q)])
        c_r2 = rstd_col(c_q, 1, 1.0 / 256)
        P.op("dve", lambda e, c_r2=c_r2: e.scalar_tensor_tensor(out=mixed[:, 0:256], in0=po, scalar=st[:, c_r2:c_r2 + 1],
                                                              in1=gbrB[:, 0:256], op0=ALU.mult, op1=ALU.mult),
             reads=[("bank", 2), ("st", c_r2), "gbrB"], writes=["mixed0"])
        pg = C.banks[3]
        for ct in range(2):
            P.op("pe", lambda e, ct=ct, tc=tc: e.matmul(pg, lhsT=yT[:, ct, tc], rhs=gluw[:, ct, :], start=(ct == 0), stop=(ct == 1)),
                 reads=[("yT", j) for j in range(8)] + [("gluw", ct)], writes=[("bank", 3)])
        P.op("act", lambda e: e.activation(out=ge, in_=pg[:, 256:512], func=AF.Exp, scale=-1.0), reads=[("bank", 3)], writes=["ge"])
        P.op("dve", lambda e: e.tensor_scalar(out=ge, in0=ge, scalar1=1.0, scalar2=None, op0=ALU.add), reads=["ge"], writes=["ge"])
        P.op("dve", lambda e: e.reciprocal(out=ge, in_=ge), reads=["ge"], writes=["ge"])
        P.op("dve", lambda e: e.tensor_tensor(out=osm, in0=pg[:, 0:256], in1=ge, op=ALU.mult), reads=[("bank", 3), "ge"], writes=["osm"])
        c_q3 = stcol()
        P.op("act", lambda e, c_q3=c_q3: e.activation(out=junk[:, 0:256], in_=osm, func=AF.Square, accum_out=st[:, c_q3:c_q3 + 1]),
             reads=["osm"], writes=["junk", ("st", c_q3)])
        c_r3 = rstd_col(c_q3, 1, 1.0 / 256)
        P.op("dve", lambda e, c_r3=c_r3: e.scalar_tensor_tensor(out=mixed[:, 256:512], in0=osm, scalar=st[:, c_r3:c_r3 + 1],
                                                              in1=gbrB[:, 256:512], op0=ALU.mult, op1=ALU.mult),
             reads=["osm", ("st", c_r3), "gbrB"], writes=["mixed1"])
        mT = tT[0]
        transposes(mixed, 4, mT[:, 0:4, :], 3, ["mixed0", "mixed1"], ["mT"])
        P.op("act", lambda e, par=par: e.activation(out=osq, in_=oT[par], func=AF.Square), reads=[("oT", par)], writes=["osq"])
        pss = C.banks[2][:, 0:1]
        for k in range(4):
            P.op("pe", lambda e, k=k: e.matmul(pss, lhsT=osq[:, k, :], rhs=onesF, start=(k == 0), stop=(k == 3)),
                 reads=["osq", "ones"], writes=[("bank", 2)])
        c_q1 = stcol()
        P.op("dve", lambda e, c_q1=c_q1: e.tensor_copy(out=st[:, c_q1:c_q1 + 1], in_=pss), reads=[("bank", 2)], writes=[("st", c_q1)])
        c_r1 = rstd_col(c_q1, 1, 1.0 / 512)
        P.op("dve", lambda e, par=par: e.tensor_tensor(out=og, in0=oT[par], in1=gbrT[:, 0:4].unsqueeze(2).to_broadcast([128, 4, 128]),
                                                      op=ALU.mult), reads=[("oT", par), "gbrT"], writes=["og"])
        for n in range(2):
            p1 = C.banks[4 + n]
            p2 = C.banks[6 + n]
            for k in range(4):
                P.op("pe", lambda e, k=k, n=n, p1=p1: e.matmul(p1, lhsT=og[:, k, :], rhs=wout[:, k, n * 512:(n + 1) * 512],
                                                               start=(k == 0), stop=(k == 3)),
                     reads=["og", ("wout", k)], writes=[("bank", 4 + n)])
            for k in range(4):
                P.op("pe", lambda e, k=k, n=n, p2=p2: e.matmul(p2, lhsT=mT[:, k, :], rhs=wout[:, 4 + k, n * 512:(n + 1) * 512],
                                                               start=(k == 0), stop=(k == 3)),
                     reads=["mT", ("wout", 4 + k)], writes=[("bank", 6 + n)])
            xs = xres[:, t, n * 512:(n + 1) * 512]
            P.op("dve", lambda e, xs=xs, p2=p2: e.tensor_tensor(out=xs, in0=xs, in1=p2, op=ALU.add),
                 reads=[("bank", 6 + n), ("x", t)], writes=[("x", t)])
            P.op("dve", lambda e, xs=xs, p1=p1, c_r1=c_r1: e.scalar_tensor_tensor(
                out=xs, in0=p1, scalar=st[:, c_r1:c_r1 + 1], in1=xs, op0=ALU.mult, op1=ALU.add),
                reads=[("bank", 4 + n), ("st", c_r1), ("x", t)], writes=[("x", t)])

    P.barrier()
    C.release(s12_mark)
    kTp = C.sb([128, 8, 256], BF16, "C_kTp")
    Vm = C.sb([128, 2, D], BF16, "C_V")
    gxaB = C.sb([128, D], F32, "C_gxa")
    gkq = C.sb([128, 4, 256], F32, "C_gkq")
    qb = C.sb([128, D], BF16, "C_qb")
    Pb = C.sb([128, 4, 256], BF16, "C_Pb")
    PT = C.sb([128, 8, 128], BF16, "C_PT")
    ob = C.sb([128, D], BF16, "C_ob")
    s2_mark = C.mark()
    wk_ = C.sb([128, 8, D], BF16, "C_wk")
    wv_ = C.sb([128, 8, D], BF16, "C_wv")
    gmB = C.sb([128, D], F32, "C_gm")
    memt = C.sb([128, D], F32, "C_memt")
    hmT = C.sb([128, 8, 256], BF16, "C_hmT")
    kf = C.sb([128, D], F32, "C_kf")
    gq4 = C.sb([128, 256], F32, "C_gq4")

    kwk = load_w(wk_, io["xa_wk"], 8, "wk")
    kwv = load_w(wv_, io["xa_wv"], 8, "wv")
    P.op("sp", lambda e: e.dma_start(out=gxaB, in_=bcast(io["norm_xa_g"], D)), writes=["gxa"], dma=True)
    P.op("sp", lambda e: e.dma_start(out=gmB, in_=bcast(io["norm_mem_g"], D)), writes=["gm"], dma=True)
    P.op("sp", lambda e: e.dma_start(out=gkq[:, 0, :], in_=bcast(io["xa_k_g"], 256)), writes=["gkq"], dma=True)
    P.op("sp", lambda e: e.dma_start(out=gq4, in_=bcast(io["xa_q_g"], 256)), writes=["gq4"], dma=True)
    P.op("dve", lambda e: e.scalar_tensor_tensor(out=gkq[:, 0, :], in0=gkq[:, 0, :], scalar=1.0 / 16, in1=gq4,
                                                 op0=ALU.mult, op1=ALU.mult), reads=["gkq", "gq4"], writes=["gkq"])
    for h in range(1, 4):
        P.op("dve", lambda e, h=h: e.tensor_copy(out=gkq[:, h, :], in_=gkq[:, 0, :]), reads=["gkq"], writes=[("gkq", h)])
    gkq_keys = ["gkq"] + [("gkq", h) for h in range(1, 4)]
    for mt in range(2):
        P.op("sp", lambda e, mt=mt: e.dma_start(out=memt, in_=mem[mt * 128:(mt + 1) * 128, :]), writes=["memt"], dma=True)
        norm_T(memt, gmB, "gm", tT[0], 4, ["memt"], ["tT0"])
        P.op("act", lambda e, mt=mt: e.copy(out=hmT[:, :, mt * 128:(mt + 1) * 128], in_=tT[0]), reads=["tT0"], writes=[("hmT", mt)])
        for n in range(2):
            pk = C.banks[n]
            for k in range(8):
                P.op("pe", lambda e, k=k, n=n, pk=pk: e.matmul(pk, lhsT=tT[0][:, k, :], rhs=wk_[:, k, n * 512:(n + 1) * 512],
                                                               start=(k == 0), stop=(k == 7)),
                     reads=["tT0", ("wk", k)], writes=[("bank", n)])
            P.op("act", lambda e, n=n, pk=pk: e.copy(out=kf[:, n * 512:(n + 1) * 512], in_=pk), reads=[("bank", n)], writes=[("kf", n)])
        c_k = stcol(4)
        for h in range(4):
            P.op("act", lambda e, h=h, c_k=c_k: e.activation(out=junk[:, 0:256], in_=kf[:, h * 256:(h + 1) * 256], func=AF.Square,
                                                            accum_out=st[:, c_k + h:c_k + h + 1]),
                 reads=[("kf", h // 2)], writes=["junk", ("st", c_k + h)])
        outc = rstd_col(c_k, 4, 1.0 / 256)
        P.op("dve", lambda e, outc=outc: e.tensor_tensor(
            out=kf.rearrange("p (h d) -> p h d", h=4), in0=kf.rearrange("p (h d) -> p h d", h=4),
            in1=st[:, outc:outc + 4].unsqueeze(2).to_broadcast([128, 4, 256]), op=ALU.mult),
            reads=[("kf", 0), ("kf", 1)] + sk(outc, 4), writes=[("kf", 0), ("kf", 1)])
        P.op("dve", lambda e: e.tensor_tensor(out=hb, in0=kf, in1=gkq.rearrange("p h d -> p (h d)"), op=ALU.mult),
             reads=[("kf", 0), ("kf", 1)] + gkq_keys, writes=["hb"])
        transposes(hb, 8, kTp[:, :, mt * 128:(mt + 1) * 128], 5, ["hb"], [("kTp", mt)])
        for n in range(2):
            pv_ = C.banks[2 + n]
            for k in range(8):
                P.op("pe", lambda e, k=k, n=n, pv_=pv_: e.matmul(pv_, lhsT=tT[0][:, k, :], rhs=wv_[:, k, n * 512:(n + 1) * 512],
                                                                 start=(k == 0), stop=(k == 7)),
                     reads=["tT0", ("wv", k)], writes=[("bank", 2 + n)])
            P.op("act", lambda e, n=n, mt=mt, pv_=pv_: e.copy(out=Vm[:, mt, n * 512:(n + 1) * 512], in_=pv_),
                 reads=[("bank", 2 + n)], writes=[("V", mt)])

    for t in range(NT):
        xt_ = xres[:, t, :]
        norm_T(xt_, gxaB, "gxa", tT[1], 4, [("x", t)], ["tT1"])
        for n in range(2):
            pq = C.banks[n]
            for k in range(8):
                P.op("pe", lambda e, k=k, n=n, pq=pq: e.matmul(pq, lhsT=tT[1][:, k, :], rhs=wq[:, k, n * 512:(n + 1) * 512],
                                                               start=(k == 0), stop=(k == 7)),
                     reads=["tT1", ("wq", k)], writes=[("bank", n)])
        c_q = stcol(4)
        for h in range(4):
            pqh = C.banks[h // 2][:, (h % 2) * 256:(h % 2) * 256 + 256]
            P.op("act", lambda e, h=h, c_q=c_q, pqh=pqh: e.activation(out=junk[:, 0:256], in_=pqh, func=AF.Square,
                                                                    accum_out=st[:, c_q + h:c_q + h + 1]),
                 reads=[("bank", h // 2)], writes=["junk", ("st", c_q + h)])
        rq = rstd_col(c_q, 4, 1.0 / 256)
        for n in range(2):
            P.op("act", lambda e, n=n: e.copy(out=qb[:, n * 512:(n + 1) * 512], in_=C.banks[n]), reads=[("bank", n)], writes=[("qb", n)])
        transposes(qb, 8, tT[0], 5, [("qb", 0), ("qb", 1)], ["tT0"])
        for h in range(4):
            psc = C.banks[2 + h // 2][:, (h % 2) * 256:(h % 2) * 256 + 256]
            for j in range(2):
                P.op("pe", lambda e, h=h, j=j, psc=psc: e.matmul(psc, lhsT=tT[0][:, 2 * h + j, :], rhs=kTp[:, 2 * h + j, :],
                                                                 start=(j == 0), stop=(j == 1)),
                     reads=["tT0", ("kTp", 0), ("kTp", 1)], writes=[("bank", 2 + h // 2)])
        c_mx = stcol(4)
        for b2 in range(2):
            P.op("dve", lambda e, b2=b2, c_mx=c_mx: e.tensor_reduce(
                out=st[:, c_mx + 2 * b2:c_mx + 2 * b2 + 2], in_=C.banks[2 + b2].rearrange("p (h m) -> p h m", h=2),
                axis=AX.X, op=ALU.max), reads=[("bank", 2 + b2)], writes=sk(c_mx + 2 * b2, 2))
        c_nb = stcol(4)
        P.op("dve", lambda e, c_mx=c_mx, c_nb=c_nb, rq=rq: e.scalar_tensor_tensor(
            out=st[:, c_nb:c_nb + 4], in0=st[:, c_mx:c_mx + 4], scalar=-1.0, in1=st[:, rq:rq + 4], op0=ALU.mult, op1=ALU.mult),
            reads=sk(c_mx, 4) + sk(rq, 4), writes=sk(c_nb, 4))
        c_rs = stcol(4)
        for h in range(4):
            psc = C.banks[2 + h // 2][:, (h % 2) * 256:(h % 2) * 256 + 256]
            P.op("act", lambda e, h=h, psc=psc, rq=rq, c_nb=c_nb, c_rs=c_rs: e.activation(
                out=Pb[:, h, :], in_=psc, func=AF.Exp, scale=st[:, rq + h:rq + h + 1], bias=st[:, c_nb + h:c_nb + h + 1],
                accum_out=st[:, c_rs + h:c_rs + h + 1]),
                reads=[("bank", 2 + h // 2), ("st", rq + h), ("st", c_nb + h)], writes=[("Pb", h), ("st", c_rs + h)])
        c_ri = stcol(4)
        P.op("dve", lambda e, c_rs=c_rs, c_ri=c_ri: e.reciprocal(out=st[:, c_ri:c_ri + 4], in_=st[:, c_rs:c_rs + 4]),
             reads=sk(c_rs, 4), writes=sk(c_ri, 4))
        transposes(Pb.rearrange("p h m -> p (h m)"), 8, PT, 6, [("Pb", h) for h in range(4)], ["PT"])
        for h in range(4):
            po_ = C.banks[h // 2][:, (h % 2) * 256:(h % 2) * 256 + 256]
            for mt in range(2):
                P.op("pe", lambda e, h=h, mt=mt, po_=po_: e.matmul(po_, lhsT=PT[:, 2 * h + mt, :], rhs=Vm[:, mt, h * 256:(h + 1) * 256],
                                                                   start=(mt == 0), stop=(mt == 1)),
                     reads=["PT", ("V", mt)], writes=[("bank", h // 2)])
        for n in range(2):
            P.op("dve", lambda e, n=n, c_ri=c_ri: e.tensor_tensor(
                out=ob[:, n * 512:(n + 1) * 512].rearrange("p (h d) -> p h d", h=2),
                in0=C.banks[n].rearrange("p (h d) -> p h d", h=2),
                in1=st[:, c_ri + 2 * n:c_ri + 2 * n + 2].unsqueeze(2).to_broadcast([128, 2, 256]), op=ALU.mult),
                reads=[("bank", n)] + sk(c_ri + 2 * n, 2), writes=[("ob", n)])
        transposes(ob, 8, tT[1], 7, [("ob", 0), ("ob", 1)], ["tT1"])
        for n in range(2):
            pw_ = C.banks[2 + n]
            for k in range(8):
                P.op("pe", lambda e, k=k, n=n, pw_=pw_: e.matmul(pw_, lhsT=tT[1][:, k, :], rhs=wo[:, k, n * 512:(n + 1) * 512],
                                                                 start=(k == 0), stop=(k == 7)),
                     reads=["tT1", ("wo", k)], writes=[("bank", 2 + n)])
            xs = xres[:, t, n * 512:(n + 1) * 512]
            P.op("dve", lambda e, xs=xs, pw_=pw_: e.tensor_tensor(out=xs, in0=xs, in1=pw_, op=ALU.add),
                 reads=[("bank", 2 + n), ("x", t)], writes=[("x", t)])

    P.barrier()
    C.release(base_mark)
    gfB = C.sb([128, D], F32, "C_gf")
    hfT = C.sb([128, 8, 1024], BF16, "C_hfT")
    hid = C.sb([128, 22, 1024], BF16, "C_hid")
    Wo_ = C.sb([128, 22, D], BF16, "C_Wo")
    Wg = [C.sb([128, 8, 128], BF16, "C_Wg%d" % i) for i in range(2)]
    Wu = [C.sb([128, 8, 128], BF16, "C_Wu%d" % i) for i in range(2)]
    sg = [C.sb([128, 512], F32, "C_sg%d" % i) for i in range(2)]
    P.op("sp", lambda e: e.dma_start(out=gfB, in_=bcast(io["norm_ffn_g"], D)), writes=["gf"], dma=True)
    w_in_v = io["ffn_w_in"].rearrange("(k p) n -> p k n", p=128)

    def load_wo(j):
        P.op("pool", lambda e, j=j: e.dma_start(out=Wo_[:, j, :], in_=io["ffn_w_out"][j * 128:(j + 1) * 128, :]),
             writes=[("Wo", j)], dma=True)

    def load_gu(j):
        par = j % 2
        P.op("pool", lambda e, j=j, par=par: e.dma_start(out=Wg[par], in_=w_in_v[:, :, j * 128:(j + 1) * 128]),
             writes=[("Wg", par)], dma=True)
        P.op("pool", lambda e, j=j, par=par: e.dma_start(out=Wu[par], in_=w_in_v[:, :, FFN_H + j * 128:FFN_H + (j + 1) * 128]),
             writes=[("Wu", par)], dma=True)

    nsg = [0]
    for half in range(2):
        for tt in range(8):
            t = half * 8 + tt
            norm_T(xres[:, t, :], gfB, "gf", hfT[:, :, tt * 128:(tt + 1) * 128], 4 + tt % 2, [("x", t)], [("hfT", tt)])
        hkeys = [("hfT", tt) for tt in range(8)]
        load_gu(0)
        for j in range(22):
            par = j % 2
            if j + 1 < 22:
                load_gu(j + 1)
            if half == 0:
                load_wo(j)
            for tc2 in range(2):
                pg_ = C.banks[0 + tc2]
                pu_ = C.banks[2 + tc2]
                for k in range(8):
                    P.op("pe", lambda e, k=k, par=par, tc2=tc2, pg_=pg_: e.matmul(
                        pg_, lhsT=Wg[par][:, k, :], rhs=hfT[:, k, tc2 * 512:(tc2 + 1) * 512], start=(k == 0), stop=(k == 7)),
                        reads=[("Wg", par)] + hkeys[tc2 * 4:tc2 * 4 + 4], writes=[("bank", tc2)])
                for k in range(8):
                    P.op("pe", lambda e, k=k, par=par, tc2=tc2, pu_=pu_: e.matmul(
                        pu_, lhsT=Wu[par][:, k, :], rhs=hfT[:, k, tc2 * 512:(tc2 + 1) * 512], start=(k == 0), stop=(k == 7)),
                        reads=[("Wu", par)] + hkeys[tc2 * 4:tc2 * 4 + 4], writes=[("bank", 2 + tc2)])
                sp_ = nsg[0] % 2
                nsg[0] += 1
                P.op("act", lambda e, sp_=sp_, pg_=pg_: e.activation(out=sg[sp_], in_=pg_, func=AF.Silu),
                     reads=[("bank", tc2)], writes=[("sg", sp_)])
                P.op("dve", lambda e, sp_=sp_, pu_=pu_, j=j, tc2=tc2: e.tensor_tensor(
                    out=hid[:, j, tc2 * 512:(tc2 + 1) * 512], in0=sg[sp_], in1=pu_, op=ALU.mult),
                    reads=[("sg", sp_), ("bank", 2 + tc2)], writes=[("hid", j, tc2)])
        for tt in range(8):
            t = half * 8 + tt
            for n in range(2):
                bk = 4 + (2 * tt + n) % 4
                pd = C.banks[bk]
                for j in range(22):
                    P.op("pe", lambda e, j=j, n=n, tt=tt, pd=pd: e.matmul(
                        pd, lhsT=hid[:, j, tt * 128:(tt + 1) * 128], rhs=Wo_[:, j, n * 512:(n + 1) * 512],
                        start=(j == 0), stop=(j == 21)),
                        reads=[("hid", j, tt // 4), ("Wo", j)], writes=[("bank", bk)])
                xs = xres[:, t, n * 512:(n + 1) * 512]
                P.op("dve", lambda e, xs=xs, pd=pd: e.tensor_tensor(out=xs, in0=xs, in1=pd, op=ALU.add),
                     reads=[("bank", bk), ("x", t)], writes=[("x", t)])
            P.op("sp", lambda e, t=t: e.dma_start(out=x_out[t * 128:(t + 1) * 128, :], in_=xres[:, t, :]),
                 reads=[("x", t)], dma=True)


C_IN = [("x", [TOK, D], F32), ("recv_o", [8, 96, TOK], F32), ("hc_loc", [256, TOK], BF16), ("halo", [256, 32], BF16),
        ("mem", [256, D], F32), ("ident", [128, 128], F32),
        ("conv_dw_wT", [256, 31], F32), ("conv_dw_b", [256, 1], F32), ("conv_ln_g", [1, 256], F32), ("conv_ln_b", [1, 256], F32),
        ("conv_pw2_w", [256, 256], F32), ("ssm_glu_w", [256, 512], F32), ("branch_g", [1, D], F32), ("branch_gT", [128, 8], F32),
        ("w_out", [D, D], F32), ("norm_xa_g", [1, D], F32), ("norm_mem_g", [1, D], F32),
        ("xa_wq", [D, D], F32), ("xa_wk", [D, D], F32), ("xa_wv", [D, D], F32), ("xa_wo", [D, D], F32),
        ("xa_q_g", [1, 256], F32), ("xa_k_g", [1, 256], F32), ("norm_ffn_g", [1, D], F32),
        ("ffn_w_in", [D, 2 * FFN_H], F32), ("ffn_w_out", [FFN_H, D], F32)]


def build_C():
    nc = bass.Bass("TRN2", target_bir_lowering=False)
    es = ExitStack()
    C = Ctx(nc, es)
    io = {n: C.din(n, s, d) for n, s, d in C_IN}
    io["x_out"] = C.dout("x_out", [TOK, D], F32)
    phase_C(C, io)
    C.P.emit(es)
    es.close()
    return nc


_PROGS = {}


def _prog(name, fn):
    if name not in _PROGS:
        _PROGS[name] = fn()
    return _PROGS[name]


def _run(nc, in_maps):
    res = run_bass_kernel_spmd(nc, in_maps, core_ids=list(range(NCORES)))
    return res.results


def _c_weights(w, l):
    f = np.ascontiguousarray
    return dict(
        conv_dw_wT=f(w["conv_dw_w"][l].T), conv_dw_b=f(w["conv_dw_b"][l][:, None]),
        conv_ln_g=f(w["conv_ln_g"][l][None]), conv_ln_b=f(w["conv_ln_b"][l][None]),
        conv_pw2_w=f(w["conv_pw2_w"][l]), ssm_glu_w=f(w["ssm_glu_w"][l]),
        branch_g=f(w["branch_norm_g"][l][None]), branch_gT=f(w["branch_norm_g"][l].reshape(8, 128).T),
        w_out=f(w["w_out"][l]), norm_xa_g=f(w["norm_xa_g"][l][None]), norm_mem_g=f(w["norm_mem_g"][l][None]),
        xa_wq=f(w["xa_wq"][l]), xa_wk=f(w["xa_wk"][l]), xa_wv=f(w["xa_wv"][l]), xa_wo=f(w["xa_wo"][l]),
        xa_q_g=f(w["xa_q_norm_g"][l][None]), xa_k_g=f(w["xa_k_norm_g"][l][None]),
        norm_ffn_g=f(w["norm_ffn_g"][l][None]), ffn_w_in=f(w["ffn_w_in"][l]), ffn_w_out=f(w["ffn_w_out"][l]))


def kernel(**inputs):
    w = {k: np.asarray(v, dtype=np.float32) for k, v in inputs.items()}
    x = w["x"]
    mem = w["mem"]
    bsz, L, _ = x.shape
    assert (bsz, L) == (2, SEQ)
    ident = np.eye(128, dtype=np.float32)
    consts = attn_consts()
    xs = [np.ascontiguousarray(x.reshape(NCORES, TOK, D)[c]) for c in range(NCORES)]
    ncA = _prog("A", build_A)
    ncB = _prog("B", build_B)
    ncC = _prog("C", build_C)
    for l in range(2):
        ra = _run(ncA, [dict(x=xs[c], w_in=w["w_in"][l], g_mix=w["norm_mix_g"][l][None], gq=w["sb_q_norm_g"][l][None],
                             gk=w["sb_k_norm_g"][l][None], ident=ident) for c in range(NCORES)])
        send_bf = [np.asarray(ra[c]["send_bf"]) for c in range(NCORES)]
        send_u = [np.asarray(ra[c]["send_u"]) for c in range(NCORES)]
        hc_loc = [np.asarray(ra[c]["hc_loc"]) for c in range(NCORES)]
        in_b = []
        for j in range(NCORES):
            in_b.append(dict(
                recv_bf=np.ascontiguousarray(np.stack([send_bf[c][j] for c in range(NCORES)])),
                recv_u=np.ascontiguousarray(np.stack([send_u[c][j] for c in range(NCORES)])),
                consts_bf=consts,
                ssm_par=ssm_pack(w["ssm_lam_re"][l], w["ssm_lam_im"][l], w["ssm_log_dt"][l], w["ssm_b_re"][l],
                                 w["ssm_b_im"][l], w["ssm_c_re"][l], w["ssm_c_im"][l], w["ssm_d"][l], j)))
        rb = _run(ncB, in_b)
        send_o = [np.asarray(rb[j]["send_o"]) for j in range(NCORES)]
        cw = _c_weights(w, l)
        in_c = []
        for c in range(NCORES):
            if c % 4 == 0:
                halo = np.zeros((256, 32), dtype=send_bf[0].dtype)
            else:
                halo = np.ascontiguousarray(send_bf[c - 1][0][HALO_OFF:].reshape(256, 32))
            m = dict(x=xs[c], recv_o=np.ascontiguousarray(np.stack([send_o[j][c] for j in range(NCORES)])),
                     hc_loc=hc_loc[c], halo=halo, mem=np.ascontiguousarray(mem[c // 4]), ident=ident)
            m.update(cw)
            in_c.append(m)
        rc = _run(ncC, in_c)
        xs = [np.asarray(rc[c]["x_out"]) for c in range(NCORES)]
    return np.stack(xs).reshape(bsz, L, D).astype(np.float32)
```

```python
import numpy as np
from contextlib import ExitStack
import concourse.bass as bass
import concourse.mybir as mybir
from concourse.bass_utils import run_bass_kernel_spmd

F32 = mybir.dt.float32
BF16 = mybir.dt.bfloat16
AF = mybir.ActivationFunctionType
ALU = mybir.AluOpType
AX = mybir.AxisListType

NCORES = 8
D = 1024
TOK = 2048
NT = TOK // 128
SEQ = 8192
NINP = 2304
FFN_H = 2816
EPS = 1e-6
QKV_SZ = 64 * TOK
HALO_OFF = 3 * QKV_SZ
SB_SZ = HALO_OFF + 256 * 32
NDMA_SEM = 12


class _Op:
    __slots__ = ("eng", "fn", "deps", "dma", "signal", "seq", "dsem", "dval", "qi")

    def __init__(self, eng, fn, dma):
        self.eng = eng
        self.fn = fn
        self.deps = []
        self.dma = dma
        self.signal = False
        self.seq = 0
        self.dsem = None
        self.dval = 0
        self.qi = 0


class Prog:
    ENGS = ("pe", "act", "dve", "pool", "sp")

    def __init__(self, nc):
        self.nc = nc
        self.ops = {e: [] for e in self.ENGS}
        self.last_w = {}
        self.readers = {}
        self.ndma = {e: 0 for e in self.ENGS}
        self.bar = []
        self._pid = {}

    def op(self, eng, fn, reads=(), writes=(), dma=False):
        o = _Op(eng, fn, dma)
        if dma:
            o.qi = self.ndma[eng]
            self.ndma[eng] += 1

        def add(p, kind):
            if p is o:
                return
            if not p.dma and p.eng == eng and not dma:
                if eng == "pe":
                    return
            if p not in o.deps:
                o.deps.append(p)
                p.signal = True

        for p in self.bar:
            add(p, "bar")
        for k in reads:
            p = self.last_w.get(k)
            if p is not None:
                add(p, "raw")
        for k in writes:
            p = self.last_w.get(k)
            if p is not None:
                add(p, "waw")
            for r in self.readers.get(k, ()):
                add(r, "war")
        for k in reads:
            self.readers.setdefault(k, []).append(o)
        for k in writes:
            self.last_w[k] = o
            self.readers[k] = []
        self.ops[eng].append(o)
        return o

    def pid(self, eng):
        k = id(eng)
        if k not in self._pid:
            self._pid[k] = eng.partition_id() % NCORES
        return self._pid[k]

    def barrier(self):
        self.bar = [self.ops[e][-1] for e in self.ENGS if self.ops[e]]
        for e in self.ENGS:
            for o in self.ops[e][-NDMA_SEM:]:
                if o.dma and o not in self.bar:
                    self.bar.append(o)
        self.last_w = {}
        self.readers = {}

    def emit(self, es):
        nc = self.nc
        esem = {e: es.enter_context(nc.semaphore("s_" + e)) for e in self.ENGS}
        dsems = {}
        for e in self.ENGS:
            if self.ndma[e]:
                dsems[e] = [es.enter_context(nc.semaphore("d_%s%d" % (e, i)))
                            for i in range(min(NDMA_SEM, self.ndma[e]))]
        for e in self.ENGS:
            n = 0
            for o in self.ops[e]:
                if o.dma:
                    o.dsem = dsems[e][o.qi % NDMA_SEM]
                    o.dval = 16 * (o.qi // NDMA_SEM + 1)
                elif o.signal:
                    n += 1
                    o.seq = n
        final_waits = []
        for e in self.ENGS:
            if self.ndma[e]:
                for i, s in enumerate(dsems[e]):
                    cnt = len(range(i, self.ndma[e], NDMA_SEM))
                    final_waits.append((s, 16 * cnt))

        def run(e, eng):
            waited = {}

            def wait(sem, val):
                key = id(sem)
                if waited.get(key, 0) >= val:
                    return
                eng.wait_ge(sem, val)
                waited[key] = val

            for o in self.ops[e]:
                for p in o.deps:
                    if p.dma:
                        wait(p.dsem, p.dval)
                    else:
                        wait(esem[p.eng], p.seq)
                if o.dma:
                    if o.qi >= NDMA_SEM:
                        wait(o.dsem, o.dval - 16)
                    o.fn(eng).then_inc(o.dsem, 16)
                else:
                    ins = o.fn(eng)
                    if o.signal:
                        ins.then_inc(esem[e], 1)
            if e == "sp":
                for s, v in final_waits:
                    wait(s, v)

        with nc.Block() as block:
            @block.tensor
            def _(eng):
                run("pe", eng)

            @block.scalar
            def _(eng):
                run("act", eng)

            @block.vector
            def _(eng):
                run("dve", eng)

            @block.gpsimd
            def _(eng):
                run("pool", eng)

            @block.sync
            def _(eng):
                run("sp", eng)


ARENA_F32 = 50688


class Ctx:
    def __init__(self, nc, es):
        self.nc = nc
        self.es = es
        self.P = Prog(nc)
        self.n = 0
        self.banks = [es.enter_context(nc.psum_tensor("bank%d" % i, [128, 512], F32))[:]
                      for i in range(8)]
        self.arena = es.enter_context(nc.sbuf_tensor("arena", [128, ARENA_F32], F32))[:]
        self.off = 0

    def sb(self, shape, dtype, name=None):
        shape = list(shape)
        esz = mybir.dt.size(dtype)
        n = 1
        for d in shape[1:]:
            n *= d
        nf = (n * esz + 31) // 32 * 8
        assert self.off + nf <= ARENA_F32, "SBUF arena overflow (%s)" % name
        v = self.arena[0:shape[0], self.off:self.off + nf]
        self.off += nf
        if dtype != F32:
            v = v.bitcast(dtype)
        v = v[:, 0:n]
        if len(shape) > 2:
            names = " ".join("d%d" % i for i in range(1, len(shape)))
            v = v.rearrange("p (%s) -> p %s" % (names, names),
                            **{"d%d" % i: shape[i] for i in range(1, len(shape))})
        return v

    def mark(self):
        return self.off

    def release(self, m):
        self.off = m

    def din(self, name, shape, dtype):
        return self.nc.dram_tensor(name, list(shape), dtype, kind="ExternalInput").ap()

    def dout(self, name, shape, dtype):
        return self.nc.dram_tensor(name, list(shape), dtype, kind="ExternalOutput").ap()

    def dint(self, name, shape, dtype):
        return self.nc.dram_tensor(name, list(shape), dtype, kind="Internal").ap()


def bank_bf(bank):
    return bank.bitcast(BF16)


def rstd_ops(P, st_in, st_tmp, st_out, inv_n, key_in, key_tmp, key_out):
    kl = lambda k: list(k) if isinstance(k, list) else [k]
    key_in, key_tmp, key_out = kl(key_in), kl(key_tmp), kl(key_out)
    P.op("dve", lambda e: e.tensor_scalar(out=st_tmp, in0=st_in, scalar1=inv_n, scalar2=EPS,
                                          op0=ALU.mult, op1=ALU.add),
         reads=key_in, writes=key_tmp)
    P.op("act", lambda e: e.activation(out=st_tmp, in_=st_tmp, func=AF.Ln),
         reads=key_tmp, writes=key_tmp)
    P.op("act", lambda e: e.activation(out=st_out, in_=st_tmp, func=AF.Exp, scale=-0.5),
         reads=key_tmp, writes=key_out)


def phase_A(C, x, w_in, g_mix, gq, gk, ident, send_bf, send_u, hc_loc):
    P = C.P
    W = C.sb([128, 8, NINP], BF16, "A_W")
    gB = C.sb([128, D], F32, "A_gB")
    gqB = C.sb([128, 8, 64], F32, "A_gqB")
    gkB = C.sb([128, 8, 64], F32, "A_gkB")
    idB = C.sb([128, 128], BF16, "A_idB")
    idF = C.sb([128, 128], F32, "A_idF")
    xt = [C.sb([128, D], F32, "A_xt%d" % i) for i in range(3)]
    junk_ = [C.sb([128, D], BF16, "A_junk%d" % i) for i in range(3)]
    hb = [C.sb([128, D], BF16, "A_hb%d" % i) for i in range(3)]
    hT = [C.sb([128, 8, 128], BF16, "A_hT%d" % i) for i in range(3)]
    st = [C.sb([128, 32], F32, "A_st%d" % i) for i in range(3)]
    qsq_ = [C.sb([128, 512], F32, "A_qsq%d" % i) for i in range(3)]
    qtmp_ = [C.sb([128, 512], F32, "A_qtmp%d" % i) for i in range(3)]
    qn_ = [C.sb([128, 512], BF16, "A_qn%d" % i) for i in range(3)]
    ge_ = [C.sb([128, 256], F32, "A_ge%d" % i) for i in range(3)]
    hcn_ = [C.sb([128, 256], BF16, "A_hcn%d" % i) for i in range(3)]
    uf_ = [C.sb([128, 256], F32, "A_uf%d" % i) for i in range(3)]
    qT_all = C.sb([128, 4, TOK], BF16, "A_qT")
    kT_all = C.sb([128, 4, TOK], BF16, "A_kT")
    v_all = C.sb([128, NT, 512], BF16, "A_v")
    hcT_all = C.sb([128, 2, TOK], BF16, "A_hcT")
    uT_all = C.sb([128, 2, TOK], F32, "A_uT")

    for ci, (c0, cw) in enumerate([(0, 512), (512, 512), (1024, 512), (1536, 512), (2048, 256)]):
        for k in range(8):
            P.op("pool", lambda e, k=k, c0=c0, cw=cw: e.dma_start(out=W[:, k, c0:c0 + cw],
                                                                 in_=w_in[k * 128:(k + 1) * 128, c0:c0 + cw]),
                 writes=[("W", k, ci)], dma=True)
    P.op("sp", lambda e: e.dma_start(out=gB, in_=g_mix.to_broadcast([128, D])), writes=["gB"], dma=True)
    P.op("sp", lambda e: e.dma_start(out=gqB, in_=gq.unsqueeze(1).to_broadcast([128, 8, 64])),
         writes=["gqB"], dma=True)
    P.op("sp", lambda e: e.dma_start(out=gkB, in_=gk.unsqueeze(1).to_broadcast([128, 8, 64])),
         writes=["gkB"], dma=True)
    P.op("pool", lambda e: e.dma_start(out=idB, in_=ident), writes=["idB"], dma=True)
    P.op("sp", lambda e: e.dma_start(out=idF, in_=ident), writes=["idF"], dma=True)
    P.op("dve", lambda e: e.tensor_scalar(out=gqB, in0=gqB, scalar1=0.125, scalar2=None, op0=ALU.mult),
         reads=["gqB"], writes=["gqB"])

    chunks = [(0, 512), (512, 512), (1024, 512), (1536, 512), (2048, 256)]
    nbank = [0]

    def obank():
        i = 3 + (nbank[0] % 3)
        nbank[0] += 1
        return i
    ntb = [0]

    def tbank():
        i = 6 + (ntb[0] % 2)
        ntb[0] += 1
        return i

    def load_x(t):
        p3 = t % 3
        P.op("sp", lambda e: e.dma_start(out=xt[p3], in_=x[t * 128:(t + 1) * 128, :]),
             writes=[("xt", p3)], dma=True)

    def tile_gen(t):
        par = t % 3
        p3 = t % 3
        if t + 2 < NT and t >= 1:
            load_x(t + 2)
        s = st[par]
        junk, qsq, qtmp, qn, ge, hcn, uf = junk_[par], qsq_[par], qtmp_[par], qn_[par], ge_[par], hcn_[par], uf_[par]
        kj, kqs, kqt, kqn, kge, khc, kuf = (("junk", par), ("qsq", par), ("qtmp", par), ("qn", par), ("ge", par),
                                            ("hcn", par), ("uf", par))
        P.op("act", lambda e, par=par, s=s: e.activation(out=junk, in_=xt[p3], func=AF.Square,
                                                          accum_out=s[:, 0:1]),
             reads=[("xt", p3)], writes=[kj, ("st", par, 0)])
        rstd_ops(P, s[:, 0:1], s[:, 1:2], s[:, 2:3], 1.0 / D, ("st", par, 0), ("st", par, 1), ("st", par, 2))
        P.op("dve", lambda e, par=par, s=s: e.scalar_tensor_tensor(
            out=hb[par], in0=xt[p3], scalar=s[:, 2:3], in1=gB, op0=ALU.mult, op1=ALU.mult),
            reads=[("xt", p3), ("st", par, 2), "gB"], writes=[("hb", par)])
        pT = bank_bf(C.banks[par]).rearrange("p (k t) -> p k t", k=8)
        for k in range(8):
            P.op("pe", lambda e, k=k, pT=pT, par=par: e.transpose(
                out=pT[:, k, :], in_=hb[par][:, k * 128:(k + 1) * 128], identity=idB),
                reads=[("hb", par), "idB"], writes=[("bank", par)])
        P.op("act", lambda e, pT=pT, par=par: e.copy(out=hT[par], in_=pT),
             reads=[("bank", par)], writes=[("hT", par)])
        yield
        for ci, (c0, cw) in enumerate(chunks):
            bi = obank()
            pO = C.banks[bi][:, 0:cw]
            for k in range(8):
                P.op("pe", lambda e, k=k, pO=pO, par=par, c0=c0, cw=cw: e.matmul(
                    pO, lhsT=hT[par][:, k, :], rhs=W[:, k, c0:c0 + cw], start=(k == 0), stop=(k == 7)),
                    reads=[("hT", par), ("W", k, ci)], writes=[("bank", bi)])
            bk = ("bank", bi)
            yield
            if ci in (0, 1):
                dstT = qT_all if ci == 0 else kT_all
                gBt = gqB if ci == 0 else gkB
                gkey = "gqB" if ci == 0 else "gkB"
                P.op("act", lambda e, pO=pO: e.activation(out=qsq, in_=pO, func=AF.Square),
                     reads=[bk], writes=[kqs])
                P.op("dve", lambda e, s=s: e.tensor_reduce(
                    out=s[:, 8:16], in_=qsq.rearrange("p (h d) -> p h d", h=8), axis=AX.X, op=ALU.add),
                    reads=[kqs], writes=[("st", par, 3)])
                rstd_ops(P, s[:, 8:16], s[:, 16:24], s[:, 24:32], 1.0 / 64,
                         ("st", par, 3), ("st", par, 4), ("st", par, 5))
                P.op("dve", lambda e, pO=pO, s=s: e.tensor_tensor(
                    out=qtmp.rearrange("p (h d) -> p h d", h=8),
                    in0=pO.rearrange("p (h d) -> p h d", h=8),
                    in1=s[:, 24:32].unsqueeze(2).to_broadcast([128, 8, 64]), op=ALU.mult),
                    reads=[bk, ("st", par, 5)], writes=[kqt])
                P.op("dve", lambda e, gBt=gBt: e.tensor_tensor(
                    out=qn, in0=qtmp, in1=gBt.rearrange("p h d -> p (h d)"), op=ALU.mult),
                    reads=[kqt, gkey], writes=[kqn])
                tb = tbank()
                pQ = bank_bf(C.banks[tb])[:, 0:512].rearrange("p (j t) -> p j t", j=4)
                for j in range(4):
                    P.op("pe", lambda e, j=j, pQ=pQ: e.transpose(
                        out=pQ[:, j, :], in_=qn[:, j * 128:(j + 1) * 128], identity=idB),
                        reads=[kqn, "idB"], writes=[("bank", tb)])
                P.op("act", lambda e, pQ=pQ, dstT=dstT, t=t: e.copy(
                    out=dstT[:, :, t * 128:(t + 1) * 128], in_=pQ),
                    reads=[("bank", tb)], writes=[("qkT", ci, t)])
            elif ci == 2:
                P.op("act", lambda e, pO=pO, t=t: e.copy(out=v_all[:, t, :], in_=pO),
                     reads=[bk], writes=[("v", t)])
            elif ci == 3:
                P.op("act", lambda e, pO=pO: e.activation(out=ge, in_=pO[:, 256:512], func=AF.Exp, scale=-1.0),
                     reads=[bk], writes=[kge])
                P.op("dve", lambda e: e.tensor_scalar(out=ge, in0=ge, scalar1=1.0, scalar2=None, op0=ALU.add),
                     reads=[kge], writes=[kge])
                P.op("dve", lambda e: e.reciprocal(out=ge, in_=ge), reads=[kge], writes=[kge])
                P.op("dve", lambda e, pO=pO: e.tensor_tensor(out=hcn, in0=pO[:, 0:256], in1=ge, op=ALU.mult),
                     reads=[bk, kge], writes=[khc])
                tb = tbank()
                pQ = bank_bf(C.banks[tb])[:, 0:256].rearrange("p (j t) -> p j t", j=2)
                for j in range(2):
                    P.op("pe", lambda e, j=j, pQ=pQ: e.transpose(
                        out=pQ[:, j, :], in_=hcn[:, j * 128:(j + 1) * 128], identity=idB),
                        reads=[khc, "idB"], writes=[("bank", tb)])
                P.op("act", lambda e, pQ=pQ, t=t: e.copy(out=hcT_all[:, :, t * 128:(t + 1) * 128], in_=pQ),
                     reads=[("bank", tb)], writes=[("hcT", t)])
            else:
                P.op("act", lambda e, pO=pO: e.copy(out=uf, in_=pO), reads=[bk], writes=[kuf])
                tb = tbank()
                pQ = C.banks[tb][:, 0:256].rearrange("p (j t) -> p j t", j=2)
                for j in range(2):
                    P.op("pe", lambda e, j=j, pQ=pQ: e.transpose(
                        out=pQ[:, j, :], in_=uf[:, j * 128:(j + 1) * 128], identity=idF),
                        reads=[kuf, "idF"], writes=[("bank", tb)])
                P.op("act", lambda e, pQ=pQ, t=t: e.copy(out=uT_all[:, :, t * 128:(t + 1) * 128], in_=pQ),
                     reads=[("bank", tb)], writes=[("uT", t)])
        yield

    load_x(0)
    load_x(1)
    load_x(2)
    pending = list(range(NT))
    active = []
    while pending or active:
        while pending and len(active) < 3:
            active.append(tile_gen(pending.pop(0)))
        for g in list(active):
            try:
                next(g)
            except StopIteration:
                active.remove(g)

    allq = [("qkT", 0, t) for t in range(NT)]
    allk = [("qkT", 1, t) for t in range(NT)]
    allv = [("v", t) for t in range(NT)]
    allh = [("hcT", t) for t in range(NT)]
    allu = [("uT", t) for t in range(NT)]
    for j in range(8):
        r0 = (j % 2) * 64
        P.op("sp", lambda e, j=j, r0=r0: e.dma_start(
            out=send_bf[j, 0:QKV_SZ].rearrange("(d t) -> d t", t=TOK), in_=qT_all[r0:r0 + 64, j // 2, :]),
            reads=allq, dma=True)
        P.op("sp", lambda e, j=j, r0=r0: e.dma_start(
            out=send_bf[j, QKV_SZ:2 * QKV_SZ].rearrange("(d t) -> d t", t=TOK), in_=kT_all[r0:r0 + 64, j // 2, :]),
            reads=allk, dma=True)
        P.op("sp", lambda e, j=j: e.dma_start(
            out=send_bf[j, 2 * QKV_SZ:3 * QKV_SZ].rearrange("(n p d) -> p n d", p=128, d=64),
            in_=v_all[:, :, j * 64:(j + 1) * 64]), reads=allv, dma=True)
        P.op("sp", lambda e, j=j: e.dma_start(
            out=send_bf[j, HALO_OFF:SB_SZ].rearrange("(c p t) -> p c t", p=128, t=32),
            in_=hcT_all[:, :, TOK - 32:TOK]), reads=allh, dma=True)
        u0 = (j % 4) * 32
        P.op("sp", lambda e, j=j, u0=u0: e.dma_start(out=send_u[j], in_=uT_all[u0:u0 + 32, j // 4, :]),
             reads=allu, dma=True)
    P.op("sp", lambda e: e.dma_start(out=hc_loc.rearrange("(c p) t -> p c t", p=128), in_=hcT_all),
         reads=allh, dma=True)


def build_A():
    nc = bass.Bass("TRN2", target_bir_lowering=False)
    es = ExitStack()
    C = Ctx(nc, es)
    x = C.din("x", [TOK, D], F32)
    w_in = C.din("w_in", [D, NINP], F32)
    g_mix = C.din("g_mix", [1, D], F32)
    gq = C.din("gq", [1, 64], F32)
    gk = C.din("gk", [1, 64], F32)
    ident = C.din("ident", [128, 128], F32)
    send_bf = C.dout("send_bf", [8, SB_SZ], BF16)
    send_u = C.dout("send_u", [8, 32, TOK], F32)
    hc_loc = C.dout("hc_loc", [256, TOK], BF16)
    phase_A(C, x, w_in, g_mix, gq, gk, ident, send_bf, send_u, hc_loc)
    C.P.emit(es)
    es.close()
    return nc


def phase_B_attn(C, recv_bf, consts_bf, send_o, side=None, side_rate=1):
    P = C.P
    qT = C.sb([64, 2 * SEQ], BF16, "B_qT")
    kT = C.sb([64, 2 * SEQ], BF16, "B_kT")
    v = C.sb([128, 128, 64], BF16, "B_v")
    cst = C.sb([128, 256 + 4 * 512], BF16, "B_cst")
    negtri = cst[:, 0:128]
    negones = cst[:, 128:256]
    mask = cst[:, 256:].rearrange("p (i q) -> p i q", i=4)
    eb = [C.sb([128, 512], F32, "B_e%d" % i) for i in range(2)]
    spb = [C.sb([128, 512], BF16, "B_sp%d" % i) for i in range(3)]
    wb = [C.sb([128, 512], BF16, "B_w%d" % i) for i in range(3)]
    ssb = [C.sb([128, 512], BF16, "B_ss%d" % i) for i in range(3)]
    ost = [C.sb([64, 512], F32, "B_ost%d" % i) for i in range(2)]

    P.op("pool", lambda e: e.dma_start(out=cst, in_=consts_bf), writes=["cst"], dma=True)
    for c in range(8):
        P.op("sp", lambda e, c=c: e.dma_start(
            out=kT[:, c * TOK:(c + 1) * TOK], in_=recv_bf[c, QKV_SZ:2 * QKV_SZ].rearrange("(d t) -> d t", t=TOK)),
            writes=[("kT", c)], dma=True)
        P.op("sp", lambda e, c=c: e.dma_start(
            out=qT[:, c * TOK:(c + 1) * TOK], in_=recv_bf[c, 0:QKV_SZ].rearrange("(d t) -> d t", t=TOK)),
            writes=[("qT", c)], dma=True)
        P.op("sp", lambda e, c=c: e.dma_start(
            out=v[:, c * 16:(c + 1) * 16, :],
            in_=recv_bf[c, 2 * QKV_SZ:3 * QKV_SZ].rearrange("(n p d) -> p n d", p=128, d=64)),
            writes=[("v", c)], dma=True)

    units = []
    for b in range(2):
        for qc in range(16):
            kbs = [(4 * qc + i, i) for i in (3, 2, 1, 0)] + [(kb, None) for kb in range(4 * qc - 1, -1, -1)]
            for n, (kb, di) in enumerate(kbs):
                units.append(dict(idx=len(units), b=b, qc=qc, kb=kb, diag=di, first=(n == 0),
                                  last=(n == len(kbs) - 1)))

    def unit_cols(u):
        kc = u["b"] * SEQ + u["kb"] * 128
        qc0 = u["b"] * SEQ + u["qc"] * 512
        return kc, qc0

    def emit_Z(u):
        i = u["idx"]
        kc, qc0 = unit_cols(u)
        zb = i % 4
        Z = C.banks[zb]
        P.op("pe", lambda e: e.matmul(Z, lhsT=kT[:, kc:kc + 128], rhs=qT[:, qc0:qc0 + 512], start=True, stop=True),
             reads=[("kT", kc // TOK), ("qT", qc0 // TOK)], writes=[("bank", zb)])

    def emit_expZ(u):
        i = u["idx"]
        zb = i % 4
        Z = C.banks[zb]
        ee = eb[i % 2]
        P.op("act", lambda e: e.activation(out=ee, in_=Z, func=AF.Exp), reads=[("bank", zb)], writes=[("e", i % 2)])

    def emit_ln(u):
        i = u["idx"]
        ee = eb[i % 2]
        sp = spb[i % 3]
        P.op("act", lambda e: e.activation(out=sp, in_=ee, func=AF.Ln, bias=1.0),
             reads=[("e", i % 2)], writes=[("sp", i % 3)])
        if u["diag"] is not None:
            m = mask[:, u["diag"], :]
            P.op("dve", lambda e: e.tensor_tensor(out=sp, in0=sp, in1=m, op=ALU.mult),
                 reads=[("sp", i % 3), "cst"], writes=[("sp", i % 3)])
        ss = ssb[i % 3]
        if u["first"]:
            P.op("dve", lambda e: e.tensor_copy(out=ss, in_=sp), reads=[("sp", i % 3)], writes=[("ss", i % 3)])
        else:
            sprev = ssb[(i - 1) % 3]
            P.op("dve", lambda e: e.tensor_tensor(out=ss, in0=sprev, in1=sp, op=ALU.add),
                 reads=[("sp", i % 3), ("ss", (i - 1) % 3)], writes=[("ss", i % 3)])

    def emit_W(u):
        i = u["idx"]
        kc, qc0 = unit_cols(u)
        wbk = i % 4
        Wp = C.banks[wbk]
        sp = spb[i % 3]
        first = u["first"]
        P.op("pe", lambda e: e.matmul(Wp, lhsT=negtri, rhs=sp, start=False, stop=first, skip_group_check=True),
             reads=["cst", ("sp", i % 3)], writes=[("bank", wbk)])
        if not first:
            sprev = ssb[(i - 1) % 3]
            P.op("pe", lambda e: e.matmul(Wp, lhsT=negones, rhs=sprev, start=False, stop=True, skip_group_check=True),
                 reads=["cst", ("ss", (i - 1) % 3)], writes=[("bank", wbk)])

    def emit_expW(u):
        i = u["idx"]
        wbk = i % 4
        Wp = C.banks[wbk]
        w = wb[i % 3]
        P.op("act", lambda e: e.activation(out=w, in_=Wp, func=AF.Exp), reads=[("bank", wbk)], writes=[("w", i % 3)])
        if u["diag"] is not None:
            m = mask[:, u["diag"], :]
            P.op("dve", lambda e: e.tensor_tensor(out=w, in0=w, in1=m, op=ALU.mult),
                 reads=[("w", i % 3), "cst"], writes=[("w", i % 3)])

    def emit_PV(u):
        i = u["idx"]
        qc = u["qc"]
        G = u["b"] * 64 + u["kb"]
        ob = 4 + qc % 2
        O = C.banks[ob][0:64, :]
        w = wb[i % 3]
        P.op("pe", lambda e: e.matmul(O, lhsT=v[:, G, :], rhs=w, start=u["first"], stop=u["last"]),
             reads=[("v", G // 16), ("w", i % 3)], writes=[("bank", ob)])
        if u["last"]:
            os_ = ost[qc % 2]
            P.op("dve", lambda e: e.tensor_copy(out=os_, in_=O), reads=[("bank", ob)], writes=[("ost", qc % 2)])
            dest = u["b"] * 4 + qc // 4
            c0 = (qc % 4) * 512
            P.op("sp", lambda e: e.dma_start(out=send_o[dest, 0:64, c0:c0 + 512], in_=os_),
                 reads=[("ost", qc % 2)], dma=True)

    n = len(units)
    U = lambda i: units[i] if 0 <= i < n else None
    emit_Z(units[0])
    emit_Z(units[1])
    emit_expZ(units[0])
    for s in range(n + 2):
        if U(s + 2):
            emit_Z(U(s + 2))
        if U(s + 1):
            emit_expZ(U(s + 1))
        if U(s):
            emit_ln(U(s))
        if U(s - 1):
            emit_W(U(s - 1))
        if U(s - 2):
            emit_PV(U(s - 2))
        if U(s - 1):
            emit_expW(U(s - 1))
        if side is not None and s % 6 != 5:
            for _ in range(side_rate):
                next(side, None)
    if side is not None:
        for _ in side:
            pass


def attn_consts():
    k = np.arange(128)
    negtri = -(k[:, None] >= k[None, :]).astype(np.float32)
    negones = -np.ones((128, 128), np.float32)
    q = np.arange(512)
    mask = np.stack([(128 * i + k[:, None] < q[None, :]).astype(np.float32) for i in range(4)], 1)
    return np.concatenate([negtri, negones, mask.reshape(128, 2048)], 1)


def build_B(with_ssm=True):
    nc = bass.Bass("TRN2", target_bir_lowering=False)
    es = ExitStack()
    C = Ctx(nc, es)
    recv_bf = C.din("recv_bf", [8, SB_SZ], BF16)
    consts_bf = C.din("consts_bf", [128, 256 + 2048], F32)
    send_o = C.dout("send_o", [8, 96, TOK], F32)
    side = None
    if with_ssm:
        ssm_io = ssm_decl(C)
        side = phase_B_ssm(C, ssm_io, send_o)
    phase_B_attn(C, recv_bf, consts_bf, send_o, side=side)
    C.P.emit(es)
    es.close()
    return nc


import math

SSM_NP = 6 + 1 + 32 + 32 + 32 + 2 + 128 + 128
PC_VEC, PC_SGN, PC_BRI, PC_BIR, PC_CT, PC_D, PC_I, PC_SW = 0, 6, 7, 39, 71, 103, 105, 233
NLEV = 13


def ssm_decl(C):
    return dict(recv_u=C.din("recv_u", [8, 32, TOK], F32), ssm_par=C.din("ssm_par", [128, SSM_NP], F32))


def ssm_pack(lam_re, lam_im, log_dt, b_re, b_im, c_re, c_im, d, core):
    par = np.zeros((128, SSM_NP), np.float32)
    for gi in range(2):
        g = 2 * core + gi
        par[:, PC_VEC + 3 * gi + 0] = np.concatenate([lam_re[g], lam_re[g]])
        par[:, PC_VEC + 3 * gi + 1] = np.concatenate([lam_im[g], lam_im[g]])
        par[:, PC_VEC + 3 * gi + 2] = log_dt[g]
        par[:, PC_BRI + 16 * gi:PC_BRI + 16 * gi + 16] = np.concatenate([b_re[g], b_im[g]], 0)
        par[:, PC_BIR + 16 * gi:PC_BIR + 16 * gi + 16] = np.concatenate([b_im[g], b_re[g]], 0)
        par[:, PC_CT + 16 * gi:PC_CT + 16 * gi + 16] = np.concatenate([c_re[g].T, c_im[g].T], 0)
        par[16 * gi:16 * gi + 16, PC_D + gi] = d[16 * g:16 * g + 16]
    par[:64, PC_SGN] = 1.0
    par[64:, PC_SGN] = -1.0
    par[:, PC_I:PC_I + 128] = np.eye(128, dtype=np.float32)
    par[:, PC_SW:PC_SW + 128] = np.roll(np.eye(128, dtype=np.float32), 64, axis=1)
    return par


def phase_B_ssm(C, io, send_o):
    P = C.P
    recv_u, ssm_par = io["recv_u"], io["ssm_par"]
    par = C.sb([128, SSM_NP], F32, "S_par")
    sc = C.sb([128, 128], F32, "S_sc")
    sci = C.sb([128, 4], mybir.dt.int32, "S_sci")
    AT = C.sb([128, 2, NLEV, 128], F32, "S_AT")
    Bpad = C.sb([128, 2, 32], F32, "S_Bpad")
    Cpad = C.sb([128, 2, 32], F32, "S_Cpad")
    lB = C.sb([32, 2, 128], F32, "S_lB")
    X = C.sb([128, SEQ], F32, "S_X")
    ub = C.sb([32, SEQ], F32, "S_u")
    yst = [C.sb([32, 512], F32, "S_y%d" % i) for i in range(2)]
    I128 = par[:, PC_I:PC_I + 128]
    SW = par[:, PC_SW:PC_SW + 128]
    sgn = par[:, PC_SGN:PC_SGN + 1]
    TWO_PI = 2.0 * math.pi

    P.op("sp", lambda e: e.dma_start(out=par, in_=ssm_par), writes=["par"], dma=True)
    P.op("dve", lambda e: e.memset(Bpad, 0.0), writes=["Bpad"])
    P.op("dve", lambda e: e.memset(Cpad, 0.0), writes=["Cpad"])

    ncol = [0]

    def col():
        ncol[0] += 1
        assert ncol[0] <= 128
        return ncol[0] - 1

    def cs(i):
        return sc[:, i:i + 1]

    def K(i):
        return ("sc", i)

    def ts(o, a, s1, s2, op0, op1=None, extra=()):
        if op1 is None:
            P.op("dve", lambda e: e.tensor_scalar(out=cs(o), in0=cs(a), scalar1=s1, scalar2=None, op0=op0),
                 reads=[K(a)] + list(extra), writes=[K(o)])
        else:
            P.op("dve", lambda e: e.tensor_scalar(out=cs(o), in0=cs(a), scalar1=s1, scalar2=s2, op0=op0, op1=op1),
                 reads=[K(a)] + list(extra), writes=[K(o)])

    def tt(o, a, b, op):
        P.op("dve", lambda e: e.tensor_tensor(out=cs(o), in0=cs(a), in1=cs(b), op=op),
             reads=[K(a), K(b)], writes=[K(o)])

    for gi in range(2):
        ncol[0] = 0
        lr = par[:, PC_VEC + 3 * gi:PC_VEC + 3 * gi + 1]
        li = par[:, PC_VEC + 3 * gi + 1:PC_VEC + 3 * gi + 2]
        ldt = par[:, PC_VEC + 3 * gi + 2:PC_VEC + 3 * gi + 3]
        c_dt, c_mag, c_th = col(), col(), col()
        P.op("act", lambda e, c_dt=c_dt, ldt=ldt: e.activation(out=cs(c_dt), in_=ldt, func=AF.Exp),
             reads=["par"], writes=[K(c_dt)])
        P.op("act", lambda e, c_mag=c_mag, c_dt=c_dt, lr=lr: e.activation(out=cs(c_mag), in_=lr, func=AF.Exp, scale=cs(c_dt)),
             reads=["par", K(c_dt)], writes=[K(c_mag)])
        P.op("dve", lambda e, c_th=c_th, c_dt=c_dt, li=li: e.tensor_tensor(out=cs(c_th), in0=li, in1=cs(c_dt), op=ALU.mult),
             reads=["par", K(c_dt)], writes=[K(c_th)])
        trig = []
        for shift in (0.5 * math.pi, 0.0):
            c_a, c_y, c_n, c_r, c_w, c_o = col(), col(), col(), col(), col(), col()
            ts(c_a, c_th, shift, None, ALU.add)
            ts(c_y, c_a, 1.0 / TWO_PI, None, ALU.mult)
            ii = 0 if shift else 1
            P.op("dve", lambda e, ii=ii, c_y=c_y: e.tensor_copy(out=sci[:, ii:ii + 1], in_=cs(c_y)),
                 reads=[K(c_y)], writes=[("sci", ii)])
            P.op("dve", lambda e, ii=ii, c_n=c_n: e.tensor_copy(out=cs(c_n), in_=sci[:, ii:ii + 1]),
                 reads=[("sci", ii)], writes=[K(c_n)])
            P.op("dve", lambda e, c_r=c_r, c_n=c_n, c_a=c_a: e.scalar_tensor_tensor(
                out=cs(c_r), in0=cs(c_n), scalar=-TWO_PI, in1=cs(c_a), op0=ALU.mult, op1=ALU.add),
                reads=[K(c_n), K(c_a)], writes=[K(c_r)])
            ts(c_w, c_r, math.pi, -TWO_PI, ALU.is_gt, ALU.mult)
            tt(c_r, c_r, c_w, ALU.add)
            P.op("act", lambda e, c_o=c_o, c_r=c_r: e.activation(out=cs(c_o), in_=cs(c_r), func=AF.Sin),
                 reads=[K(c_r)], writes=[K(c_o)])
            trig.append(c_o)
        c_ar, c_ai = col(), col()
        tt(c_ar, c_mag, trig[0], ALU.mult)
        tt(c_ai, c_mag, trig[1], ALU.mult)
        c_l2, c_i2, c_den, c_am1, c_t1, c_t2, c_fr, c_fi, c_f2 = [col() for _ in range(9)]
        P.op("dve", lambda e, lr=lr, c_l2=c_l2: e.tensor_tensor(out=cs(c_l2), in0=lr, in1=lr, op=ALU.mult),
             reads=["par"], writes=[K(c_l2)])
        P.op("dve", lambda e, li=li, c_i2=c_i2: e.tensor_tensor(out=cs(c_i2), in0=li, in1=li, op=ALU.mult),
             reads=["par"], writes=[K(c_i2)])
        tt(c_den, c_l2, c_i2, ALU.add)
        P.op("dve", lambda e, c_den=c_den: e.reciprocal(out=cs(c_den), in_=cs(c_den)), reads=[K(c_den)], writes=[K(c_den)])
        ts(c_am1, c_ar, -1.0, None, ALU.add)
        P.op("dve", lambda e, lr=lr, c_t1=c_t1, c_am1=c_am1: e.tensor_tensor(out=cs(c_t1), in0=cs(c_am1), in1=lr, op=ALU.mult),
             reads=["par", K(c_am1)], writes=[K(c_t1)])
        P.op("dve", lambda e, li=li, c_t2=c_t2, c_ai=c_ai: e.tensor_tensor(out=cs(c_t2), in0=cs(c_ai), in1=li, op=ALU.mult),
             reads=["par", K(c_ai)], writes=[K(c_t2)])
        tt(c_fr, c_t1, c_t2, ALU.add)
        tt(c_fr, c_fr, c_den, ALU.mult)
        P.op("dve", lambda e, lr=lr, c_t1=c_t1, c_ai=c_ai: e.tensor_tensor(out=cs(c_t1), in0=cs(c_ai), in1=lr, op=ALU.mult),
             reads=["par", K(c_ai)], writes=[K(c_t1)])
        P.op("dve", lambda e, li=li, c_t2=c_t2, c_am1=c_am1: e.tensor_tensor(out=cs(c_t2), in0=cs(c_am1), in1=li, op=ALU.mult),
             reads=["par", K(c_am1)], writes=[K(c_t2)])
        tt(c_fi, c_t1, c_t2, ALU.subtract)
        tt(c_fi, c_fi, c_den, ALU.mult)
        P.op("dve", lambda e, c_f2=c_f2, c_fi=c_fi: e.scalar_tensor_tensor(
            out=cs(c_f2), in0=cs(c_fi), scalar=-1.0, in1=sgn, op0=ALU.mult, op1=ALU.mult),
            reads=[K(c_fi), "par"], writes=[K(c_f2)])
        bsl = Bpad[:, gi, 16 * gi:16 * gi + 16]
        P.op("dve", lambda e, bsl=bsl, gi=gi, c_fr=c_fr: e.tensor_scalar(
            out=bsl, in0=par[:, PC_BRI + 16 * gi:PC_BRI + 16 * gi + 16], scalar1=cs(c_fr), scalar2=None, op0=ALU.mult),
            reads=["par", K(c_fr), "Bpad"], writes=[("Bpad", gi)])
        P.op("dve", lambda e, bsl=bsl, gi=gi, c_f2=c_f2: e.scalar_tensor_tensor(
            out=bsl, in0=par[:, PC_BIR + 16 * gi:PC_BIR + 16 * gi + 16], scalar=cs(c_f2), in1=bsl, op0=ALU.mult, op1=ALU.add),
            reads=["par", K(c_f2), ("Bpad", gi)], writes=[("Bpad", gi)])
        pb = C.banks[6][0:32, 0:128]
        P.op("pe", lambda e, pb=pb, gi=gi: e.transpose(out=pb, in_=Bpad[:, gi, :], identity=I128),
             reads=[("Bpad", gi), "par"], writes=[("bank", 6)])
        P.op("dve", lambda e, pb=pb, gi=gi: e.tensor_copy(out=lB[:, gi, :], in_=pb), reads=[("bank", 6)], writes=[("lB", gi)])
        P.op("dve", lambda e, gi=gi: e.tensor_scalar(
            out=Cpad[:, gi, 16 * gi:16 * gi + 16], in0=par[:, PC_CT + 16 * gi:PC_CT + 16 * gi + 16],
            scalar1=sgn, scalar2=None, op0=ALU.mult), reads=["par", "Cpad"], writes=[("Cpad", gi)])
        c_pr, c_pi = c_ar, c_ai
        for k in range(NLEV):
            c_s2 = col()
            P.op("dve", lambda e, c_s2=c_s2, c_pi=c_pi: e.tensor_tensor(out=cs(c_s2), in0=cs(c_pi), in1=sgn, op=ALU.mult),
                 reads=[K(c_pi), "par"], writes=[K(c_s2)])
            A = AT[:, gi, k, :]
            P.op("dve", lambda e, A=A, c_pr=c_pr: e.tensor_scalar(out=A, in0=I128, scalar1=cs(c_pr), scalar2=None, op0=ALU.mult),
                 reads=["par", K(c_pr)], writes=[("AT", gi, k)])
            P.op("dve", lambda e, A=A, c_s2=c_s2: e.scalar_tensor_tensor(
                out=A, in0=SW, scalar=cs(c_s2), in1=A, op0=ALU.mult, op1=ALU.add),
                reads=["par", K(c_s2), ("AT", gi, k)], writes=[("AT", gi, k)])
            if k + 1 < NLEV:
                c_a2, c_b2, c_nr, c_ni = col(), col(), col(), col()
                tt(c_a2, c_pr, c_pr, ALU.mult)
                tt(c_b2, c_pi, c_pi, ALU.mult)
                tt(c_nr, c_a2, c_b2, ALU.subtract)
                P.op("dve", lambda e, c_ni=c_ni, c_pr=c_pr, c_pi=c_pi: e.scalar_tensor_tensor(
                    out=cs(c_ni), in0=cs(c_pr), scalar=2.0, in1=cs(c_pi), op0=ALU.mult, op1=ALU.mult),
                    reads=[K(c_pr), K(c_pi)], writes=[K(c_ni)])
                c_pr, c_pi = c_nr, c_ni

    def scan_gen():
      if True:
        nb = [0]

        def sbank():
            nb[0] += 1
            return 6 + nb[0] % 2

        for b in range(2):
            for j in range(4):
                P.op("sp", lambda e, b=b, j=j: e.dma_start(out=ub[:, j * TOK:(j + 1) * TOK], in_=recv_u[4 * b + j]),
                     writes=[("ub", j)], dma=True)
            for gi in range(2):
                for ch in range(16):
                    bk = sbank()
                    ps = C.banks[bk]
                    P.op("pe", lambda e, ps=ps, gi=gi, ch=ch: e.matmul(
                        ps, lhsT=lB[:, gi, :], rhs=ub[:, ch * 512:(ch + 1) * 512], start=True, stop=True),
                        reads=[("lB", gi), ("ub", ch // 4)], writes=[("bank", bk)])
                    P.op("dve", lambda e, ps=ps, ch=ch: e.tensor_copy(out=X[:, ch * 512:(ch + 1) * 512], in_=ps),
                         reads=[("bank", bk)], writes=[("X", ch)])
                    yield

                for k in range(NLEV):
                    s_ = 1 << k
                    for ch in range(15, -1, -1):
                        lo = max(512 * ch, s_)
                        hi = 512 * (ch + 1)
                        if lo >= hi:
                            continue
                        n_ = hi - lo
                        bk = sbank()
                        ps = C.banks[bk][:, 0:n_]
                        src = X[:, lo - s_:hi - s_]
                        dst = X[:, lo:hi]
                        rk = sorted(set([("X", (lo - s_) // 512), ("X", (hi - s_ - 1) // 512)]))
                        P.op("pe", lambda e, ps=ps, src=src, k=k, gi=gi: e.matmul(
                            ps, lhsT=AT[:, gi, k, :], rhs=src, start=True, stop=True),
                            reads=[("AT", gi, k)] + rk, writes=[("bank", bk)])
                        P.op("dve", lambda e, ps=ps, dst=dst: e.tensor_tensor(out=dst, in0=dst, in1=ps, op=ALU.add),
                             reads=[("bank", bk), ("X", ch)], writes=[("X", ch)])
                        yield
                for ch in range(16):
                    bk = sbank()
                    ps = C.banks[bk][0:32, :]
                    P.op("pe", lambda e, ps=ps, ch=ch, gi=gi: e.matmul(
                        ps, lhsT=Cpad[:, gi, :], rhs=X[:, ch * 512:(ch + 1) * 512], start=True, stop=True),
                        reads=[("Cpad", gi), ("X", ch)], writes=[("bank", bk)])
                    ys = yst[ch % 2]
                    P.op("dve", lambda e, ps=ps, ch=ch, gi=gi, ys=ys: e.scalar_tensor_tensor(
                        out=ys, in0=ub[:, ch * 512:(ch + 1) * 512], scalar=par[0:32, PC_D + gi:PC_D + gi + 1], in1=ps,
                        op0=ALU.mult, op1=ALU.add),
                        reads=[("bank", bk), ("ub", ch // 4), "par"], writes=[("yst", ch % 2)])
                    dest = 4 * b + ch // 4
                    c0 = (ch % 4) * 512
                    P.op("sp", lambda e, ys=ys, dest=dest, c0=c0, gi=gi: e.dma_start(
                        out=send_o[dest, 64 + 16 * gi:64 + 16 * gi + 16, c0:c0 + 512], in_=ys[16 * gi:16 * gi + 16, :]),
                        reads=[("yst", ch % 2)], dma=True)
                    yield

    return scan_gen()


def phase_C(C, io):
    P = C.P
    x, x_out, recv_o, hc_loc, halo, mem = io["x"], io["x_out"], io["recv_o"], io["hc_loc"], io["halo"], io["mem"]
    bcast = lambda ap, n: ap.to_broadcast([128, n])

    xres = C.sb([128, NT, D], F32, "C_x")
    idB = C.sb([128, 128], BF16, "C_idB")
    idF = C.sb([128, 128], F32, "C_idF")
    onesF = C.sb([128, 1], F32, "C_ones")
    st = C.sb([128, 64], F32, "C_st")
    junk = C.sb([128, D], BF16, "C_junk")
    hb = C.sb([128, D], BF16, "C_hb")
    tT = [C.sb([128, 8, 128], BF16, "C_tT%d" % i) for i in range(2)]
    base_mark = C.mark()
    wq = C.sb([128, 8, D], BF16, "C_wq")
    wo = C.sb([128, 8, D], BF16, "C_wo")
    s12_mark = C.mark()

    for t in range(NT):
        P.op("sp", lambda e, t=t: e.dma_start(out=xres[:, t, :], in_=x[t * 128:(t + 1) * 128, :]),
             writes=[("x", t)], dma=True)
    P.op("pool", lambda e: e.dma_start(out=idB, in_=io["ident"]), writes=["idB"], dma=True)
    P.op("sp", lambda e: e.dma_start(out=idF, in_=io["ident"]), writes=["idF"], dma=True)
    P.op("dve", lambda e: e.memset(onesF, 1.0), writes=["ones"])

    nst = [0]

    def stcol(n=1):
        c = nst[0] % 64
        if c + n > 64:
            c = 0
        nst[0] = c + n
        return c

    def sk(c, n=1):
        return [("st", c + i) for i in range(n)]

    def rstd_col(ss_c, n, inv_n):
        tmp, out = stcol(n), stcol(n)
        rstd_ops(P, st[:, ss_c:ss_c + n], st[:, tmp:tmp + n], st[:, out:out + n], inv_n,
                 sk(ss_c, n), sk(tmp, n), sk(out, n))
        return out

    def load_w(dst, src, nk, key, eng="pool"):
        for k in range(nk):
            P.op(eng, lambda e, k=k: e.dma_start(out=dst[:, k, :], in_=src[k * 128:(k + 1) * 128, :]),
                 writes=[(key, k)], dma=True)
        return [(key, k) for k in range(nk)]

    def transposes(src, n, dst, bank, rk, wk, ident=None, f32=False, evac="act"):
        if f32:
            pv = C.banks[bank][:, 0:n * 128].rearrange("p (i t) -> p i t", i=n)
        else:
            pv = bank_bf(C.banks[bank])[:, 0:n * 128].rearrange("p (i t) -> p i t", i=n)
        idt, idk = (idF, "idF") if f32 else (idB, "idB")
        for i in range(n):
            P.op("pe", lambda e, i=i: e.transpose(out=pv[:, i, :], in_=src[:, i * 128:(i + 1) * 128], identity=idt),
                 reads=list(rk) + [idk], writes=[("bank", bank)])
        if evac == "act":
            P.op("act", lambda e: e.copy(out=dst, in_=pv), reads=[("bank", bank)], writes=list(wk))
        else:
            P.op("dve", lambda e: e.tensor_copy(out=dst, in_=pv), reads=[("bank", bank)], writes=list(wk))

    def norm_T(src, gB, gkey, dstT, bank, rk, wk):
        c = stcol()
        P.op("act", lambda e: e.activation(out=junk, in_=src, func=AF.Square, accum_out=st[:, c:c + 1]),
             reads=list(rk), writes=["junk", ("st", c)])
        r = rstd_col(c, 1, 1.0 / D)
        P.op("dve", lambda e: e.scalar_tensor_tensor(out=hb, in0=src, scalar=st[:, r:r + 1], in1=gB,
                                                     op0=ALU.mult, op1=ALU.mult),
             reads=list(rk) + [("st", r), gkey], writes=["hb"])
        transposes(hb, 8, dstT, bank, ["hb"], wk)

    hcT = C.sb([128, 2, 32 + TOK], BF16, "C_hcT")
    Dg = C.sb([128, 2, 31, 128], BF16, "C_Dg")
    cw = C.sb([128, 2, 36], F32, "C_cw")
    cT = C.sb([128, 2, TOK], F32, "C_cT")
    lngB = C.sb([128, 256], F32, "C_lng")
    lnbB = C.sb([128, 256], F32, "C_lnb")
    gbrB = C.sb([128, 512], F32, "C_gbrB")
    gbrT = C.sb([128, 8], F32, "C_gbrT")
    pw2 = C.sb([128, 2, 256], BF16, "C_pw2")
    gluw = C.sb([128, 2, 512], BF16, "C_gluw")
    wout = C.sb([128, 8, D], BF16, "C_wout")
    yT = C.sb([128, 2, TOK], BF16, "C_yT")
    oT = [C.sb([128, 4, 128], F32, "C_oT%d" % i) for i in range(2)]
    osq = C.sb([128, 4, 128], F32, "C_osq")
    og = C.sb([128, 4, 128], BF16, "C_og")
    yn = C.sb([128, 256], F32, "C_yn")
    ge = C.sb([128, 256], F32, "C_ge")
    sw = C.sb([128, 256], BF16, "C_sw")
    swT = C.sb([128, 2, 128], BF16, "C_swT")
    osm = C.sb([128, 256], F32, "C_osm")
    mixed = C.sb([128, 512], BF16, "C_mixed")

    P.op("sp", lambda e: e.dma_start(out=hcT[:, :, 0:32], in_=halo.rearrange("(c p) t -> p c t", p=128)),
         writes=["hcT_h"], dma=True)
    P.op("sp", lambda e: e.dma_start(out=hcT[:, :, 32:], in_=hc_loc.rearrange("(c p) t -> p c t", p=128)),
         writes=["hcT"], dma=True)
    P.op("sp", lambda e: e.dma_start(out=cw[:, :, 0:31], in_=io["conv_dw_wT"].rearrange("(c p) t -> p c t", p=128)),
         writes=["cw"], dma=True)
    for ct in range(2):
        P.op("sp", lambda e, ct=ct: e.dma_start(out=cw[:, ct, 31:32], in_=io["conv_dw_b"][ct * 128:(ct + 1) * 128, :]),
             writes=[("cwb", ct)], dma=True)
    P.op("sp", lambda e: e.dma_start(out=lngB, in_=bcast(io["conv_ln_g"], 256)), writes=["lng"], dma=True)
    P.op("sp", lambda e: e.dma_start(out=lnbB, in_=bcast(io["conv_ln_b"], 256)), writes=["lnb"], dma=True)
    P.op("sp", lambda e: e.dma_start(out=gbrB, in_=bcast(io["branch_g"][:, 512:1024], 512)), writes=["gbrB"], dma=True)
    P.op("sp", lambda e: e.dma_start(out=gbrT, in_=io["branch_gT"]), writes=["gbrT"], dma=True)
    load_w(pw2, io["conv_pw2_w"], 2, "pw2")
    load_w(gluw, io["ssm_glu_w"], 2, "gluw")
    kw_out = load_w(wout, io["w_out"], 8, "wout")
    for j in range(8):
        P.op("pool", lambda e, j=j: e.dma_start(out=yT[(j % 4) * 32:(j % 4) * 32 + 32, j // 4, :], in_=recv_o[j, 64:96, :]),
             writes=[("yT", j)], dma=True)
    load_w(wq, io["xa_wq"], 8, "wq")
    load_w(wo, io["xa_wo"], 8, "wo")
    for ct in range(2):
        for j in range(31):
            eng = "dve" if (j % 2) else "pool"
            P.op(eng, lambda e, ct=ct, j=j: e.tensor_scalar(out=Dg[:, ct, j, :], in0=idF, scalar1=cw[:, ct, j:j + 1],
                                                           scalar2=None, op0=ALU.mult),
                 reads=["idF", "cw"], writes=[("Dg", ct, j)])
    nb = [0]
    for ct in range(2):
        for ch in range(4):
            bk = nb[0] % 2
            nb[0] += 1
            ps = C.banks[bk]
            for j in range(31):
                c0 = 512 * ch + j + 2
                P.op("pe", lambda e, ct=ct, j=j, c0=c0, ps=ps: e.matmul(
                    ps, lhsT=Dg[:, ct, j, :], rhs=hcT[:, ct, c0:c0 + 512], start=(j == 0), stop=(j == 30)),
                    reads=[("Dg", ct, j), "hcT", "hcT_h"], writes=[("bank", bk)])
            P.op("act", lambda e, ct=ct, ch=ch, ps=ps: e.activation(
                out=cT[:, ct, ch * 512:(ch + 1) * 512], in_=ps, func=AF.Identity, bias=cw[:, ct, 31:32]),
                reads=[("bank", bk), ("cwb", ct)], writes=[("cT", ct, ch)])

    def load_o(t):
        par = t % 2
        for k in range(4):
            for hh in range(2):
                P.op("sp", lambda e, k=k, hh=hh, par=par, t=t: e.dma_start(
                    out=oT[par][hh * 64:hh * 64 + 64, k, :], in_=recv_o[2 * k + hh, 0:64, t * 128:(t + 1) * 128]),
                    writes=[("oT", par)], dma=True)

    load_o(0)
    for t in range(NT):
        par = t % 2
        if t + 1 < NT:
            load_o(t + 1)
        tc = slice(t * 128, (t + 1) * 128)
        pc = C.banks[2][:, 0:256]
        for ct in range(2):
            P.op("pe", lambda e, ct=ct, tc=tc: e.transpose(out=pc[:, ct * 128:(ct + 1) * 128], in_=cT[:, ct, tc], identity=idF),
                 reads=[("cT", ct, t // 4), "idF"], writes=[("bank", 2)])
        c_s = stcol()
        P.op("dve", lambda e, c_s=c_s: e.tensor_reduce(out=st[:, c_s:c_s + 1], in_=pc, axis=AX.X, op=ALU.add),
             reads=[("bank", 2)], writes=[("st", c_s)])
        c_m = stcol()
        P.op("dve", lambda e, c_s=c_s, c_m=c_m: e.tensor_scalar(out=st[:, c_m:c_m + 1], in0=st[:, c_s:c_s + 1],
                                                              scalar1=-1.0 / 256, scalar2=None, op0=ALU.mult),
             reads=[("st", c_s)], writes=[("st", c_m)])
        c_v = stcol()
        P.op("act", lambda e, c_m=c_m, c_v=c_v: e.activation(out=junk[:, 0:256], in_=pc, func=AF.Square,
                                                            bias=st[:, c_m:c_m + 1], accum_out=st[:, c_v:c_v + 1]),
             reads=[("bank", 2), ("st", c_m)], writes=["junk", ("st", c_v)])
        c_r = rstd_col(c_v, 1, 1.0 / 256)
        P.op("dve", lambda e, c_m=c_m, c_r=c_r: e.tensor_scalar(out=yn, in0=pc, scalar1=st[:, c_m:c_m + 1],
                                                              scalar2=st[:, c_r:c_r + 1], op0=ALU.add, op1=ALU.mult),
             reads=[("bank", 2), ("st", c_m), ("st", c_r)], writes=["yn"])
        P.op("dve", lambda e: e.tensor_tensor(out=yn, in0=yn, in1=lngB, op=ALU.mult), reads=["yn", "lng"], writes=["yn"])
        P.op("dve", lambda e: e.tensor_tensor(out=yn, in0=yn, in1=lnbB, op=ALU.add), reads=["yn", "lnb"], writes=["yn"])
        P.op("act", lambda e: e.activation(out=ge, in_=yn, func=AF.Exp, scale=-1.0), reads=["yn"], writes=["ge"])
        P.op("dve", lambda e: e.tensor_scalar(out=ge, in0=ge, scalar1=1.0, scalar2=None, op0=ALU.add), reads=["ge"], writes=["ge"])
        P.op("dve", lambda e: e.reciprocal(out=ge, in_=ge), reads=["ge"], writes=["ge"])
        P.op("dve", lambda e: e.tensor_tensor(out=sw, in0=yn, in1=ge, op=ALU.mult), reads=["yn", "ge"], writes=["sw"])
        transposes(sw, 2, swT, 3, ["sw"], ["swT"])
        po = C.banks[2][:, 256:512]
        for ct in range(2):
            P.op("pe", lambda e, ct=ct: e.matmul(po, lhsT=swT[:, ct, :], rhs=pw2[:, ct, :], start=(ct == 0), stop=(ct == 1)),
                 reads=["swT", ("pw2", ct)], writes=[("bank", 2)])
        c_q = stcol()
        P.op("act", lambda e, c_q=c_q: e.activation(out=junk[:, 0:256], in_=po, func=AF.Square, accum_out=st[:, c_q:c_q + 1]),
             reads=[("bank", 2)], writes=["junk", ("st", c_q)])
        c_r2 = rstd_col(c_q, 1, 1.0 / 256)
        P.op("dve", lambda e, c_r2=c_r2: e.scalar_tensor_tensor(out=mixed[:, 0:256], in0=po, scalar=st[:, c_r2:c_r2 + 1],
                                                              in1=gbrB[:, 0:256], op0=ALU.mult, op1=ALU.mult),
             reads=[("bank", 2), ("st", c_r2), "gbrB"], writes=["mixed0"])
        pg = C.banks[3]
        for ct in range(2):
            P.op("pe", lambda e, ct=ct, tc=tc: e.matmul(pg, lhsT=yT[:, ct, tc], rhs=gluw[:, ct, :], start=(ct == 0), stop=(ct == 1)),
                 reads=[("yT", j) for j in range(8)] + [("gluw", ct)], writes=[("bank", 3)])
        P.op("act", lambda e: e.activation(out=ge, in_=pg[:, 256:512], func=AF.Exp, scale=-1.0), reads=[("bank", 3)], writes=["ge"])
        P.op("dve", lambda e: e.tensor_scalar(out=ge, in0=ge, scalar1=1.0, scalar2=None, op0=ALU.add), reads=["ge"], writes=["ge"])
        P.op("dve", lambda e: e.reciprocal(out=ge, in_=ge), reads=["ge"], writes=["ge"])
        P.op("dve", lambda e: e.tensor_tensor(out=osm, in0=pg[:, 0:256], in1=ge, op=ALU.mult), reads=[("bank", 3), "ge"], writes=["osm"])
        c_q3 = stcol()
        P.op("act", lambda e, c_q3=c_q3: e.activation(out=junk[:, 0:256], in_=osm, func=AF.Square, accum_out=st[:, c_q3:c_q3 + 1]),
             reads=["osm"], writes=["junk", ("st", c_q3)])
        c_r3 = rstd_col(c_q3, 1, 1.0 / 256)
        P.op("dve", lambda e, c_r3=c_r3: e.scalar_tensor_tensor(out=mixed[:, 256:512], in0=osm, scalar=st[:, c_r3:c_r3 + 1],
                                                              in1=gbrB[:, 256:512], op0=ALU.mult, op1=ALU.mult),
             reads=["osm", ("st", c_r3), "gbrB"], writes=["mixed1"])
        mT = tT[0]
        transposes(mixed, 4, mT[:, 0:4, :], 3, ["mixed0", "mixed1"], ["mT"])
        P.op("act", lambda e, par=par: e.activation(out=osq, in_=oT[par], func=AF.Square), reads=[("oT", par)], writes=["osq"])
        pss = C.banks[2][:, 0:1]
        for k in range(4):
            P.op("pe", lambda e, k=k: e.matmul(pss, lhsT=osq[:, k, :], rhs=onesF, start=(k == 0), stop=(k == 3)),
                 reads=["osq", "ones"], writes=[("bank", 2)])
        c_q1 = stcol()
        P.op("dve", lambda e, c_q1=c_q1: e.tensor_copy(out=st[:, c_q1:c_q1 + 1], in_=pss), reads=[("bank", 2)], writes=[("st", c_q1)])
        c_r1 = rstd_col(c_q1, 1, 1.0 / 512)
        P.op("dve", lambda e, par=par: e.tensor_tensor(out=og, in0=oT[par], in1=gbrT[:, 0:4].unsqueeze(2).to_broadcast([128, 4, 128]),
                                                      op=ALU.mult), reads=[("oT", par), "gbrT"], writes=["og"])
        for n in range(2):
            p1 = C.banks[4 + n]
            p2 = C.banks[6 + n]
            for k in range(4):
                P.op("pe", lambda e, k=k, n=n, p1=p1: e.matmul(p1, lhsT=og[:, k, :], rhs=wout[:, k, n * 512:(n + 1) * 512],
                                                               start=(k == 0), stop=(k == 3)),
                     reads=["og", ("wout", k)], writes=[("bank", 4 + n)])
            for k in range(4):
                P.op("pe", lambda e, k=k, n=n, p2=p2: e.matmul(p2, lhsT=mT[:, k, :], rhs=wout[:, 4 + k, n * 512:(n + 1) * 512],
                                                               start=(k == 0), stop=(k == 3)),
                     reads=["mT", ("wout", 4 + k)], writes=[("bank", 6 + n)])
            xs = xres[:, t, n * 512:(n + 1) * 512]
            P.op("dve", lambda e, xs=xs, p2=p2: e.tensor_tensor(out=xs, in0=xs, in1=p2, op=ALU.add),
                 reads=[("bank", 6 + n), ("x", t)], writes=[("x", t)])
            P.op("dve", lambda e, xs=xs, p1=p1, c_r1=c_r1: e.scalar_tensor_tensor(
                out=xs, in0=p1, scalar=st[:, c_r1:c_r1 + 1], in1=xs, op0=ALU.mult, op1=ALU.add),
                reads=[("bank", 4 + n), ("st", c_r1), ("x", t)], writes=[("x", t)])

    P.barrier()
    C.release(s12_mark)
    kTp = C.sb([128, 8, 256], BF16, "C_kTp")
    Vm = C.sb([128, 2, D], BF16, "C_V")
    gxaB = C.sb([128, D], F32, "C_gxa")
    gkq = C.sb([128, 4, 256], F32, "C_gkq")
    qb = C.sb([128, D], BF16, "C_qb")
    Pb = C.sb([128, 4, 256], BF16, "C_Pb")
    PT = C.sb([128, 8, 128], BF16, "C_PT")
    ob = C.sb([128, D], BF16, "C_ob")
    s2_mark = C.mark()
    wk_ = C.sb([128, 8, D], BF16, "C_wk")
    wv_ = C.sb([128, 8, D], BF16, "C_wv")
    gmB = C.sb([128, D], F32, "C_gm")
    memt = C.sb([128, D], F32, "C_memt")
    hmT = C.sb([128, 8, 256], BF16, "C_hmT")
    kf = C.sb([128, D], F32, "C_kf")
    gq4 = C.sb([128, 256], F32, "C_gq4")

    kwk = load_w(wk_, io["xa_wk"], 8, "wk")
    kwv = load_w(wv_, io["xa_wv"], 8, "wv")
    P.op("sp", lambda e: e.dma_start(out=gxaB, in_=bcast(io["norm_xa_g"], D)), writes=["gxa"], dma=True)
    P.op("sp", lambda e: e.dma_start(out=gmB, in_=bcast(io["norm_mem_g"], D)), writes=["gm"], dma=True)
    P.op("sp", lambda e: e.dma_start(out=gkq[:, 0, :], in_=bcast(io["xa_k_g"], 256)), writes=["gkq"], dma=True)
    P.op("sp", lambda e: e.dma_start(out=gq4, in_=bcast(io["xa_q_g"], 256)), writes=["gq4"], dma=True)
    P.op("dve", lambda e: e.scalar_tensor_tensor(out=gkq[:, 0, :], in0=gkq[:, 0, :], scalar=1.0 / 16, in1=gq4,
                                                 op0=ALU.mult, op1=ALU.mult), reads=["gkq", "gq4"], writes=["gkq"])
    for h in range(1, 4):
        P.op("dve", lambda e, h=h: e.tensor_copy(out=gkq[:, h, :], in_=gkq[:, 0, :]), reads=["gkq"], writes=[("gkq", h)])
    gkq_keys = ["gkq"] + [("gkq", h) for h in range(1, 4)]
    for mt in range(2):
        P.op("sp", lambda e, mt=mt: e.dma_start(out=memt, in_=mem[mt * 128:(mt + 1) * 128, :]), writes=["memt"], dma=True)
        norm_T(memt, gmB, "gm", tT[0], 4, ["memt"], ["tT0"])
        P.op("act", lambda e, mt=mt: e.copy(out=hmT[:, :, mt * 128:(mt + 1) * 128], in_=tT[0]), reads=["tT0"], writes=[("hmT", mt)])
        for n in range(2):
            pk = C.banks[n]
            for k in range(8):
                P.op("pe", lambda e, k=k, n=n, pk=pk: e.matmul(pk, lhsT=tT[0][:, k, :], rhs=wk_[:, k, n * 512:(n + 1) * 512],
                                                               start=(k == 0), stop=(k == 7)),
                     reads=["tT0", ("wk", k)], writes=[("bank", n)])
            P.op("act", lambda e, n=n, pk=pk: e.copy(out=kf[:, n * 512:(n + 1) * 512], in_=pk), reads=[("bank", n)], writes=[("kf", n)])
        c_k = stcol(4)
        for h in range(4):
            P.op("act", lambda e, h=h, c_k=c_k: e.activation(out=junk[:, 0:256], in_=kf[:, h * 256:(h + 1) * 256], func=AF.Square,
                                                            accum_out=st[:, c_k + h:c_k + h + 1]),
                 reads=[("kf", h // 2)], writes=["junk", ("st", c_k + h)])
        outc = rstd_col(c_k, 4, 1.0 / 256)
        P.op("dve", lambda e, outc=outc: e.tensor_tensor(
            out=kf.rearrange("p (h d) -> p h d", h=4), in0=kf.rearrange("p (h d) -> p h d", h=4),
            in1=st[:, outc:outc + 4].unsqueeze(2).to_broadcast([128, 4, 256]), op=ALU.mult),
            reads=[("kf", 0), ("kf", 1)] + sk(outc, 4), writes=[("kf", 0), ("kf", 1)])
        P.op("dve", lambda e: e.tensor_tensor(out=hb, in0=kf, in1=gkq.rearrange("p h d -> p (h d)"), op=ALU.mult),
             reads=[("kf", 0), ("kf", 1)] + gkq_keys, writes=["hb"])
        transposes(hb, 8, kTp[:, :, mt * 128:(mt + 1) * 128], 5, ["hb"], [("kTp", mt)])
        for n in range(2):
            pv_ = C.banks[2 + n]
            for k in range(8):
                P.op("pe", lambda e, k=k, n=n, pv_=pv_: e.matmul(pv_, lhsT=tT[0][:, k, :], rhs=wv_[:, k, n * 512:(n + 1) * 512],
                                                                 start=(k == 0), stop=(k == 7)),
                     reads=["tT0", ("wv", k)], writes=[("bank", 2 + n)])
            P.op("act", lambda e, n=n, mt=mt, pv_=pv_: e.copy(out=Vm[:, mt, n * 512:(n + 1) * 512], in_=pv_),
                 reads=[("bank", 2 + n)], writes=[("V", mt)])

    for t in range(NT):
        xt_ = xres[:, t, :]
        norm_T(xt_, gxaB, "gxa", tT[1], 4, [("x", t)], ["tT1"])
        for n in range(2):
            pq = C.banks[n]
            for k in range(8):
                P.op("pe", lambda e, k=k, n=n, pq=pq: e.matmul(pq, lhsT=tT[1][:, k, :], rhs=wq[:, k, n * 512:(n + 1) * 512],
                                                               start=(k == 0), stop=(k == 7)),
                     reads=["tT1", ("wq", k)], writes=[("bank", n)])
        c_q = stcol(4)
        for h in range(4):
            pqh = C.banks[h // 2][:, (h % 2) * 256:(h % 2) * 256 + 256]
            P.op("act", lambda e, h=h, c_q=c_q, pqh=pqh: e.activation(out=junk[:, 0:256], in_=pqh, func=AF.Square,
                                                                    accum_out=st[:, c_q + h:c_q + h + 1]),
                 reads=[("bank", h // 2)], writes=["junk", ("st", c_q + h)])
        rq = rstd_col(c_q, 4, 1.0 / 256)
        for n in range(2):
            P.op("act", lambda e, n=n: e.copy(out=qb[:, n * 512:(n + 1) * 512], in_=C.banks[n]), reads=[("bank", n)], writes=[("qb", n)])
        transposes(qb, 8, tT[0], 5, [("qb", 0), ("qb", 1)], ["tT0"])
        for h in range(4):
            psc = C.banks[2 + h // 2][:, (h % 2) * 256:(h % 2) * 256 + 256]
            for j in range(2):
                P.op("pe", lambda e, h=h, j=j, psc=psc: e.matmul(psc, lhsT=tT[0][:, 2 * h + j, :], rhs=kTp[:, 2 * h + j, :],
                                                                 start=(j == 0), stop=(j == 1)),
                     reads=["tT0", ("kTp", 0), ("kTp", 1)], writes=[("bank", 2 + h // 2)])
        c_mx = stcol(4)
        for b2 in range(2):
            P.op("dve", lambda e, b2=b2, c_mx=c_mx: e.tensor_reduce(
                out=st[:, c_mx + 2 * b2:c_mx + 2 * b2 + 2], in_=C.banks[2 + b2].rearrange("p (h m) -> p h m", h=2),
                axis=AX.X, op=ALU.max), reads=[("bank", 2 + b2)], writes=sk(c_mx + 2 * b2, 2))
        c_nb = stcol(4)
        P.op("dve", lambda e, c_mx=c_mx, c_nb=c_nb, rq=rq: e.scalar_tensor_tensor(
            out=st[:, c_nb:c_nb + 4], in0=st[:, c_mx:c_mx + 4], scalar=-1.0, in1=st[:, rq:rq + 4], op0=ALU.mult, op1=ALU.mult),
            reads=sk(c_mx, 4) + sk(rq, 4), writes=sk(c_nb, 4))
        c_rs = stcol(4)
        for h in range(4):
            psc = C.banks[2 + h // 2][:, (h % 2) * 256:(h % 2) * 256 + 256]
            P.op("act", lambda e, h=h, psc=psc, rq=rq, c_nb=c_nb, c_rs=c_rs: e.activation(
                out=Pb[:, h, :], in_=psc, func=AF.Exp, scale=st[:, rq + h:rq + h + 1], bias=st[:, c_nb + h:c_nb + h + 1],
                accum_out=st[:, c_rs + h:c_rs + h + 1]),
                reads=[("bank", 2 + h // 2), ("st", rq + h), ("st", c_nb + h)], writes=[("Pb", h), ("st", c_rs + h)])
        c_ri = stcol(4)
        P.op("dve", lambda e, c_rs=c_rs, c_ri=c_ri: e.reciprocal(out=st[:, c_ri:c_ri + 4], in_=st[:, c_rs:c_rs + 4]),
             reads=sk(c_rs, 4), writes=sk(c_ri, 4))
        transposes(Pb.rearrange("p h m -> p (h m)"), 8, PT, 6, [("Pb", h) for h in range(4)], ["PT"])
        for h in range(4):
            po_ = C.banks[h // 2][:, (h % 2) * 256:(h % 2) * 256 + 256]
            for mt in range(2):
                P.op("pe", lambda e, h=h, mt=mt, po_=po_: e.matmul(po_, lhsT=PT[:, 2 * h + mt, :], rhs=Vm[:, mt, h * 256:(h + 1) * 256],
                                                                   start=(mt == 0), stop=(mt == 1)),
                     reads=["PT", ("V", mt)], writes=[("bank", h // 2)])
        for n in range(2):
            P.op("dve", lambda e, n=n, c_ri=c_ri: e.tensor_tensor(
                out=ob[:, n * 512:(n + 1) * 512].rearrange("p (h d) -> p h d", h=2),
                in0=C.banks[n].rearrange("p (h d) -> p h d", h=2),
                in1=st[:, c_ri + 2 * n:c_ri + 2 * n + 2].unsqueeze(2).to_broadcast([128, 2, 256]), op=ALU.mult),
                reads=[("bank", n)] + sk(c_ri + 2 * n, 2), writes=[("ob", n)])
        transposes(ob, 8, tT[1], 7, [("ob", 0), ("ob", 1)], ["tT1"])
        for n in range(2):
            pw_ = C.banks[2 + n]
            for k in range(8):
                P.op("pe", lambda e, k=k, n=n, pw_=pw_: e.matmul(pw_, lhsT=tT[1][:, k, :], rhs=wo[:, k, n * 512:(n + 1) * 512],
                                                                 start=(k == 0), stop=(k == 7)),
                     reads=["tT1", ("wo", k)], writes=[("bank", 2 + n)])
            xs = xres[:, t, n * 512:(n + 1) * 512]
            P.op("dve", lambda e, xs=xs, pw_=pw_: e.tensor_tensor(out=xs, in0=xs, in1=pw_, op=ALU.add),
                 reads=[("bank", 2 + n), ("x", t)], writes=[("x", t)])

    P.barrier()
    C.release(base_mark)
    gfB = C.sb([128, D], F32, "C_gf")
    hfT = C.sb([128, 8, 1024], BF16, "C_hfT")
    hid = C.sb([128, 22, 1024], BF16, "C_hid")
    Wo_ = C.sb([128, 22, D], BF16, "C_Wo")
    Wg = [C.sb([128, 8, 128], BF16, "C_Wg%d" % i) for i in range(2)]
    Wu = [C.sb([128, 8, 128], BF16, "C_Wu%d" % i) for i in range(2)]
    sg = [C.sb([128, 512], F32, "C_sg%d" % i) for i in range(2)]
    P.op("sp", lambda e: e.dma_start(out=gfB, in_=bcast(io["norm_ffn_g"], D)), writes=["gf"], dma=True)
    w_in_v = io["ffn_w_in"].rearrange("(k p) n -> p k n", p=128)

    def load_wo(j):
        P.op("pool", lambda e, j=j: e.dma_start(out=Wo_[:, j, :], in_=io["ffn_w_out"][j * 128:(j + 1) * 128, :]),
             writes=[("Wo", j)], dma=True)

    def load_gu(j):
        par = j % 2
        P.op("pool", lambda e, j=j, par=par: e.dma_start(out=Wg[par], in_=w_in_v[:, :, j * 128:(j + 1) * 128]),
             writes=[("Wg", par)], dma=True)
        P.op("pool", lambda e, j=j, par=par: e.dma_start(out=Wu[par], in_=w_in_v[:, :, FFN_H + j * 128:FFN_H + (j + 1) * 128]),
             writes=[("Wu", par)], dma=True)

    nsg = [0]
    for half in range(2):
        for tt in range(8):
            t = half * 8 + tt
            norm_T(xres[:, t, :], gfB, "gf", hfT[:, :, tt * 128:(tt + 1) * 128], 4 + tt % 2, [("x", t)], [("hfT", tt)])
        hkeys = [("hfT", tt) for tt in range(8)]
        load_gu(0)
        for j in range(22):
            par = j % 2
            if j + 1 < 22:
                load_gu(j + 1)
            if half == 0:
                load_wo(j)
            for tc2 in range(2):
                pg_ = C.banks[0 + tc2]
                pu_ = C.banks[2 + tc2]
                for k in range(8):
                    P.op("pe", lambda e, k=k, par=par, tc2=tc2, pg_=pg_: e.matmul(
                        pg_, lhsT=Wg[par][:, k, :], rhs=hfT[:, k, tc2 * 512:(tc2 + 1) * 512], start=(k == 0), stop=(k == 7)),
                        reads=[("Wg", par)] + hkeys[tc2 * 4:tc2 * 4 + 4], writes=[("bank", tc2)])
                for k in range(8):
                    P.op("pe", lambda e, k=k, par=par, tc2=tc2, pu_=pu_: e.matmul(
                        pu_, lhsT=Wu[par][:, k, :], rhs=hfT[:, k, tc2 * 512:(tc2 + 1) * 512], start=(k == 0), stop=(k == 7)),
                        reads=[("Wu", par)] + hkeys[tc2 * 4:tc2 * 4 + 4], writes=[("bank", 2 + tc2)])
                sp_ = nsg[0] % 2
                nsg[0] += 1
                P.op("act", lambda e, sp_=sp_, pg_=pg_: e.activation(out=sg[sp_], in_=pg_, func=AF.Silu),
                     reads=[("bank", tc2)], writes=[("sg", sp_)])
                P.op("dve", lambda e, sp_=sp_, pu_=pu_, j=j, tc2=tc2: e.tensor_tensor(
                    out=hid[:, j, tc2 * 512:(tc2 + 1) * 512], in0=sg[sp_], in1=pu_, op=ALU.mult),
                    reads=[("sg", sp_), ("bank", 2 + tc2)], writes=[("hid", j, tc2)])
        for tt in range(8):
            t = half * 8 + tt
            for n in range(2):
                bk = 4 + (2 * tt + n) % 4
                pd = C.banks[bk]
                for j in range(22):
                    P.op("pe", lambda e, j=j, n=n, tt=tt, pd=pd: e.matmul(
                        pd, lhsT=hid[:, j, tt * 128:(tt + 1) * 128], rhs=Wo_[:, j, n * 512:(n + 1) * 512],
                        start=(j == 0), stop=(j == 21)),
                        reads=[("hid", j, tt // 4), ("Wo", j)], writes=[("bank", bk)])
                xs = xres[:, t, n * 512:(n + 1) * 512]
                P.op("dve", lambda e, xs=xs, pd=pd: e.tensor_tensor(out=xs, in0=xs, in1=pd, op=ALU.add),
                     reads=[("bank", bk), ("x", t)], writes=[("x", t)])
            P.op("sp", lambda e, t=t: e.dma_start(out=x_out[t * 128:(t + 1) * 128, :], in_=xres[:, t, :]),
                 reads=[("x", t)], dma=True)


C_IN = [("x", [TOK, D], F32), ("recv_o", [8, 96, TOK], F32), ("hc_loc", [256, TOK], BF16), ("halo", [256, 32], BF16),
        ("mem", [256, D], F32), ("ident", [128, 128], F32),
        ("conv_dw_wT", [256, 31], F32), ("conv_dw_b", [256, 1], F32), ("conv_ln_g", [1, 256], F32), ("conv_ln_b", [1, 256], F32),
        ("conv_pw2_w", [256, 256], F32), ("ssm_glu_w", [256, 512], F32), ("branch_g", [1, D], F32), ("branch_gT", [128, 8], F32),
        ("w_out", [D, D], F32), ("norm_xa_g", [1, D], F32), ("norm_mem_g", [1, D], F32),
        ("xa_wq", [D, D], F32), ("xa_wk", [D, D], F32), ("xa_wv", [D, D], F32), ("xa_wo", [D, D], F32),
        ("xa_q_g", [1, 256], F32), ("xa_k_g", [1, 256], F32), ("norm_ffn_g", [1, D], F32),
        ("ffn_w_in", [D, 2 * FFN_H], F32), ("ffn_w_out", [FFN_H, D], F32)]


def build_C():
    nc = bass.Bass("TRN2", target_bir_lowering=False)
    es = ExitStack()
    C = Ctx(nc, es)
    io = {n: C.din(n, s, d) for n, s, d in C_IN}
    io["x_out"] = C.dout("x_out", [TOK, D], F32)
    phase_C(C, io)
    C.P.emit(es)
    es.close()
    return nc


_PROGS = {}


def _prog(name, fn):
    if name not in _PROGS:
        _PROGS[name] = fn()
    return _PROGS[name]


def _run(nc, in_maps):
    res = run_bass_kernel_spmd(nc, in_maps, core_ids=list(range(NCORES)))
    return res.results


def _c_weights(w, l):
    f = np.ascontiguousarray
    return dict(
        conv_dw_wT=f(w["conv_dw_w"][l].T), conv_dw_b=f(w["conv_dw_b"][l][:, None]),
        conv_ln_g=f(w["conv_ln_g"][l][None]), conv_ln_b=f(w["conv_ln_b"][l][None]),
        conv_pw2_w=f(w["conv_pw2_w"][l]), ssm_glu_w=f(w["ssm_glu_w"][l]),
        branch_g=f(w["branch_norm_g"][l][None]), branch_gT=f(w["branch_norm_g"][l].reshape(8, 128).T),
        w_out=f(w["w_out"][l]), norm_xa_g=f(w["norm_xa_g"][l][None]), norm_mem_g=f(w["norm_mem_g"][l][None]),
        xa_wq=f(w["xa_wq"][l]), xa_wk=f(w["xa_wk"][l]), xa_wv=f(w["xa_wv"][l]), xa_wo=f(w["xa_wo"][l]),
        xa_q_g=f(w["xa_q_norm_g"][l][None]), xa_k_g=f(w["xa_k_norm_g"][l][None]),
        norm_ffn_g=f(w["norm_ffn_g"][l][None]), ffn_w_in=f(w["ffn_w_in"][l]), ffn_w_out=f(w["ffn_w_out"][l]))


def kernel(**inputs):
    w = {k: np.asarray(v, dtype=np.float32) for k, v in inputs.items()}
    x = w["x"]
    mem = w["mem"]
    bsz, L, _ = x.shape
    assert (bsz, L) == (2, SEQ)
    ident = np.eye(128, dtype=np.float32)
    consts = attn_consts()
    xs = [np.ascontiguousarray(x.reshape(NCORES, TOK, D)[c]) for c in range(NCORES)]
    ncA = _prog("A", build_A)
    ncB = _prog("B", build_B)
    ncC = _prog("C", build_C)
    for l in range(2):
        ra = _run(ncA, [dict(x=xs[c], w_in=w["w_in"][l], g_mix=w["norm_mix_g"][l][None], gq=w["sb_q_norm_g"][l][None],
                             gk=w["sb_k_norm_g"][l][None], ident=ident) for c in range(NCORES)])
        send_bf = [np.asarray(ra[c]["send_bf"]) for c in range(NCORES)]
        send_u = [np.asarray(ra[c]["send_u"]) for c in range(NCORES)]
        hc_loc = [np.asarray(ra[c]["hc_loc"]) for c in range(NCORES)]
        in_b = []
        for j in range(NCORES):
            in_b.append(dict(
                recv_bf=np.ascontiguousarray(np.stack([send_bf[c][j] for c in range(NCORES)])),
                recv_u=np.ascontiguousarray(np.stack([send_u[c][j] for c in range(NCORES)])),
                consts_bf=consts,
                ssm_par=ssm_pack(w["ssm_lam_re"][l], w["ssm_lam_im"][l], w["ssm_log_dt"][l], w["ssm_b_re"][l],
                                 w["ssm_b_im"][l], w["ssm_c_re"][l], w["ssm_c_im"][l], w["ssm_d"][l], j)))
        rb = _run(ncB, in_b)
        send_o = [np.asarray(rb[j]["send_o"]) for j in range(NCORES)]
        cw = _c_weights(w, l)
        in_c = []
        for c in range(NCORES):
            if c % 4 == 0:
                halo = np.zeros((256, 32), dtype=send_bf[0].dtype)
            else:
                halo = np.ascontiguousarray(send_bf[c - 1][0][HALO_OFF:].reshape(256, 32))
            m = dict(x=xs[c], recv_o=np.ascontiguousarray(np.stack([send_o[j][c] for j in range(NCORES)])),
                     hc_loc=hc_loc[c], halo=halo, mem=np.ascontiguousarray(mem[c // 4]), ident=ident)
            m.update(cw)
            in_c.append(m)
        rc = _run(ncC, in_c)
        xs = [np.asarray(rc[c]["x_out"]) for c in range(NCORES)]
    return np.stack(xs).reshape(bsz, L, D).astype(np.float32)
```

```python
import numpy as np
from contextlib import ExitStack
import concourse.bass as bass
import concourse.mybir as mybir
from concourse.bass_utils import run_bass_kernel_spmd

F32 = mybir.dt.float32
BF16 = mybir.dt.bfloat16
AF = mybir.ActivationFunctionType
ALU = mybir.AluOpType
AX = mybir.AxisListType

NCORES = 8
D = 1024
TOK = 2048
NT = TOK // 128
SEQ = 8192
NINP = 2304
FFN_H = 2816
EPS = 1e-6
QKV_SZ = 64 * TOK
HALO_OFF = 3 * QKV_SZ
SB_SZ = HALO_OFF + 256 * 32
NDMA_SEM = 12


class _Op:
    __slots__ = ("eng", "fn", "deps", "dma", "signal", "seq", "dsem", "dval", "qi")

    def __init__(self, eng, fn, dma):
        self.eng = eng
        self.fn = fn
        self.deps = []
        self.dma = dma
        self.signal = False
        self.seq = 0
        self.dsem = None
        self.dval = 0
        self.qi = 0


class Prog:
    ENGS = ("pe", "act", "dve", "pool", "sp")

    def __init__(self, nc):
        self.nc = nc
        self.ops = {e: [] for e in self.ENGS}
        self.last_w = {}
        self.readers = {}
        self.ndma = {e: 0 for e in self.ENGS}
        self.bar = []
        self._pid = {}

    def op(self, eng, fn, reads=(), writes=(), dma=False):
        o = _Op(eng, fn, dma)
        if dma:
            o.qi = self.ndma[eng]
            self.ndma[eng] += 1

        def add(p, kind):
            if p is o:
                return
            if not p.dma and p.eng == eng and not dma:
                if eng == "pe":
                    return
            if p not in o.deps:
                o.deps.append(p)
                p.signal = True

        for p in self.bar:
            add(p, "bar")
        for k in reads:
            p = self.last_w.get(k)
            if p is not None:
                add(p, "raw")
        for k in writes:
            p = self.last_w.get(k)
            if p is not None:
                add(p, "waw")
            for r in self.readers.get(k, ()):
                add(r, "war")
        for k in reads:
            self.readers.setdefault(k, []).append(o)
        for k in writes:
            self.last_w[k] = o
            self.readers[k] = []
        self.ops[eng].append(o)
        return o

    def pid(self, eng):
        k = id(eng)
        if k not in self._pid:
            self._pid[k] = eng.partition_id() % NCORES
        return self._pid[k]

    def barrier(self):
        self.bar = [self.ops[e][-1] for e in self.ENGS if self.ops[e]]
        for e in self.ENGS:
            for o in self.ops[e][-NDMA_SEM:]:
                if o.dma and o not in self.bar:
                    self.bar.append(o)
        self.last_w = {}
        self.readers = {}

    def emit(self, es):
        nc = self.nc
        esem = {e: es.enter_context(nc.semaphore("s_" + e)) for e in self.ENGS}
        dsems = {}
        for e in self.ENGS:
            if self.ndma[e]:
                dsems[e] = [es.enter_context(nc.semaphore("d_%s%d" % (e, i)))
                            for i in range(min(NDMA_SEM, self.ndma[e]))]
        for e in self.ENGS:
            n = 0
            for o in self.ops[e]:
                if o.dma:
                    o.dsem = dsems[e][o.qi % NDMA_SEM]
                    o.dval = 16 * (o.qi // NDMA_SEM + 1)
                elif o.signal:
                    n += 1
                    o.seq = n
        final_waits = []
        for e in self.ENGS:
            if self.ndma[e]:
                for i, s in enumerate(dsems[e]):
                    cnt = len(range(i, self.ndma[e], NDMA_SEM))
                    final_waits.append((s, 16 * cnt))

        def run(e, eng):
            waited = {}

            def wait(sem, val):
                key = id(sem)
                if waited.get(key, 0) >= val:
                    return
                eng.wait_ge(sem, val)
                waited[key] = val

            for o in self.ops[e]:
                for p in o.deps:
                    if p.dma:
                        wait(p.dsem, p.dval)
                    else:
                        wait(esem[p.eng], p.seq)
                if o.dma:
                    if o.qi >= NDMA_SEM:
                        wait(o.dsem, o.dval - 16)
                    o.fn(eng).then_inc(o.dsem, 16)
                else:
                    ins = o.fn(eng)
                    if o.signal:
                        ins.then_inc(esem[e], 1)
            if e == "sp":
                for s, v in final_waits:
                    wait(s, v)

        with nc.Block() as block:
            @block.tensor
            def _(eng):
                run("pe", eng)

            @block.scalar
            def _(eng):
                run("act", eng)

            @block.vector
            def _(eng):
                run("dve", eng)

            @block.gpsimd
            def _(eng):
                run("pool", eng)

            @block.sync
            def _(eng):
                run("sp", eng)


ARENA_F32 = 50688


class Ctx:
    def __init__(self, nc, es):
        self.nc = nc
        self.es = es
        self.P = Prog(nc)
        self.n = 0
        self.banks = [es.enter_context(nc.psum_tensor("bank%d" % i, [128, 512], F32))[:]
                      for i in range(8)]
        self.arena = es.enter_context(nc.sbuf_tensor("arena", [128, ARENA_F32], F32))[:]
        self.off = 0

    def sb(self, shape, dtype, name=None):
        shape = list(shape)
        esz = mybir.dt.size(dtype)
        n = 1
        for d in shape[1:]:
            n *= d
        nf = (n * esz + 31) // 32 * 8
        assert self.off + nf <= ARENA_F32, "SBUF arena overflow (%s)" % name
        v = self.arena[0:shape[0], self.off:self.off + nf]
        self.off += nf
        if dtype != F32:
            v = v.bitcast(dtype)
        v = v[:, 0:n]
        if len(shape) > 2:
            names = " ".join("d%d" % i for i in range(1, len(shape)))
            v = v.rearrange("p (%s) -> p %s" % (names, names),
                            **{"d%d" % i: shape[i] for i in range(1, len(shape))})
        return v

    def mark(self):
        return self.off

    def release(self, m):
        self.off = m

    def din(self, name, shape, dtype):
        return self.nc.dram_tensor(name, list(shape), dtype, kind="ExternalInput").ap()

    def dout(self, name, shape, dtype):
        return self.nc.dram_tensor(name, list(shape), dtype, kind="ExternalOutput").ap()

    def dint(self, name, shape, dtype):
        return self.nc.dram_tensor(name, list(shape), dtype, kind="Internal").ap()


def bank_bf(bank):
    return bank.bitcast(BF16)


def rstd_ops(P, st_in, st_tmp, st_out, inv_n, key_in, key_tmp, key_out):
    kl = lambda k: list(k) if isinstance(k, list) else [k]
    key_in, key_tmp, key_out = kl(key_in), kl(key_tmp), kl(key_out)
    P.op("dve", lambda e: e.tensor_scalar(out=st_tmp, in0=st_in, scalar1=inv_n, scalar2=EPS,
                                          op0=ALU.mult, op1=ALU.add),
         reads=key_in, writes=key_tmp)
    P.op("act", lambda e: e.activation(out=st_tmp, in_=st_tmp, func=AF.Ln),
         reads=key_tmp, writes=key_tmp)
    P.op("act", lambda e: e.activation(out=st_out, in_=st_tmp, func=AF.Exp, scale=-0.5),
         reads=key_tmp, writes=key_out)


def phase_A(C, x, w_in, g_mix, gq, gk, ident, send_bf, send_u, hc_loc):
    P = C.P
    W = C.sb([128, 8, NINP], BF16, "A_W")
    gB = C.sb([128, D], F32, "A_gB")
    gqB = C.sb([128, 8, 64], F32, "A_gqB")
    gkB = C.sb([128, 8, 64], F32, "A_gkB")
    idB = C.sb([128, 128], BF16, "A_idB")
    idF = C.sb([128, 128], F32, "A_idF")
    xt = [C.sb([128, D], F32, "A_xt%d" % i) for i in range(3)]
    junk_ = [C.sb([128, D], BF16, "A_junk%d" % i) for i in range(3)]
    hb = [C.sb([128, D], BF16, "A_hb%d" % i) for i in range(3)]
    hT = [C.sb([128, 8, 128], BF16, "A_hT%d" % i) for i in range(3)]
    st = [C.sb([128, 32], F32, "A_st%d" % i) for i in range(3)]
    qsq_ = [C.sb([128, 512], F32, "A_qsq%d" % i) for i in range(3)]
    qtmp_ = [C.sb([128, 512], F32, "A_qtmp%d" % i) for i in range(3)]
    qn_ = [C.sb([128, 512], BF16, "A_qn%d" % i) for i in range(3)]
    ge_ = [C.sb([128, 256], F32, "A_ge%d" % i) for i in range(3)]
    hcn_ = [C.sb([128, 256], BF16, "A_hcn%d" % i) for i in range(3)]
    uf_ = [C.sb([128, 256], F32, "A_uf%d" % i) for i in range(3)]
    qT_all = C.sb([128, 4, TOK], BF16, "A_qT")
    kT_all = C.sb([128, 4, TOK], BF16, "A_kT")
    v_all = C.sb([128, NT, 512], BF16, "A_v")
    hcT_all = C.sb([128, 2, TOK], BF16, "A_hcT")
    uT_all = C.sb([128, 2, TOK], F32, "A_uT")

    for ci, (c0, cw) in enumerate([(0, 512), (512, 512), (1024, 512), (1536, 512), (2048, 256)]):
        for k in range(8):
            P.op("pool", lambda e, k=k, c0=c0, cw=cw: e.dma_start(out=W[:, k, c0:c0 + cw],
                                                                 in_=w_in[k * 128:(k + 1) * 128, c0:c0 + cw]),
                 writes=[("W", k, ci)], dma=True)
    P.op("sp", lambda e: e.dma_start(out=gB, in_=g_mix.to_broadcast([128, D])), writes=["gB"], dma=True)
    P.op("sp", lambda e: e.dma_start(out=gqB, in_=gq.unsqueeze(1).to_broadcast([128, 8, 64])),
         writes=["gqB"], dma=True)
    P.op("sp", lambda e: e.dma_start(out=gkB, in_=gk.unsqueeze(1).to_broadcast([128, 8, 64])),
         writes=["gkB"], dma=True)
    P.op("pool", lambda e: e.dma_start(out=idB, in_=ident), writes=["idB"], dma=True)
    P.op("sp", lambda e: e.dma_start(out=idF, in_=ident), writes=["idF"], dma=True)
    P.op("dve", lambda e: e.tensor_scalar(out=gqB, in0=gqB, scalar1=0.125, scalar2=None, op0=ALU.mult),
         reads=["gqB"], writes=["gqB"])

    chunks = [(0, 512), (512, 512), (1024, 512), (1536, 512), (2048, 256)]
    nbank = [0]

    def obank():
        i = 3 + (nbank[0] % 3)
        nbank[0] += 1
        return i
    ntb = [0]

    def tbank():
        i = 6 + (ntb[0] % 2)
        ntb[0] += 1
        return i

    def load_x(t):
        p3 = t % 3
        P.op("sp", lambda e: e.dma_start(out=xt[p3], in_=x[t * 128:(t + 1) * 128, :]),
             writes=[("xt", p3)], dma=True)

    def tile_gen(t):
        par = t % 3
        p3 = t % 3
        if t + 2 < NT and t >= 1:
            load_x(t + 2)
        s = st[par]
        junk, qsq, qtmp, qn, ge, hcn, uf = junk_[par], qsq_[par], qtmp_[par], qn_[par], ge_[par], hcn_[par], uf_[par]
        kj, kqs, kqt, kqn, kge, khc, kuf = (("junk", par), ("qsq", par), ("qtmp", par), ("qn", par), ("ge", par),
                                            ("hcn", par), ("uf", par))
        P.op("act", lambda e, par=par, s=s: e.activation(out=junk, in_=xt[p3], func=AF.Square,
                                                          accum_out=s[:, 0:1]),
             reads=[("xt", p3)], writes=[kj, ("st", par, 0)])
        rstd_ops(P, s[:, 0:1], s[:, 1:2], s[:, 2:3], 1.0 / D, ("st", par, 0), ("st", par, 1), ("st", par, 2))
        P.op("dve", lambda e, par=par, s=s: e.scalar_tensor_tensor(
            out=hb[par], in0=xt[p3], scalar=s[:, 2:3], in1=gB, op0=ALU.mult, op1=ALU.mult),
            reads=[("xt", p3), ("st", par, 2), "gB"], writes=[("hb", par)])
        pT = bank_bf(C.banks[par]).rearrange("p (k t) -> p k t", k=8)
        for k in range(8):
            P.op("pe", lambda e, k=k, pT=pT, par=par: e.transpose(
                out=pT[:, k, :], in_=hb[par][:, k * 128:(k + 1) * 128], identity=idB),
                reads=[("hb", par), "idB"], writes=[("bank", par)])
        P.op("act", lambda e, pT=pT, par=par: e.copy(out=hT[par], in_=pT),
             reads=[("bank", par)], writes=[("hT", par)])
        yield
        for ci, (c0, cw) in enumerate(chunks):
            bi = obank()
            pO = C.banks[bi][:, 0:cw]
            for k in range(8):
                P.op("pe", lambda e, k=k, pO=pO, par=par, c0=c0, cw=cw: e.matmul(
                    pO, lhsT=hT[par][:, k, :], rhs=W[:, k, c0:c0 + cw], start=(k == 0), stop=(k == 7)),
                    reads=[("hT", par), ("W", k, ci)], writes=[("bank", bi)])
            bk = ("bank", bi)
            yield
            if ci in (0, 1):
                dstT = qT_all if ci == 0 else kT_all
                gBt = gqB if ci == 0 else gkB
                gkey = "gqB" if ci == 0 else "gkB"
                P.op("act", lambda e, pO=pO: e.activation(out=qsq, in_=pO, func=AF.Square),
                     reads=[bk], writes=[kqs])
                P.op("dve", lambda e, s=s: e.tensor_reduce(
                    out=s[:, 8:16], in_=qsq.rearrange("p (h d) -> p h d", h=8), axis=AX.X, op=ALU.add),
                    reads=[kqs], writes=[("st", par, 3)])
                rstd_ops(P, s[:, 8:16], s[:, 16:24], s[:, 24:32], 1.0 / 64,
                         ("st", par, 3), ("st", par, 4), ("st", par, 5))
                P.op("dve", lambda e, pO=pO, s=s: e.tensor_tensor(
                    out=qtmp.rearrange("p (h d) -> p h d", h=8),
                    in0=pO.rearrange("p (h d) -> p h d", h=8),
                    in1=s[:, 24:32].unsqueeze(2).to_broadcast([128, 8, 64]), op=ALU.mult),
                    reads=[bk, ("st", par, 5)], writes=[kqt])
                P.op("dve", lambda e, gBt=gBt: e.tensor_tensor(
                    out=qn, in0=qtmp, in1=gBt.rearrange("p h d -> p (h d)"), op=ALU.mult),
                    reads=[kqt, gkey], writes=[kqn])
                tb = tbank()
                pQ = bank_bf(C.banks[tb])[:, 0:512].rearrange("p (j t) -> p j t", j=4)
                for j in range(4):
                    P.op("pe", lambda e, j=j, pQ=pQ: e.transpose(
                        out=pQ[:, j, :], in_=qn[:, j * 128:(j + 1) * 128], identity=idB),
                        reads=[kqn, "idB"], writes=[("bank", tb)])
                P.op("act", lambda e, pQ=pQ, dstT=dstT, t=t: e.copy(
                    out=dstT[:, :, t * 128:(t + 1) * 128], in_=pQ),
                    reads=[("bank", tb)], writes=[("qkT", ci, t)])
            elif ci == 2:
                P.op("act", lambda e, pO=pO, t=t: e.copy(out=v_all[:, t, :], in_=pO),
                     reads=[bk], writes=[("v", t)])
            elif ci == 3:
                P.op("act", lambda e, pO=pO: e.activation(out=ge, in_=pO[:, 256:512], func=AF.Exp, scale=-1.0),
                     reads=[bk], writes=[kge])
                P.op("dve", lambda e: e.tensor_scalar(out=ge, in0=ge, scalar1=1.0, scalar2=None, op0=ALU.add),
                     reads=[kge], writes=[kge])
                P.op("dve", lambda e: e.reciprocal(out=ge, in_=ge), reads=[kge], writes=[kge])
                P.op("dve", lambda e, pO=pO: e.tensor_tensor(out=hcn, in0=pO[:, 0:256], in1=ge, op=ALU.mult),
                     reads=[bk, kge], writes=[khc])
                tb = tbank()
                pQ = bank_bf(C.banks[tb])[:, 0:256].rearrange("p (j t) -> p j t", j=2)
                for j in range(2):
                    P.op("pe", lambda e, j=j, pQ=pQ: e.transpose(
                        out=pQ[:, j, :], in_=hcn[:, j * 128:(j + 1) * 128], identity=idB),
                        reads=[khc, "idB"], writes=[("bank", tb)])
                P.op("act", lambda e, pQ=pQ, t=t: e.copy(out=hcT_all[:, :, t * 128:(t + 1) * 128], in_=pQ),
                     reads=[("bank", tb)], writes=[("hcT", t)])
            else:
                P.op("act", lambda e, pO=pO: e.copy(out=uf, in_=pO), reads=[bk], writes=[kuf])
                tb = tbank()
                pQ = C.banks[tb][:, 0:256].rearrange("p (j t) -> p j t", j=2)
                for j in range(2):
                    P.op("pe", lambda e, j=j, pQ=pQ: e.transpose(
                        out=pQ[:, j, :], in_=uf[:, j * 128:(j + 1) * 128], identity=idF),
                        reads=[kuf, "idF"], writes=[("bank", tb)])
                P.op("act", lambda e, pQ=pQ, t=t: e.copy(out=uT_all[:, :, t * 128:(t + 1) * 128], in_=pQ),
                     reads=[("bank", tb)], writes=[("uT", t)])
        yield

    load_x(0)
    load_x(1)
    load_x(2)
    pending = list(range(NT))
    active = []
    while pending or active:
        while pending and len(active) < 3:
            active.append(tile_gen(pending.pop(0)))
        for g in list(active):
            try:
                next(g)
            except StopIteration:
                active.remove(g)

    allq = [("qkT", 0, t) for t in range(NT)]
    allk = [("qkT", 1, t) for t in range(NT)]
    allv = [("v", t) for t in range(NT)]
    allh = [("hcT", t) for t in range(NT)]
    allu = [("uT", t) for t in range(NT)]
    for j in range(8):
        r0 = (j % 2) * 64
        P.op("sp", lambda e, j=j, r0=r0: e.dma_start(
            out=send_bf[j, 0:QKV_SZ].rearrange("(d t) -> d t", t=TOK), in_=qT_all[r0:r0 + 64, j // 2, :]),
            reads=allq, dma=True)
        P.op("sp", lambda e, j=j, r0=r0: e.dma_start(
            out=send_bf[j, QKV_SZ:2 * QKV_SZ].rearrange("(d t) -> d t", t=TOK), in_=kT_all[r0:r0 + 64, j // 2, :]),
            reads=allk, dma=True)
        P.op("sp", lambda e, j=j: e.dma_start(
            out=send_bf[j, 2 * QKV_SZ:3 * QKV_SZ].rearrange("(n p d) -> p n d", p=128, d=64),
            in_=v_all[:, :, j * 64:(j + 1) * 64]), reads=allv, dma=True)
        P.op("sp", lambda e, j=j: e.dma_start(
            out=send_bf[j, HALO_OFF:SB_SZ].rearrange("(c p t) -> p c t", p=128, t=32),
            in_=hcT_all[:, :, TOK - 32:TOK]), reads=allh, dma=True)
        u0 = (j % 4) * 32
        P.op("sp", lambda e, j=j, u0=u0: e.dma_start(out=send_u[j], in_=uT_all[u0:u0 + 32, j // 4, :]),
             reads=allu, dma=True)
    P.op("sp", lambda e: e.dma_start(out=hc_loc.rearrange("(c p) t -> p c t", p=128), in_=hcT_all),
         reads=allh, dma=True)


def build_A():
    nc = bass.Bass("TRN2", target_bir_lowering=False)
    es = ExitStack()
    C = Ctx(nc, es)
    x = C.din("x", [TOK, D], F32)
    w_in = C.din("w_in", [D, NINP], F32)
    g_mix = C.din("g_mix", [1, D], F32)
    gq = C.din("gq", [1, 64], F32)
    gk = C.din("gk", [1, 64], F32)
    ident = C.din("ident", [128, 128], F32)
    send_bf = C.dout("send_bf", [8, SB_SZ], BF16)
    send_u = C.dout("send_u", [8, 32, TOK], F32)
    hc_loc = C.dout("hc_loc", [256, TOK], BF16)
    phase_A(C, x, w_in, g_mix, gq, gk, ident, send_bf, send_u, hc_loc)
    C.P.emit(es)
    es.close()
    return nc


def phase_B_attn(C, recv_bf, consts_bf, send_o, side=None, side_rate=1):
    P = C.P
    qT = C.sb([64, 2 * SEQ], BF16, "B_qT")
    kT = C.sb([64, 2 * SEQ], BF16, "B_kT")
    v = C.sb([128, 128, 64], BF16, "B_v")
    cst = C.sb([128, 256 + 4 * 512], BF16, "B_cst")
    negtri = cst[:, 0:128]
    negones = cst[:, 128:256]
    mask = cst[:, 256:].rearrange("p (i q) -> p i q", i=4)
    eb = [C.sb([128, 512], F32, "B_e%d" % i) for i in range(2)]
    spb = [C.sb([128, 512], BF16, "B_sp%d" % i) for i in range(3)]
    wb = [C.sb([128, 512], BF16, "B_w%d" % i) for i in range(3)]
    ssb = [C.sb([128, 512], BF16, "B_ss%d" % i) for i in range(3)]
    ost = [C.sb([64, 512], F32, "B_ost%d" % i) for i in range(2)]

    P.op("pool", lambda e: e.dma_start(out=cst, in_=consts_bf), writes=["cst"], dma=True)
    for c in range(8):
        P.op("sp", lambda e, c=c: e.dma_start(
            out=kT[:, c * TOK:(c + 1) * TOK], in_=recv_bf[c, QKV_SZ:2 * QKV_SZ].rearrange("(d t) -> d t", t=TOK)),
            writes=[("kT", c)], dma=True)
        P.op("sp", lambda e, c=c: e.dma_start(
            out=qT[:, c * TOK:(c + 1) * TOK], in_=recv_bf[c, 0:QKV_SZ].rearrange("(d t) -> d t", t=TOK)),
            writes=[("qT", c)], dma=True)
        P.op("sp", lambda e, c=c: e.dma_start(
            out=v[:, c * 16:(c + 1) * 16, :],
            in_=recv_bf[c, 2 * QKV_SZ:3 * QKV_SZ].rearrange("(n p d) -> p n d", p=128, d=64)),
            writes=[("v", c)], dma=True)

    units = []
    for b in range(2):
        for qc in range(16):
            kbs = [(4 * qc + i, i) for i in (3, 2, 1, 0)] + [(kb, None) for kb in range(4 * qc - 1, -1, -1)]
            for n, (kb, di) in enumerate(kbs):
                units.append(dict(idx=len(units), b=b, qc=qc, kb=kb, diag=di, first=(n == 0),
                                  last=(n == len(kbs) - 1)))

    def unit_cols(u):
        kc = u["b"] * SEQ + u["kb"] * 128
        qc0 = u["b"] * SEQ + u["qc"] * 512
        return kc, qc0

    def emit_Z(u):
        i = u["idx"]
        kc, qc0 = unit_cols(u)
        zb = i % 4
        Z = C.banks[zb]
        P.op("pe", lambda e: e.matmul(Z, lhsT=kT[:, kc:kc + 128], rhs=qT[:, qc0:qc0 + 512], start=True, stop=True),
             reads=[("kT", kc // TOK), ("qT", qc0 // TOK)], writes=[("bank", zb)])

    def emit_expZ(u):
        i = u["idx"]
        zb = i % 4
        Z = C.banks[zb]
        ee = eb[i % 2]
        P.op("act", lambda e: e.activation(out=ee, in_=Z, func=AF.Exp), reads=[("bank", zb)], writes=[("e", i % 2)])

    def emit_ln(u):
        i = u["idx"]
        ee = eb[i % 2]
        sp = spb[i % 3]
        P.op("act", lambda e: e.activation(out=sp, in_=ee, func=AF.Ln, bias=1.0),
             reads=[("e", i % 2)], writes=[("sp", i % 3)])
        if u["diag"] is not None:
            m = mask[:, u["diag"], :]
            P.op("dve", lambda e: e.tensor_tensor(out=sp, in0=sp, in1=m, op=ALU.mult),
                 reads=[("sp", i % 3), "cst"], writes=[("sp", i % 3)])
        ss = ssb[i % 3]
        if u["first"]:
            P.op("dve", lambda e: e.tensor_copy(out=ss, in_=sp), reads=[("sp", i % 3)], writes=[("ss", i % 3)])
        else:
            sprev = ssb[(i - 1) % 3]
            P.op("dve", lambda e: e.tensor_tensor(out=ss, in0=sprev, in1=sp, op=ALU.add),
                 reads=[("sp", i % 3), ("ss", (i - 1) % 3)], writes=[("ss", i % 3)])

    def emit_W(u):
        i = u["idx"]
        kc, qc0 = unit_cols(u)
        wbk = i % 4
        Wp = C.banks[wbk]
        sp = spb[i % 3]
        first = u["first"]
        P.op("pe", lambda e: e.matmul(Wp, lhsT=negtri, rhs=sp, start=False, stop=first, skip_group_check=True),
             reads=["cst", ("sp", i % 3)], writes=[("bank", wbk)])
        if not first:
            sprev = ssb[(i - 1) % 3]
            P.op("pe", lambda e: e.matmul(Wp, lhsT=negones, rhs=sprev, start=False, stop=True, skip_group_check=True),
                 reads=["cst", ("ss", (i - 1) % 3)], writes=[("bank", wbk)])

    def emit_expW(u):
        i = u["idx"]
        wbk = i % 4
        Wp = C.banks[wbk]
        w = wb[i % 3]
        P.op("act", lambda e: e.activation(out=w, in_=Wp, func=AF.Exp), reads=[("bank", wbk)], writes=[("w", i % 3)])
        if u["diag"] is not None:
            m = mask[:, u["diag"], :]
            P.op("dve", lambda e: e.tensor_tensor(out=w, in0=w, in1=m, op=ALU.mult),
                 reads=[("w", i % 3), "cst"], writes=[("w", i % 3)])

    def emit_PV(u):
        i = u["idx"]
        qc = u["qc"]
        G = u["b"] * 64 + u["kb"]
        ob = 4 + qc % 2
        O = C.banks[ob][0:64, :]
        w = wb[i % 3]
        P.op("pe", lambda e: e.matmul(O, lhsT=v[:, G, :], rhs=w, start=u["first"], stop=u["last"]),
             reads=[("v", G // 16), ("w", i % 3)], writes=[("bank", ob)])
        if u["last"]:
            os_ = ost[qc % 2]
            P.op("dve", lambda e: e.tensor_copy(out=os_, in_=O), reads=[("bank", ob)], writes=[("ost", qc % 2)])
            dest = u["b"] * 4 + qc // 4
            c0 = (qc % 4) * 512
            P.op("sp", lambda e: e.dma_start(out=send_o[dest, 0:64, c0:c0 + 512], in_=os_),
                 reads=[("ost", qc % 2)], dma=True)

    n = len(units)
    U = lambda i: units[i] if 0 <= i < n else None
    emit_Z(units[0])
    emit_Z(units[1])
    emit_expZ(units[0])
    for s in range(n + 2):
        if U(s + 2):
            emit_Z(U(s + 2))
        if U(s + 1):
            emit_expZ(U(s + 1))
        if U(s):
            emit_ln(U(s))
        if U(s - 1):
            emit_W(U(s - 1))
        if U(s - 2):
            emit_PV(U(s - 2))
        if U(s - 1):
            emit_expW(U(s - 1))
        if side is not None and s % 6 != 5:
            for _ in range(side_rate):
                next(side, None)
    if side is not None:
        for _ in side:
            pass


def attn_consts():
    k = np.arange(128)
    negtri = -(k[:, None] >= k[None, :]).astype(np.float32)
    negones = -np.ones((128, 128), np.float32)
    q = np.arange(512)
    mask = np.stack([(128 * i + k[:, None] < q[None, :]).astype(np.float32) for i in range(4)], 1)
    return np.concatenate([negtri, negones, mask.reshape(128, 2048)], 1)


def build_B(with_ssm=True):
    nc = bass.Bass("TRN2", target_bir_lowering=False)
    es = ExitStack()
    C = Ctx(nc, es)
    recv_bf = C.din("recv_bf", [8, SB_SZ], BF16)
    consts_bf = C.din("consts_bf", [128, 256 + 2048], F32)
    send_o = C.dout("send_o", [8, 96, TOK], F32)
    side = None
    if with_ssm:
        ssm_io = ssm_decl(C)
        side = phase_B_ssm(C, ssm_io, send_o)
    phase_B_attn(C, recv_bf, consts_bf, send_o, side=side)
    C.P.emit(es)
    es.close()
    return nc


import math

SSM_NP = 6 + 1 + 32 + 32 + 32 + 2 + 128 + 128
PC_VEC, PC_SGN, PC_BRI, PC_BIR, PC_CT, PC_D, PC_I, PC_SW = 0, 6, 7, 39, 71, 103, 105, 233
NLEV = 13


def ssm_decl(C):
    return dict(recv_u=C.din("recv_u", [8, 32, TOK], F32), ssm_par=C.din("ssm_par", [128, SSM_NP], F32))


def ssm_pack(lam_re, lam_im, log_dt, b_re, b_im, c_re, c_im, d, core):
    par = np.zeros((128, SSM_NP), np.float32)
    for gi in range(2):
        g = 2 * core + gi
        par[:, PC_VEC + 3 * gi + 0] = np.concatenate([lam_re[g], lam_re[g]])
        par[:, PC_VEC + 3 * gi + 1] = np.concatenate([lam_im[g], lam_im[g]])
        par[:, PC_VEC + 3 * gi + 2] = log_dt[g]
        par[:, PC_BRI + 16 * gi:PC_BRI + 16 * gi + 16] = np.concatenate([b_re[g], b_im[g]], 0)
        par[:, PC_BIR + 16 * gi:PC_BIR + 16 * gi + 16] = np.concatenate([b_im[g], b_re[g]], 0)
        par[:, PC_CT + 16 * gi:PC_CT + 16 * gi + 16] = np.concatenate([c_re[g].T, c_im[g].T], 0)
        par[16 * gi:16 * gi + 16, PC_D + gi] = d[16 * g:16 * g + 16]
    par[:64, PC_SGN] = 1.0
    par[64:, PC_SGN] = -1.0
    par[:, PC_I:PC_I + 128] = np.eye(128, dtype=np.float32)
    par[:, PC_SW:PC_SW + 128] = np.roll(np.eye(128, dtype=np.float32), 64, axis=1)
    return par


def phase_B_ssm(C, io, send_o):
    P = C.P
    recv_u, ssm_par = io["recv_u"], io["ssm_par"]
    par = C.sb([128, SSM_NP], F32, "S_par")
    sc = C.sb([128, 128], F32, "S_sc")
    sci = C.sb([128, 4], mybir.dt.int32, "S_sci")
    AT = C.sb([128, 2, NLEV, 128], F32, "S_AT")
    Bpad = C.sb([128, 2, 32], F32, "S_Bpad")
    Cpad = C.sb([128, 2, 32], F32, "S_Cpad")
    lB = C.sb([32, 2, 128], F32, "S_lB")
    X = C.sb([128, SEQ], F32, "S_X")
    ub = C.sb([32, SEQ], F32, "S_u")
    yst = [C.sb([32, 512], F32, "S_y%d" % i) for i in range(2)]
    I128 = par[:, PC_I:PC_I + 128]
    SW = par[:, PC_SW:PC_SW + 128]
    sgn = par[:, PC_SGN:PC_SGN + 1]
    TWO_PI = 2.0 * math.pi

    P.op("sp", lambda e: e.dma_start(out=par, in_=ssm_par), writes=["par"], dma=True)
    P.op("dve", lambda e: e.memset(Bpad, 0.0), writes=["Bpad"])
    P.op("dve", lambda e: e.memset(Cpad, 0.0), writes=["Cpad"])

    ncol = [0]

    def col():
        ncol[0] += 1
        assert ncol[0] <= 128
        return ncol[0] - 1

    def cs(i):
        return sc[:, i:i + 1]

    def K(i):
        return ("sc", i)

    def ts(o, a, s1, s2, op0, op1=None, extra=()):
        if op1 is None:
            P.op("dve", lambda e: e.tensor_scalar(out=cs(o), in0=cs(a), scalar1=s1, scalar2=None, op0=op0),
                 reads=[K(a)] + list(extra), writes=[K(o)])
        else:
            P.op("dve", lambda e: e.tensor_scalar(out=cs(o), in0=cs(a), scalar1=s1, scalar2=s2, op0=op0, op1=op1),
                 reads=[K(a)] + list(extra), writes=[K(o)])

    def tt(o, a, b, op):
        P.op("dve", lambda e: e.tensor_tensor(out=cs(o), in0=cs(a), in1=cs(b), op=op),
             reads=[K(a), K(b)], writes=[K(o)])

    for gi in range(2):
        ncol[0] = 0
        lr = par[:, PC_VEC + 3 * gi:PC_VEC + 3 * gi + 1]
        li = par[:, PC_VEC + 3 * gi + 1:PC_VEC + 3 * gi + 2]
        ldt = par[:, PC_VEC + 3 * gi + 2:PC_VEC + 3 * gi + 3]
        c_dt, c_mag, c_th = col(), col(), col()
        P.op("act", lambda e, c_dt=c_dt, ldt=ldt: e.activation(out=cs(c_dt), in_=ldt, func=AF.Exp),
             reads=["par"], writes=[K(c_dt)])
        P.op("act", lambda e, c_mag=c_mag, c_dt=c_dt, lr=lr: e.activation(out=cs(c_mag), in_=lr, func=AF.Exp, scale=cs(c_dt)),
             reads=["par", K(c_dt)], writes=[K(c_mag)])
        P.op("dve", lambda e, c_th=c_th, c_dt=c_dt, li=li: e.tensor_tensor(out=cs(c_th), in0=li, in1=cs(c_dt), op=ALU.mult),
             reads=["par", K(c_dt)], writes=[K(c_th)])
        trig = []
        for shift in (0.5 * math.pi, 0.0):
            c_a, c_y, c_n, c_r, c_w, c_o = col(), col(), col(), col(), col(), col()
            ts(c_a, c_th, shift, None, ALU.add)
            ts(c_y, c_a, 1.0 / TWO_PI, None, ALU.mult)
            ii = 0 if shift else 1
            P.op("dve", lambda e, ii=ii, c_y=c_y: e.tensor_copy(out=sci[:, ii:ii + 1], in_=cs(c_y)),
                 reads=[K(c_y)], writes=[("sci", ii)])
            P.op("dve", lambda e, ii=ii, c_n=c_n: e.tensor_copy(out=cs(c_n), in_=sci[:, ii:ii + 1]),
                 reads=[("sci", ii)], writes=[K(c_n)])
            P.op("dve", lambda e, c_r=c_r, c_n=c_n, c_a=c_a: e.scalar_tensor_tensor(
                out=cs(c_r), in0=cs(c_n), scalar=-TWO_PI, in1=cs(c_a), op0=ALU.mult, op1=ALU.add),
                reads=[K(c_n), K(c_a)], writes=[K(c_r)])
            ts(c_w, c_r, math.pi, -TWO_PI, ALU.is_gt, ALU.mult)
            tt(c_r, c_r, c_w, ALU.add)
            P.op("act", lambda e, c_o=c_o, c_r=c_r: e.activation(out=cs(c_o), in_=cs(c_r), func=AF.Sin),
                 reads=[K(c_r)], writes=[K(c_o)])
            trig.append(c_o)
        c_ar, c_ai = col(), col()
        tt(c_ar, c_mag, trig[0], ALU.mult)
        tt(c_ai, c_mag, trig[1], ALU.mult)
        c_l2, c_i2, c_den, c_am1, c_t1, c_t2, c_fr, c_fi, c_f2 = [col() for _ in range(9)]
        P.op("dve", lambda e, lr=lr, c_l2=c_l2: e.tensor_tensor(out=cs(c_l2), in0=lr, in1=lr, op=ALU.mult),
             reads=["par"], writes=[K(c_l2)])
        P.op("dve", lambda e, li=li, c_i2=c_i2: e.tensor_tensor(out=cs(c_i2), in0=li, in1=li, op=ALU.mult),
             reads=["par"], writes=[K(c_i2)])
        tt(c_den, c_l2, c_i2, ALU.add)
        P.op("dve", lambda e, c_den=c_den: e.reciprocal(out=cs(c_den), in_=cs(c_den)), reads=[K(c_den)], writes=[K(c_den)])
        ts(c_am1, c_ar, -1.0, None, ALU.add)
        P.op("dve", lambda e, lr=lr, c_t1=c_t1, c_am1=c_am1: e.tensor_tensor(out=cs(c_t1), in0=cs(c_am1), in1=lr, op=ALU.mult),
             reads=["par", K(c_am1)], writes=[K(c_t1)])
        P.op("dve", lambda e, li=li, c_t2=c_t2, c_ai=c_ai: e.tensor_tensor(out=cs(c_t2), in0=cs(c_ai), in1=li, op=ALU.mult),
             reads=["par", K(c_ai)], writes=[K(c_t2)])
        tt(c_fr, c_t1, c_t2, ALU.add)
        tt(c_fr, c_fr, c_den, ALU.mult)
        P.op("dve", lambda e, lr=lr, c_t1=c_t1, c_ai=c_ai: e.tensor_tensor(out=cs(c_t1), in0=cs(c_ai), in1=lr, op=ALU.mult),
             reads=["par", K(c_ai)], writes=[K(c_t1)])
        P.op("dve", lambda e, li=li, c_t2=c_t2, c_am1=c_am1: e.tensor_tensor(out=cs(c_t2), in0=cs(c_am1), in1=li, op=ALU.mult),
             reads=["par", K(c_am1)], writes=[K(c_t2)])
        tt(c_fi, c_t1, c_t2, ALU.subtract)
        tt(c_fi, c_fi, c_den, ALU.mult)
        P.op("dve", lambda e, c_f2=c_f2, c_fi=c_fi: e.scalar_tensor_tensor(
            out=cs(c_f2), in0=cs(c_fi), scalar=-1.0, in1=sgn, op0=ALU.mult, op1=ALU.mult),
            reads=[K(c_fi), "par"], writes=[K(c_f2)])
        bsl = Bpad[:, gi, 16 * gi:16 * gi + 16]
        P.op("dve", lambda e, bsl=bsl, gi=gi, c_fr=c_fr: e.tensor_scalar(
            out=bsl, in0=par[:, PC_BRI + 16 * gi:PC_BRI + 16 * gi + 16], scalar1=cs(c_fr), scalar2=None, op0=ALU.mult),
            reads=["par", K(c_fr), "Bpad"], writes=[("Bpad", gi)])
        P.op("dve", lambda e, bsl=bsl, gi=gi, c_f2=c_f2: e.scalar_tensor_tensor(
            out=bsl, in0=par[:, PC_BIR + 16 * gi:PC_BIR + 16 * gi + 16], scalar=cs(c_f2), in1=bsl, op0=ALU.mult, op1=ALU.add),
            reads=["par", K(c_f2), ("Bpad", gi)], writes=[("Bpad", gi)])
        pb = C.banks[6][0:32, 0:128]
        P.op("pe", lambda e, pb=pb, gi=gi: e.transpose(out=pb, in_=Bpad[:, gi, :], identity=I128),
             reads=[("Bpad", gi), "par"], writes=[("bank", 6)])
        P.op("dve", lambda e, pb=pb, gi=gi: e.tensor_copy(out=lB[:, gi, :], in_=pb), reads=[("bank", 6)], writes=[("lB", gi)])
        P.op("dve", lambda e, gi=gi: e.tensor_scalar(
            out=Cpad[:, gi, 16 * gi:16 * gi + 16], in0=par[:, PC_CT + 16 * gi:PC_CT + 16 * gi + 16],
            scalar1=sgn, scalar2=None, op0=ALU.mult), reads=["par", "Cpad"], writes=[("Cpad", gi)])
        c_pr, c_pi = c_ar, c_ai
        for k in range(NLEV):
            c_s2 = col()
            P.op("dve", lambda e, c_s2=c_s2, c_pi=c_pi: e.tensor_tensor(out=cs(c_s2), in0=cs(c_pi), in1=sgn, op=ALU.mult),
                 reads=[K(c_pi), "par"], writes=[K(c_s2)])
            A = AT[:, gi, k, :]
            P.op("dve", lambda e, A=A, c_pr=c_pr: e.tensor_scalar(out=A, in0=I128, scalar1=cs(c_pr), scalar2=None, op0=ALU.mult),
                 reads=["par", K(c_pr)], writes=[("AT", gi, k)])
            P.op("dve", lambda e, A=A, c_s2=c_s2: e.scalar_tensor_tensor(
                out=A, in0=SW, scalar=cs(c_s2), in1=A, op0=ALU.mult, op1=ALU.add),
                reads=["par", K(c_s2), ("AT", gi, k)], writes=[("AT", gi, k)])
            if k + 1 < NLEV:
                c_a2, c_b2, c_nr, c_ni = col(), col(), col(), col()
                tt(c_a2, c_pr, c_pr, ALU.mult)
                tt(c_b2, c_pi, c_pi, ALU.mult)
                tt(c_nr, c_a2, c_b2, ALU.subtract)
                P.op("dve", lambda e, c_ni=c_ni, c_pr=c_pr, c_pi=c_pi: e.scalar_tensor_tensor(
                    out=cs(c_ni), in0=cs(c_pr), scalar=2.0, in1=cs(c_pi), op0=ALU.mult, op1=ALU.mult),
                    reads=[K(c_pr), K(c_pi)], writes=[K(c_ni)])
                c_pr, c_pi = c_nr, c_ni

    def scan_gen():
      if True:
        nb = [0]

        def sbank():
            nb[0] += 1
            return 6 + nb[0] % 2

        for b in range(2):
            for j in range(4):
                P.op("sp", lambda e, b=b, j=j: e.dma_start(out=ub[:, j * TOK:(j + 1) * TOK], in_=recv_u[4 * b + j]),
                     writes=[("ub", j)], dma=True)
            for gi in range(2):
                for ch in range(16):
                    bk = sbank()
                    ps = C.banks[bk]
                    P.op("pe", lambda e, ps=ps, gi=gi, ch=ch: e.matmul(
                        ps, lhsT=lB[:, gi, :], rhs=ub[:, ch * 512:(ch + 1) * 512], start=True, stop=True),
                        reads=[("lB", gi), ("ub", ch // 4)], writes=[("bank", bk)])
                    P.op("dve", lambda e, ps=ps, ch=ch: e.tensor_copy(out=X[:, ch * 512:(ch + 1) * 512], in_=ps),
                         reads=[("bank", bk)], writes=[("X", ch)])
                    yield

                for k in range(NLEV):
                    s_ = 1 << k
                    for ch in range(15, -1, -1):
                        lo = max(512 * ch, s_)
                        hi = 512 * (ch + 1)
                        if lo >= hi:
                            continue
                        n_ = hi - lo
                        bk = sbank()
                        ps = C.banks[bk][:, 0:n_]
                        src = X[:, lo - s_:hi - s_]
                        dst = X[:, lo:hi]
                        rk = sorted(set([("X", (lo - s_) // 512), ("X", (hi - s_ - 1) // 512)]))
                        P.op("pe", lambda e, ps=ps, src=src, k=k, gi=gi: e.matmul(
                            ps, lhsT=AT[:, gi, k, :], rhs=src, start=True, stop=True),
                            reads=[("AT", gi, k)] + rk, writes=[("bank", bk)])
                        P.op("dve", lambda e, ps=ps, dst=dst: e.tensor_tensor(out=dst, in0=dst, in1=ps, op=ALU.add),
                             reads=[("bank", bk), ("X", ch)], writes=[("X", ch)])
                        yield
                for ch in range(16):
                    bk = sbank()
                    ps = C.banks[bk][0:32, :]
                    P.op("pe", lambda e, ps=ps, ch=ch, gi=gi: e.matmul(
                        ps, lhsT=Cpad[:, gi, :], rhs=X[:, ch * 512:(ch + 1) * 512], start=True, stop=True),
                        reads=[("Cpad", gi), ("X", ch)], writes=[("bank", bk)])
                    ys = yst[ch % 2]
                    P.op("dve", lambda e, ps=ps, ch=ch, gi=gi, ys=ys: e.scalar_tensor_tensor(
                        out=ys, in0=ub[:, ch * 512:(ch + 1) * 512], scalar=par[0:32, PC_D + gi:PC_D + gi + 1], in1=ps,
                        op0=ALU.mult, op1=ALU.add),
                        reads=[("bank", bk), ("ub", ch // 4), "par"], writes=[("yst", ch % 2)])
                    dest = 4 * b + ch // 4
                    c0 = (ch % 4) * 512
                    P.op("sp", lambda e, ys=ys, dest=dest, c0=c0, gi=gi: e.dma_start(
                        out=send_o[dest, 64 + 16 * gi:64 + 16 * gi + 16, c0:c0 + 512], in_=ys[16 * gi:16 * gi + 16, :]),
                        reads=[("yst", ch % 2)], dma=True)
                    yield

    return scan_gen()


def phase_C(C, io):
    P = C.P
    x, x_out, recv_o, hc_loc, halo, mem = io["x"], io["x_out"], io["recv_o"], io["hc_loc"], io["halo"], io["mem"]
    bcast = lambda ap, n: ap.to_broadcast([128, n])

    xres = C.sb([128, NT, D], F32, "C_x")
    idB = C.sb([128, 128], BF16, "C_idB")
    idF = C.sb([128, 128], F32, "C_idF")
    onesF = C.sb([128, 1], F32, "C_ones")
    st = C.sb([128, 64], F32, "C_st")
    junk = C.sb([128, D], BF16, "C_junk")
    hb = C.sb([128, D], BF16, "C_hb")
    tT = [C.sb([128, 8, 128], BF16, "C_tT%d" % i) for i in range(2)]
    base_mark = C.mark()
    wq = C.sb([128, 8, D], BF16, "C_wq")
    wo = C.sb([128, 8, D], BF16, "C_wo")
    s12_mark = C.mark()

    for t in range(NT):
        P.op("sp", lambda e, t=t: e.dma_start(out=xres[:, t, :], in_=x[t * 128:(t + 1) * 128, :]),
             writes=[("x", t)], dma=True)
    P.op("pool", lambda e: e.dma_start(out=idB, in_=io["ident"]), writes=["idB"], dma=True)
    P.op("sp", lambda e: e.dma_start(out=idF, in_=io["ident"]), writes=["idF"], dma=True)
    P.op("dve", lambda e: e.memset(onesF, 1.0), writes=["ones"])

    nst = [0]

    def stcol(n=1):
        c = nst[0] % 64
        if c + n > 64:
            c = 0
        nst[0] = c + n
        return c

    def sk(c, n=1):
        return [("st", c + i) for i in range(n)]

    def rstd_col(ss_c, n, inv_n):
        tmp, out = stcol(n), stcol(n)
        rstd_ops(P, st[:, ss_c:ss_c + n], st[:, tmp:tmp + n], st[:, out:out + n], inv_n,
                 sk(ss_c, n), sk(tmp, n), sk(out, n))
        return out

    def load_w(dst, src, nk, key, eng="pool"):
        for k in range(nk):
            P.op(eng, lambda e, k=k: e.dma_start(out=dst[:, k, :], in_=src[k * 128:(k + 1) * 128, :]),
                 writes=[(key, k)], dma=True)
        return [(key, k) for k in range(nk)]

    def transposes(src, n, dst, bank, rk, wk, ident=None, f32=False, evac="act"):
        if f32:
            pv = C.banks[bank][:, 0:n * 128].rearrange("p (i t) -> p i t", i=n)
        else:
            pv = bank_bf(C.banks[bank])[:, 0:n * 128].rearrange("p (i t) -> p i t", i=n)
        idt, idk = (idF, "idF") if f32 else (idB, "idB")
        for i in range(n):
            P.op("pe", lambda e, i=i: e.transpose(out=pv[:, i, :], in_=src[:, i * 128:(i + 1) * 128], identity=idt),
                 reads=list(rk) + [idk], writes=[("bank", bank)])
        if evac == "act":
            P.op("act", lambda e: e.copy(out=dst, in_=pv), reads=[("bank", bank)], writes=list(wk))
        else:
            P.op("dve", lambda e: e.tensor_copy(out=dst, in_=pv), reads=[("bank", bank)], writes=list(wk))

    def norm_T(src, gB, gkey, dstT, bank, rk, wk, jb=None):
        junk_, kjunk, hb_, khb = jb if jb is not None else (junk, "junk", hb, "hb")
        c = stcol()
        P.op("act", lambda e: e.activation(out=junk_, in_=src, func=AF.Square, accum_out=st[:, c:c + 1]),
             reads=list(rk), writes=[kjunk, ("st", c)])
        r = rstd_col(c, 1, 1.0 / D)
        P.op("dve", lambda e: e.scalar_tensor_tensor(out=hb_, in0=src, scalar=st[:, r:r + 1], in1=gB,
                                                     op0=ALU.mult, op1=ALU.mult),
             reads=list(rk) + [("st", r), gkey], writes=[khb])
        transposes(hb_, 8, dstT, bank, [khb], wk)

    hcT = C.sb([128, 2, 32 + TOK], BF16, "C_hcT")
    Dg = C.sb([128, 2, 31, 128], BF16, "C_Dg")
    cw = C.sb([128, 2, 36], F32, "C_cw")
    cT = C.sb([128, 2, TOK], F32, "C_cT")
    lngB = C.sb([128, 256], F32, "C_lng")
    lnbB = C.sb([128, 256], F32, "C_lnb")
    gbrB = C.sb([128, 512], F32, "C_gbrB")
    gbrT = C.sb([128, 8], F32, "C_gbrT")
    pw2 = C.sb([128, 2, 256], BF16, "C_pw2")
    gluw = C.sb([128, 2, 512], BF16, "C_gluw")
    wout = C.sb([128, 8, D], BF16, "C_wout")
    yT = C.sb([128, 2, TOK], BF16, "C_yT")
    oT = [C.sb([128, 4, 128], F32, "C_oT%d" % i) for i in range(2)]
    osq = C.sb([128, 4, 128], F32, "C_osq")
    og = C.sb([128, 4, 128], BF16, "C_og")
    yn = C.sb([128, 256], F32, "C_yn")
    ge = C.sb([128, 256], F32, "C_ge")
    sw = C.sb([128, 256], BF16, "C_sw")
    swT = C.sb([128, 2, 128], BF16, "C_swT")
    osm = C.sb([128, 256], F32, "C_osm")
    mixed = C.sb([128, 512], BF16, "C_mixed")

    P.op("sp", lambda e: e.dma_start(out=hcT[:, :, 0:32], in_=halo.rearrange("(c p) t -> p c t", p=128)),
         writes=["hcT_h"], dma=True)
    P.op("sp", lambda e: e.dma_start(out=hcT[:, :, 32:], in_=hc_loc.rearrange("(c p) t -> p c t", p=128)),
         writes=["hcT"], dma=True)
    P.op("sp", lambda e: e.dma_start(out=cw[:, :, 0:31], in_=io["conv_dw_wT"].rearrange("(c p) t -> p c t", p=128)),
         writes=["cw"], dma=True)
    for ct in range(2):
        P.op("sp", lambda e, ct=ct: e.dma_start(out=cw[:, ct, 31:32], in_=io["conv_dw_b"][ct * 128:(ct + 1) * 128, :]),
             writes=[("cwb", ct)], dma=True)
    P.op("sp", lambda e: e.dma_start(out=lngB, in_=bcast(io["conv_ln_g"], 256)), writes=["lng"], dma=True)
    P.op("sp", lambda e: e.dma_start(out=lnbB, in_=bcast(io["conv_ln_b"], 256)), writes=["lnb"], dma=True)
    P.op("sp", lambda e: e.dma_start(out=gbrB, in_=bcast(io["branch_g"][:, 512:1024], 512)), writes=["gbrB"], dma=True)
    P.op("sp", lambda e: e.dma_start(out=gbrT, in_=io["branch_gT"]), writes=["gbrT"], dma=True)
    load_w(pw2, io["conv_pw2_w"], 2, "pw2")
    load_w(gluw, io["ssm_glu_w"], 2, "gluw")
    kw_out = load_w(wout, io["w_out"], 8, "wout")
    for j in range(8):
        P.op("pool", lambda e, j=j: e.dma_start(out=yT[(j % 4) * 32:(j % 4) * 32 + 32, j // 4, :], in_=recv_o[j, 64:96, :]),
             writes=[("yT", j)], dma=True)
    load_w(wq, io["xa_wq"], 8, "wq")
    load_w(wo, io["xa_wo"], 8, "wo")
    for ct in range(2):
        for j in range(31):
            eng = "dve" if (j % 2) else "pool"
            P.op(eng, lambda e, ct=ct, j=j: e.tensor_scalar(out=Dg[:, ct, j, :], in0=idF, scalar1=cw[:, ct, j:j + 1],
                                                           scalar2=None, op0=ALU.mult),
                 reads=["idF", "cw"], writes=[("Dg", ct, j)])
    nb = [0]
    for ct in range(2):
        for ch in range(4):
            bk = nb[0] % 2
            nb[0] += 1
            ps = C.banks[bk]
            for j in range(31):
                c0 = 512 * ch + j + 2
                P.op("pe", lambda e, ct=ct, j=j, c0=c0, ps=ps: e.matmul(
                    ps, lhsT=Dg[:, ct, j, :], rhs=hcT[:, ct, c0:c0 + 512], start=(j == 0), stop=(j == 30)),
                    reads=[("Dg", ct, j), "hcT", "hcT_h"], writes=[("bank", bk)])
            P.op("act", lambda e, ct=ct, ch=ch, ps=ps: e.activation(
                out=cT[:, ct, ch * 512:(ch + 1) * 512], in_=ps, func=AF.Identity, bias=cw[:, ct, 31:32]),
                reads=[("bank", bk), ("cwb", ct)], writes=[("cT", ct, ch)])

    def load_o(t):
        par = t % 2
        for k in range(4):
            for hh in range(2):
                P.op("sp", lambda e, k=k, hh=hh, par=par, t=t: e.dma_start(
                    out=oT[par][hh * 64:hh * 64 + 64, k, :], in_=recv_o[2 * k + hh, 0:64, t * 128:(t + 1) * 128]),
                    writes=[("oT", par)], dma=True)

    load_o(0)
    for t in range(NT):
        par = t % 2
        if t + 1 < NT:
            load_o(t + 1)
        tc = slice(t * 128, (t + 1) * 128)
        pc = C.banks[2][:, 0:256]
        for ct in range(2):
            P.op("pe", lambda e, ct=ct, tc=tc: e.transpose(out=pc[:, ct * 128:(ct + 1) * 128], in_=cT[:, ct, tc], identity=idF),
                 reads=[("cT", ct, t // 4), "idF"], writes=[("bank", 2)])
        c_s = stcol()
        P.op("dve", lambda e, c_s=c_s: e.tensor_reduce(out=st[:, c_s:c_s + 1], in_=pc, axis=AX.X, op=ALU.add),
             reads=[("bank", 2)], writes=[("st", c_s)])
        c_m = stcol()
        P.op("dve", lambda e, c_s=c_s, c_m=c_m: e.tensor_scalar(out=st[:, c_m:c_m + 1], in0=st[:, c_s:c_s + 1],
                                                              scalar1=-1.0 / 256, scalar2=None, op0=ALU.mult),
             reads=[("st", c_s)], writes=[("st", c_m)])
        c_v = stcol()
        P.op("act", lambda e, c_m=c_m, c_v=c_v: e.activation(out=junk[:, 0:256], in_=pc, func=AF.Square,
                                                            bias=st[:, c_m:c_m + 1], accum_out=st[:, c_v:c_v + 1]),
             reads=[("bank", 2), ("st", c_m)], writes=["junk", ("st", c_v)])
        c_r = rstd_col(c_v, 1, 1.0 / 256)
        P.op("dve", lambda e, c_m=c_m, c_r=c_r: e.tensor_scalar(out=yn, in0=pc, scalar1=st[:, c_m:c_m + 1],
                                                              scalar2=st[:, c_r:c_r + 1], op0=ALU.add, op1=ALU.mult),
             reads=[("bank", 2), ("st", c_m), ("st", c_r)], writes=["yn"])
        P.op("dve", lambda e: e.tensor_tensor(out=yn, in0=yn, in1=lngB, op=ALU.mult), reads=["yn", "lng"], writes=["yn"])
        P.op("dve", lambda e: e.tensor_tensor(out=yn, in0=yn, in1=lnbB, op=ALU.add), reads=["yn", "lnb"], writes=["yn"])
        P.op("act", lambda e: e.activation(out=ge, in_=yn, func=AF.Exp, scale=-1.0), reads=["yn"], writes=["ge"])
        P.op("dve", lambda e: e.tensor_scalar(out=ge, in0=ge, scalar1=1.0, scalar2=None, op0=ALU.add), reads=["ge"], writes=["ge"])
        P.op("dve", lambda e: e.reciprocal(out=ge, in_=ge), reads=["ge"], writes=["ge"])
        P.op("dve", lambda e: e.tensor_tensor(out=sw, in0=yn, in1=ge, op=ALU.mult), reads=["yn", "ge"], writes=["sw"])
        transposes(sw, 2, swT, 3, ["sw"], ["swT"])
        po = C.banks[2][:, 256:512]
        for ct in range(2):
            P.op("pe", lambda e, ct=ct: e.matmul(po, lhsT=swT[:, ct, :], rhs=pw2[:, ct, :], start=(ct == 0), stop=(ct == 1)),
                 reads=["swT", ("pw2", ct)], writes=[("bank", 2)])
        c_q = stcol()
        P.op("act", lambda e, c_q=c_q: e.activation(out=junk[:, 0:256], in_=po, func=AF.Square, accum_out=st[:, c_q:c_q + 1]),
             reads=[("bank", 2)], writes=["junk", ("st", c_q)])
        c_r2 = rstd_col(c_q, 1, 1.0 / 256)
        P.op("dve", lambda e, c_r2=c_r2: e.scalar_tensor_tensor(out=mixed[:, 0:256], in0=po, scalar=st[:, c_r2:c_r2 + 1],
                                                              in1=gbrB[:, 0:256], op0=ALU.mult, op1=ALU.mult),
             reads=[("bank", 2), ("st", c_r2), "gbrB"], writes=["mixed0"])
        pg = C.banks[3]
        for ct in range(2):
            P.op("pe", lambda e, ct=ct, tc=tc: e.matmul(pg, lhsT=yT[:, ct, tc], rhs=gluw[:, ct, :], start=(ct == 0), stop=(ct == 1)),
                 reads=[("yT", j) for j in range(8)] + [("gluw", ct)], writes=[("bank", 3)])
        P.op("act", lambda e: e.activation(out=ge, in_=pg[:, 256:512], func=AF.Exp, scale=-1.0), reads=[("bank", 3)], writes=["ge"])
        P.op("dve", lambda e: e.tensor_scalar(out=ge, in0=ge, scalar1=1.0, scalar2=None, op0=ALU.add), reads=["ge"], writes=["ge"])
        P.op("dve", lambda e: e.reciprocal(out=ge, in_=ge), reads=["ge"], writes=["ge"])
        P.op("dve", lambda e: e.tensor_tensor(out=osm, in0=pg[:, 0:256], in1=ge, op=ALU.mult), reads=[("bank", 3), "ge"], writes=["osm"])
        c_q3 = stcol()
        P.op("act", lambda e, c_q3=c_q3: e.activation(out=junk[:, 0:256], in_=osm, func=AF.Square, accum_out=st[:, c_q3:c_q3 + 1]),
             reads=["osm"], writes=["junk", ("st", c_q3)])
        c_r3 = rstd_col(c_q3, 1, 1.0 / 256)
        P.op("dve", lambda e, c_r3=c_r3: e.scalar_tensor_tensor(out=mixed[:, 256:512], in0=osm, scalar=st[:, c_r3:c_r3 + 1],
                                                              in1=gbrB[:, 256:512], op0=ALU.mult, op1=ALU.mult),
             reads=["osm", ("st", c_r3), "gbrB"], writes=["mixed1"])
        mT = tT[0]
        transposes(mixed, 4, mT[:, 0:4, :], 3, ["mixed0", "mixed1"], ["mT"])
        P.op("act", lambda e, par=par: e.activation(out=osq, in_=oT[par], func=AF.Square), reads=[("oT", par)], writes=["osq"])
        pss = C.banks[2][:, 0:1]
        for k in range(4):
            P.op("pe", lambda e, k=k: e.matmul(pss, lhsT=osq[:, k, :], rhs=onesF, start=(k == 0), stop=(k == 3)),
                 reads=["osq", "ones"], writes=[("bank", 2)])
        c_q1 = stcol()
        P.op("dve", lambda e, c_q1=c_q1: e.tensor_copy(out=st[:, c_q1:c_q1 + 1], in_=pss), reads=[("bank", 2)], writes=[("st", c_q1)])
        c_r1 = rstd_col(c_q1, 1, 1.0 / 512)
        P.op("dve", lambda e, par=par: e.tensor_tensor(out=og, in0=oT[par], in1=gbrT[:, 0:4].unsqueeze(2).to_broadcast([128, 4, 128]),
                                                      op=ALU.mult), reads=[("oT", par), "gbrT"], writes=["og"])
        for n in range(2):
            p1 = C.banks[4 + n]
            p2 = C.banks[6 + n]
            for k in range(4):
                P.op("pe", lambda e, k=k, n=n, p1=p1: e.matmul(p1, lhsT=og[:, k, :], rhs=wout[:, k, n * 512:(n + 1) * 512],
                                                               start=(k == 0), stop=(k == 3)),
                     reads=["og", ("wout", k)], writes=[("bank", 4 + n)])
            for k in range(4):
                P.op("pe", lambda e, k=k, n=n, p2=p2: e.matmul(p2, lhsT=mT[:, k, :], rhs=wout[:, 4 + k, n * 512:(n + 1) * 512],
                                                               start=(k == 0), stop=(k == 3)),
                     reads=["mT", ("wout", 4 + k)], writes=[("bank", 6 + n)])
            xs = xres[:, t, n * 512:(n + 1) * 512]
            P.op("dve", lambda e, xs=xs, p2=p2: e.tensor_tensor(out=xs, in0=xs, in1=p2, op=ALU.add),
                 reads=[("bank", 6 + n), ("x", t)], writes=[("x", t)])
            P.op("dve", lambda e, xs=xs, p1=p1, c_r1=c_r1: e.scalar_tensor_tensor(
                out=xs, in0=p1, scalar=st[:, c_r1:c_r1 + 1], in1=xs, op0=ALU.mult, op1=ALU.add),
                reads=[("bank", 4 + n), ("st", c_r1), ("x", t)], writes=[("x", t)])

    P.barrier()
    C.release(s12_mark)
    kTp = C.sb([128, 8, 256], BF16, "C_kTp")
    Vm = C.sb([128, 2, D], BF16, "C_V")
    gxaB = C.sb([128, D], F32, "C_gxa")
    gkq = C.sb([128, 4, 256], F32, "C_gkq")
    qb = C.sb([128, D], BF16, "C_qb")
    Pb = C.sb([128, 4, 256], BF16, "C_Pb")
    PT = C.sb([128, 8, 128], BF16, "C_PT")
    ob = C.sb([128, D], BF16, "C_ob")
    tTa2 = [tT[0], C.sb([128, 8, 128], BF16, "C_tTa1")]
    tTb2 = [tT[1], C.sb([128, 8, 128], BF16, "C_tTb1")]
    qb2 = [qb, C.sb([128, D], BF16, "C_qb1")]
    Pb2 = [Pb, C.sb([128, 4, 256], BF16, "C_Pb1")]
    PT2 = [PT, C.sb([128, 8, 128], BF16, "C_PT1")]
    ob2 = [ob, C.sb([128, D], BF16, "C_ob1")]
    junk2 = [junk, C.sb([128, D], BF16, "C_junk1")]
    hb2 = [hb, C.sb([128, D], BF16, "C_hb1")]
    s2_mark = C.mark()
    wk_ = C.sb([128, 8, D], BF16, "C_wk")
    wv_ = C.sb([128, 8, D], BF16, "C_wv")
    gmB = C.sb([128, D], F32, "C_gm")
    memt = C.sb([128, D], F32, "C_memt")
    hmT = C.sb([128, 8, 256], BF16, "C_hmT")
    kf = C.sb([128, D], F32, "C_kf")
    gq4 = C.sb([128, 256], F32, "C_gq4")

    kwk = load_w(wk_, io["xa_wk"], 8, "wk")
    kwv = load_w(wv_, io["xa_wv"], 8, "wv")
    P.op("sp", lambda e: e.dma_start(out=gxaB, in_=bcast(io["norm_xa_g"], D)), writes=["gxa"], dma=True)
    P.op("sp", lambda e: e.dma_start(out=gmB, in_=bcast(io["norm_mem_g"], D)), writes=["gm"], dma=True)
    P.op("sp", lambda e: e.dma_start(out=gkq[:, 0, :], in_=bcast(io["xa_k_g"], 256)), writes=["gkq"], dma=True)
    P.op("sp", lambda e: e.dma_start(out=gq4, in_=bcast(io["xa_q_g"], 256)), writes=["gq4"], dma=True)
    P.op("dve", lambda e: e.scalar_tensor_tensor(out=gkq[:, 0, :], in0=gkq[:, 0, :], scalar=1.0 / 16, in1=gq4,
                                                 op0=ALU.mult, op1=ALU.mult), reads=["gkq", "gq4"], writes=["gkq"])
    for h in range(1, 4):
        P.op("dve", lambda e, h=h: e.tensor_copy(out=gkq[:, h, :], in_=gkq[:, 0, :]), reads=["gkq"], writes=[("gkq", h)])
    gkq_keys = ["gkq"] + [("gkq", h) for h in range(1, 4)]
    for mt in range(2):
        P.op("sp", lambda e, mt=mt: e.dma_start(out=memt, in_=mem[mt * 128:(mt + 1) * 128, :]), writes=["memt"], dma=True)
        norm_T(memt, gmB, "gm", tT[0], 4, ["memt"], ["tT0"])
        P.op("act", lambda e, mt=mt: e.copy(out=hmT[:, :, mt * 128:(mt + 1) * 128], in_=tT[0]), reads=["tT0"], writes=[("hmT", mt)])
        for n in range(2):
            pk = C.banks[n]
            for k in range(8):
                P.op("pe", lambda e, k=k, n=n, pk=pk: e.matmul(pk, lhsT=tT[0][:, k, :], rhs=wk_[:, k, n * 512:(n + 1) * 512],
                                                               start=(k == 0), stop=(k == 7)),
                     reads=["tT0", ("wk", k)], writes=[("bank", n)])
            P.op("act", lambda e, n=n, pk=pk: e.copy(out=kf[:, n * 512:(n + 1) * 512], in_=pk), reads=[("bank", n)], writes=[("kf", n)])
        c_k = stcol(4)
        for h in range(4):
            P.op("act", lambda e, h=h, c_k=c_k: e.activation(out=junk[:, 0:256], in_=kf[:, h * 256:(h + 1) * 256], func=AF.Square,
                                                            accum_out=st[:, c_k + h:c_k + h + 1]),
                 reads=[("kf", h // 2)], writes=["junk", ("st", c_k + h)])
        outc = rstd_col(c_k, 4, 1.0 / 256)
        P.op("dve", lambda e, outc=outc: e.tensor_tensor(
            out=kf.rearrange("p (h d) -> p h d", h=4), in0=kf.rearrange("p (h d) -> p h d", h=4),
            in1=st[:, outc:outc + 4].unsqueeze(2).to_broadcast([128, 4, 256]), op=ALU.mult),
            reads=[("kf", 0), ("kf", 1)] + sk(outc, 4), writes=[("kf", 0), ("kf", 1)])
        P.op("dve", lambda e: e.tensor_tensor(out=hb, in0=kf, in1=gkq.rearrange("p h d -> p (h d)"), op=ALU.mult),
             reads=[("kf", 0), ("kf", 1)] + gkq_keys, writes=["hb"])
        transposes(hb, 8, kTp[:, :, mt * 128:(mt + 1) * 128], 5, ["hb"], [("kTp", mt)])
        for n in range(2):
            pv_ = C.banks[2 + n]
            for k in range(8):
                P.op("pe", lambda e, k=k, n=n, pv_=pv_: e.matmul(pv_, lhsT=tT[0][:, k, :], rhs=wv_[:, k, n * 512:(n + 1) * 512],
                                                                 start=(k == 0), stop=(k == 7)),
                     reads=["tT0", ("wv", k)], writes=[("bank", 2 + n)])
            P.op("act", lambda e, n=n, mt=mt, pv_=pv_: e.copy(out=Vm[:, mt, n * 512:(n + 1) * 512], in_=pv_),
                 reads=[("bank", 2 + n)], writes=[("V", mt)])

    def xa_tile(t):
        par = t % 2
        xt_ = xres[:, t, :]
        tA, tB = tTa2[par], tTb2[par]
        kA, kB = ("tTa", par), ("tTb", par)
        qb, Pb, PT, ob = qb2[par], Pb2[par], PT2[par], ob2[par]
        W0, W1 = 2 * par, 2 * par + 1
        TA, TB = 4 + 2 * par, 5 + 2 * par
        norm_T(xt_, gxaB, "gxa", tB, TA, [("x", t)], [kB], jb=(junk2[par], ("junk", par), hb2[par], ("hb", par)))
        yield
        for n in range(2):
            pq = C.banks[W0 + n]
            for k in range(8):
                P.op("pe", lambda e, k=k, n=n, pq=pq: e.matmul(pq, lhsT=tB[:, k, :], rhs=wq[:, k, n * 512:(n + 1) * 512],
                                                               start=(k == 0), stop=(k == 7)),
                     reads=[kB, ("wq", k)], writes=[("bank", W0 + n)])
        yield
        c_q = stcol(4)
        for h in range(4):
            pqh = C.banks[W0 + h // 2][:, (h % 2) * 256:(h % 2) * 256 + 256]
            P.op("act", lambda e, h=h, c_q=c_q, pqh=pqh: e.activation(out=junk2[par][:, 0:256], in_=pqh, func=AF.Square,
                                                                    accum_out=st[:, c_q + h:c_q + h + 1]),
                 reads=[("bank", W0 + h // 2)], writes=[("junk", par), ("st", c_q + h)])
        rq = rstd_col(c_q, 4, 1.0 / 256)
        for n in range(2):
            P.op("act", lambda e, n=n: e.copy(out=qb[:, n * 512:(n + 1) * 512], in_=C.banks[W0 + n]), reads=[("bank", W0 + n)], writes=[("qb", par, n)])
        yield
        transposes(qb, 8, tA, TB, [("qb", par, 0), ("qb", par, 1)], [kA])
        yield
        for h in range(4):
            psc = C.banks[W0 + h // 2][:, (h % 2) * 256:(h % 2) * 256 + 256]
            for j in range(2):
                P.op("pe", lambda e, h=h, j=j, psc=psc: e.matmul(psc, lhsT=tA[:, 2 * h + j, :], rhs=kTp[:, 2 * h + j, :],
                                                                 start=(j == 0), stop=(j == 1)),
                     reads=[kA, ("kTp", 0), ("kTp", 1)], writes=[("bank", W0 + h // 2)])
        yield
        c_mx = stcol(4)
        for b2 in range(2):
            P.op("dve", lambda e, b2=b2, c_mx=c_mx: e.tensor_reduce(
                out=st[:, c_mx + 2 * b2:c_mx + 2 * b2 + 2], in_=C.banks[W0 + b2].rearrange("p (h m) -> p h m", h=2),
                axis=AX.X, op=ALU.max), reads=[("bank", W0 + b2)], writes=sk(c_mx + 2 * b2, 2))
        c_nb = stcol(4)
        P.op("dve", lambda e, c_mx=c_mx, c_nb=c_nb, rq=rq: e.scalar_tensor_tensor(
            out=st[:, c_nb:c_nb + 4], in0=st[:, c_mx:c_mx + 4], scalar=-1.0, in1=st[:, rq:rq + 4], op0=ALU.mult, op1=ALU.mult),
            reads=sk(c_mx, 4) + sk(rq, 4), writes=sk(c_nb, 4))
        c_rs = stcol(4)
        for h in range(4):
            psc = C.banks[W0 + h // 2][:, (h % 2) * 256:(h % 2) * 256 + 256]
            P.op("act", lambda e, h=h, psc=psc, rq=rq, c_nb=c_nb, c_rs=c_rs: e.activation(
                out=Pb[:, h, :], in_=psc, func=AF.Exp, scale=st[:, rq + h:rq + h + 1], bias=st[:, c_nb + h:c_nb + h + 1],
                accum_out=st[:, c_rs + h:c_rs + h + 1]),
                reads=[("bank", W0 + h // 2), ("st", rq + h), ("st", c_nb + h)], writes=[("Pb", par, h), ("st", c_rs + h)])
        c_ri = stcol(4)
        P.op("dve", lambda e, c_rs=c_rs, c_ri=c_ri: e.reciprocal(out=st[:, c_ri:c_ri + 4], in_=st[:, c_rs:c_rs + 4]),
             reads=sk(c_rs, 4), writes=sk(c_ri, 4))
        yield
        transposes(Pb.rearrange("p h m -> p (h m)"), 8, PT, TA, [("Pb", par, h) for h in range(4)], [("PT", par)])
        yield
        for h in range(4):
            po_ = C.banks[W0 + h // 2][:, (h % 2) * 256:(h % 2) * 256 + 256]
            for mt in range(2):
                P.op("pe", lambda e, h=h, mt=mt, po_=po_: e.matmul(po_, lhsT=PT[:, 2 * h + mt, :], rhs=Vm[:, mt, h * 256:(h + 1) * 256],
                                                                   start=(mt == 0), stop=(mt == 1)),
                     reads=[("PT", par), ("V", mt)], writes=[("bank", W0 + h // 2)])
        for n in range(2):
            P.op("dve", lambda e, n=n, c_ri=c_ri: e.tensor_tensor(
                out=ob[:, n * 512:(n + 1) * 512].rearrange("p (h d) -> p h d", h=2),
                in0=C.banks[W0 + n].rearrange("p (h d) -> p h d", h=2),
                in1=st[:, c_ri + 2 * n:c_ri + 2 * n + 2].unsqueeze(2).to_broadcast([128, 2, 256]), op=ALU.mult),
                reads=[("bank", W0 + n)] + sk(c_ri + 2 * n, 2), writes=[("ob", par, n)])
        yield
        transposes(ob, 8, tB, TB, [("ob", par, 0), ("ob", par, 1)], [kB])
        yield
        for n in range(2):
            pw_ = C.banks[W0 + n]
            for k in range(8):
                P.op("pe", lambda e, k=k, n=n, pw_=pw_: e.matmul(pw_, lhsT=tB[:, k, :], rhs=wo[:, k, n * 512:(n + 1) * 512],
                                                                 start=(k == 0), stop=(k == 7)),
                     reads=[kB, ("wo", k)], writes=[("bank", W0 + n)])
            xs = xres[:, t, n * 512:(n + 1) * 512]
            P.op("dve", lambda e, xs=xs, pw_=pw_: e.tensor_tensor(out=xs, in0=xs, in1=pw_, op=ALU.add),
                 reads=[("bank", W0 + n), ("x", t)], writes=[("x", t)])
        yield

    pending = list(range(NT))
    active = []
    while pending or active:
        while pending and len(active) < 2:
            active.append(xa_tile(pending.pop(0)))
        for g in list(active):
            try:
                next(g)
            except StopIteration:
                active.remove(g)

    P.barrier()
    C.release(base_mark)
    gfB = C.sb([128, D], F32, "C_gf")
    hfT = C.sb([128, 8, 1024], BF16, "C_hfT")
    hid = C.sb([128, 22, 1024], BF16, "C_hid")
    Wo_ = C.sb([128, 22, D], BF16, "C_Wo")
    Wg = [C.sb([128, 8, 128], BF16, "C_Wg%d" % i) for i in range(2)]
    Wu = [C.sb([128, 8, 128], BF16, "C_Wu%d" % i) for i in range(2)]
    sg = [C.sb([128, 512], F32, "C_sg%d" % i) for i in range(2)]
    P.op("sp", lambda e: e.dma_start(out=gfB, in_=bcast(io["norm_ffn_g"], D)), writes=["gf"], dma=True)
    w_in_v = io["ffn_w_in"].rearrange("(k p) n -> p k n", p=128)

    def load_wo(j):
        P.op("pool", lambda e, j=j: e.dma_start(out=Wo_[:, j, :], in_=io["ffn_w_out"][j * 128:(j + 1) * 128, :]),
             writes=[("Wo", j)], dma=True)

    def load_gu(j):
        par = j % 2
        P.op("pool", lambda e, j=j, par=par: e.dma_start(out=Wg[par], in_=w_in_v[:, :, j * 128:(j + 1) * 128]),
             writes=[("Wg", par)], dma=True)
        P.op("pool", lambda e, j=j, par=par: e.dma_start(out=Wu[par], in_=w_in_v[:, :, FFN_H + j * 128:FFN_H + (j + 1) * 128]),
             writes=[("Wu", par)], dma=True)

    nsg = [0]
    for half in range(2):
        for tt in range(8):
            t = half * 8 + tt
            norm_T(xres[:, t, :], gfB, "gf", hfT[:, :, tt * 128:(tt + 1) * 128], 4 + tt % 2, [("x", t)], [("hfT", tt)])
        hkeys = [("hfT", tt) for tt in range(8)]
        load_gu(0)
        for j in range(22):
            par = j % 2
            if j + 1 < 22:
                load_gu(j + 1)
            if half == 0:
                load_wo(j)
            for tc2 in range(2):
                pg_ = C.banks[0 + tc2]
                pu_ = C.banks[2 + tc2]
                for k in range(8):
                    P.op("pe", lambda e, k=k, par=par, tc2=tc2, pg_=pg_: e.matmul(
                        pg_, lhsT=Wg[par][:, k, :], rhs=hfT[:, k, tc2 * 512:(tc2 + 1) * 512], start=(k == 0), stop=(k == 7)),
                        reads=[("Wg", par)] + hkeys[tc2 * 4:tc2 * 4 + 4], writes=[("bank", tc2)])
                for k in range(8):
                    P.op("pe", lambda e, k=k, par=par, tc2=tc2, pu_=pu_: e.matmul(
                        pu_, lhsT=Wu[par][:, k, :], rhs=hfT[:, k, tc2 * 512:(tc2 + 1) * 512], start=(k == 0), stop=(k == 7)),
                        reads=[("Wu", par)] + hkeys[tc2 * 4:tc2 * 4 + 4], writes=[("bank", 2 + tc2)])
                sp_ = nsg[0] % 2
                nsg[0] += 1
                P.op("act", lambda e, sp_=sp_, pg_=pg_: e.activation(out=sg[sp_], in_=pg_, func=AF.Silu),
                     reads=[("bank", tc2)], writes=[("sg", sp_)])
                P.op("dve", lambda e, sp_=sp_, pu_=pu_, j=j, tc2=tc2: e.tensor_tensor(
                    out=hid[:, j, tc2 * 512:(tc2 + 1) * 512], in0=sg[sp_], in1=pu_, op=ALU.mult),
                    reads=[("sg", sp_), ("bank", 2 + tc2)], writes=[("hid", j, tc2)])
        for tt in range(8):
            t = half * 8 + tt
            for n in range(2):
                bk = 4 + (2 * tt + n) % 4
                pd = C.banks[bk]
                for j in range(22):
                    P.op("pe", lambda e, j=j, n=n, tt=tt, pd=pd: e.matmul(
                        pd, lhsT=hid[:, j, tt * 128:(tt + 1) * 128], rhs=Wo_[:, j, n * 512:(n + 1) * 512],
                        start=(j == 0), stop=(j == 21)),
                        reads=[("hid", j, tt // 4), ("Wo", j)], writes=[("bank", bk)])
                xs = xres[:, t, n * 512:(n + 1) * 512]
                P.op("dve", lambda e, xs=xs, pd=pd: e.tensor_tensor(out=xs, in0=xs, in1=pd, op=ALU.add),
                     reads=[("bank", bk), ("x", t)], writes=[("x", t)])
            P.op("sp", lambda e, t=t: e.dma_start(out=x_out[t * 128:(t + 1) * 128, :], in_=xres[:, t, :]),
                 reads=[("x", t)], dma=True)


C_IN = [("x", [TOK, D], F32), ("recv_o", [8, 96, TOK], F32), ("hc_loc", [256, TOK], BF16), ("halo", [256, 32], BF16),
        ("mem", [256, D], F32), ("ident", [128, 128], F32),
        ("conv_dw_wT", [256, 31], F32), ("conv_dw_b", [256, 1], F32), ("conv_ln_g", [1, 256], F32), ("conv_ln_b", [1, 256], F32),
        ("conv_pw2_w", [256, 256], F32), ("ssm_glu_w", [256, 512], F32), ("branch_g", [1, D], F32), ("branch_gT", [128, 8], F32),
        ("w_out", [D, D], F32), ("norm_xa_g", [1, D], F32), ("norm_mem_g", [1, D], F32),
        ("xa_wq", [D, D], F32), ("xa_wk", [D, D], F32), ("xa_wv", [D, D], F32), ("xa_wo", [D, D], F32),
        ("xa_q_g", [1, 256], F32), ("xa_k_g", [1, 256], F32), ("norm_ffn_g", [1, D], F32),
        ("ffn_w_in", [D, 2 * FFN_H], F32), ("ffn_w_out", [FFN_H, D], F32)]


def build_C():
    nc = bass.Bass("TRN2", target_bir_lowering=False)
    es = ExitStack()
    C = Ctx(nc, es)
    io = {n: C.din(n, s, d) for n, s, d in C_IN}
    io["x_out"] = C.dout("x_out", [TOK, D], F32)
    phase_C(C, io)
    C.P.emit(es)
    es.close()
    return nc


_PROGS = {}


def _prog(name, fn):
    if name not in _PROGS:
        _PROGS[name] = fn()
    return _PROGS[name]


def _run(nc, in_maps):
    res = run_bass_kernel_spmd(nc, in_maps, core_ids=list(range(NCORES)))
    return res.results


def _c_weights(w, l):
    f = np.ascontiguousarray
    return dict(
        conv_dw_wT=f(w["conv_dw_w"][l].T), conv_dw_b=f(w["conv_dw_b"][l][:, None]),
        conv_ln_g=f(w["conv_ln_g"][l][None]), conv_ln_b=f(w["conv_ln_b"][l][None]),
        conv_pw2_w=f(w["conv_pw2_w"][l]), ssm_glu_w=f(w["ssm_glu_w"][l]),
        branch_g=f(w["branch_norm_g"][l][None]), branch_gT=f(w["branch_norm_g"][l].reshape(8, 128).T),
        w_out=f(w["w_out"][l]), norm_xa_g=f(w["norm_xa_g"][l][None]), norm_mem_g=f(w["norm_mem_g"][l][None]),
        xa_wq=f(w["xa_wq"][l]), xa_wk=f(w["xa_wk"][l]), xa_wv=f(w["xa_wv"][l]), xa_wo=f(w["xa_wo"][l]),
        xa_q_g=f(w["xa_q_norm_g"][l][None]), xa_k_g=f(w["xa_k_norm_g"][l][None]),
        norm_ffn_g=f(w["norm_ffn_g"][l][None]), ffn_w_in=f(w["ffn_w_in"][l]), ffn_w_out=f(w["ffn_w_out"][l]))


def kernel(**inputs):
    w = {k: np.asarray(v, dtype=np.float32) for k, v in inputs.items()}
    x = w["x"]
    mem = w["mem"]
    bsz, L, _ = x.shape
    assert (bsz, L) == (2, SEQ)
    ident = np.eye(128, dtype=np.float32)
    consts = attn_consts()
    xs = [np.ascontiguousarray(x.reshape(NCORES, TOK, D)[c]) for c in range(NCORES)]
    ncA = _prog("A", build_A)
    ncB = _prog("B", build_B)
    ncC = _prog("C", build_C)
    for l in range(2):
        ra = _run(ncA, [dict(x=xs[c], w_in=w["w_in"][l], g_mix=w["norm_mix_g"][l][None], gq=w["sb_q_norm_g"][l][None],
                             gk=w["sb_k_norm_g"][l][None], ident=ident) for c in range(NCORES)])
        send_bf = [np.asarray(ra[c]["send_bf"]) for c in range(NCORES)]
        send_u = [np.asarray(ra[c]["send_u"]) for c in range(NCORES)]
        hc_loc = [np.asarray(ra[c]["hc_loc"]) for c in range(NCORES)]
        in_b = []
        for j in range(NCORES):
            in_b.append(dict(
                recv_bf=np.ascontiguousarray(np.stack([send_bf[c][j] for c in range(NCORES)])),
                recv_u=np.ascontiguousarray(np.stack([send_u[c][j] for c in range(NCORES)])),
                consts_bf=consts,
                ssm_par=ssm_pack(w["ssm_lam_re"][l], w["ssm_lam_im"][l], w["ssm_log_dt"][l], w["ssm_b_re"][l],
                                 w["ssm_b_im"][l], w["ssm_c_re"][l], w["ssm_c_im"][l], w["ssm_d"][l], j)))
        rb = _run(ncB, in_b)
        send_o = [np.asarray(rb[j]["send_o"]) for j in range(NCORES)]
        cw = _c_weights(w, l)
        in_c = []
        for c in range(NCORES):
            if c % 4 == 0:
                halo = np.zeros((256, 32), dtype=send_bf[0].dtype)
            else:
                halo = np.ascontiguousarray(send_bf[c - 1][0][HALO_OFF:].reshape(256, 32))
            m = dict(x=xs[c], recv_o=np.ascontiguousarray(np.stack([send_o[j][c] for j in range(NCORES)])),
                     hc_loc=hc_loc[c], halo=halo, mem=np.ascontiguousarray(mem[c // 4]), ident=ident)
            m.update(cw)
            in_c.append(m)
        rc = _run(ncC, in_c)
        xs = [np.asarray(rc[c]["x_out"]) for c in range(NCORES)]
    return np.stack(xs).reshape(bsz, L, D).astype(np.float32)
```
